# Optimizing a Trainium2 kernel written in Bass

```python
import math
import jax, jax.numpy as jnp
from jax import lax
import numpy as np

D_MODEL = 1024
BATCH = 4
SEQ = 4096
DEPTH = 2

N_MIXERS = 2
N_SSM_LAYERS = (DEPTH + 1) // 2
N_ATTN_LAYERS = DEPTH // 2
N_SUB = 3
D_FF = 2816
SSM_EXPAND = 2
D_INNER = SSM_EXPAND * D_MODEL
SSM_HEADDIM = 64
SSM_HEADS = D_INNER // SSM_HEADDIM
SSM_GROUPS = 4
SSM_STATE = 128
CONV_WIDTH = 3
CHUNK = 128
CONV_CH = D_INNER + 2 * SSM_GROUPS * SSM_STATE
SSM_IN = D_INNER + CONV_CH + 2 * SSM_HEADS
DA_HEADS = 8
DA_HEAD_DIM = 64
DA_V_DIM = 2 * DA_HEAD_DIM
DA_QK = DA_HEADS * 2 * DA_HEAD_DIM
DA_QKV = 2 * DA_QK + DA_HEADS * DA_V_DIM
Q_BLOCK = 128
ALPHA = (2 * DEPTH) ** 0.25
BETA = (8 * DEPTH) ** -0.25
LN_EPS = 1e-5

kernel_name = "hybrid_ssd_diffattn_macaron_deepnorm_adaln"


def layer_norm(x, g, b):
    xf = x.astype(jnp.float32)
    mu = jnp.mean(xf, axis=-1, keepdims=True)
    var = jnp.mean(jnp.square(xf - mu), axis=-1, keepdims=True)
    y = (xf - mu) * lax.rsqrt(var + LN_EPS) * g.astype(jnp.float32) + b.astype(jnp.float32)
    return y.astype(x.dtype)


def rms_norm(x, g):
    xf = x.astype(jnp.float32)
    y = xf * lax.rsqrt(jnp.mean(jnp.square(xf), axis=-1, keepdims=True) + LN_EPS)
    return (y * g.astype(jnp.float32)).astype(x.dtype)


def modulate(x, m):
    return x * (1 + m[:, 1][:, None, :]) + m[:, 0][:, None, :]


def swiglu(u, w_gate, w_up, w_down):
    return (jax.nn.silu(u @ w_gate) * (u @ w_up)) @ w_down


def depthwise_conv(x, w, bias):
    ch = x.shape[-1]
    y = lax.conv_general_dilated(
        x, w[:, None, :].astype(x.dtype), window_strides=(1,),
        padding=[(CONV_WIDTH // 2, CONV_WIDTH // 2)],
        dimension_numbers=('NWC', 'WIO', 'NWC'), feature_group_count=ch)
    return y + bias.astype(x.dtype)


def segsum(a):
    t = a.shape[-1]
    cs = jnp.cumsum(a, axis=-1)
    seg = cs[..., :, None] - cs[..., None, :]
    mask = jnp.tril(jnp.ones((t, t), dtype=bool))
    return jnp.where(mask, seg, -jnp.inf)


def ssd_chunked(X, dt, A, Bm, Cm):
    b, t, h, p = X.shape
    g, n = Bm.shape[-2:]
    j = h // g
    c = t // CHUNK
    Xc = (X * dt[..., None]).reshape(b, c, CHUNK, g, j, p)
    a = (dt * A).reshape(b, c, CHUNK, g, j).transpose(0, 3, 4, 1, 2)
    Bc = Bm.reshape(b, c, CHUNK, g, n)
    Cc = Cm.reshape(b, c, CHUNK, g, n)
    a_cs = jnp.cumsum(a, axis=-1)
    L = jnp.exp(segsum(a))
    CB = jnp.einsum('bclgn,bcsgn->bgcls', Cc, Bc)
    y_diag = jnp.einsum('bgcls,bgjcls,bcsgjp->bclgjp', CB, L, Xc)
    decay_states = jnp.exp(a_cs[..., -1:] - a_cs)
    states = jnp.einsum('bclgn,bgjcl,bclgjp->bcgjpn', Bc, decay_states, Xc)
    states = jnp.concatenate([jnp.zeros_like(states[:, :1]), states], axis=1)
    chunk_decay = jnp.exp(segsum(jnp.pad(a_cs[..., -1], ((0, 0), (0, 0), (0, 0), (1, 0)))))
    states = jnp.einsum('bgjzc,bcgjpn->bzgjpn', chunk_decay, states)[:, :-1]
    y_off = jnp.einsum('bclgn,bcgjpn,bgjcl->bclgjp', Cc, states, jnp.exp(a_cs))
    return (y_diag + y_off).reshape(b, t, h, p)


def mamba2_bidir(u, w_in, conv_w, conv_b, dt_bias, a_log, d_skip, norm_g, w_out):
    b, t, _ = u.shape
    f32 = jnp.float32
    proj = u @ w_in
    z, xbc, dt_raw = jnp.split(proj, [D_INNER, D_INNER + CONV_CH], axis=-1)
    xbc = jax.nn.silu(depthwise_conv(xbc, conv_w, conv_b))
    xs, Bm, Cm = jnp.split(xbc, [D_INNER, D_INNER + SSM_GROUPS * SSM_STATE], axis=-1)
    X = xs.reshape(b, t, SSM_HEADS, SSM_HEADDIM).astype(f32)
    Bm = Bm.reshape(b, t, SSM_GROUPS, SSM_STATE).astype(f32)
    Cm = Cm.reshape(b, t, SSM_GROUPS, SSM_STATE).astype(f32)
    dt = jax.nn.softplus(dt_raw.astype(f32).reshape(b, t, 2, SSM_HEADS) + dt_bias.astype(f32))
    A = -jnp.exp(a_log.astype(f32))
    flip = lambda v: jnp.flip(v, axis=1)
    y_f = ssd_chunked(X, dt[:, :, 0], A[0], Bm, Cm)
    y_b = flip(ssd_chunked(flip(X), flip(dt[:, :, 1]), A[1], flip(Bm), flip(Cm)))
    y = y_f + y_b + X * d_skip.astype(f32)[:, None]
    y = y.reshape(b, t, D_INNER) * jax.nn.silu(z.astype(f32))
    yg = y.reshape(b, t, SSM_GROUPS, D_INNER // SSM_GROUPS)
    yg = yg * lax.rsqrt(jnp.mean(jnp.square(yg), axis=-1, keepdims=True) + LN_EPS)
    y = yg.reshape(b, t, D_INNER) * norm_g.astype(f32)
    return y.astype(u.dtype) @ w_out


def alibi_slopes(n_heads):
    return jnp.asarray(np.array([2.0 ** (-8.0 * (h + 1) / n_heads) for h in range(n_heads)], dtype=np.float32))


def lambda_init_fn(layer_idx):
    return 0.8 - 0.6 * math.exp(-0.3 * layer_idx)


def diff_attention(u, w_qkv, lam, subln_g, w_out, lambda_init):
    b, t, _ = u.shape
    f32 = jnp.float32
    qkv = u @ w_qkv
    q, k, v = jnp.split(qkv, [DA_QK, 2 * DA_QK], axis=-1)
    q = q.reshape(b, t, DA_HEADS, 2, DA_HEAD_DIM)
    k = k.reshape(b, t, DA_HEADS, 2, DA_HEAD_DIM)
    v = v.reshape(b, t, DA_HEADS, DA_V_DIM)
    lamf = lam.astype(f32)
    lam_full = jnp.exp(jnp.sum(lamf[0] * lamf[1])) - jnp.exp(jnp.sum(lamf[2] * lamf[3])) + lambda_init
    slopes = alibi_slopes(DA_HEADS)
    scale = DA_HEAD_DIM ** -0.5
    n_blk = t // Q_BLOCK
    qb = q.reshape(b, n_blk, Q_BLOCK, DA_HEADS, 2, DA_HEAD_DIM).transpose(1, 0, 2, 3, 4, 5)
    pos_k = jnp.arange(t, dtype=f32)

    def block(args):
        q_blk, i = args
        s = jnp.einsum('bqhmd,bkhmd->bhmqk', q_blk, k).astype(f32) * scale
        pos_q = (i * Q_BLOCK + jnp.arange(Q_BLOCK)).astype(f32)
        dist = jnp.abs(pos_q[:, None] - pos_k[None, :])
        s = s - slopes[None, :, None, None, None] * dist
        p = jax.nn.softmax(s, axis=-1)
        w = p[:, :, 0] - lam_full * p[:, :, 1]
        return jnp.einsum('bhqk,bkhe->bqhe', w.astype(v.dtype), v)

    o = lax.map(block, (qb, jnp.arange(n_blk)))
    o = o.transpose(1, 0, 2, 3, 4).reshape(b, t, DA_HEADS, DA_V_DIM)
    o = rms_norm(o, subln_g) * (1 - lambda_init)
    return o.reshape(b, t, DA_HEADS * DA_V_DIM) @ w_out


def setup_inputs(seed: int = 0) -> dict:
    key = jax.random.key(seed)
    ks = jax.random.split(key, 24)
    f32 = jnp.float32
    nrm = lambda k, s, sc: jax.random.normal(k, s, f32) * sc
    NS, NA, H = N_SSM_LAYERS, N_ATTN_LAYERS, SSM_HEADS
    dt0 = jnp.exp(jax.random.uniform(ks[12], (NS, 2, H), f32) * (math.log(0.1) - math.log(0.001)) + math.log(0.001))
    return {
        "x": nrm(ks[0], (BATCH, SEQ, D_MODEL), 1.0),
        "c": nrm(ks[1], (BATCH, D_MODEL), 1.0),
        "ada_w": nrm(ks[2], (DEPTH, D_MODEL, N_SUB * 3 * D_MODEL), 0.1 * D_MODEL ** -0.5),
        "ada_b": nrm(ks[3], (DEPTH, N_SUB * 3 * D_MODEL), 0.01),
        "ln_g": 1.0 + nrm(ks[4], (DEPTH, N_SUB, D_MODEL), 0.01),
        "ln_b": nrm(ks[5], (DEPTH, N_SUB, D_MODEL), 0.01),
        "ffn_w_gate": nrm(ks[6], (DEPTH, 2, D_MODEL, D_FF), D_MODEL ** -0.5),
        "ffn_w_up": nrm(ks[7], (DEPTH, 2, D_MODEL, D_FF), D_MODEL ** -0.5),
        "ffn_w_down": nrm(ks[8], (DEPTH, 2, D_FF, D_MODEL), BETA * D_FF ** -0.5),
        "ssm_w_in": nrm(ks[9], (NS, D_MODEL, SSM_IN), D_MODEL ** -0.5),
        "ssm_conv_w": nrm(ks[10], (NS, CONV_WIDTH, CONV_CH), CONV_WIDTH ** -0.5),
        "ssm_conv_b": nrm(ks[11], (NS, CONV_CH), 0.01),
        "ssm_dt_bias": dt0 + jnp.log(-jnp.expm1(-dt0)),
        "ssm_a_log": jnp.log(jax.random.uniform(ks[13], (NS, 2, H), f32, 1.0, 16.0)),
        "ssm_d": 1.0 + nrm(ks[14], (NS, H), 0.01),
        "ssm_norm_g": 1.0 + nrm(ks[15], (NS, D_INNER), 0.01),
        "ssm_w_out": nrm(ks[16], (NS, D_INNER, D_MODEL), BETA * D_INNER ** -0.5),
        "attn_w_qkv": nrm(ks[17], (NA, D_MODEL, DA_QKV), D_MODEL ** -0.5),
        "attn_lambda": nrm(ks[18], (NA, 4, DA_HEAD_DIM), 0.1),
        "attn_subln_g": 1.0 + nrm(ks[19], (NA, DA_V_DIM), 0.01),
        "attn_w_out": nrm(ks[20], (NA, DA_HEADS * DA_V_DIM, D_MODEL), BETA * (DA_HEADS * DA_V_DIM) ** -0.5),
    }


def reference(x, c, ada_w, ada_b, ln_g, ln_b, ffn_w_gate, ffn_w_up, ffn_w_down,
              ssm_w_in, ssm_conv_w, ssm_conv_b, ssm_dt_bias, ssm_a_log, ssm_d, ssm_norm_g, ssm_w_out,
              attn_w_qkv, attn_lambda, attn_subln_g, attn_w_out):
    b = x.shape[0]
    cond = jax.nn.silu(c)
    for i in range(DEPTH):
        mods = (cond @ ada_w[i] + ada_b[i]).reshape(b, N_SUB, 3, D_MODEL)
        gate = lambda s: (1 + mods[:, s, 2])[:, None, :]
        y = swiglu(modulate(x, mods[:, 0]), ffn_w_gate[i, 0], ffn_w_up[i, 0], ffn_w_down[i, 0])
        x = layer_norm(ALPHA * x + 0.5 * gate(0) * y, ln_g[i, 0], ln_b[i, 0])
        u = modulate(x, mods[:, 1])
        li = i // N_MIXERS
        if i % N_MIXERS == 0:
            y = mamba2_bidir(u, ssm_w_in[li], ssm_conv_w[li], ssm_conv_b[li], ssm_dt_bias[li],
                             ssm_a_log[li], ssm_d[li], ssm_norm_g[li], ssm_w_out[li])
        else:
            y = diff_attention(u, attn_w_qkv[li], attn_lambda[li], attn_subln_g[li], attn_w_out[li],
                               lambda_init_fn(i))
        x = layer_norm(ALPHA * x + gate(1) * y, ln_g[i, 1], ln_b[i, 1])
        y = swiglu(modulate(x, mods[:, 2]), ffn_w_gate[i, 1], ffn_w_up[i, 1], ffn_w_down[i, 1])
        x = layer_norm(ALPHA * x + 0.5 * gate(2) * y, ln_g[i, 2], ln_b[i, 2])
    return x
```

```python
from contextlib import ExitStack
import numpy as np
import concourse.bass as bass
import concourse.mybir as mybir
from concourse.bass_utils import run_bass_kernel_spmd

DT = mybir.dt
F32 = DT.float32
BF16 = DT.bfloat16
AF = mybir.ActivationFunctionType
ALU = mybir.AluOpType
AX = mybir.AxisListType

ENGS = ("pe", "act", "dve", "pool", "sp")
N_DMA_SEMS = 40


class Op:
    __slots__ = ("eng", "fn", "deps", "signal", "seq", "is_dma", "slot", "slot_val",
                 "prev_slot_user", "idx", "is_nop")


class Sched:
    def __init__(self):
        self.ops = {e: [] for e in ENGS}
        self.last_write = {}
        self.readers = {}
        self.n_dma = 0
        self.slot_last = [None] * N_DMA_SEMS
        self.slot_count = [0] * N_DMA_SEMS
        self.all = []

    def _add(self, eng, fn, reads, writes, is_dma):
        o = Op()
        o.eng, o.fn, o.deps, o.signal, o.seq, o.is_dma = eng, fn, [], False, 0, is_dma
        o.slot = o.slot_val = None
        o.prev_slot_user = None
        o.is_nop = False
        o.idx = len(self.all)
        deps = {}
        for r in reads:
            w = self.last_write.get(r)
            if w is not None:
                deps[id(w)] = w
        for r in writes:
            w = self.last_write.get(r)
            if w is not None:
                deps[id(w)] = w
            for rd in self.readers.get(r, ()):
                if rd is not o:
                    deps[id(rd)] = rd
        for d in deps.values():
            if d.eng == eng and not d.is_dma and not is_dma:
                if eng == "pe":
                    continue
            o.deps.append(d)
            d.signal = True
        for r in reads:
            self.readers.setdefault(r, []).append(o)
        for r in writes:
            self.last_write[r] = o
            self.readers[r] = []
        if is_dma:
            s = self.n_dma % N_DMA_SEMS
            self.n_dma += 1
            o.prev_slot_user = self.slot_last[s]
            self.slot_last[s] = o
            self.slot_count[s] += 1
            o.slot, o.slot_val = s, 16 * self.slot_count[s]
            o.signal = True
        self.ops[eng].append(o)
        self.all.append(o)
        return o

    def op(self, eng, fn, reads=(), writes=()):
        return self._add(eng, fn, reads, writes, False)

    def dma(self, eng, fn, reads=(), writes=()):
        return self._add(eng, fn, reads, writes, True)

    def barrier(self):
        key = ("__barrier__",)
        lasts = []
        for e in ENGS:
            for o in reversed(self.ops[e]):
                if not o.is_dma and not o.is_nop:
                    lasts.append(o)
                    break
        dmas = [o for o in self.slot_last if o is not None]
        for e in ("pe", "act", "dve", "pool", "sp"):
            o = self._add(e, lambda eng: eng.nop(), (), (), False)
            o.is_nop = True
            for d in lasts + dmas:
                if d.eng == e and e == "pe" and not d.is_dma:
                    continue
                o.deps.append(d)
                d.signal = True
        self.last_write = {}
        self.readers = {}

    def emit(self, nc, es):
        eng_sem = {e: es.enter_context(nc.semaphore("prog_" + e)) for e in ENGS}
        dma_sem = [es.enter_context(nc.semaphore("dma%d" % i)) for i in range(N_DMA_SEMS)]
        for e in ENGS:
            n = 0
            for o in self.ops[e]:
                if o.signal and not o.is_dma:
                    n += 1
                    o.seq = n
        block = es.enter_context(nc.Block())
        sched = self

        def body_for(e):
            def body(eng):
                waited_eng = {f: 0 for f in ENGS}
                waited_slot = [0] * N_DMA_SEMS
                for o in sched.ops[e]:
                    need_eng = {}
                    need_slot = {}
                    deps = list(o.deps)
                    if o.is_dma and o.prev_slot_user is not None:
                        deps.append(o.prev_slot_user)
                    for d in deps:
                        if d.is_dma:
                            if d.slot_val > waited_slot[d.slot]:
                                need_slot[d.slot] = max(need_slot.get(d.slot, 0), d.slot_val)
                        else:
                            if d.seq > waited_eng[d.eng]:
                                need_eng[d.eng] = max(need_eng.get(d.eng, 0), d.seq)
                    for f, v in need_eng.items():
                        eng.wait_ge(eng_sem[f], v)
                        waited_eng[f] = v
                    for s, v in need_slot.items():
                        eng.wait_ge(dma_sem[s], v)
                        waited_slot[s] = v
                    ins = o.fn(eng)
                    if o.is_dma:
                        ins.then_inc(dma_sem[o.slot], 16)
                    elif o.signal:
                        ins.then_inc(eng_sem[e], 1)
            return body

        block.tensor(body_for("pe"))
        block.scalar(body_for("act"))
        block.vector(body_for("dve"))
        block.gpsimd(body_for("pool"))
        block.sync(body_for("sp"))


D = 1024
TOK = 2048
NT = TOK // 128
DFF = 2816
NF = DFF // 128
KC = D // 128
ALPHA = (2 * 2) ** 0.25
LN_EPS = 1e-5
FFN_GROUPS = [(0, 4), (4, 4), (8, 4), (12, 4), (16, 4), (20, 2)]


class Ctx:
    pass


_UC = [0]


def _U():
    _UC[0] += 1
    return "u%d_" % _UC[0]


def alloc_globals(nc, es, C):
    C.x = es.enter_context(nc.sbuf_tensor(_U() + "x_res", [128, NT, D], F32))
    C.uT = es.enter_context(nc.sbuf_tensor(_U() + "uT", [128, KC, TOK + 2], BF16))
    C.ident = es.enter_context(nc.sbuf_tensor(_U() + "ident", [128, 128], F32))
    C.identb = es.enter_context(nc.sbuf_tensor(_U() + "identb", [128, 128], BF16))
    C.gh = es.enter_context(nc.sbuf_tensor(_U() + "gh", [128, 3, D], F32))
    C.modcol = es.enter_context(nc.sbuf_tensor(_U() + "modcol", [128, 3, 2, KC], F32))
    C.tri = es.enter_context(nc.sbuf_tensor(_U() + "tri", [128, 5, 128], F32))
    C.stat = es.enter_context(nc.sbuf_tensor(_U() + "stat", [128, 2, 16], F32))
    C.epsc = es.enter_context(nc.sbuf_tensor(_U() + "epsc", [128, 1], F32))
    C.ps = [es.enter_context(nc.psum_tensor("ps%d" % b, [128, 512], F32)) for b in range(8)]


def load_consts(nc, S, C, ident_d):
    S.dma("sp", lambda e: e.dma_start(out=C.tri[:], in_=ident_d), (), [("tri",)])
    S.op("dve", lambda e: e.tensor_copy(out=C.ident[:], in_=C.tri[:, 0, :]), [("tri",)], [("ident",)])
    S.op("dve", lambda e: e.tensor_copy(out=C.identb[:], in_=C.ident[:]), [("ident",)], [("identb",)])
    S.op("dve", lambda e: e.memset(C.epsc[:], LN_EPS), (), [("epsc",)])


def phase_mods(nc, S, C, ccol_d, ada_w_d, ada_b_d):
    with ExitStack() as es:
        ccol = es.enter_context(nc.sbuf_tensor(_U() + "ccol", [128, KC], F32))
        condb = es.enter_context(nc.sbuf_tensor(_U() + "condb", [128, KC], BF16))
        condrep = es.enter_context(nc.sbuf_tensor(_U() + "condrep", [128, KC, 128], BF16))
        wa = [es.enter_context(nc.sbuf_tensor(_U() + "wa%d" % i, [128, KC, 512], BF16)) for i in range(3)]
        brow = [es.enter_context(nc.sbuf_tensor(_U() + "brow%d" % i, [128, 512], F32)) for i in range(3)]
        mrow = [es.enter_context(nc.sbuf_tensor(_U() + "mrow%d" % i, [128, 512], F32)) for i in range(2)]
        S.dma("sp", lambda e: e.dma_start(out=ccol[:], in_=ccol_d), (), [("ccol",)])
        S.op("act", lambda e: e.activation(out=condb[:], in_=ccol[:], func=AF.Silu), [("ccol",)], [("condb",)])
        S.op("dve", lambda e: e.tensor_copy(out=condrep[:], in_=condb[:].unsqueeze(2).to_broadcast([128, KC, 128])),
             [("condb",)], [("condrep",)])
        wv = ada_w_d.rearrange("(c p) n -> p c n", p=128)
        for nb in range(18):
            b3, b2 = nb % 3, nb % 2
            sl = slice(nb * 512, (nb + 1) * 512)
            S.dma("pool", lambda e, b3=b3, sl=sl: e.dma_start(out=wa[b3][:], in_=wv[:, :, sl]), (), [("wa", b3)])
            S.dma("sp", lambda e, b3=b3, sl=sl: e.dma_start(out=brow[b3][:], in_=ada_b_d[0:1, sl].to_broadcast([128, 512])),
                  (), [("brow", b3)])
            pb = nb % 2
            for k in range(KC):
                S.op("pe", lambda e, k=k, b3=b3, pb=pb: e.matmul(C.ps[pb][:], condrep[:, k, :], wa[b3][:, k, :],
                                                                  start=(k == 0), stop=(k == KC - 1)),
                     [("condrep",), ("wa", b3)], [("ps", pb)])
            S.op("dve", lambda e, b2=b2, b3=b3, pb=pb: e.tensor_tensor(out=mrow[b2][:], in0=C.ps[pb][:], in1=brow[b3][:], op=ALU.add),
                 [("ps", pb), ("brow", b3)], [("mrow", b2)])
            sub, kind, half = nb // 6, (nb % 6) // 2, nb % 2
            if kind == 2:
                f = 1.0 if sub == 1 else 0.5
                S.op("dve", lambda e, b2=b2, sub=sub, half=half, f=f: e.tensor_scalar(
                    out=C.gh[:, sub, half * 512:(half + 1) * 512], in0=mrow[b2][:], scalar1=1.0, scalar2=f,
                    op0=ALU.add, op1=ALU.mult), [("mrow", b2)], [("gh", sub, half)])
            else:
                tb = 2 + (nb % 2)
                for j in range(4):
                    S.op("pe", lambda e, j=j, b2=b2, tb=tb: e.transpose(C.ps[tb][:, j * 128:(j + 1) * 128],
                                                                       mrow[b2][:, j * 128:(j + 1) * 128], C.ident[:]),
                         [("mrow", b2), ("ident",)], [("ps", tb)])
                for j in range(4):
                    cidx = half * 4 + j
                    S.op("dve", lambda e, j=j, tb=tb, sub=sub, kind=kind, cidx=cidx: e.tensor_scalar(
                        out=C.modcol[:, sub, kind, cidx:cidx + 1], in0=C.ps[tb][:, j * 128:j * 128 + 1],
                        scalar1=(1.0 if kind == 1 else 0.0), scalar2=None, op0=ALU.add),
                        [("ps", tb)], [("modcol", sub)])
        S.barrier()


def ln_modulate_stage(nc, S, C, sub_next, do_ln, lng_d=None, lnb_d=None, tiles=None):
    es = ExitStack()
    if do_ln:
        C.lng = es.enter_context(nc.sbuf_tensor(_U() + "lng", [128, D], F32))
        C.lnb = es.enter_context(nc.sbuf_tensor(_U() + "lnb", [128, D], F32))
        S.dma("sp", lambda e: e.dma_start(out=C.lng[:], in_=lng_d.to_broadcast([128, D])), (), [("lng",)])
        S.dma("sp", lambda e: e.dma_start(out=C.lnb[:], in_=lnb_d.to_broadcast([128, D])), (), [("lnb",)])
    for t in (tiles if tiles is not None else range(NT)):
        xr = ("x", t)
        if do_ln:
            st = ("stat", t % 2)
            sv = C.stat[:, t % 2, :]
            for h in range(2):
                S.op("dve", lambda e, t=t, h=h, sv=sv: e.bn_stats(out=sv[:, h * 6:(h + 1) * 6], in_=C.x[:, t, h * 512:(h + 1) * 512]),
                     [xr], [st])
            S.op("dve", lambda e, sv=sv: e.bn_aggr(out=sv[:, 12:14], in_=sv[:, 0:12]), [st], [st])
            S.op("act", lambda e, sv=sv: e.activation(out=sv[:, 14:15], in_=sv[:, 13:14], func=AF.Sqrt, bias=C.epsc[:, 0:1], scale=1.0),
                 [st, ("epsc",)], [st])
            S.op("dve", lambda e, sv=sv: e.reciprocal(out=sv[:, 15:16], in_=sv[:, 14:15]), [st], [st])
            S.op("dve", lambda e, t=t, sv=sv: e.tensor_scalar(out=C.x[:, t, :], in0=C.x[:, t, :], scalar1=sv[:, 12:13],
                                                           scalar2=sv[:, 15:16], op0=ALU.subtract, op1=ALU.mult),
                 [xr, st], [xr])
            S.op("pool", lambda e, t=t: e.tensor_tensor(out=C.x[:, t, :], in0=C.x[:, t, :], in1=C.lng[:], op=ALU.mult),
                 [xr, ("lng",)], [xr])
            S.op("pool", lambda e, t=t: e.tensor_tensor(out=C.x[:, t, :], in0=C.x[:, t, :], in1=C.lnb[:], op=ALU.add),
                 [xr, ("lnb",)], [xr])
        if sub_next is not None:
            for half in range(2):
                pb = 2 * (t % 2) + half
                for j in range(4):
                    c = half * 4 + j
                    S.op("pe", lambda e, t=t, c=c, j=j, pb=pb: e.transpose(C.ps[pb][:, j * 128:(j + 1) * 128],
                                                                       C.x[:, t, c * 128:(c + 1) * 128], C.ident[:]),
                         [xr, ("ident",)], [("ps", pb)])
                for j in range(4):
                    c = half * 4 + j
                    S.op("act", lambda e, t=t, c=c, j=j, pb=pb: e.activation(
                        out=C.uT[:, c, 1 + t * 128:1 + (t + 1) * 128], in_=C.ps[pb][:, j * 128:(j + 1) * 128],
                        func=AF.Identity, scale=C.modcol[:, sub_next, 1, c:c + 1], bias=C.modcol[:, sub_next, 0, c:c + 1]),
                        [("ps", pb), ("modcol", sub_next)], [("uT", t)])
    if do_ln:
        S.barrier()
    es.close()


def phase_ffn(nc, S, C, sub, wg_d, wu_d, wd_d):
    with ExitStack() as es:
        wg = [es.enter_context(nc.sbuf_tensor(_U() + "wg%d" % i, [128, KC, 512], BF16)) for i in range(2)]
        wu = [es.enter_context(nc.sbuf_tensor(_U() + "wu%d" % i, [128, KC, 512], BF16)) for i in range(2)]
        wd = [es.enter_context(nc.sbuf_tensor(_U() + "wd%d" % i, [128, 4, D], BF16)) for i in range(2)]
        hT = [es.enter_context(nc.sbuf_tensor(_U() + "hT%d" % i, [128, 4, 512], BF16)) for i in range(2)]
        sg = [es.enter_context(nc.sbuf_tensor(_U() + "sg%d" % i, [128, 512], F32)) for i in range(2)]
        wgv = wg_d.rearrange("(c p) n -> p c n", p=128)
        wuv = wu_d.rearrange("(c p) n -> p c n", p=128)
        wdv = wd_d.rearrange("(c p) n -> p c n", p=128)
        cnt = 0
        ycnt = 0
        for gi, (f0, nf) in enumerate(FFN_GROUPS):
            b = gi % 2
            fs = slice(f0 * 128, (f0 + nf) * 128)
            S.dma("pool", lambda e, b=b, fs=fs, nf=nf: e.dma_start(out=wg[b][:, :, 0:nf * 128], in_=wgv[:, :, fs]), (), [("wg", b)])
            S.dma("pool", lambda e, b=b, fs=fs, nf=nf: e.dma_start(out=wu[b][:, :, 0:nf * 128], in_=wuv[:, :, fs]), (), [("wu", b)])
            S.dma("pool", lambda e, b=b, f0=f0, nf=nf: e.dma_start(out=wd[b][:, 0:nf, :], in_=wdv[:, f0:f0 + nf, :]), (), [("wd", b)])
            for j in range(nf):
                S.op("pool", lambda e, b=b, j=j: e.tensor_tensor(out=wd[b][:, j, :], in0=wd[b][:, j, :], in1=C.gh[:, sub, :], op=ALU.mult),
                     [("wd", b), ("gh", sub, 0), ("gh", sub, 1)], [("wd", b)])
            for tb in range(4):
                hb = tb % 2
                tsl = slice(1 + tb * 512, 1 + (tb + 1) * 512)
                for j in range(nf):
                    pg, pu = (cnt % 2), 2 + (cnt % 2)
                    sgb = cnt % 2
                    cnt += 1
                    ur = [("uT", tb * 4 + q) for q in range(4)]
                    for k in range(KC):
                        S.op("pe", lambda e, k=k, b=b, j=j, pg=pg, tsl=tsl: e.matmul(
                            C.ps[pg][:], wg[b][:, k, j * 128:(j + 1) * 128], C.uT[:, k, tsl], start=(k == 0), stop=(k == KC - 1)),
                            [("wg", b)] + ur, [("ps", pg)])
                    for k in range(KC):
                        S.op("pe", lambda e, k=k, b=b, j=j, pu=pu, tsl=tsl: e.matmul(
                            C.ps[pu][:], wu[b][:, k, j * 128:(j + 1) * 128], C.uT[:, k, tsl], start=(k == 0), stop=(k == KC - 1)),
                            [("wu", b)] + ur, [("ps", pu)])
                    S.op("act", lambda e, pg=pg, sgb=sgb: e.activation(out=sg[sgb][:], in_=C.ps[pg][:], func=AF.Silu),
                         [("ps", pg)], [("sg", sgb)])
                    S.op("dve", lambda e, pu=pu, sgb=sgb, hb=hb, j=j: e.tensor_tensor(
                        out=hT[hb][:, j, :], in0=sg[sgb][:], in1=C.ps[pu][:], op=ALU.mult),
                        [("ps", pu), ("sg", sgb)], [("hT", hb)])
                for tt in range(4):
                    t = tb * 4 + tt
                    for half in range(2):
                        py = 4 + (ycnt % 4)
                        ycnt += 1
                        for j in range(nf):
                            S.op("pe", lambda e, hb=hb, j=j, tt=tt, b=b, half=half, py=py: e.matmul(
                                C.ps[py][:], hT[hb][:, j, tt * 128:(tt + 1) * 128], wd[b][:, j, half * 512:(half + 1) * 512],
                                start=(j == 0), stop=(j == nf - 1)), [("hT", hb), ("wd", b)], [("ps", py)])
                        xs = C.x[:, t, half * 512:(half + 1) * 512]
                        if gi == 0:
                            S.op("dve", lambda e, xs=xs, py=py: e.scalar_tensor_tensor(
                                out=xs, in0=xs, scalar=ALPHA, in1=C.ps[py][:], op0=ALU.mult, op1=ALU.add),
                                [("x", t), ("ps", py)], [("x", t)])
                        else:
                            S.op("dve", lambda e, xs=xs, py=py: e.tensor_tensor(out=xs, in0=xs, in1=C.ps[py][:], op=ALU.add),
                                 [("x", t), ("ps", py)], [("x", t)])
        S.barrier()


NH = 8
SLOPES = [2.0 ** (-8.0 * (h + 1) / NH) for h in range(NH)]
LAMBDA_INIT1 = 0.8 - 0.6 * float(np.exp(-0.3 * 1))
BANDW = 3968
SKIP_T = 105.0


def phase_attn(nc, S, C, uTo_d, wqkv_d, lam_d, subg_d, wout_d):
    scale = 64 ** -0.5
    with ExitStack() as es:
        sb = lambda name, shape, dt: es.enter_context(nc.sbuf_tensor(name, shape, dt))
        uTo = [sb("uTo%d" % i, [128, KC, 512], BF16) for i in range(2)]
        wq = [sb("wqkv%d" % i, [128, KC, 3, 128], BF16) for i in range(2)]
        wo = [sb("wo%d" % i, [128, D], BF16) for i in range(2)]
        qT = sb("qT", [128, TOK], BF16)
        kT = sb("kT", [128, 2 * TOK], BF16)
        v = sb("v_h", [128, 32, 132], BF16)
        band = [sb("band%d" % i, [128, BANDW], BF16) for i in range(2)]
        itmp = [sb("itmp%d" % i, [128, 496], F32) for i in range(2)]
        sq = [sb("sq%d" % i, [128, 512], BF16) for i in range(2)]
        onesb = sb("onesb", [128, 128], BF16)
        nstat = sb("nstat", [128, 2, 16], F32)
        nbias = sb("nbias", [128, 2], F32)
        lamt = sb("lamt", [128, 4, 64], F32)
        lam2 = sb("lam2", [128, 2, 64], F32)
        lams = sb("lams", [128, 4], F32)
        subg = sb("subg", [128, 128], F32)
        PT = [sb("PT%d" % i, [128, 512], BF16) for i in range(3)]
        osum = [sb("osum%d" % i, [128, 128], F32) for i in range(4)]
        otmp = [sb("otmp%d" % i, [128, 128], F32) for i in range(2)]
        rs = sb("rs", [128, 16], F32)
        onb = [sb("onb%d" % i, [128, 128], BF16) for i in range(2)]
        oT = sb("oT", [128, TOK], BF16)

        S.op("dve", lambda e: e.memset(onesb[:], 1.0), (), [("onesb",)])
        S.op("dve", lambda e: e.memset(v[:, :, 128:129], 1.0), (), [("vone",)])
        S.dma("sp", lambda e: e.dma_start(out=lamt[:].rearrange("p a b -> p (a b)"),
                                          in_=lam_d.rearrange("(o a) b -> o (a b)", o=1).to_broadcast([128, 256])), (), [("lamt",)])
        S.dma("sp", lambda e: e.dma_start(out=subg[:], in_=subg_d.to_broadcast([128, 128])), (), [("subg",)])
        S.op("dve", lambda e: e.tensor_tensor(out=lam2[:, 0, :], in0=lamt[:, 0, :], in1=lamt[:, 1, :], op=ALU.mult), [("lamt",)], [("lam2",)])
        S.op("dve", lambda e: e.tensor_tensor(out=lam2[:, 1, :], in0=lamt[:, 2, :], in1=lamt[:, 3, :], op=ALU.mult), [("lamt",)], [("lam2",)])
        S.op("dve", lambda e: e.tensor_reduce(out=lams[:, 0:2], in_=lam2[:], axis=AX.X, op=ALU.add), [("lam2",)], [("lams",)])
        S.op("act", lambda e: e.activation(out=lams[:, 0:2], in_=lams[:, 0:2], func=AF.Exp), [("lams",)], [("lams",)])
        S.op("dve", lambda e: e.tensor_tensor(out=lams[:, 2:3], in0=lams[:, 1:2], in1=lams[:, 0:1], op=ALU.subtract), [("lams",)], [("lams",)])
        S.op("dve", lambda e: e.tensor_scalar(out=lams[:, 2:3], in0=lams[:, 2:3], scalar1=-LAMBDA_INIT1, scalar2=None, op0=ALU.add),
             [("lams",)], [("lams",)])
        S.op("dve", lambda e: e.tensor_scalar(out=subg[:], in0=subg[:], scalar1=(1.0 - LAMBDA_INIT1), scalar2=None, op0=ALU.mult),
             [("subg",)], [("subg",)])
        wv_ = wqkv_d.rearrange("(c p) n -> p c n", p=128)
        pcnt = [0]
        scnt = [0]
        ptc = [0]
        uocnt = [0]

        def proj_bank():
            pcnt[0] += 1
            return 6 + (pcnt[0] % 2)

        for h in range(NH):
            wb = h % 2
            for part in range(3):
                cs = slice(part * 1024 + h * 128, part * 1024 + (h + 1) * 128)
                S.dma("pool", lambda e, wb=wb, part=part, cs=cs: e.dma_start(out=wq[wb][:, :, part, :], in_=wv_[:, :, cs]),
                      (), [("wq", wb)])
            S.dma("pool", lambda e, wb=wb, h=h: e.dma_start(out=wo[wb][:], in_=wout_d[h * 128:(h + 1) * 128, :]), (), [("wo", wb)])
            S.op("pool", lambda e, wb=wb: e.tensor_tensor(out=wo[wb][:], in0=wo[wb][:], in1=C.gh[:, 1, :], op=ALU.mult),
                 [("wo", wb), ("gh", 1, 0), ("gh", 1, 1)], [("wo", wb)])
            def proj_block(part, own, tb, srcbuf, col, dst, dr, blk, wb=wb):
                wqb = wq[wb]
                pb = proj_bank()
                for k in range(KC):
                    if own:
                        src = C.uT[:, k, 1 + tb * 512:1 + (tb + 1) * 512]
                        rr = [("uT", tb * 4 + q) for q in range(4)]
                    else:
                        src = uTo[srcbuf][:, k, :]
                        rr = [("uTo", srcbuf)]
                    S.op("pe", lambda e, k=k, pb=pb, src=src: e.matmul(
                        C.ps[pb][:], wqb[:, k, part, :], src, start=(k == 0), stop=(k == KC - 1)),
                        [("wq", wb)] + rr, [("ps", pb)])
                if blk % 2 == 0:
                    S.op("act", lambda e, pb=pb: e.copy(out=dst, in_=C.ps[pb][:]), [("ps", pb)], [dr])
                else:
                    S.op("dve", lambda e, pb=pb: e.tensor_copy(out=dst, in_=C.ps[pb][:]), [("ps", pb)], [dr])
                sb_ = blk % 2
                S.op("pool", lambda e, sb_=sb_: e.tensor_tensor(out=sq[sb_][:], in0=dst, in1=dst, op=ALU.mult), [dr], [("sq", sb_)])
                for m in range(2):
                    pb2 = proj_bank()
                    ms = slice(m * 64, (m + 1) * 64)
                    S.op("pe", lambda e, ms=ms, sb_=sb_, pb2=pb2: e.matmul(C.ps[pb2][:], onesb[ms, :], sq[sb_][ms, :], start=True, stop=True),
                         [("onesb",), ("sq", sb_)], [("ps", pb2)])
                    S.op("dve", lambda e, m=m, pb2=pb2: e.tensor_reduce(out=nstat[:, m, col:col + 1], in_=C.ps[pb2][:], axis=AX.X, op=ALU.max),
                         [("ps", pb2)], [("nstat",)])

            def v_tile(kt, own, srcbuf, off, wb=wb):
                wqb = wq[wb]
                pb = proj_bank()
                for k in range(KC):
                    if own:
                        src = C.uT[:, k, 1 + kt * 128:1 + (kt + 1) * 128]
                        rr = [("uT", kt)]
                    else:
                        src = uTo[srcbuf][:, k, off * 128:(off + 1) * 128]
                        rr = [("uTo", srcbuf)]
                    S.op("pe", lambda e, k=k, pb=pb, src=src: e.matmul(
                        C.ps[pb][:, 0:128], src, wqb[:, k, 2, :], start=(k == 0), stop=(k == KC - 1)),
                        [("wq", wb)] + rr, [("ps", pb)])
                if kt % 2 == 0:
                    S.op("act", lambda e, pb=pb: e.copy(out=v[:, kt, 0:128], in_=C.ps[pb][:, 0:128]), [("ps", pb)], [("v", kt)])
                else:
                    S.op("dve", lambda e, pb=pb: e.tensor_copy(out=v[:, kt, 0:128], in_=C.ps[pb][:, 0:128]), [("ps", pb)], [("v", kt)])

            for tb in range(4):
                proj_block(0, True, tb, None, tb, qT[:, tb * 512:(tb + 1) * 512], ("qT", tb), tb)
            for tb in range(4):
                proj_block(1, True, tb, None, 4 + tb, kT[:, tb * 512:(tb + 1) * 512], ("kT", tb), 4 + tb)
            for kt in range(16):
                v_tile(kt, True, None, None)
            for tb in range(4):
                ub = uocnt[0] % 2
                uocnt[0] += 1
                S.dma("sp", lambda e, ub=ub, tb=tb: e.dma_start(out=uTo[ub][:], in_=uTo_d[:, :, tb * 512:(tb + 1) * 512]), (), [("uTo", ub)])
                proj_block(1, False, tb, ub, 8 + tb, kT[:, (4 + tb) * 512:(5 + tb) * 512], ("kT", 4 + tb), 8 + tb)
                for off in range(4):
                    v_tile(16 + tb * 4 + off, False, ub, off)
            for m in range(2):
                S.op("dve", lambda e, m=m: e.tensor_reduce(out=nstat[:, m, 12:13], in_=nstat[:, m, 0:4], axis=AX.X, op=ALU.max), [("nstat",)], [("nstat",)])
                S.op("dve", lambda e, m=m: e.tensor_reduce(out=nstat[:, m, 13:14], in_=nstat[:, m, 4:12], axis=AX.X, op=ALU.max), [("nstat",)], [("nstat",)])
                S.op("dve", lambda e, m=m: e.tensor_tensor(out=nstat[:, m, 14:15], in0=nstat[:, m, 12:13], in1=nstat[:, m, 13:14], op=ALU.mult),
                     [("nstat",)], [("nstat",)])
                S.op("act", lambda e, m=m: e.activation(out=nstat[:, m, 15:16], in_=nstat[:, m, 14:15], func=AF.Sqrt, scale=scale * scale),
                     [("nstat",)], [("nstat",)])
                S.op("dve", lambda e, m=m: e.tensor_scalar(out=nbias[:, m:m + 1], in0=nstat[:, m, 15:16], scalar1=-1.0, scalar2=None, op0=ALU.mult),
                     [("nstat",)], [("nbias", m)])
            slope = SLOPES[h]
            for bi in range(2):
                for cchunk in range(8):
                    ib = cchunk % 2
                    c0 = cchunk * 496
                    if bi == 0:
                        S.op("pool", lambda e, ib=ib, c0=c0: e.iota(itmp[ib][:], pattern=[[1, 496]], base=c0 - 1920, channel_multiplier=-1,
                                                                    allow_small_or_imprecise_dtypes=True), (), [("itmp", ib)])
                        S.op("act", lambda e, ib=ib: e.activation(out=itmp[ib][:], in_=itmp[ib][:], func=AF.Abs), [("itmp", ib)], [("itmp", ib)])
                    else:
                        S.op("pool", lambda e, ib=ib, c0=c0: e.iota(itmp[ib][:], pattern=[[-1, 496]], base=4095 - c0, channel_multiplier=-1,
                                                                    allow_small_or_imprecise_dtypes=True), (), [("itmp", ib)])
                    S.op("act", lambda e, ib=ib, bi=bi, c0=c0, slope=slope: e.activation(out=band[bi][:, c0:c0 + 496], in_=itmp[ib][:],
                                                                                 func=AF.Exp, scale=-slope), [("itmp", ib)], [("band", bi)])
            for qb in range(4):
                q0 = qb * 512
                for m in range(2):
                    ms = slice(m * 64, (m + 1) * 64)
                    act_kts = []
                    for kt in range(32):
                        if kt < 16:
                            k0 = kt * 128
                            if k0 > q0 + 511:
                                dmin = k0 - (q0 + 511)
                            elif k0 + 127 < q0:
                                dmin = q0 - (k0 + 127)
                            else:
                                dmin = 0
                        else:
                            k0 = (kt - 16) * 128
                            dmin = 4095 - (q0 + 511) - (k0 + 127)
                        if slope * dmin < SKIP_T:
                            act_kts.append(kt)
                    for ai, kt in enumerate(act_kts):
                        sbk = scnt[0] % 2
                        scnt[0] += 1
                        pt = ptc[0] % 3
                        ptc[0] += 1
                        S.op("pe", lambda e, ms=ms, kt=kt, q0=q0, sbk=sbk: e.matmul(
                            C.ps[sbk][:], kT[ms, kt * 128:(kt + 1) * 128], qT[ms, q0:q0 + 512], start=True, stop=True),
                            [("kT", kt // 4), ("qT", qb)], [("ps", sbk)])
                        S.op("act", lambda e, sbk=sbk, pt=pt, m=m: e.activation(out=PT[pt][:], in_=C.ps[sbk][:], func=AF.Exp,
                                                                              bias=nbias[:, m:m + 1], scale=scale),
                             [("ps", sbk), ("nbias", m)], [("PT", pt)])
                        if kt < 16:
                            st = q0 - kt * 128 + 1920
                            bsl = band[0][:, st:st + 512]
                        else:
                            st = q0 + (kt - 16) * 128
                            bsl = band[1][:, st:st + 512]
                        bi = 0 if kt < 16 else 1
                        eng = "dve" if (ai % 2 == 0) else "pool"
                        S.op(eng, lambda e, pt=pt, bsl=bsl: e.tensor_tensor(out=PT[pt][:], in0=PT[pt][:], in1=bsl, op=ALU.mult),
                             [("PT", pt), ("band", bi)], [("PT", pt)])
                        for qt in range(4):
                            S.op("pe", lambda e, pt=pt, qt=qt, kt=kt, ai=ai, n=len(act_kts): e.matmul(
                                C.ps[2 + qt][:, 0:129], PT[pt][:, qt * 128:(qt + 1) * 128], v[:, kt, 0:129],
                                start=(ai == 0), stop=(ai == n - 1)), [("PT", pt), ("v", kt), ("vone",)], [("ps", 2 + qt)])
                    for qt in range(4):
                        col = m * 4 + qt
                        S.op("dve", lambda e, qt=qt, col=col: e.reciprocal(out=rs[:, col:col + 1], in_=C.ps[2 + qt][:, 128:129]),
                             [("ps", 2 + qt)], [("rs", col)])
                        if m == 0:
                            S.op("dve", lambda e, qt=qt, col=col: e.tensor_scalar(out=osum[qt][:], in0=C.ps[2 + qt][:, 0:128],
                                                                               scalar1=rs[:, col:col + 1], scalar2=None, op0=ALU.mult),
                                 [("ps", 2 + qt), ("rs", col)], [("osum", qt)])
                        else:
                            ob = qt % 2
                            S.op("dve", lambda e, qt=qt, col=col, ob=ob: e.tensor_scalar(out=otmp[ob][:], in0=C.ps[2 + qt][:, 0:128],
                                                                                     scalar1=rs[:, col:col + 1], scalar2=lams[:, 2:3],
                                                                                     op0=ALU.mult, op1=ALU.mult),
                                 [("ps", 2 + qt), ("rs", col), ("lams",)], [("otmp", ob)])
                            S.op("pool", lambda e, qt=qt, ob=ob: e.tensor_tensor(out=osum[qt][:], in0=osum[qt][:], in1=otmp[ob][:], op=ALU.add),
                                 [("osum", qt), ("otmp", ob)], [("osum", qt)])
                for qt in range(4):
                    col = 8 + qt
                    ob = qt % 2
                    S.op("act", lambda e, qt=qt, col=col, ob=ob: e.activation(out=otmp[ob][:], in_=osum[qt][:], func=AF.Square, accum_out=rs[:, col:col + 1]),
                         [("osum", qt)], [("otmp", ob), ("rs", col)])
                    S.op("act", lambda e, col=col: e.activation(out=rs[:, col:col + 1], in_=rs[:, col:col + 1], func=AF.Sqrt,
                                                               bias=C.epsc[:, 0:1], scale=1.0 / 128.0), [("rs", col), ("epsc",)], [("rs", col)])
                    S.op("dve", lambda e, col=col: e.reciprocal(out=rs[:, col:col + 1], in_=rs[:, col:col + 1]), [("rs", col)], [("rs", col)])
                    S.op("dve", lambda e, qt=qt, col=col, ob=ob: e.scalar_tensor_tensor(out=onb[ob][:], in0=osum[qt][:], scalar=rs[:, col:col + 1],
                                                                                    in1=subg[:], op0=ALU.mult, op1=ALU.mult),
                         [("osum", qt), ("rs", col), ("subg",)], [("onb", ob)])
                    pb = proj_bank()
                    pbv = C.ps[pb][:].bitcast(BF16)
                    S.op("pe", lambda e, ob=ob, pbv=pbv: e.transpose(pbv[:, 0:128], onb[ob][:], C.identb[:]),
                         [("onb", ob), ("identb",)], [("ps", pb)])
                    tcol = q0 + qt * 128
                    S.op("act", lambda e, pbv=pbv, tcol=tcol: e.copy(out=oT[:, tcol:tcol + 128], in_=pbv[:, 0:128]),
                         [("ps", pb)], [("oT", tcol // 128)])
            for t in range(NT):
                for half in range(2):
                    pb = proj_bank()
                    S.op("pe", lambda e, t=t, half=half, wb=wb, pb=pb: e.matmul(
                        C.ps[pb][:], oT[:, t * 128:(t + 1) * 128], wo[wb][:, half * 512:(half + 1) * 512], start=True, stop=True),
                        [("oT", t), ("wo", wb)], [("ps", pb)])
                    xs = C.x[:, t, half * 512:(half + 1) * 512]
                    if h == 0:
                        S.op("dve", lambda e, xs=xs, pb=pb: e.scalar_tensor_tensor(out=xs, in0=xs, scalar=ALPHA, in1=C.ps[pb][:],
                                                                                 op0=ALU.mult, op1=ALU.add), [("x", t), ("ps", pb)], [("x", t)])
                    else:
                        S.op("dve", lambda e, xs=xs, pb=pb: e.tensor_tensor(out=xs, in0=xs, in1=C.ps[pb][:], op=ALU.add),
                             [("x", t), ("ps", pb)], [("x", t)])
        S.barrier()


def _op(S, eng, f, reads, writes):
    return S.op(eng, f, reads, writes)


def MM(S, out, lhsT, rhs, start, stop, reads, writes):
    S.op("pe", lambda e: e.matmul(out, lhsT, rhs, start=start, stop=stop), reads, writes)


def TT(S, eng, out, in0, in1, op, reads, writes):
    S.op(eng, lambda e: e.tensor_tensor(out=out, in0=in0, in1=in1, op=op), reads, writes)


def TS(S, eng, out, in0, s1, s2, op0, op1, reads, writes):
    if op1 is None:
        S.op(eng, lambda e: e.tensor_scalar(out=out, in0=in0, scalar1=s1, scalar2=None, op0=op0), reads, writes)
    else:
        S.op(eng, lambda e: e.tensor_scalar(out=out, in0=in0, scalar1=s1, scalar2=s2, op0=op0, op1=op1), reads, writes)


def ACTF(S, out, in_, func, reads, writes, bias=None, scale=None, accum_out=None):
    kw = {}
    if bias is not None:
        kw["bias"] = bias
    if scale is not None:
        kw["scale"] = scale
    if accum_out is not None:
        kw["accum_out"] = accum_out
    S.op("act", lambda e: e.activation(out=out, in_=in_, func=func, **kw), reads, writes)


def TR(S, out, in_, ident, reads, writes):
    S.op("pe", lambda e: e.transpose(out, in_, ident), reads, writes)


def phase_ssm_pass(nc, S, C, p, win_d, wdt_d, dtb_d, alog_d, convw_d, convb_d, dsk_d, normg_d, wout_d,
                   y1_d, sin_d, sout_d, scale_x):
    last = (p == 1)
    IDN, TLE, TGE, TGT, TLT = 0, 1, 2, 3, 4
    BT_, CPB, NBLK = 256, 2, 8
    Tm = C.tri[:, TLE if p == 0 else TGE, :]
    Um = C.tri[:, TGT if p == 0 else TLT, :]
    ones_f = None
    with ExitStack() as es:
        sb = lambda name, shape, dt: es.enter_context(nc.sbuf_tensor(_U() + "s%d_%s" % (p, name), shape, dt))
        xbcT = sb("xbcT", [128, 24, BT_], BF16)
        pre = [sb("pre%d" % i, [128, BT_ + 4], F32) for i in range(2)]
        cva = [sb("cva%d" % i, [128, BT_], F32) for i in range(2)]
        wfc = [sb("wfc%d" % i, [128, KC, 128], BF16) for i in range(3)]
        wdt = sb("wdt", [128, KC, 32], BF16)
        convw = sb("convw", [128, 24, 3], F32)
        convb = sb("convb", [128, 24], F32)
        rows = sb("rows", [128, 4, 32], F32)
        onesf = sb("onesf", [128, 128], F32)
        sm = sb("sm", [128, 8, 32], F32)
        Rb = sb("Rb", [128, 8, 128], F32)
        LT = sb("LT", [128, 8, 128], BF16)
        Gm = sb("Gm", [128, 128], BF16)
        Xtok = sb("Xtok", [128, 512], BF16)
        Btok = sb("Btok", [128, 128], BF16)
        Xt1 = sb("Xt1", [128, 512], BF16)
        Xt2 = sb("Xt2", [128, 512], BF16)
        Sst = sb("Sst", [128, 4, 512], F32)
        Sb = sb("Sb", [128, 4, 512], BF16)
        ych = sb("ych", [128, 2048], F32)
        t1 = sb("t1", [128, 512], F32)
        if last:
            wz = [sb("wz%d" % i, [128, KC, 128], BF16) for i in range(2)]
            sz = sb("sz", [128, CPB, 2048], BF16)
            yT = sb("yT", [128, 16, BT_], BF16)
            ynb = sb("ynb", [128, 2048], BF16)
            wo = [sb("wo%d" % i, [128, D], BF16) for i in range(2)]
            ngc = sb("ngc", [128, 16], F32)
            gs = sb("gs", [128, 16], F32)
        psb = [C.ps[i][:].bitcast(BF16) for i in range(8)]
        wv_ = win_d.rearrange("(c q) n -> q c n", q=128)

        S.dma("sp", lambda e: e.dma_start(out=convw[:], in_=convw_d), (), [("convw",)])
        S.dma("sp", lambda e: e.dma_start(out=convb[:], in_=convb_d), (), [("convb",)])
        S.dma("pool", lambda e: e.dma_start(out=wdt[:], in_=wdt_d[p].rearrange("(c q) n -> q c n", q=128)), (), [("wdt",)])
        S.dma("sp", lambda e: e.dma_start(out=rows[:, 0, :], in_=dtb_d[p:p + 1, :].to_broadcast([128, 32])), (), [("rows", 0)])
        S.dma("sp", lambda e: e.dma_start(out=rows[:, 1, :], in_=alog_d[p:p + 1, :].to_broadcast([128, 32])), (), [("rows", 1)])
        S.dma("sp", lambda e: e.dma_start(out=rows[:, 2, :], in_=dsk_d.to_broadcast([128, 32])), (), [("rows", 2)])
        ACTF(S, rows[:, 1, :], rows[:, 1, :], AF.Exp, [("rows", 1)], [("rows", 1)])
        TS(S, "dve", rows[:, 1, :], rows[:, 1, :], -1.0, None, ALU.mult, None, [("rows", 1)], [("rows", 1)])
        S.op("dve", lambda e: e.memset(onesf[:], 1.0), (), [("onesf",)])
        if scale_x:
            for t in range(NT):
                TS(S, "pool", C.x[:, t, :], C.x[:, t, :], ALPHA, None, ALU.mult, None, [("x", t)], [("x", t)])
        if p == 0:
            S.op("dve", lambda e: e.memset(Sst[:], 0.0), (), [("Sst", g) for g in range(4)])
            S.op("pool", lambda e: e.memset(Sb[:], 0.0), (), [("Sb", g) for g in range(4)])
        else:
            S.dma("sp", lambda e: e.dma_start(out=Sst[:], in_=sin_d), (), [("Sst", g) for g in range(4)])
            for g in range(4):
                S.op("act", lambda e, g=g: e.copy(out=Sb[:, g, :], in_=Sst[:, g, :]), [("Sst", g)], [("Sb", g)])
            S.dma("sp", lambda e: e.dma_start(out=ngc[:], in_=normg_d), (), [("ngc",)])
        blocks = range(NBLK) if p == 0 else range(NBLK - 1, -1, -1)
        wcnt = [0]
        for blk in blocks:
            c0 = blk * BT_
            for fc in range(24):
                wb = wcnt[0] % 3
                pa = wcnt[0] % 2
                wcnt[0] += 1
                cs = slice(2048 + fc * 128, 2048 + (fc + 1) * 128)
                S.dma("pool", lambda e, wb=wb, cs=cs: e.dma_start(out=wfc[wb][:], in_=wv_[:, :, cs]), (), [("wfc", wb)])
                ur = [("uT", min(max(t, 0), NT - 1)) for t in range(blk * CPB - 1, blk * CPB + CPB + 1)] + [("uThalo",)]
                for k in range(KC):
                    MM(S, C.ps[pa][:, 0:BT_], wfc[wb][:, k, :], C.uT[:, k, c0:c0 + BT_], k == 0, k == KC - 1, [("wfc", wb)] + ur, [("ps", pa)])
                for k in range(KC):
                    MM(S, C.ps[2][:, pa * 8:pa * 8 + 2], wfc[wb][:, k, :], C.uT[:, k, c0 + BT_:c0 + BT_ + 2], k == 0, k == KC - 1,
                       [("wfc", wb)] + ur, [("ps", 2)])
                S.op("act", lambda e, pa=pa: e.copy(out=pre[pa][:, 0:BT_], in_=C.ps[pa][:, 0:BT_]), [("ps", pa)], [("pre", pa)])
                S.op("dve", lambda e, pa=pa: e.tensor_copy(out=pre[pa][:, BT_:BT_ + 2], in_=C.ps[2][:, pa * 8:pa * 8 + 2]), [("ps", 2)], [("pre", pa)])
                TS(S, "pool", cva[pa][:], pre[pa][:, 0:BT_], convw[:, fc, 0:1], None, ALU.mult, None, [("pre", pa), ("convw",)], [("cva", pa)])
                S.op("dve", lambda e, pa=pa, fc=fc: e.scalar_tensor_tensor(out=cva[pa][:], in0=pre[pa][:, 1:BT_ + 1], scalar=convw[:, fc, 1:2],
                                                                         in1=cva[pa][:], op0=ALU.mult, op1=ALU.add),
                     [("pre", pa), ("cva", pa), ("convw",)], [("cva", pa)])
                S.op("dve", lambda e, pa=pa, fc=fc: e.scalar_tensor_tensor(out=cva[pa][:], in0=pre[pa][:, 2:BT_ + 2], scalar=convw[:, fc, 2:3],
                                                                         in1=cva[pa][:], op0=ALU.mult, op1=ALU.add),
                     [("pre", pa), ("cva", pa), ("convw",)], [("cva", pa)])
                ACTF(S, xbcT[:, fc, :], cva[pa][:], AF.Silu, [("cva", pa), ("convb",)], [("xbcT", fc)], bias=convb[:, fc:fc + 1])
            if last:
                for zc in range(16):
                    zb = zc % 2
                    cs = slice(zc * 128, (zc + 1) * 128)
                    S.dma("pool", lambda e, zb=zb, cs=cs: e.dma_start(out=wz[zb][:], in_=wv_[:, :, cs]), (), [("wz", zb)])
                    for cc in range(CPB):
                        tcol = 1 + c0 + cc * 128
                        for k in range(KC):
                            MM(S, C.ps[7][:, 0:128], C.uT[:, k, tcol:tcol + 128], wz[zb][:, k, :], k == 0, k == KC - 1,
                               [("wz", zb), ("uT", blk * CPB + cc)], [("ps", 7)])
                        ACTF(S, sz[:, cc, zc * 128:(zc + 1) * 128], C.ps[7][:, 0:128], AF.Silu, [("ps", 7)], [("sz", cc)])
            chunks = range(CPB) if p == 0 else range(CPB - 1, -1, -1)
            for cc in chunks:
                c = blk * CPB + cc
                off = cc * 128
                tcol = 1 + c * 128
                for k in range(KC):
                    MM(S, C.ps[3][:, 0:32], C.uT[:, k, tcol:tcol + 128], wdt[:, k, :], k == 0, k == KC - 1, [("wdt",), ("uT", c)], [("ps", 3)])
                TT(S, "dve", sm[:, 0, :], C.ps[3][:, 0:32], rows[:, 0, :], ALU.add, [("ps", 3), ("rows", 0)], [("sm", 0)])
                ACTF(S, sm[:, 0, :], sm[:, 0, :], AF.Exp, [("sm", 0)], [("sm", 0)])
                ACTF(S, sm[:, 0, :], sm[:, 0, :], AF.Ln, [("sm", 0)], [("sm", 0)], bias=1.0)
                TT(S, "dve", sm[:, 1, :], sm[:, 0, :], rows[:, 1, :], ALU.mult, [("sm", 0), ("rows", 1)], [("sm", 1)])
                MM(S, C.ps[3][:, 32:64], Tm, sm[:, 1, :], True, True, [("tri",), ("sm", 1)], [("ps", 3)])
                MM(S, C.ps[3][:, 64:96], onesf[:], sm[:, 1, :], True, True, [("onesf",), ("sm", 1)], [("ps", 3)])
                S.op("act", lambda e: e.copy(out=sm[:, 2, :], in_=C.ps[3][:, 32:64]), [("ps", 3)], [("sm", 2)])
                ACTF(S, sm[:, 3, :], C.ps[3][:, 32:64], AF.Exp, [("ps", 3)], [("sm", 3)])
                ACTF(S, sm[:, 5, :], C.ps[3][:, 64:96], AF.Exp, [("ps", 3)], [("sm", 5)])
                TT(S, "dve", sm[:, 6, :], C.ps[3][:, 64:96], sm[:, 2, :], ALU.subtract, [("ps", 3), ("sm", 2)], [("sm", 6)])
                ACTF(S, sm[:, 6, :], sm[:, 6, :], AF.Exp, [("sm", 6)], [("sm", 6)])
                TT(S, "dve", sm[:, 4, :], sm[:, 6, :], sm[:, 0, :], ALU.mult, [("sm", 6), ("sm", 0)], [("sm", 4)])
                if last:
                    S.dma("sp", lambda e, c=c: e.dma_start(out=ych[:], in_=y1_d[c]), (), [("ych", g) for g in range(4)])
                for g in range(4):
                    hs = slice(8 * g, 8 * g + 8)
                    BT = xbcT[:, 16 + g, off:off + 128]
                    CT = xbcT[:, 20 + g, off:off + 128]
                    TT(S, "pool", Rb[:], Tm.unsqueeze(1).to_broadcast([128, 8, 128]), sm[:, 1, hs].unsqueeze(2).to_broadcast([128, 8, 128]),
                       ALU.mult, [("tri",), ("sm", 1)], [("Rb",)])
                    for j in range(2):
                        MM(S, C.ps[4 + j][:], Um, Rb[:, 4 * j:4 * j + 4, :].rearrange("q a b -> q (a b)"), True, True, [("tri",), ("Rb",)], [("ps", 4 + j)])
                        ACTF(S, LT[:, 4 * j:4 * j + 4, :].rearrange("q a b -> q (a b)"), C.ps[4 + j][:], AF.Exp, [("ps", 4 + j)], [("LT",)])
                    MM(S, C.ps[6][:, 0:128], BT, CT, True, True, [("xbcT", 16 + g), ("xbcT", 20 + g)], [("ps", 6)])
                    TT(S, "dve", Gm[:], C.ps[6][:, 0:128], Tm, ALU.mult, [("ps", 6), ("tri",)], [("Gm",)])
                    TT(S, "pool", LT[:], LT[:], Gm[:].unsqueeze(1).to_broadcast([128, 8, 128]), ALU.mult, [("LT",), ("Gm",)], [("LT",)])
                    for j in range(4):
                        TR(S, psb[7][:, j * 128:(j + 1) * 128], xbcT[:, 4 * g + j, off:off + 128], C.identb[:], [("xbcT", 4 * g + j), ("identb",)], [("ps", 7)])
                    TR(S, psb[7][:, 512:640], xbcT[:, 16 + g, off:off + 128], C.identb[:], [("xbcT", 16 + g), ("identb",)], [("ps", 7)])
                    S.op("act", lambda e: e.copy(out=Xtok[:], in_=psb[7][:, 0:512]), [("ps", 7)], [("Xtok",)])
                    S.op("act", lambda e: e.copy(out=Btok[:], in_=psb[7][:, 512:640]), [("ps", 7)], [("Btok",)])
                    X3 = Xtok[:].rearrange("q (a b) -> q a b", a=8)
                    TT(S, "dve", Xt1[:].rearrange("q (a b) -> q a b", a=8), X3, sm[:, 0, hs].unsqueeze(2).to_broadcast([128, 8, 64]), ALU.mult,
                       [("Xtok",), ("sm", 0)], [("Xt1",)])
                    TT(S, "pool", Xt2[:].rearrange("q (a b) -> q a b", a=8), X3, sm[:, 4, hs].unsqueeze(2).to_broadcast([128, 8, 64]), ALU.mult,
                       [("Xtok",), ("sm", 4)], [("Xt2",)])
                    for hh in range(8):
                        MM(S, C.ps[0][:, hh * 64:(hh + 1) * 64], LT[:, hh, :], Xt1[:, hh * 64:(hh + 1) * 64], True, True, [("LT",), ("Xt1",)], [("ps", 0)])
                    MM(S, C.ps[1][:], CT, Sb[:, g, :], True, True, [("xbcT", 20 + g), ("Sb", g)], [("ps", 1)])
                    TT(S, "dve", t1[:].rearrange("q (a b) -> q a b", a=8), C.ps[1][:].rearrange("q (a b) -> q a b", a=8),
                       sm[:, 3, hs].unsqueeze(2).to_broadcast([128, 8, 64]), ALU.mult, [("ps", 1), ("sm", 3)], [("t1",)])
                    yg = ych[:, g * 512:(g + 1) * 512]
                    if not last:
                        TT(S, "dve", yg, t1[:], C.ps[0][:], ALU.add, [("t1",), ("ps", 0)], [("ych", g)])
                        TT(S, "pool", t1[:].rearrange("q (a b) -> q a b", a=8), X3, rows[:, 2, hs].unsqueeze(2).to_broadcast([128, 8, 64]), ALU.mult,
                           [("Xtok",), ("rows", 2), ("t1",)], [("t1",)])
                        TT(S, "pool", yg, yg, t1[:], ALU.add, [("t1",), ("ych", g)], [("ych", g)])
                    else:
                        TT(S, "dve", t1[:], t1[:], C.ps[0][:], ALU.add, [("t1",), ("ps", 0)], [("t1",)])
                        TT(S, "pool", yg, yg, t1[:], ALU.add, [("t1",), ("ych", g)], [("ych", g)])
                    MM(S, C.ps[2][:], Btok[:], Xt2[:], True, True, [("Btok",), ("Xt2",)], [("ps", 2)])
                    TT(S, "dve", Sst[:, g, :].rearrange("q (a b) -> q a b", a=8), Sst[:, g, :].rearrange("q (a b) -> q a b", a=8),
                       sm[:, 5, hs].unsqueeze(2).to_broadcast([128, 8, 64]), ALU.mult, [("Sst", g), ("sm", 5)], [("Sst", g)])
                    TT(S, "dve", Sst[:, g, :], Sst[:, g, :], C.ps[2][:], ALU.add, [("Sst", g), ("ps", 2)], [("Sst", g)])
                    S.op("act", lambda e, g=g: e.copy(out=Sb[:, g, :], in_=Sst[:, g, :]), [("Sst", g)], [("Sb", g)])
                if not last:
                    S.dma("sp", lambda e, c=c: e.dma_start(out=y1_d[c], in_=ych[:]), [("ych", g) for g in range(4)], [("y1d", c)])
                else:
                    TT(S, "dve", ych[:], ych[:], sz[:, cc, :], ALU.mult, [("ych", g) for g in range(4)] + [("sz", cc)], [("ych", g) for g in range(4)])
                    for g in range(4):
                        ACTF(S, t1[:], ych[:, g * 512:(g + 1) * 512], AF.Square, [("ych", g)], [("t1",), ("gs", g)], accum_out=gs[:, g:g + 1])
                        ACTF(S, gs[:, g:g + 1], gs[:, g:g + 1], AF.Sqrt, [("gs", g), ("epsc",)], [("gs", g)], bias=C.epsc[:, 0:1], scale=1.0 / 512.0)
                        S.op("dve", lambda e, g=g: e.reciprocal(out=gs[:, g:g + 1], in_=gs[:, g:g + 1]), [("gs", g)], [("gs", g)])
                        TS(S, "dve", ynb[:, g * 512:(g + 1) * 512], ych[:, g * 512:(g + 1) * 512], gs[:, g:g + 1], None, ALU.mult, None,
                           [("ych", g), ("gs", g)], [("ynb", g)])
                    for q4 in range(4):
                        for j in range(4):
                            ch = q4 * 4 + j
                            TR(S, psb[6][:, j * 128:(j + 1) * 128], ynb[:, ch * 128:(ch + 1) * 128], C.identb[:], [("ynb", q4), ("identb",)], [("ps", 6)])
                        S.op("act", lambda e, q4=q4, off=off: e.copy(out=yT[:, 4 * q4:4 * q4 + 4, off:off + 128],
                                                                    in_=psb[6][:, 0:512].rearrange("q (a b) -> q a b", a=4)),
                             [("ps", 6)], [("yT", cc)])
            if last:
                for ch in range(16):
                    wb = ch % 2
                    S.dma("pool", lambda e, wb=wb, ch=ch: e.dma_start(out=wo[wb][:], in_=wout_d[ch * 128:(ch + 1) * 128, :]), (), [("wo", wb)])
                    TS(S, "pool", wo[wb][:], wo[wb][:], ngc[:, ch:ch + 1], None, ALU.mult, None, [("wo", wb), ("ngc",)], [("wo", wb)])
                    TT(S, "pool", wo[wb][:], wo[wb][:], C.gh[:, 1, :], ALU.mult, [("wo", wb), ("gh", 1, 0), ("gh", 1, 1)], [("wo", wb)])
                    for cc in range(CPB):
                        t = blk * CPB + cc
                        for half in range(2):
                            pb = 4 + ((cc * 2 + half) % 2)
                            MM(S, C.ps[pb][:], yT[:, ch, cc * 128:(cc + 1) * 128], wo[wb][:, half * 512:(half + 1) * 512], True, True,
                               [("yT", cc), ("wo", wb)], [("ps", pb)])
                            xs = C.x[:, t, half * 512:(half + 1) * 512]
                            TT(S, "dve", xs, xs, C.ps[pb][:], ALU.add, [("x", t), ("ps", pb)], [("x", t)])
        if not last:
            S.dma("sp", lambda e: e.dma_start(out=sout_d, in_=Sst[:]), [("Sst", g) for g in range(4)], [("soutd",)])
        S.barrier()


def _dram_in(nc, name, shape, dt=F32):
    return nc.dram_tensor("d_" + name, list(shape), dt, kind="ExternalInput").ap()


def _dram_out(nc, name, shape, dt=F32):
    return nc.dram_tensor("d_" + name, list(shape), dt, kind="ExternalOutput").ap()


def build_stage(stage):
    nc = bass.Bass("TRN2", target_bir_lowering=False)
    S = Sched()
    C = Ctx()
    I = lambda name, shape, dt=F32: _dram_in(nc, name, shape, dt)
    x_d = I("x_in", [TOK, D])
    ccol_d = I("ccol", [128, KC])
    cpack_d = I("cpack", [128, 5, 128])
    lay = [0] if stage in (0, 1) else ([0, 1] if stage == 2 else [1])
    ada_w = {i: I("ada_w%d" % i, [D, 9216]) for i in lay}
    ada_b = {i: I("ada_b%d" % i, [1, 9216]) for i in lay}
    lng = {i: I("lng%d" % i, [3, D]) for i in lay}
    lnb = {i: I("lnb%d" % i, [3, D]) for i in lay}
    ffn_keys = {0: [(0, 0)], 1: [], 2: [(0, 1), (1, 0)], 3: [(1, 1)]}[stage]
    ffw = {}
    for (i, j) in ffn_keys:
        ffw[(i, j)] = (I("wg%d%d" % (i, j), [D, DFF]), I("wu%d%d" % (i, j), [D, DFF]), I("wd%d%d" % (i, j), [DFF, D]))
    if stage in (1, 2):
        w_in = I("w_in", [D, 5184])
        wdt = I("wdt", [2, D, 32])
        dtb = I("dtb", [2, 32])
        alog = I("alog", [2, 32])
        convw = I("convw", [128, 24, 3])
        convb = I("convb", [128, 24])
        dsk = I("dsk", [1, 32])
        normg = I("normg", [128, 16])
        ssm_wout = I("ssm_wout", [2048, D])
        xh = I("xh", [128, KC])
    if stage == 1:
        y1 = _dram_out(nc, "y1", [16, 128, 2048])
        s_out = _dram_out(nc, "s_out", [128, 4, 512])
        s_in = None
    if stage == 2:
        y1 = I("y1", [16, 128, 2048])
        s_in = I("s_in", [128, 4, 512])
        s_out = None
        uT_out = _dram_out(nc, "uT_out", [128, KC, TOK], BF16)
    if stage == 3:
        uTo = I("uTo", [128, KC, TOK], BF16)
        wqkv = I("wqkv", [D, 3072])
        lam = I("lam", [4, 64])
        subg = I("subg", [1, 128])
        attn_wout = I("attn_wout", [D, D])
    if stage != 1:
        y_d = _dram_out(nc, "y_out", [TOK, D])

    with ExitStack() as es:
        alloc_globals(nc, es, C)
        load_consts(nc, S, C, cpack_d)
        S.op("dve", lambda e: e.memset(C.uT[:, :, 0:1], 0.0), (), [("uT0",)])
        xv = x_d.rearrange("(t q) d -> q t d", q=128)
        for t in range(NT):
            S.dma("sp", lambda e, t=t: e.dma_start(out=C.x[:, t, :], in_=xv[:, t, :]), (), [("x", t)])

        def ffn(i, j):
            wg, wu, wd = ffw[(i, j)]
            phase_ffn(nc, S, C, 2 * j, wg, wu, wd)

        def halo():
            with ExitStack() as hs:
                xht = hs.enter_context(nc.sbuf_tensor(_U() + "xht", [128, KC], F32))
                S.dma("sp", lambda e: e.dma_start(out=xht[:], in_=xh), (), [("xht",)])
                TT(S, "dve", xht[:], xht[:], C.modcol[:, 1, 1, :], ALU.mult, [("xht",), ("modcol", 1)], [("xht",)])
                TT(S, "dve", C.uT[:, :, TOK + 1:TOK + 2], xht[:].unsqueeze(2), C.modcol[:, 1, 0, :].unsqueeze(2), ALU.add,
                   [("xht",), ("modcol", 1)], [("uThalo",)])
                S.barrier()

        if stage == 0:
            phase_mods(nc, S, C, ccol_d, ada_w[0], ada_b[0])
            ln_modulate_stage(nc, S, C, 0, False)
            S.barrier()
            ffn(0, 0)
            ln_modulate_stage(nc, S, C, None, True, lng[0][0:1, :], lnb[0][0:1, :])
        elif stage == 1:
            phase_mods(nc, S, C, ccol_d, ada_w[0], ada_b[0])
            ln_modulate_stage(nc, S, C, 1, False)
            halo()
            phase_ssm_pass(nc, S, C, 0, w_in, wdt, dtb, alog, convw, convb, dsk, normg, ssm_wout, y1, None, s_out, False)
        elif stage == 2:
            phase_mods(nc, S, C, ccol_d, ada_w[0], ada_b[0])
            ln_modulate_stage(nc, S, C, 1, False)
            halo()
            phase_ssm_pass(nc, S, C, 1, w_in, wdt, dtb, alog, convw, convb, dsk, normg, ssm_wout, y1, s_in, None, True)
            ln_modulate_stage(nc, S, C, 2, True, lng[0][1:2, :], lnb[0][1:2, :])
            ffn(0, 1)
            ln_modulate_stage(nc, S, C, None, True, lng[0][2:3, :], lnb[0][2:3, :])
            phase_mods(nc, S, C, ccol_d, ada_w[1], ada_b[1])
            ln_modulate_stage(nc, S, C, 0, False)
            S.barrier()
            ffn(1, 0)
            ln_modulate_stage(nc, S, C, 1, True, lng[1][0:1, :], lnb[1][0:1, :])
            S.dma("sp", lambda e: e.dma_start(out=uT_out, in_=C.uT[:, :, 1:1 + TOK]), [("uT", t) for t in range(NT)], [("uTout",)])
        else:
            phase_mods(nc, S, C, ccol_d, ada_w[1], ada_b[1])
            ln_modulate_stage(nc, S, C, 1, False)
            S.barrier()
            phase_attn(nc, S, C, uTo, wqkv, lam, subg, attn_wout)
            ln_modulate_stage(nc, S, C, 2, True, lng[1][1:2, :], lnb[1][1:2, :])
            ffn(1, 1)
            ln_modulate_stage(nc, S, C, None, True, lng[1][2:3, :], lnb[1][2:3, :])
        S.barrier()
        if stage != 1:
            yv = y_d.rearrange("(t q) d -> q t d", q=128)
            for t in range(NT):
                S.dma("sp", lambda e, t=t: e.dma_start(out=yv[:, t, :], in_=C.x[:, t, :]), [("x", t)], [("yout", t)])
        S.barrier()
        S.emit(nc, es)
    return nc


def _cpack():
    r = np.arange(128)[:, None]
    c = np.arange(128)[None, :]
    mats = [r == c, r <= c, r >= c, r > c, r < c]
    return np.ascontiguousarray(np.stack([m.astype(np.float32) for m in mats], axis=1))


def _col(v):
    v = np.asarray(v)
    return np.ascontiguousarray(v.reshape(-1, 128).T)


_PROGS = {}


def _prog(stage):
    if stage not in _PROGS:
        _PROGS[stage] = build_stage(stage)
    return _PROGS[stage]


DEBUG_STOP = None


def kernel(x, c, ada_w, ada_b, ln_g, ln_b, ffn_w_gate, ffn_w_up, ffn_w_down,
           ssm_w_in, ssm_conv_w, ssm_conv_b, ssm_dt_bias, ssm_a_log, ssm_d, ssm_norm_g, ssm_w_out,
           attn_w_qkv, attn_lambda, attn_subln_g, attn_w_out):
    f = lambda a: np.ascontiguousarray(np.asarray(a, dtype=np.float32))
    x, c = f(x), f(c)
    ada_w, ada_b, ln_g, ln_b = f(ada_w), f(ada_b), f(ln_g), f(ln_b)
    wgate, wup, wdown = f(ffn_w_gate), f(ffn_w_up), f(ffn_w_down)
    w_in, conv_w, conv_b = f(ssm_w_in)[0], f(ssm_conv_w)[0], f(ssm_conv_b)[0]
    dt_bias, a_log, dsk, norm_g, ssm_wout = f(ssm_dt_bias)[0], f(ssm_a_log)[0], f(ssm_d), f(ssm_norm_g)[0], f(ssm_w_out)[0]
    wqkv, lam, subg, attn_wout = f(attn_w_qkv)[0], f(attn_lambda)[0], f(attn_subln_g), f(attn_w_out)[0]
    cores = list(range(8))
    cpack = _cpack()

    def local(arr, core):
        b, h = core // 2, core % 2
        a = arr[b, h * TOK:(h + 1) * TOK]
        return np.ascontiguousarray(a[::-1] if h else a)

    def common(core, lays):
        b = core // 2
        m = {"d_ccol": _col(c[b]), "d_cpack": cpack}
        for i in lays:
            m["d_ada_w%d" % i] = ada_w[i]
            m["d_ada_b%d" % i] = ada_b[i:i + 1]
            m["d_lng%d" % i] = ln_g[i]
            m["d_lnb%d" % i] = ln_b[i]
        return m

    def ffn_in(m, keys):
        for (i, j) in keys:
            m["d_wg%d%d" % (i, j)] = wgate[i, j]
            m["d_wu%d%d" % (i, j)] = wup[i, j]
            m["d_wd%d%d" % (i, j)] = wdown[i, j]

    def ssm_in(m, core, x1loc):
        h = core % 2
        sets = [0, 1] if h == 0 else [1, 0]
        m["d_w_in"] = w_in
        m["d_wdt"] = np.ascontiguousarray(np.stack([w_in[:, 5120 + 32 * s:5152 + 32 * s] for s in sets]))
        m["d_dtb"] = np.ascontiguousarray(np.stack([dt_bias[s] for s in sets]))
        m["d_alog"] = np.ascontiguousarray(np.stack([a_log[s] for s in sets]))
        cw = conv_w if h == 0 else conv_w[::-1]
        m["d_convw"] = np.ascontiguousarray(cw.reshape(3, 24, 128).transpose(2, 1, 0))
        m["d_convb"] = _col(conv_b)
        m["d_dsk"] = dsk
        m["d_normg"] = _col(norm_g)
        m["d_ssm_wout"] = ssm_wout
        m["d_xh"] = _col(x1loc[core ^ 1][TOK - 1])

    maps = []
    for core in cores:
        m = common(core, [0])
        m["d_x_in"] = local(x, core)
        ffn_in(m, [(0, 0)])
        maps.append(m)
    r0 = run_bass_kernel_spmd(_prog(0), maps, core_ids=cores)
    x1 = [np.asarray(r0.results[k]["d_y_out"]) for k in cores]
    if DEBUG_STOP == 0:
        return x1
    maps = []
    for core in cores:
        m = common(core, [0])
        m["d_x_in"] = x1[core]
        ssm_in(m, core, x1)
        maps.append(m)
    r1 = run_bass_kernel_spmd(_prog(1), maps, core_ids=cores)
    y1 = [np.asarray(r1.results[k]["d_y1"]) for k in cores]
    so = [np.asarray(r1.results[k]["d_s_out"]) for k in cores]
    maps = []
    for core in cores:
        m = common(core, [0, 1])
        m["d_x_in"] = x1[core]
        ssm_in(m, core, x1)
        m["d_y1"] = y1[core]
        m["d_s_in"] = so[core ^ 1]
        ffn_in(m, [(0, 1), (1, 0)])
        maps.append(m)
    r2 = run_bass_kernel_spmd(_prog(2), maps, core_ids=cores)
    x4 = [np.asarray(r2.results[k]["d_y_out"]) for k in cores]
    uT4 = [np.asarray(r2.results[k]["d_uT_out"]) for k in cores]
    if DEBUG_STOP == 2:
        return x4
    maps = []
    for core in cores:
        m = common(core, [1])
        m["d_x_in"] = x4[core]
        m["d_uTo"] = uT4[core ^ 1]
        m["d_wqkv"] = wqkv
        m["d_lam"] = lam
        m["d_subg"] = subg
        m["d_attn_wout"] = attn_wout
        ffn_in(m, [(1, 1)])
        maps.append(m)
    r3 = run_bass_kernel_spmd(_prog(3), maps, core_ids=cores)
    out = np.empty((4, 2 * TOK, D), dtype=np.float32)
    for core in cores:
        b, h = core // 2, core % 2
        y = np.asarray(r3.results[core]["d_y_out"])
        out[b, h * TOK:(h + 1) * TOK] = y[::-1] if h else y
    return out
```

```python
from contextlib import ExitStack
import numpy as np
import concourse.bass as bass
import concourse.mybir as mybir
from concourse.bass_utils import run_bass_kernel_spmd

DT = mybir.dt
F32 = DT.float32
BF16 = DT.bfloat16
AF = mybir.ActivationFunctionType
ALU = mybir.AluOpType
AX = mybir.AxisListType

ENGS = ("pe", "act", "dve", "pool", "sp")
N_DMA_SEMS = 40
MAX_OUTSTANDING_DMA = 10 ** 9


class Op:
    __slots__ = ("eng", "fn", "deps", "signal", "seq", "is_dma", "slot", "slot_val",
                 "prev_slot_user", "idx", "is_nop")


class Sched:
    def __init__(self):
        self.ops = {e: [] for e in ENGS}
        self.last_write = {}
        self.readers = {}
        self.n_dma = 0
        self.slot_last = [None] * N_DMA_SEMS
        self.slot_count = [0] * N_DMA_SEMS
        self.all = []
        self.dma_hist = {}

    def _add(self, eng, fn, reads, writes, is_dma):
        o = Op()
        o.eng, o.fn, o.deps, o.signal, o.seq, o.is_dma = eng, fn, [], False, 0, is_dma
        o.slot = o.slot_val = None
        o.prev_slot_user = None
        o.is_nop = False
        o.idx = len(self.all)
        deps = {}
        for r in reads:
            w = self.last_write.get(r)
            if w is not None:
                deps[id(w)] = w
        for r in writes:
            w = self.last_write.get(r)
            if w is not None:
                deps[id(w)] = w
            for rd in self.readers.get(r, ()):
                if rd is not o:
                    deps[id(rd)] = rd
        for d in deps.values():
            if d.eng == eng and not d.is_dma and not is_dma:
                if eng == "pe":
                    continue
            o.deps.append(d)
            d.signal = True
        for r in reads:
            self.readers.setdefault(r, []).append(o)
        for r in writes:
            self.last_write[r] = o
            self.readers[r] = []
        if is_dma:
            q = self.dma_hist.setdefault(eng, [])
            if len(q) >= MAX_OUTSTANDING_DMA:
                o.deps.append(q[-MAX_OUTSTANDING_DMA])
            q.append(o)
            s = self.n_dma % N_DMA_SEMS
            self.n_dma += 1
            o.prev_slot_user = self.slot_last[s]
            self.slot_last[s] = o
            self.slot_count[s] += 1
            o.slot, o.slot_val = s, 16 * self.slot_count[s]
            o.signal = True
        self.ops[eng].append(o)
        self.all.append(o)
        return o

    def op(self, eng, fn, reads=(), writes=()):
        return self._add(eng, fn, reads, writes, False)

    def dma(self, eng, fn, reads=(), writes=()):
        return self._add(eng, fn, reads, writes, True)

    def barrier(self):
        key = ("__barrier__",)
        lasts = []
        for e in ENGS:
            for o in reversed(self.ops[e]):
                if not o.is_dma and not o.is_nop:
                    lasts.append(o)
                    break
        dmas = [o for o in self.slot_last if o is not None]
        for e in ("pe", "act", "dve", "pool", "sp"):
            o = self._add(e, lambda eng: eng.nop(), (), (), False)
            o.is_nop = True
            for d in lasts + dmas:
                if d.eng == e and e == "pe" and not d.is_dma:
                    continue
                o.deps.append(d)
                d.signal = True
        self.last_write = {}
        self.readers = {}

    def emit(self, nc, es):
        eng_sem = {e: es.enter_context(nc.semaphore("prog_" + e)) for e in ENGS}
        dma_sem = [es.enter_context(nc.semaphore("dma%d" % i)) for i in range(N_DMA_SEMS)]
        for e in ENGS:
            n = 0
            for o in self.ops[e]:
                if o.signal and not o.is_dma:
                    n += 1
                    o.seq = n
        block = es.enter_context(nc.Block())
        sched = self

        def body_for(e):
            def body(eng):
                waited_eng = {f: 0 for f in ENGS}
                waited_slot = [0] * N_DMA_SEMS
                for o in sched.ops[e]:
                    need_eng = {}
                    need_slot = {}
                    deps = list(o.deps)
                    if o.is_dma and o.prev_slot_user is not None:
                        deps.append(o.prev_slot_user)
                    for d in deps:
                        if d.is_dma:
                            if d.slot_val > waited_slot[d.slot]:
                                need_slot[d.slot] = max(need_slot.get(d.slot, 0), d.slot_val)
                        else:
                            if d.seq > waited_eng[d.eng]:
                                need_eng[d.eng] = max(need_eng.get(d.eng, 0), d.seq)
                    for f, v in need_eng.items():
                        eng.wait_ge(eng_sem[f], v)
                        waited_eng[f] = v
                    for s, v in need_slot.items():
                        eng.wait_ge(dma_sem[s], v)
                        waited_slot[s] = v
                    ins = o.fn(eng)
                    if o.is_dma:
                        ins.then_inc(dma_sem[o.slot], 16)
                    elif o.signal:
                        ins.then_inc(eng_sem[e], 1)
            return body

        block.tensor(body_for("pe"))
        block.scalar(body_for("act"))
        block.vector(body_for("dve"))
        block.gpsimd(body_for("pool"))
        block.sync(body_for("sp"))


D = 1024
TOK = 2048
NT = TOK // 128
DFF = 2816
NF = DFF // 128
KC = D // 128
ALPHA = (2 * 2) ** 0.25
LN_EPS = 1e-5
FFN_GROUPS = [(0, 4), (4, 4), (8, 4), (12, 4), (16, 4), (20, 2)]


class Ctx:
    pass


_UC = [0]


def _U():
    _UC[0] += 1
    return "u%d_" % _UC[0]


def alloc_globals(nc, es, C):
    C.x = es.enter_context(nc.sbuf_tensor(_U() + "x_res", [128, NT, D], F32))
    C.uT = es.enter_context(nc.sbuf_tensor(_U() + "uT", [128, KC, TOK + 2], BF16))
    C.ident = es.enter_context(nc.sbuf_tensor(_U() + "ident", [128, 128], F32))
    C.identb = es.enter_context(nc.sbuf_tensor(_U() + "identb", [128, 128], BF16))
    C.gh = es.enter_context(nc.sbuf_tensor(_U() + "gh", [128, 3, D], F32))
    C.modcol = es.enter_context(nc.sbuf_tensor(_U() + "modcol", [128, 3, 2, KC], F32))
    C.tri = es.enter_context(nc.sbuf_tensor(_U() + "tri", [128, 5, 128], F32))
    C.stat = es.enter_context(nc.sbuf_tensor(_U() + "stat", [128, 2, 16], F32))
    C.epsc = es.enter_context(nc.sbuf_tensor(_U() + "epsc", [128, 1], F32))
    C.ps = [es.enter_context(nc.psum_tensor("ps%d" % b, [128, 512], F32)) for b in range(8)]


def load_consts(nc, S, C, ident_d):
    S.dma("sp", lambda e: e.dma_start(out=C.tri[:], in_=ident_d), (), [("tri",)])
    S.op("dve", lambda e: e.tensor_copy(out=C.ident[:], in_=C.tri[:, 0, :]), [("tri",)], [("ident",)])
    S.op("dve", lambda e: e.tensor_copy(out=C.identb[:], in_=C.ident[:]), [("ident",)], [("identb",)])
    S.op("dve", lambda e: e.memset(C.epsc[:], LN_EPS), (), [("epsc",)])


def phase_mods(nc, S, C, ccol_d, ada_w_d, ada_b_d):
    with ExitStack() as es:
        ccol = es.enter_context(nc.sbuf_tensor(_U() + "ccol", [128, KC], F32))
        condb = es.enter_context(nc.sbuf_tensor(_U() + "condb", [128, KC], BF16))
        condrep = es.enter_context(nc.sbuf_tensor(_U() + "condrep", [128, KC, 128], BF16))
        wa = [es.enter_context(nc.sbuf_tensor(_U() + "wa%d" % i, [128, KC, 512], BF16)) for i in range(3)]
        brow = [es.enter_context(nc.sbuf_tensor(_U() + "brow%d" % i, [128, 512], F32)) for i in range(3)]
        mrow = [es.enter_context(nc.sbuf_tensor(_U() + "mrow%d" % i, [128, 512], F32)) for i in range(2)]
        S.dma("sp", lambda e: e.dma_start(out=ccol[:], in_=ccol_d), (), [("ccol",)])
        S.op("act", lambda e: e.activation(out=condb[:], in_=ccol[:], func=AF.Silu), [("ccol",)], [("condb",)])
        S.op("dve", lambda e: e.tensor_copy(out=condrep[:], in_=condb[:].unsqueeze(2).to_broadcast([128, KC, 128])),
             [("condb",)], [("condrep",)])
        wv = ada_w_d.rearrange("(c p) n -> p c n", p=128)
        for nb in range(18):
            b3, b2 = nb % 3, nb % 2
            sl = slice(nb * 512, (nb + 1) * 512)
            S.dma("pool", lambda e, b3=b3, sl=sl: e.dma_start(out=wa[b3][:], in_=wv[:, :, sl]), (), [("wa", b3)])
            S.dma("sp", lambda e, b3=b3, sl=sl: e.dma_start(out=brow[b3][:], in_=ada_b_d[0:1, sl].to_broadcast([128, 512])),
                  (), [("brow", b3)])
            pb = nb % 2
            for k in range(KC):
                S.op("pe", lambda e, k=k, b3=b3, pb=pb: e.matmul(C.ps[pb][:], condrep[:, k, :], wa[b3][:, k, :],
                                                                  start=(k == 0), stop=(k == KC - 1)),
                     [("condrep",), ("wa", b3)], [("ps", pb)])
            S.op("dve", lambda e, b2=b2, b3=b3, pb=pb: e.tensor_tensor(out=mrow[b2][:], in0=C.ps[pb][:], in1=brow[b3][:], op=ALU.add),
                 [("ps", pb), ("brow", b3)], [("mrow", b2)])
            sub, kind, half = nb // 6, (nb % 6) // 2, nb % 2
            if kind == 2:
                f = 1.0 if sub == 1 else 0.5
                S.op("dve", lambda e, b2=b2, sub=sub, half=half, f=f: e.tensor_scalar(
                    out=C.gh[:, sub, half * 512:(half + 1) * 512], in0=mrow[b2][:], scalar1=1.0, scalar2=f,
                    op0=ALU.add, op1=ALU.mult), [("mrow", b2)], [("gh", sub, half)])
            else:
                tb = 2 + (nb % 2)
                for j in range(4):
                    S.op("pe", lambda e, j=j, b2=b2, tb=tb: e.transpose(C.ps[tb][:, j * 128:(j + 1) * 128],
                                                                       mrow[b2][:, j * 128:(j + 1) * 128], C.ident[:]),
                         [("mrow", b2), ("ident",)], [("ps", tb)])
                for j in range(4):
                    cidx = half * 4 + j
                    S.op("dve", lambda e, j=j, tb=tb, sub=sub, kind=kind, cidx=cidx: e.tensor_scalar(
                        out=C.modcol[:, sub, kind, cidx:cidx + 1], in0=C.ps[tb][:, j * 128:j * 128 + 1],
                        scalar1=(1.0 if kind == 1 else 0.0), scalar2=None, op0=ALU.add),
                        [("ps", tb)], [("modcol", sub)])
        S.barrier()


def ln_modulate_stage(nc, S, C, sub_next, do_ln, lng_d=None, lnb_d=None, tiles=None):
    es = ExitStack()
    if do_ln:
        C.lng = es.enter_context(nc.sbuf_tensor(_U() + "lng", [128, D], F32))
        C.lnb = es.enter_context(nc.sbuf_tensor(_U() + "lnb", [128, D], F32))
        S.dma("sp", lambda e: e.dma_start(out=C.lng[:], in_=lng_d.to_broadcast([128, D])), (), [("lng",)])
        S.dma("sp", lambda e: e.dma_start(out=C.lnb[:], in_=lnb_d.to_broadcast([128, D])), (), [("lnb",)])
    for t in (tiles if tiles is not None else range(NT)):
        xr = ("x", t)
        if do_ln:
            st = ("stat", t % 2)
            sv = C.stat[:, t % 2, :]
            for h in range(2):
                S.op("dve", lambda e, t=t, h=h, sv=sv: e.bn_stats(out=sv[:, h * 6:(h + 1) * 6], in_=C.x[:, t, h * 512:(h + 1) * 512]),
                     [xr], [st])
            S.op("dve", lambda e, sv=sv: e.bn_aggr(out=sv[:, 12:14], in_=sv[:, 0:12]), [st], [st])
            S.op("act", lambda e, sv=sv: e.activation(out=sv[:, 14:15], in_=sv[:, 13:14], func=AF.Sqrt, bias=C.epsc[:, 0:1], scale=1.0),
                 [st, ("epsc",)], [st])
            S.op("dve", lambda e, sv=sv: e.reciprocal(out=sv[:, 15:16], in_=sv[:, 14:15]), [st], [st])
            S.op("dve", lambda e, t=t, sv=sv: e.tensor_scalar(out=C.x[:, t, :], in0=C.x[:, t, :], scalar1=sv[:, 12:13],
                                                           scalar2=sv[:, 15:16], op0=ALU.subtract, op1=ALU.mult),
                 [xr, st], [xr])
            S.op("pool", lambda e, t=t: e.tensor_tensor(out=C.x[:, t, :], in0=C.x[:, t, :], in1=C.lng[:], op=ALU.mult),
                 [xr, ("lng",)], [xr])
            S.op("pool", lambda e, t=t: e.tensor_tensor(out=C.x[:, t, :], in0=C.x[:, t, :], in1=C.lnb[:], op=ALU.add),
                 [xr, ("lnb",)], [xr])
        if sub_next is not None:
            for half in range(2):
                pb = 2 * (t % 2) + half
                for j in range(4):
                    c = half * 4 + j
                    S.op("pe", lambda e, t=t, c=c, j=j, pb=pb: e.transpose(C.ps[pb][:, j * 128:(j + 1) * 128],
                                                                       C.x[:, t, c * 128:(c + 1) * 128], C.ident[:]),
                         [xr, ("ident",)], [("ps", pb)])
                for j in range(4):
                    c = half * 4 + j
                    S.op("act", lambda e, t=t, c=c, j=j, pb=pb: e.activation(
                        out=C.uT[:, c, 1 + t * 128:1 + (t + 1) * 128], in_=C.ps[pb][:, j * 128:(j + 1) * 128],
                        func=AF.Identity, scale=C.modcol[:, sub_next, 1, c:c + 1], bias=C.modcol[:, sub_next, 0, c:c + 1]),
                        [("ps", pb), ("modcol", sub_next)], [("uT", t)])
    if do_ln:
        S.barrier()
    es.close()


def phase_ffn(nc, S, C, sub, wg_d, wu_d, wd_d):
    with ExitStack() as es:
        wg = [es.enter_context(nc.sbuf_tensor(_U() + "wg%d" % i, [128, KC, 512], BF16)) for i in range(2)]
        wu = [es.enter_context(nc.sbuf_tensor(_U() + "wu%d" % i, [128, KC, 512], BF16)) for i in range(2)]
        wd = [es.enter_context(nc.sbuf_tensor(_U() + "wd%d" % i, [128, 4, D], BF16)) for i in range(2)]
        hT = [es.enter_context(nc.sbuf_tensor(_U() + "hT%d" % i, [128, 4, 512], BF16)) for i in range(2)]
        sg = [es.enter_context(nc.sbuf_tensor(_U() + "sg%d" % i, [128, 512], F32)) for i in range(2)]
        wgv = wg_d.rearrange("(c p) n -> p c n", p=128)
        wuv = wu_d.rearrange("(c p) n -> p c n", p=128)
        wdv = wd_d.rearrange("(c p) n -> p c n", p=128)
        cnt = 0
        ycnt = 0
        for gi, (f0, nf) in enumerate(FFN_GROUPS):
            b = gi % 2
            fs = slice(f0 * 128, (f0 + nf) * 128)
            S.dma("pool", lambda e, b=b, fs=fs, nf=nf: e.dma_start(out=wg[b][:, :, 0:nf * 128], in_=wgv[:, :, fs]), (), [("wg", b)])
            S.dma("pool", lambda e, b=b, fs=fs, nf=nf: e.dma_start(out=wu[b][:, :, 0:nf * 128], in_=wuv[:, :, fs]), (), [("wu", b)])
            S.dma("pool", lambda e, b=b, f0=f0, nf=nf: e.dma_start(out=wd[b][:, 0:nf, :], in_=wdv[:, f0:f0 + nf, :]), (), [("wd", b)])
            for j in range(nf):
                S.op("pool", lambda e, b=b, j=j: e.tensor_tensor(out=wd[b][:, j, :], in0=wd[b][:, j, :], in1=C.gh[:, sub, :], op=ALU.mult),
                     [("wd", b), ("gh", sub, 0), ("gh", sub, 1)], [("wd", b)])
            for tb in range(4):
                hb = tb % 2
                tsl = slice(1 + tb * 512, 1 + (tb + 1) * 512)
                for j in range(nf):
                    pg, pu = (cnt % 2), 2 + (cnt % 2)
                    sgb = cnt % 2
                    cnt += 1
                    ur = [("uT", tb * 4 + q) for q in range(4)]
                    for k in range(KC):
                        S.op("pe", lambda e, k=k, b=b, j=j, pg=pg, tsl=tsl: e.matmul(
                            C.ps[pg][:], wg[b][:, k, j * 128:(j + 1) * 128], C.uT[:, k, tsl], start=(k == 0), stop=(k == KC - 1)),
                            [("wg", b)] + ur, [("ps", pg)])
                    for k in range(KC):
                        S.op("pe", lambda e, k=k, b=b, j=j, pu=pu, tsl=tsl: e.matmul(
                            C.ps[pu][:], wu[b][:, k, j * 128:(j + 1) * 128], C.uT[:, k, tsl], start=(k == 0), stop=(k == KC - 1)),
                            [("wu", b)] + ur, [("ps", pu)])
                    S.op("act", lambda e, pg=pg, sgb=sgb: e.activation(out=sg[sgb][:], in_=C.ps[pg][:], func=AF.Silu),
                         [("ps", pg)], [("sg", sgb)])
                    S.op("dve", lambda e, pu=pu, sgb=sgb, hb=hb, j=j: e.tensor_tensor(
                        out=hT[hb][:, j, :], in0=sg[sgb][:], in1=C.ps[pu][:], op=ALU.mult),
                        [("ps", pu), ("sg", sgb)], [("hT", hb)])
                for tt in range(4):
                    t = tb * 4 + tt
                    for half in range(2):
                        py = 4 + (ycnt % 4)
                        ycnt += 1
                        for j in range(nf):
                            S.op("pe", lambda e, hb=hb, j=j, tt=tt, b=b, half=half, py=py: e.matmul(
                                C.ps[py][:], hT[hb][:, j, tt * 128:(tt + 1) * 128], wd[b][:, j, half * 512:(half + 1) * 512],
                                start=(j == 0), stop=(j == nf - 1)), [("hT", hb), ("wd", b)], [("ps", py)])
                        xs = C.x[:, t, half * 512:(half + 1) * 512]
                        if gi == 0:
                            S.op("dve", lambda e, xs=xs, py=py: e.scalar_tensor_tensor(
                                out=xs, in0=xs, scalar=ALPHA, in1=C.ps[py][:], op0=ALU.mult, op1=ALU.add),
                                [("x", t), ("ps", py)], [("x", t)])
                        else:
                            S.op("dve", lambda e, xs=xs, py=py: e.tensor_tensor(out=xs, in0=xs, in1=C.ps[py][:], op=ALU.add),
                                 [("x", t), ("ps", py)], [("x", t)])
        S.barrier()


NH = 8
SLOPES = [2.0 ** (-8.0 * (h + 1) / NH) for h in range(NH)]
LAMBDA_INIT1 = 0.8 - 0.6 * float(np.exp(-0.3 * 1))
BANDW = 3968
SKIP_T = 105.0


def phase_attn(nc, S, C, uTo_d, wqkv_d, lam_d, subg_d, wout_d):
    scale = 64 ** -0.5
    with ExitStack() as es:
        sb = lambda name, shape, dt: es.enter_context(nc.sbuf_tensor(name, shape, dt))
        uTo = [sb("uTo%d" % i, [128, KC, 512], BF16) for i in range(2)]
        wq = [sb("wqkv%d" % i, [128, KC, 3, 128], BF16) for i in range(2)]
        wo = [sb("wo%d" % i, [128, D], BF16) for i in range(2)]
        qT = sb("qT", [128, TOK], BF16)
        kT = sb("kT", [128, 2 * TOK], BF16)
        v = sb("v_h", [128, 32, 132], BF16)
        band = [sb("band%d" % i, [128, BANDW], BF16) for i in range(2)]
        itmp = [sb("itmp%d" % i, [128, 496], F32) for i in range(2)]
        sq = [sb("sq%d" % i, [128, 512], BF16) for i in range(2)]
        onesb = sb("onesb", [128, 128], BF16)
        nstat = sb("nstat", [128, 2, 16], F32)
        nbias = sb("nbias", [128, 2], F32)
        lamt = sb("lamt", [128, 4, 64], F32)
        lam2 = sb("lam2", [128, 2, 64], F32)
        lams = sb("lams", [128, 4], F32)
        subg = sb("subg", [128, 128], F32)
        PT = [sb("PT%d" % i, [128, 512], BF16) for i in range(4)]
        osum = [sb("osum%d" % i, [128, 128], F32) for i in range(4)]
        otmp = [sb("otmp%d" % i, [128, 128], F32) for i in range(2)]
        rs = sb("rs", [128, 16], F32)
        onb = [sb("onb%d" % i, [128, 128], BF16) for i in range(2)]
        oT = sb("oT", [128, TOK], BF16)

        S.op("dve", lambda e: e.memset(onesb[:], 1.0), (), [("onesb",)])
        S.op("dve", lambda e: e.memset(v[:, :, 128:129], 1.0), (), [("vone",)])
        S.dma("sp", lambda e: e.dma_start(out=lamt[:].rearrange("p a b -> p (a b)"),
                                          in_=lam_d.rearrange("(o a) b -> o (a b)", o=1).to_broadcast([128, 256])), (), [("lamt",)])
        S.dma("sp", lambda e: e.dma_start(out=subg[:], in_=subg_d.to_broadcast([128, 128])), (), [("subg",)])
        S.op("dve", lambda e: e.tensor_tensor(out=lam2[:, 0, :], in0=lamt[:, 0, :], in1=lamt[:, 1, :], op=ALU.mult), [("lamt",)], [("lam2",)])
        S.op("dve", lambda e: e.tensor_tensor(out=lam2[:, 1, :], in0=lamt[:, 2, :], in1=lamt[:, 3, :], op=ALU.mult), [("lamt",)], [("lam2",)])
        S.op("dve", lambda e: e.tensor_reduce(out=lams[:, 0:2], in_=lam2[:], axis=AX.X, op=ALU.add), [("lam2",)], [("lams",)])
        S.op("act", lambda e: e.activation(out=lams[:, 0:2], in_=lams[:, 0:2], func=AF.Exp), [("lams",)], [("lams",)])
        S.op("dve", lambda e: e.tensor_tensor(out=lams[:, 2:3], in0=lams[:, 1:2], in1=lams[:, 0:1], op=ALU.subtract), [("lams",)], [("lams",)])
        S.op("dve", lambda e: e.tensor_scalar(out=lams[:, 2:3], in0=lams[:, 2:3], scalar1=-LAMBDA_INIT1, scalar2=None, op0=ALU.add),
             [("lams",)], [("lams",)])
        S.op("dve", lambda e: e.tensor_scalar(out=subg[:], in0=subg[:], scalar1=(1.0 - LAMBDA_INIT1), scalar2=None, op0=ALU.mult),
             [("subg",)], [("subg",)])
        wv_ = wqkv_d.rearrange("(c p) n -> p c n", p=128)
        pcnt = [0]
        scnt = [0]
        ptc = [0]
        uocnt = [0]

        def proj_bank():
            pcnt[0] += 1
            return 6 + (pcnt[0] % 2)

        for h in range(NH):
            wb = h % 2
            for part in range(3):
                cs = slice(part * 1024 + h * 128, part * 1024 + (h + 1) * 128)
                S.dma("pool", lambda e, wb=wb, part=part, cs=cs: e.dma_start(out=wq[wb][:, :, part, :], in_=wv_[:, :, cs]),
                      (), [("wq", wb)])
            S.dma("pool", lambda e, wb=wb, h=h: e.dma_start(out=wo[wb][:], in_=wout_d[h * 128:(h + 1) * 128, :]), (), [("wo", wb)])
            S.op("pool", lambda e, wb=wb: e.tensor_tensor(out=wo[wb][:], in0=wo[wb][:], in1=C.gh[:, 1, :], op=ALU.mult),
                 [("wo", wb), ("gh", 1, 0), ("gh", 1, 1)], [("wo", wb)])
            def proj_block(part, own, tb, srcbuf, col, dst, dr, blk, wb=wb):
                wqb = wq[wb]
                pb = proj_bank()
                for k in range(KC):
                    if own:
                        src = C.uT[:, k, 1 + tb * 512:1 + (tb + 1) * 512]
                        rr = [("uT", tb * 4 + q) for q in range(4)]
                    else:
                        src = uTo[srcbuf][:, k, :]
                        rr = [("uTo", srcbuf)]
                    S.op("pe", lambda e, k=k, pb=pb, src=src: e.matmul(
                        C.ps[pb][:], wqb[:, k, part, :], src, start=(k == 0), stop=(k == KC - 1)),
                        [("wq", wb)] + rr, [("ps", pb)])
                if blk % 2 == 0:
                    S.op("act", lambda e, pb=pb: e.copy(out=dst, in_=C.ps[pb][:]), [("ps", pb)], [dr])
                else:
                    S.op("dve", lambda e, pb=pb: e.tensor_copy(out=dst, in_=C.ps[pb][:]), [("ps", pb)], [dr])
                sb_ = blk % 2
                S.op("pool", lambda e, sb_=sb_: e.tensor_tensor(out=sq[sb_][:], in0=dst, in1=dst, op=ALU.mult), [dr], [("sq", sb_)])
                for m in range(2):
                    pb2 = proj_bank()
                    ms = slice(m * 64, (m + 1) * 64)
                    S.op("pe", lambda e, ms=ms, sb_=sb_, pb2=pb2: e.matmul(C.ps[pb2][:], onesb[ms, :], sq[sb_][ms, :], start=True, stop=True),
                         [("onesb",), ("sq", sb_)], [("ps", pb2)])
                    S.op("dve", lambda e, m=m, pb2=pb2: e.tensor_reduce(out=nstat[:, m, col:col + 1], in_=C.ps[pb2][:], axis=AX.X, op=ALU.max),
                         [("ps", pb2)], [("nstat",)])

            def v_tile(kt, own, srcbuf, off, wb=wb):
                wqb = wq[wb]
                pb = proj_bank()
                for k in range(KC):
                    if own:
                        src = C.uT[:, k, 1 + kt * 128:1 + (kt + 1) * 128]
                        rr = [("uT", kt)]
                    else:
                        src = uTo[srcbuf][:, k, off * 128:(off + 1) * 128]
                        rr = [("uTo", srcbuf)]
                    S.op("pe", lambda e, k=k, pb=pb, src=src: e.matmul(
                        C.ps[pb][:, 0:128], src, wqb[:, k, 2, :], start=(k == 0), stop=(k == KC - 1)),
                        [("wq", wb)] + rr, [("ps", pb)])
                if kt % 2 == 0:
                    S.op("act", lambda e, pb=pb: e.copy(out=v[:, kt, 0:128], in_=C.ps[pb][:, 0:128]), [("ps", pb)], [("v", kt)])
                else:
                    S.op("dve", lambda e, pb=pb: e.tensor_copy(out=v[:, kt, 0:128], in_=C.ps[pb][:, 0:128]), [("ps", pb)], [("v", kt)])

            for tb in range(4):
                proj_block(0, True, tb, None, tb, qT[:, tb * 512:(tb + 1) * 512], ("qT", tb), tb)
            for tb in range(4):
                proj_block(1, True, tb, None, 4 + tb, kT[:, tb * 512:(tb + 1) * 512], ("kT", tb), 4 + tb)
            for kt in range(16):
                v_tile(kt, True, None, None)
            for tb in range(4):
                ub = uocnt[0] % 2
                uocnt[0] += 1
                S.dma("sp", lambda e, ub=ub, tb=tb: e.dma_start(out=uTo[ub][:], in_=uTo_d[:, :, tb * 512:(tb + 1) * 512]), (), [("uTo", ub)])
                proj_block(1, False, tb, ub, 8 + tb, kT[:, (4 + tb) * 512:(5 + tb) * 512], ("kT", 4 + tb), 8 + tb)
                for off in range(4):
                    v_tile(16 + tb * 4 + off, False, ub, off)
            for m in range(2):
                S.op("dve", lambda e, m=m: e.tensor_reduce(out=nstat[:, m, 12:13], in_=nstat[:, m, 0:4], axis=AX.X, op=ALU.max), [("nstat",)], [("nstat",)])
                S.op("dve", lambda e, m=m: e.tensor_reduce(out=nstat[:, m, 13:14], in_=nstat[:, m, 4:12], axis=AX.X, op=ALU.max), [("nstat",)], [("nstat",)])
                S.op("dve", lambda e, m=m: e.tensor_tensor(out=nstat[:, m, 14:15], in0=nstat[:, m, 12:13], in1=nstat[:, m, 13:14], op=ALU.mult),
                     [("nstat",)], [("nstat",)])
                S.op("act", lambda e, m=m: e.activation(out=nstat[:, m, 15:16], in_=nstat[:, m, 14:15], func=AF.Sqrt, scale=scale * scale),
                     [("nstat",)], [("nstat",)])
                S.op("dve", lambda e, m=m: e.tensor_scalar(out=nbias[:, m:m + 1], in0=nstat[:, m, 15:16], scalar1=-1.0, scalar2=None, op0=ALU.mult),
                     [("nstat",)], [("nbias", m)])
            slope = SLOPES[h]
            for bi in range(2):
                for cchunk in range(8):
                    ib = cchunk % 2
                    c0 = cchunk * 496
                    if bi == 0:
                        S.op("pool", lambda e, ib=ib, c0=c0: e.iota(itmp[ib][:], pattern=[[1, 496]], base=c0 - 1920, channel_multiplier=-1,
                                                                    allow_small_or_imprecise_dtypes=True), (), [("itmp", ib)])
                        S.op("act", lambda e, ib=ib: e.activation(out=itmp[ib][:], in_=itmp[ib][:], func=AF.Abs), [("itmp", ib)], [("itmp", ib)])
                    else:
                        S.op("pool", lambda e, ib=ib, c0=c0: e.iota(itmp[ib][:], pattern=[[-1, 496]], base=4095 - c0, channel_multiplier=-1,
                                                                    allow_small_or_imprecise_dtypes=True), (), [("itmp", ib)])
                    S.op("act", lambda e, ib=ib, bi=bi, c0=c0, slope=slope: e.activation(out=band[bi][:, c0:c0 + 496], in_=itmp[ib][:],
                                                                                 func=AF.Exp, scale=-slope), [("itmp", ib)], [("band", bi)])
            for qb in range(4):
                q0 = qb * 512
                for m in range(2):
                    ms = slice(m * 64, (m + 1) * 64)
                    act_kts = []
                    for kt in range(32):
                        if kt < 16:
                            k0 = kt * 128
                            if k0 > q0 + 511:
                                dmin = k0 - (q0 + 511)
                            elif k0 + 127 < q0:
                                dmin = q0 - (k0 + 127)
                            else:
                                dmin = 0
                        else:
                            k0 = (kt - 16) * 128
                            dmin = 4095 - (q0 + 511) - (k0 + 127)
                        if slope * dmin < SKIP_T:
                            act_kts.append(kt)
                    LAG = 2
                    nact = len(act_kts)
                    pts = {}

                    def emit_pv(ai):
                        kt = act_kts[ai]
                        pt = pts[ai]
                        for qt in range(4):
                            S.op("pe", lambda e, pt=pt, qt=qt, kt=kt, ai=ai: e.matmul(
                                C.ps[2 + qt][:, 0:129], PT[pt][:, qt * 128:(qt + 1) * 128], v[:, kt, 0:129],
                                start=(ai == 0), stop=(ai == nact - 1)), [("PT", pt), ("v", kt), ("vone",)], [("ps", 2 + qt)])

                    for ai, kt in enumerate(act_kts):
                        sbk = scnt[0] % 2
                        scnt[0] += 1
                        pt = ptc[0] % 4
                        ptc[0] += 1
                        pts[ai] = pt
                        S.op("pe", lambda e, ms=ms, kt=kt, q0=q0, sbk=sbk: e.matmul(
                            C.ps[sbk][:], kT[ms, kt * 128:(kt + 1) * 128], qT[ms, q0:q0 + 512], start=True, stop=True),
                            [("kT", kt // 4), ("qT", qb)], [("ps", sbk)])
                        S.op("act", lambda e, sbk=sbk, pt=pt, m=m: e.activation(out=PT[pt][:], in_=C.ps[sbk][:], func=AF.Exp,
                                                                              bias=nbias[:, m:m + 1], scale=scale),
                             [("ps", sbk), ("nbias", m)], [("PT", pt)])
                        if kt < 16:
                            st = q0 - kt * 128 + 1920
                            bsl = band[0][:, st:st + 512]
                        else:
                            st = q0 + (kt - 16) * 128
                            bsl = band[1][:, st:st + 512]
                        bi = 0 if kt < 16 else 1
                        S.op("dve", lambda e, pt=pt, bsl=bsl: e.tensor_tensor(out=PT[pt][:], in0=PT[pt][:], in1=bsl, op=ALU.mult),
                             [("PT", pt), ("band", bi)], [("PT", pt)])
                        if ai >= LAG:
                            emit_pv(ai - LAG)
                    for ai in range(max(0, nact - LAG), nact):
                        emit_pv(ai)
                    for qt in range(4):
                        col = m * 4 + qt
                        S.op("dve", lambda e, qt=qt, col=col: e.reciprocal(out=rs[:, col:col + 1], in_=C.ps[2 + qt][:, 128:129]),
                             [("ps", 2 + qt)], [("rs", col)])
                        if m == 0:
                            S.op("dve", lambda e, qt=qt, col=col: e.tensor_scalar(out=osum[qt][:], in0=C.ps[2 + qt][:, 0:128],
                                                                               scalar1=rs[:, col:col + 1], scalar2=None, op0=ALU.mult),
                                 [("ps", 2 + qt), ("rs", col)], [("osum", qt)])
                        else:
                            ob = qt % 2
                            S.op("dve", lambda e, qt=qt, col=col, ob=ob: e.tensor_scalar(out=otmp[ob][:], in0=C.ps[2 + qt][:, 0:128],
                                                                                     scalar1=rs[:, col:col + 1], scalar2=lams[:, 2:3],
                                                                                     op0=ALU.mult, op1=ALU.mult),
                                 [("ps", 2 + qt), ("rs", col), ("lams",)], [("otmp", ob)])
                            S.op("pool", lambda e, qt=qt, ob=ob: e.tensor_tensor(out=osum[qt][:], in0=osum[qt][:], in1=otmp[ob][:], op=ALU.add),
                                 [("osum", qt), ("otmp", ob)], [("osum", qt)])
                for qt in range(4):
                    col = 8 + qt
                    ob = qt % 2
                    S.op("act", lambda e, qt=qt, col=col, ob=ob: e.activation(out=otmp[ob][:], in_=osum[qt][:], func=AF.Square, accum_out=rs[:, col:col + 1]),
                         [("osum", qt)], [("otmp", ob), ("rs", col)])
                    S.op("act", lambda e, col=col: e.activation(out=rs[:, col:col + 1], in_=rs[:, col:col + 1], func=AF.Sqrt,
                                                               bias=C.epsc[:, 0:1], scale=1.0 / 128.0), [("rs", col), ("epsc",)], [("rs", col)])
                    S.op("dve", lambda e, col=col: e.reciprocal(out=rs[:, col:col + 1], in_=rs[:, col:col + 1]), [("rs", col)], [("rs", col)])
                    S.op("dve", lambda e, qt=qt, col=col, ob=ob: e.scalar_tensor_tensor(out=onb[ob][:], in0=osum[qt][:], scalar=rs[:, col:col + 1],
                                                                                    in1=subg[:], op0=ALU.mult, op1=ALU.mult),
                         [("osum", qt), ("rs", col), ("subg",)], [("onb", ob)])
                    pb = proj_bank()
                    pbv = C.ps[pb][:].bitcast(BF16)
                    S.op("pe", lambda e, ob=ob, pbv=pbv: e.transpose(pbv[:, 0:128], onb[ob][:], C.identb[:]),
                         [("onb", ob), ("identb",)], [("ps", pb)])
                    tcol = q0 + qt * 128
                    S.op("act", lambda e, pbv=pbv, tcol=tcol: e.copy(out=oT[:, tcol:tcol + 128], in_=pbv[:, 0:128]),
                         [("ps", pb)], [("oT", tcol // 128)])
            for t in range(NT):
                for half in range(2):
                    pb = proj_bank()
                    S.op("pe", lambda e, t=t, half=half, wb=wb, pb=pb: e.matmul(
                        C.ps[pb][:], oT[:, t * 128:(t + 1) * 128], wo[wb][:, half * 512:(half + 1) * 512], start=True, stop=True),
                        [("oT", t), ("wo", wb)], [("ps", pb)])
                    xs = C.x[:, t, half * 512:(half + 1) * 512]
                    if h == 0:
                        S.op("dve", lambda e, xs=xs, pb=pb: e.scalar_tensor_tensor(out=xs, in0=xs, scalar=ALPHA, in1=C.ps[pb][:],
                                                                                 op0=ALU.mult, op1=ALU.add), [("x", t), ("ps", pb)], [("x", t)])
                    else:
                        S.op("dve", lambda e, xs=xs, pb=pb: e.tensor_tensor(out=xs, in0=xs, in1=C.ps[pb][:], op=ALU.add),
                             [("x", t), ("ps", pb)], [("x", t)])
        S.barrier()


def _op(S, eng, f, reads, writes):
    return S.op(eng, f, reads, writes)


def MM(S, out, lhsT, rhs, start, stop, reads, writes):
    S.op("pe", lambda e: e.matmul(out, lhsT, rhs, start=start, stop=stop), reads, writes)


def TT(S, eng, out, in0, in1, op, reads, writes):
    S.op(eng, lambda e: e.tensor_tensor(out=out, in0=in0, in1=in1, op=op), reads, writes)


def TS(S, eng, out, in0, s1, s2, op0, op1, reads, writes):
    if op1 is None:
        S.op(eng, lambda e: e.tensor_scalar(out=out, in0=in0, scalar1=s1, scalar2=None, op0=op0), reads, writes)
    else:
        S.op(eng, lambda e: e.tensor_scalar(out=out, in0=in0, scalar1=s1, scalar2=s2, op0=op0, op1=op1), reads, writes)


def ACTF(S, out, in_, func, reads, writes, bias=None, scale=None, accum_out=None):
    kw = {}
    if bias is not None:
        kw["bias"] = bias
    if scale is not None:
        kw["scale"] = scale
    if accum_out is not None:
        kw["accum_out"] = accum_out
    S.op("act", lambda e: e.activation(out=out, in_=in_, func=func, **kw), reads, writes)


def TR(S, out, in_, ident, reads, writes):
    S.op("pe", lambda e: e.transpose(out, in_, ident), reads, writes)


def phase_ssm_pass(nc, S, C, p, win_d, wdt_d, dtb_d, alog_d, convw_d, convb_d, dsk_d, normg_d, wout_d,
                   y1_d, sin_d, sout_d, scale_x):
    last = (p == 1)
    IDN, TLE, TGE, TGT, TLT = 0, 1, 2, 3, 4
    BT_, CPB, NBLK = 256, 2, 8
    Tm = C.tri[:, TLE if p == 0 else TGE, :]
    Um = C.tri[:, TGT if p == 0 else TLT, :]
    ones_f = None
    with ExitStack() as es:
        sb = lambda name, shape, dt: es.enter_context(nc.sbuf_tensor(_U() + "s%d_%s" % (p, name), shape, dt))
        xbcT = sb("xbcT", [128, 24, BT_], BF16)
        pre = [sb("pre%d" % i, [128, BT_ + 4], F32) for i in range(2)]
        cva = [sb("cva%d" % i, [128, BT_], F32) for i in range(2)]
        wfc = [sb("wfc%d" % i, [128, KC, 128], BF16) for i in range(3)]
        wdt = sb("wdt", [128, KC, 32], BF16)
        convw = sb("convw", [128, 24, 3], F32)
        convb = sb("convb", [128, 24], F32)
        rows = sb("rows", [128, 4, 32], F32)
        onesf = sb("onesf", [128, 128], F32)
        sm = sb("sm", [128, 8, 32], F32)
        Rb = sb("Rb", [128, 8, 128], F32)
        LT = sb("LT", [128, 8, 128], BF16)
        Gm = sb("Gm", [128, 128], BF16)
        Xtok = sb("Xtok", [128, 512], BF16)
        Btok = sb("Btok", [128, 128], BF16)
        Xt1 = sb("Xt1", [128, 512], BF16)
        Xt2 = sb("Xt2", [128, 512], BF16)
        Sst = sb("Sst", [128, 4, 512], F32)
        Sb = sb("Sb", [128, 4, 512], BF16)
        ych = sb("ych", [128, 2048], F32)
        t1 = sb("t1", [128, 512], F32)
        if last:
            wz = [sb("wz%d" % i, [128, KC, 128], BF16) for i in range(2)]
            sz = sb("sz", [128, CPB, 2048], BF16)
            yT = sb("yT", [128, 16, BT_], BF16)
            ynb = sb("ynb", [128, 2048], BF16)
            wo = [sb("wo%d" % i, [128, D], BF16) for i in range(2)]
            ngc = sb("ngc", [128, 16], F32)
            gs = sb("gs", [128, 16], F32)
        psb = [C.ps[i][:].bitcast(BF16) for i in range(8)]
        wv_ = win_d.rearrange("(c q) n -> q c n", q=128)

        S.dma("sp", lambda e: e.dma_start(out=convw[:], in_=convw_d), (), [("convw",)])
        S.dma("sp", lambda e: e.dma_start(out=convb[:], in_=convb_d), (), [("convb",)])
        S.dma("pool", lambda e: e.dma_start(out=wdt[:], in_=wdt_d[p].rearrange("(c q) n -> q c n", q=128)), (), [("wdt",)])
        S.dma("sp", lambda e: e.dma_start(out=rows[:, 0, :], in_=dtb_d[p:p + 1, :].to_broadcast([128, 32])), (), [("rows", 0)])
        S.dma("sp", lambda e: e.dma_start(out=rows[:, 1, :], in_=alog_d[p:p + 1, :].to_broadcast([128, 32])), (), [("rows", 1)])
        S.dma("sp", lambda e: e.dma_start(out=rows[:, 2, :], in_=dsk_d.to_broadcast([128, 32])), (), [("rows", 2)])
        ACTF(S, rows[:, 1, :], rows[:, 1, :], AF.Exp, [("rows", 1)], [("rows", 1)])
        TS(S, "dve", rows[:, 1, :], rows[:, 1, :], -1.0, None, ALU.mult, None, [("rows", 1)], [("rows", 1)])
        S.op("dve", lambda e: e.memset(onesf[:], 1.0), (), [("onesf",)])
        if scale_x:
            for t in range(NT):
                TS(S, "pool", C.x[:, t, :], C.x[:, t, :], ALPHA, None, ALU.mult, None, [("x", t)], [("x", t)])
        if p == 0:
            S.op("dve", lambda e: e.memset(Sst[:], 0.0), (), [("Sst", g) for g in range(4)])
            S.op("pool", lambda e: e.memset(Sb[:], 0.0), (), [("Sb", g) for g in range(4)])
        else:
            S.dma("sp", lambda e: e.dma_start(out=Sst[:], in_=sin_d), (), [("Sst", g) for g in range(4)])
            for g in range(4):
                S.op("act", lambda e, g=g: e.copy(out=Sb[:, g, :], in_=Sst[:, g, :]), [("Sst", g)], [("Sb", g)])
            S.dma("sp", lambda e: e.dma_start(out=ngc[:], in_=normg_d), (), [("ngc",)])
        blocks = range(NBLK) if p == 0 else range(NBLK - 1, -1, -1)
        wcnt = [0]
        for blk in blocks:
            c0 = blk * BT_
            for fc in range(24):
                wb = wcnt[0] % 3
                pa = wcnt[0] % 2
                wcnt[0] += 1
                cs = slice(2048 + fc * 128, 2048 + (fc + 1) * 128)
                S.dma("pool", lambda e, wb=wb, cs=cs: e.dma_start(out=wfc[wb][:], in_=wv_[:, :, cs]), (), [("wfc", wb)])
                ur = [("uT", min(max(t, 0), NT - 1)) for t in range(blk * CPB - 1, blk * CPB + CPB + 1)] + [("uThalo",)]
                for k in range(KC):
                    MM(S, C.ps[pa][:, 0:BT_], wfc[wb][:, k, :], C.uT[:, k, c0:c0 + BT_], k == 0, k == KC - 1, [("wfc", wb)] + ur, [("ps", pa)])
                for k in range(KC):
                    MM(S, C.ps[2][:, pa * 8:pa * 8 + 2], wfc[wb][:, k, :], C.uT[:, k, c0 + BT_:c0 + BT_ + 2], k == 0, k == KC - 1,
                       [("wfc", wb)] + ur, [("ps", 2)])
                S.op("act", lambda e, pa=pa: e.copy(out=pre[pa][:, 0:BT_], in_=C.ps[pa][:, 0:BT_]), [("ps", pa)], [("pre", pa)])
                S.op("dve", lambda e, pa=pa: e.tensor_copy(out=pre[pa][:, BT_:BT_ + 2], in_=C.ps[2][:, pa * 8:pa * 8 + 2]), [("ps", 2)], [("pre", pa)])
                TS(S, "pool", cva[pa][:], pre[pa][:, 0:BT_], convw[:, fc, 0:1], None, ALU.mult, None, [("pre", pa), ("convw",)], [("cva", pa)])
                S.op("dve", lambda e, pa=pa, fc=fc: e.scalar_tensor_tensor(out=cva[pa][:], in0=pre[pa][:, 1:BT_ + 1], scalar=convw[:, fc, 1:2],
                                                                         in1=cva[pa][:], op0=ALU.mult, op1=ALU.add),
                     [("pre", pa), ("cva", pa), ("convw",)], [("cva", pa)])
                S.op("dve", lambda e, pa=pa, fc=fc: e.scalar_tensor_tensor(out=cva[pa][:], in0=pre[pa][:, 2:BT_ + 2], scalar=convw[:, fc, 2:3],
                                                                         in1=cva[pa][:], op0=ALU.mult, op1=ALU.add),
                     [("pre", pa), ("cva", pa), ("convw",)], [("cva", pa)])
                ACTF(S, xbcT[:, fc, :], cva[pa][:], AF.Silu, [("cva", pa), ("convb",)], [("xbcT", fc)], bias=convb[:, fc:fc + 1])
            if last:
                for zc in range(16):
                    zb = zc % 2
                    cs = slice(zc * 128, (zc + 1) * 128)
                    S.dma("pool", lambda e, zb=zb, cs=cs: e.dma_start(out=wz[zb][:], in_=wv_[:, :, cs]), (), [("wz", zb)])
                    for cc in range(CPB):
                        tcol = 1 + c0 + cc * 128
                        for k in range(KC):
                            MM(S, C.ps[7][:, 0:128], C.uT[:, k, tcol:tcol + 128], wz[zb][:, k, :], k == 0, k == KC - 1,
                               [("wz", zb), ("uT", blk * CPB + cc)], [("ps", 7)])
                        ACTF(S, sz[:, cc, zc * 128:(zc + 1) * 128], C.ps[7][:, 0:128], AF.Silu, [("ps", 7)], [("sz", cc)])
            chunks = range(CPB) if p == 0 else range(CPB - 1, -1, -1)
            for cc in chunks:
                c = blk * CPB + cc
                off = cc * 128
                tcol = 1 + c * 128
                for k in range(KC):
                    MM(S, C.ps[3][:, 0:32], C.uT[:, k, tcol:tcol + 128], wdt[:, k, :], k == 0, k == KC - 1, [("wdt",), ("uT", c)], [("ps", 3)])
                TT(S, "dve", sm[:, 0, :], C.ps[3][:, 0:32], rows[:, 0, :], ALU.add, [("ps", 3), ("rows", 0)], [("sm", 0)])
                ACTF(S, sm[:, 0, :], sm[:, 0, :], AF.Exp, [("sm", 0)], [("sm", 0)])
                ACTF(S, sm[:, 0, :], sm[:, 0, :], AF.Ln, [("sm", 0)], [("sm", 0)], bias=1.0)
                TT(S, "dve", sm[:, 1, :], sm[:, 0, :], rows[:, 1, :], ALU.mult, [("sm", 0), ("rows", 1)], [("sm", 1)])
                MM(S, C.ps[3][:, 32:64], Tm, sm[:, 1, :], True, True, [("tri",), ("sm", 1)], [("ps", 3)])
                MM(S, C.ps[3][:, 64:96], onesf[:], sm[:, 1, :], True, True, [("onesf",), ("sm", 1)], [("ps", 3)])
                S.op("act", lambda e: e.copy(out=sm[:, 2, :], in_=C.ps[3][:, 32:64]), [("ps", 3)], [("sm", 2)])
                ACTF(S, sm[:, 3, :], C.ps[3][:, 32:64], AF.Exp, [("ps", 3)], [("sm", 3)])
                ACTF(S, sm[:, 5, :], C.ps[3][:, 64:96], AF.Exp, [("ps", 3)], [("sm", 5)])
                TT(S, "dve", sm[:, 6, :], C.ps[3][:, 64:96], sm[:, 2, :], ALU.subtract, [("ps", 3), ("sm", 2)], [("sm", 6)])
                ACTF(S, sm[:, 6, :], sm[:, 6, :], AF.Exp, [("sm", 6)], [("sm", 6)])
                TT(S, "dve", sm[:, 4, :], sm[:, 6, :], sm[:, 0, :], ALU.mult, [("sm", 6), ("sm", 0)], [("sm", 4)])
                if last:
                    S.dma("sp", lambda e, c=c: e.dma_start(out=ych[:], in_=y1_d[c]), (), [("ych", g) for g in range(4)])
                for g in range(4):
                    hs = slice(8 * g, 8 * g + 8)
                    BT = xbcT[:, 16 + g, off:off + 128]
                    CT = xbcT[:, 20 + g, off:off + 128]
                    TT(S, "dve", Rb[:], Tm.unsqueeze(1).to_broadcast([128, 8, 128]), sm[:, 1, hs].unsqueeze(2).to_broadcast([128, 8, 128]),
                       ALU.mult, [("tri",), ("sm", 1)], [("Rb",)])
                    for j in range(2):
                        MM(S, C.ps[4 + j][:], Um, Rb[:, 4 * j:4 * j + 4, :].rearrange("q a b -> q (a b)"), True, True, [("tri",), ("Rb",)], [("ps", 4 + j)])
                        ACTF(S, LT[:, 4 * j:4 * j + 4, :].rearrange("q a b -> q (a b)"), C.ps[4 + j][:], AF.Exp, [("ps", 4 + j)], [("LT",)])
                    MM(S, C.ps[6][:, 0:128], BT, CT, True, True, [("xbcT", 16 + g), ("xbcT", 20 + g)], [("ps", 6)])
                    TT(S, "dve", Gm[:], C.ps[6][:, 0:128], Tm, ALU.mult, [("ps", 6), ("tri",)], [("Gm",)])
                    TT(S, "dve", LT[:], LT[:], Gm[:].unsqueeze(1).to_broadcast([128, 8, 128]), ALU.mult, [("LT",), ("Gm",)], [("LT",)])
                    for j in range(4):
                        TR(S, psb[7][:, j * 128:(j + 1) * 128], xbcT[:, 4 * g + j, off:off + 128], C.identb[:], [("xbcT", 4 * g + j), ("identb",)], [("ps", 7)])
                    TR(S, psb[7][:, 512:640], xbcT[:, 16 + g, off:off + 128], C.identb[:], [("xbcT", 16 + g), ("identb",)], [("ps", 7)])
                    S.op("act", lambda e: e.copy(out=Xtok[:], in_=psb[7][:, 0:512]), [("ps", 7)], [("Xtok",)])
                    S.op("act", lambda e: e.copy(out=Btok[:], in_=psb[7][:, 512:640]), [("ps", 7)], [("Btok",)])
                    X3 = Xtok[:].rearrange("q (a b) -> q a b", a=8)
                    TT(S, "dve", Xt1[:].rearrange("q (a b) -> q a b", a=8), X3, sm[:, 0, hs].unsqueeze(2).to_broadcast([128, 8, 64]), ALU.mult,
                       [("Xtok",), ("sm", 0)], [("Xt1",)])
                    TT(S, "pool", Xt2[:].rearrange("q (a b) -> q a b", a=8), X3, sm[:, 4, hs].unsqueeze(2).to_broadcast([128, 8, 64]), ALU.mult,
                       [("Xtok",), ("sm", 4)], [("Xt2",)])
                    for hh in range(8):
                        MM(S, C.ps[0][:, hh * 64:(hh + 1) * 64], LT[:, hh, :], Xt1[:, hh * 64:(hh + 1) * 64], True, True, [("LT",), ("Xt1",)], [("ps", 0)])
                    MM(S, C.ps[1][:], CT, Sb[:, g, :], True, True, [("xbcT", 20 + g), ("Sb", g)], [("ps", 1)])
                    TT(S, "dve", t1[:].rearrange("q (a b) -> q a b", a=8), C.ps[1][:].rearrange("q (a b) -> q a b", a=8),
                       sm[:, 3, hs].unsqueeze(2).to_broadcast([128, 8, 64]), ALU.mult, [("ps", 1), ("sm", 3)], [("t1",)])
                    yg = ych[:, g * 512:(g + 1) * 512]
                    if not last:
                        TT(S, "dve", yg, t1[:], C.ps[0][:], ALU.add, [("t1",), ("ps", 0)], [("ych", g)])
                        TT(S, "pool", t1[:].rearrange("q (a b) -> q a b", a=8), X3, rows[:, 2, hs].unsqueeze(2).to_broadcast([128, 8, 64]), ALU.mult,
                           [("Xtok",), ("rows", 2), ("t1",)], [("t1",)])
                        TT(S, "pool", yg, yg, t1[:], ALU.add, [("t1",), ("ych", g)], [("ych", g)])
                    else:
                        TT(S, "dve", t1[:], t1[:], C.ps[0][:], ALU.add, [("t1",), ("ps", 0)], [("t1",)])
                        TT(S, "pool", yg, yg, t1[:], ALU.add, [("t1",), ("ych", g)], [("ych", g)])
                    MM(S, C.ps[2][:], Btok[:], Xt2[:], True, True, [("Btok",), ("Xt2",)], [("ps", 2)])
                    TT(S, "dve", Sst[:, g, :].rearrange("q (a b) -> q a b", a=8), Sst[:, g, :].rearrange("q (a b) -> q a b", a=8),
                       sm[:, 5, hs].unsqueeze(2).to_broadcast([128, 8, 64]), ALU.mult, [("Sst", g), ("sm", 5)], [("Sst", g)])
                    TT(S, "dve", Sst[:, g, :], Sst[:, g, :], C.ps[2][:], ALU.add, [("Sst", g), ("ps", 2)], [("Sst", g)])
                    S.op("act", lambda e, g=g: e.copy(out=Sb[:, g, :], in_=Sst[:, g, :]), [("Sst", g)], [("Sb", g)])
                if not last:
                    S.dma("sp", lambda e, c=c: e.dma_start(out=y1_d[c], in_=ych[:]), [("ych", g) for g in range(4)], [("y1d", c)])
                else:
                    TT(S, "dve", ych[:], ych[:], sz[:, cc, :], ALU.mult, [("ych", g) for g in range(4)] + [("sz", cc)], [("ych", g) for g in range(4)])
                    for g in range(4):
                        ACTF(S, t1[:], ych[:, g * 512:(g + 1) * 512], AF.Square, [("ych", g)], [("t1",), ("gs", g)], accum_out=gs[:, g:g + 1])
                        ACTF(S, gs[:, g:g + 1], gs[:, g:g + 1], AF.Sqrt, [("gs", g), ("epsc",)], [("gs", g)], bias=C.epsc[:, 0:1], scale=1.0 / 512.0)
                        S.op("dve", lambda e, g=g: e.reciprocal(out=gs[:, g:g + 1], in_=gs[:, g:g + 1]), [("gs", g)], [("gs", g)])
                        TS(S, "dve", ynb[:, g * 512:(g + 1) * 512], ych[:, g * 512:(g + 1) * 512], gs[:, g:g + 1], None, ALU.mult, None,
                           [("ych", g), ("gs", g)], [("ynb", g)])
                    for q4 in range(4):
                        for j in range(4):
                            ch = q4 * 4 + j
                            TR(S, psb[6][:, j * 128:(j + 1) * 128], ynb[:, ch * 128:(ch + 1) * 128], C.identb[:], [("ynb", q4), ("identb",)], [("ps", 6)])
                        S.op("act", lambda e, q4=q4, off=off: e.copy(out=yT[:, 4 * q4:4 * q4 + 4, off:off + 128],
                                                                    in_=psb[6][:, 0:512].rearrange("q (a b) -> q a b", a=4)),
                             [("ps", 6)], [("yT", cc)])
            if last:
                for ch in range(16):
                    wb = ch % 2
                    S.dma("pool", lambda e, wb=wb, ch=ch: e.dma_start(out=wo[wb][:], in_=wout_d[ch * 128:(ch + 1) * 128, :]), (), [("wo", wb)])
                    TS(S, "pool", wo[wb][:], wo[wb][:], ngc[:, ch:ch + 1], None, ALU.mult, None, [("wo", wb), ("ngc",)], [("wo", wb)])
                    TT(S, "pool", wo[wb][:], wo[wb][:], C.gh[:, 1, :], ALU.mult, [("wo", wb), ("gh", 1, 0), ("gh", 1, 1)], [("wo", wb)])
                    for cc in range(CPB):
                        t = blk * CPB + cc
                        for half in range(2):
                            pb = 4 + ((cc * 2 + half) % 2)
                            MM(S, C.ps[pb][:], yT[:, ch, cc * 128:(cc + 1) * 128], wo[wb][:, half * 512:(half + 1) * 512], True, True,
                               [("yT", cc), ("wo", wb)], [("ps", pb)])
                            xs = C.x[:, t, half * 512:(half + 1) * 512]
                            TT(S, "dve", xs, xs, C.ps[pb][:], ALU.add, [("x", t), ("ps", pb)], [("x", t)])
        if not last:
            S.dma("sp", lambda e: e.dma_start(out=sout_d, in_=Sst[:]), [("Sst", g) for g in range(4)], [("soutd",)])
        S.barrier()


def _dram_in(nc, name, shape, dt=F32):
    return nc.dram_tensor("d_" + name, list(shape), dt, kind="ExternalInput").ap()


def _dram_out(nc, name, shape, dt=F32):
    return nc.dram_tensor("d_" + name, list(shape), dt, kind="ExternalOutput").ap()


def build_stage(stage):
    nc = bass.Bass("TRN2", target_bir_lowering=False)
    S = Sched()
    C = Ctx()
    I = lambda name, shape, dt=F32: _dram_in(nc, name, shape, dt)
    x_d = I("x_in", [TOK, D])
    ccol_d = I("ccol", [128, KC])
    cpack_d = I("cpack", [128, 5, 128])
    lay = [0] if stage in (0, 1) else ([0, 1] if stage == 2 else [1])
    ada_w = {i: I("ada_w%d" % i, [D, 9216]) for i in lay}
    ada_b = {i: I("ada_b%d" % i, [1, 9216]) for i in lay}
    lng = {i: I("lng%d" % i, [3, D]) for i in lay}
    lnb = {i: I("lnb%d" % i, [3, D]) for i in lay}
    ffn_keys = {0: [(0, 0)], 1: [], 2: [(0, 1), (1, 0)], 3: [(1, 1)]}[stage]
    ffw = {}
    for (i, j) in ffn_keys:
        ffw[(i, j)] = (I("wg%d%d" % (i, j), [D, DFF]), I("wu%d%d" % (i, j), [D, DFF]), I("wd%d%d" % (i, j), [DFF, D]))
    if stage in (1, 2):
        w_in = I("w_in", [D, 5184])
        wdt = I("wdt", [2, D, 32])
        dtb = I("dtb", [2, 32])
        alog = I("alog", [2, 32])
        convw = I("convw", [128, 24, 3])
        convb = I("convb", [128, 24])
        dsk = I("dsk", [1, 32])
        normg = I("normg", [128, 16])
        ssm_wout = I("ssm_wout", [2048, D])
        xh = I("xh", [128, KC])
    if stage == 1:
        y1 = _dram_out(nc, "y1", [16, 128, 2048])
        s_out = _dram_out(nc, "s_out", [128, 4, 512])
        s_in = None
    if stage == 2:
        y1 = I("y1", [16, 128, 2048])
        s_in = I("s_in", [128, 4, 512])
        s_out = None
        uT_out = _dram_out(nc, "uT_out", [128, KC, TOK], BF16)
    if stage == 3:
        uTo = I("uTo", [128, KC, TOK], BF16)
        wqkv = I("wqkv", [D, 3072])
        lam = I("lam", [4, 64])
        subg = I("subg", [1, 128])
        attn_wout = I("attn_wout", [D, D])
    if stage != 1:
        y_d = _dram_out(nc, "y_out", [TOK, D])

    with ExitStack() as es:
        alloc_globals(nc, es, C)
        load_consts(nc, S, C, cpack_d)
        S.op("dve", lambda e: e.memset(C.uT[:, :, 0:1], 0.0), (), [("uT0",)])
        xv = x_d.rearrange("(t q) d -> q t d", q=128)
        for t in range(NT):
            S.dma("sp", lambda e, t=t: e.dma_start(out=C.x[:, t, :], in_=xv[:, t, :]), (), [("x", t)])

        def ffn(i, j):
            wg, wu, wd = ffw[(i, j)]
            phase_ffn(nc, S, C, 2 * j, wg, wu, wd)

        def halo():
            with ExitStack() as hs:
                xht = hs.enter_context(nc.sbuf_tensor(_U() + "xht", [128, KC], F32))
                S.dma("sp", lambda e: e.dma_start(out=xht[:], in_=xh), (), [("xht",)])
                TT(S, "dve", xht[:], xht[:], C.modcol[:, 1, 1, :], ALU.mult, [("xht",), ("modcol", 1)], [("xht",)])
                TT(S, "dve", C.uT[:, :, TOK + 1:TOK + 2], xht[:].unsqueeze(2), C.modcol[:, 1, 0, :].unsqueeze(2), ALU.add,
                   [("xht",), ("modcol", 1)], [("uThalo",)])
                S.barrier()

        if stage == 0:
            phase_mods(nc, S, C, ccol_d, ada_w[0], ada_b[0])
            ln_modulate_stage(nc, S, C, 0, False)
            S.barrier()
            ffn(0, 0)
            ln_modulate_stage(nc, S, C, None, True, lng[0][0:1, :], lnb[0][0:1, :])
        elif stage == 1:
            phase_mods(nc, S, C, ccol_d, ada_w[0], ada_b[0])
            ln_modulate_stage(nc, S, C, 1, False)
            halo()
            phase_ssm_pass(nc, S, C, 0, w_in, wdt, dtb, alog, convw, convb, dsk, normg, ssm_wout, y1, None, s_out, False)
        elif stage == 2:
            phase_mods(nc, S, C, ccol_d, ada_w[0], ada_b[0])
            ln_modulate_stage(nc, S, C, 1, False)
            halo()
            phase_ssm_pass(nc, S, C, 1, w_in, wdt, dtb, alog, convw, convb, dsk, normg, ssm_wout, y1, s_in, None, True)
            ln_modulate_stage(nc, S, C, 2, True, lng[0][1:2, :], lnb[0][1:2, :])
            ffn(0, 1)
            ln_modulate_stage(nc, S, C, None, True, lng[0][2:3, :], lnb[0][2:3, :])
            phase_mods(nc, S, C, ccol_d, ada_w[1], ada_b[1])
            ln_modulate_stage(nc, S, C, 0, False)
            S.barrier()
            ffn(1, 0)
            ln_modulate_stage(nc, S, C, 1, True, lng[1][0:1, :], lnb[1][0:1, :])
            S.dma("sp", lambda e: e.dma_start(out=uT_out, in_=C.uT[:, :, 1:1 + TOK]), [("uT", t) for t in range(NT)], [("uTout",)])
        else:
            phase_mods(nc, S, C, ccol_d, ada_w[1], ada_b[1])
            ln_modulate_stage(nc, S, C, 1, False)
            S.barrier()
            phase_attn(nc, S, C, uTo, wqkv, lam, subg, attn_wout)
            ln_modulate_stage(nc, S, C, 2, True, lng[1][1:2, :], lnb[1][1:2, :])
            ffn(1, 1)
            ln_modulate_stage(nc, S, C, None, True, lng[1][2:3, :], lnb[1][2:3, :])
        S.barrier()
        if stage != 1:
            yv = y_d.rearrange("(t q) d -> q t d", q=128)
            for t in range(NT):
                S.dma("sp", lambda e, t=t: e.dma_start(out=yv[:, t, :], in_=C.x[:, t, :]), [("x", t)], [("yout", t)])
        S.barrier()
        S.emit(nc, es)
    return nc


def _cpack():
    r = np.arange(128)[:, None]
    c = np.arange(128)[None, :]
    mats = [r == c, r <= c, r >= c, r > c, r < c]
    return np.ascontiguousarray(np.stack([m.astype(np.float32) for m in mats], axis=1))


def _col(v):
    v = np.asarray(v)
    return np.ascontiguousarray(v.reshape(-1, 128).T)


_PROGS = {}


def _prog(stage):
    if stage not in _PROGS:
        _PROGS[stage] = build_stage(stage)
    return _PROGS[stage]


DEBUG_STOP = None


def kernel(x, c, ada_w, ada_b, ln_g, ln_b, ffn_w_gate, ffn_w_up, ffn_w_down,
           ssm_w_in, ssm_conv_w, ssm_conv_b, ssm_dt_bias, ssm_a_log, ssm_d, ssm_norm_g, ssm_w_out,
           attn_w_qkv, attn_lambda, attn_subln_g, attn_w_out):
    f = lambda a: np.ascontiguousarray(np.asarray(a, dtype=np.float32))
    x, c = f(x), f(c)
    ada_w, ada_b, ln_g, ln_b = f(ada_w), f(ada_b), f(ln_g), f(ln_b)
    wgate, wup, wdown = f(ffn_w_gate), f(ffn_w_up), f(ffn_w_down)
    w_in, conv_w, conv_b = f(ssm_w_in)[0], f(ssm_conv_w)[0], f(ssm_conv_b)[0]
    dt_bias, a_log, dsk, norm_g, ssm_wout = f(ssm_dt_bias)[0], f(ssm_a_log)[0], f(ssm_d), f(ssm_norm_g)[0], f(ssm_w_out)[0]
    wqkv, lam, subg, attn_wout = f(attn_w_qkv)[0], f(attn_lambda)[0], f(attn_subln_g), f(attn_w_out)[0]
    cores = list(range(8))
    cpack = _cpack()

    def local(arr, core):
        b, h = core // 2, core % 2
        a = arr[b, h * TOK:(h + 1) * TOK]
        return np.ascontiguousarray(a[::-1] if h else a)

    def common(core, lays):
        b = core // 2
        m = {"d_ccol": _col(c[b]), "d_cpack": cpack}
        for i in lays:
            m["d_ada_w%d" % i] = ada_w[i]
            m["d_ada_b%d" % i] = ada_b[i:i + 1]
            m["d_lng%d" % i] = ln_g[i]
            m["d_lnb%d" % i] = ln_b[i]
        return m

    def ffn_in(m, keys):
        for (i, j) in keys:
            m["d_wg%d%d" % (i, j)] = wgate[i, j]
            m["d_wu%d%d" % (i, j)] = wup[i, j]
            m["d_wd%d%d" % (i, j)] = wdown[i, j]

    def ssm_in(m, core, x1loc):
        h = core % 2
        sets = [0, 1] if h == 0 else [1, 0]
        m["d_w_in"] = w_in
        m["d_wdt"] = np.ascontiguousarray(np.stack([w_in[:, 5120 + 32 * s:5152 + 32 * s] for s in sets]))
        m["d_dtb"] = np.ascontiguousarray(np.stack([dt_bias[s] for s in sets]))
        m["d_alog"] = np.ascontiguousarray(np.stack([a_log[s] for s in sets]))
        cw = conv_w if h == 0 else conv_w[::-1]
        m["d_convw"] = np.ascontiguousarray(cw.reshape(3, 24, 128).transpose(2, 1, 0))
        m["d_convb"] = _col(conv_b)
        m["d_dsk"] = dsk
        m["d_normg"] = _col(norm_g)
        m["d_ssm_wout"] = ssm_wout
        m["d_xh"] = _col(x1loc[core ^ 1][TOK - 1])

    maps = []
    for core in cores:
        m = common(core, [0])
        m["d_x_in"] = local(x, core)
        ffn_in(m, [(0, 0)])
        maps.append(m)
    r0 = run_bass_kernel_spmd(_prog(0), maps, core_ids=cores)
    x1 = [np.asarray(r0.results[k]["d_y_out"]) for k in cores]
    if DEBUG_STOP == 0:
        return x1
    maps = []
    for core in cores:
        m = common(core, [0])
        m["d_x_in"] = x1[core]
        ssm_in(m, core, x1)
        maps.append(m)
    r1 = run_bass_kernel_spmd(_prog(1), maps, core_ids=cores)
    y1 = [np.asarray(r1.results[k]["d_y1"]) for k in cores]
    so = [np.asarray(r1.results[k]["d_s_out"]) for k in cores]
    maps = []
    for core in cores:
        m = common(core, [0, 1])
        m["d_x_in"] = x1[core]
        ssm_in(m, core, x1)
        m["d_y1"] = y1[core]
        m["d_s_in"] = so[core ^ 1]
        ffn_in(m, [(0, 1), (1, 0)])
        maps.append(m)
    r2 = run_bass_kernel_spmd(_prog(2), maps, core_ids=cores)
    x4 = [np.asarray(r2.results[k]["d_y_out"]) for k in cores]
    uT4 = [np.asarray(r2.results[k]["d_uT_out"]) for k in cores]
    if DEBUG_STOP == 2:
        return x4
    maps = []
    for core in cores:
        m = common(core, [1])
        m["d_x_in"] = x4[core]
        m["d_uTo"] = uT4[core ^ 1]
        m["d_wqkv"] = wqkv
        m["d_lam"] = lam
        m["d_subg"] = subg
        m["d_attn_wout"] = attn_wout
        ffn_in(m, [(1, 1)])
        maps.append(m)
    r3 = run_bass_kernel_spmd(_prog(3), maps, core_ids=cores)
    out = np.empty((4, 2 * TOK, D), dtype=np.float32)
    for core in cores:
        b, h = core // 2, core % 2
        y = np.asarray(r3.results[core]["d_y_out"])
        out[b, h * TOK:(h + 1) * TOK] = y[::-1] if h else y
    return out
```

```python
from contextlib import ExitStack
import numpy as np
import concourse.bass as bass
import concourse.mybir as mybir
from concourse.bass_utils import run_bass_kernel_spmd

DT = mybir.dt
F32 = DT.float32
BF16 = DT.bfloat16
AF = mybir.ActivationFunctionType
ALU = mybir.AluOpType
AX = mybir.AxisListType

ENGS = ("pe", "act", "dve", "pool", "sp")
N_DMA_SEMS = 40
MAX_OUTSTANDING_DMA = 10 ** 9


class Op:
    __slots__ = ("eng", "fn", "deps", "signal", "seq", "is_dma", "slot", "slot_val",
                 "prev_slot_user", "idx", "is_nop")


class Sched:
    def __init__(self):
        self.ops = {e: [] for e in ENGS}
        self.last_write = {}
        self.readers = {}
        self.n_dma = 0
        self.slot_last = [None] * N_DMA_SEMS
        self.slot_count = [0] * N_DMA_SEMS
        self.all = []
        self.dma_hist = {}

    def _add(self, eng, fn, reads, writes, is_dma):
        o = Op()
        o.eng, o.fn, o.deps, o.signal, o.seq, o.is_dma = eng, fn, [], False, 0, is_dma
        o.slot = o.slot_val = None
        o.prev_slot_user = None
        o.is_nop = False
        o.idx = len(self.all)
        deps = {}
        for r in reads:
            w = self.last_write.get(r)
            if w is not None:
                deps[id(w)] = w
        for r in writes:
            w = self.last_write.get(r)
            if w is not None:
                deps[id(w)] = w
            for rd in self.readers.get(r, ()):
                if rd is not o:
                    deps[id(rd)] = rd
        for d in deps.values():
            if d.eng == eng and not d.is_dma and not is_dma:
                if eng == "pe":
                    continue
            o.deps.append(d)
            d.signal = True
        for r in reads:
            self.readers.setdefault(r, []).append(o)
        for r in writes:
            self.last_write[r] = o
            self.readers[r] = []
        if is_dma:
            q = self.dma_hist.setdefault(eng, [])
            if len(q) >= MAX_OUTSTANDING_DMA:
                o.deps.append(q[-MAX_OUTSTANDING_DMA])
            q.append(o)
            s = self.n_dma % N_DMA_SEMS
            self.n_dma += 1
            o.prev_slot_user = self.slot_last[s]
            self.slot_last[s] = o
            self.slot_count[s] += 1
            o.slot, o.slot_val = s, 16 * self.slot_count[s]
            o.signal = True
        self.ops[eng].append(o)
        self.all.append(o)
        return o

    def op(self, eng, fn, reads=(), writes=()):
        return self._add(eng, fn, reads, writes, False)

    def dma(self, eng, fn, reads=(), writes=()):
        return self._add(eng, fn, reads, writes, True)

    def barrier(self):
        key = ("__barrier__",)
        lasts = []
        for e in ENGS:
            for o in reversed(self.ops[e]):
                if not o.is_dma and not o.is_nop:
                    lasts.append(o)
                    break
        dmas = [o for o in self.slot_last if o is not None]
        for e in ("pe", "act", "dve", "pool", "sp"):
            o = self._add(e, lambda eng: eng.nop(), (), (), False)
            o.is_nop = True
            for d in lasts + dmas:
                if d.eng == e and e == "pe" and not d.is_dma:
                    continue
                o.deps.append(d)
                d.signal = True
        self.last_write = {}
        self.readers = {}

    def emit(self, nc, es):
        eng_sem = {e: es.enter_context(nc.semaphore("prog_" + e)) for e in ENGS}
        dma_sem = [es.enter_context(nc.semaphore("dma%d" % i)) for i in range(N_DMA_SEMS)]
        for e in ENGS:
            n = 0
            for o in self.ops[e]:
                if o.signal and not o.is_dma:
                    n += 1
                    o.seq = n
        block = es.enter_context(nc.Block())
        sched = self

        def body_for(e):
            def body(eng):
                waited_eng = {f: 0 for f in ENGS}
                waited_slot = [0] * N_DMA_SEMS
                for o in sched.ops[e]:
                    need_eng = {}
                    need_slot = {}
                    deps = list(o.deps)
                    if o.is_dma and o.prev_slot_user is not None:
                        deps.append(o.prev_slot_user)
                    for d in deps:
                        if d.is_dma:
                            if d.slot_val > waited_slot[d.slot]:
                                need_slot[d.slot] = max(need_slot.get(d.slot, 0), d.slot_val)
                        else:
                            if d.seq > waited_eng[d.eng]:
                                need_eng[d.eng] = max(need_eng.get(d.eng, 0), d.seq)
                    for f, v in need_eng.items():
                        eng.wait_ge(eng_sem[f], v)
                        waited_eng[f] = v
                    for s, v in need_slot.items():
                        eng.wait_ge(dma_sem[s], v)
                        waited_slot[s] = v
                    ins = o.fn(eng)
                    if o.is_dma:
                        ins.then_inc(dma_sem[o.slot], 16)
                    elif o.signal:
                        ins.then_inc(eng_sem[e], 1)
            return body

        block.tensor(body_for("pe"))
        block.scalar(body_for("act"))
        block.vector(body_for("dve"))
        block.gpsimd(body_for("pool"))
        block.sync(body_for("sp"))


D = 1024
TOK = 2048
NT = TOK // 128
DFF = 2816
NF = DFF // 128
KC = D // 128
ALPHA = (2 * 2) ** 0.25
LN_EPS = 1e-5
FFN_GROUPS = [(0, 4), (4, 4), (8, 4), (12, 4), (16, 4), (20, 2)]


class Ctx:
    pass


_UC = [0]


def _U():
    _UC[0] += 1
    return "u%d_" % _UC[0]


def alloc_globals(nc, es, C):
    C.x = es.enter_context(nc.sbuf_tensor(_U() + "x_res", [128, NT, D], F32))
    C.uT = es.enter_context(nc.sbuf_tensor(_U() + "uT", [128, KC, TOK + 2], BF16))
    C.ident = es.enter_context(nc.sbuf_tensor(_U() + "ident", [128, 128], F32))
    C.identb = es.enter_context(nc.sbuf_tensor(_U() + "identb", [128, 128], BF16))
    C.gh = es.enter_context(nc.sbuf_tensor(_U() + "gh", [128, 3, D], F32))
    C.modcol = es.enter_context(nc.sbuf_tensor(_U() + "modcol", [128, 3, 2, KC], F32))
    C.tri = es.enter_context(nc.sbuf_tensor(_U() + "tri", [128, 5, 128], F32))
    C.stat = es.enter_context(nc.sbuf_tensor(_U() + "stat", [128, 2, 16], F32))
    C.epsc = es.enter_context(nc.sbuf_tensor(_U() + "epsc", [128, 1], F32))
    C.ps = [es.enter_context(nc.psum_tensor("ps%d" % b, [128, 512], F32)) for b in range(8)]


def load_consts(nc, S, C, ident_d):
    S.dma("sp", lambda e: e.dma_start(out=C.tri[:], in_=ident_d), (), [("tri",)])
    S.op("dve", lambda e: e.tensor_copy(out=C.ident[:], in_=C.tri[:, 0, :]), [("tri",)], [("ident",)])
    S.op("dve", lambda e: e.tensor_copy(out=C.identb[:], in_=C.ident[:]), [("ident",)], [("identb",)])
    S.op("dve", lambda e: e.memset(C.epsc[:], LN_EPS), (), [("epsc",)])


def phase_mods(nc, S, C, ccol_d, ada_w_d, ada_b_d):
    with ExitStack() as es:
        ccol = es.enter_context(nc.sbuf_tensor(_U() + "ccol", [128, KC], F32))
        condb = es.enter_context(nc.sbuf_tensor(_U() + "condb", [128, KC], BF16))
        condrep = es.enter_context(nc.sbuf_tensor(_U() + "condrep", [128, KC, 128], BF16))
        wa = [es.enter_context(nc.sbuf_tensor(_U() + "wa%d" % i, [128, KC, 512], BF16)) for i in range(3)]
        brow = [es.enter_context(nc.sbuf_tensor(_U() + "brow%d" % i, [128, 512], F32)) for i in range(3)]
        mrow = [es.enter_context(nc.sbuf_tensor(_U() + "mrow%d" % i, [128, 512], F32)) for i in range(2)]
        S.dma("sp", lambda e: e.dma_start(out=ccol[:], in_=ccol_d), (), [("ccol",)])
        S.op("act", lambda e: e.activation(out=condb[:], in_=ccol[:], func=AF.Silu), [("ccol",)], [("condb",)])
        S.op("dve", lambda e: e.tensor_copy(out=condrep[:], in_=condb[:].unsqueeze(2).to_broadcast([128, KC, 128])),
             [("condb",)], [("condrep",)])
        wv = ada_w_d.rearrange("(c p) n -> p c n", p=128)
        for nb in range(18):
            b3, b2 = nb % 3, nb % 2
            sl = slice(nb * 512, (nb + 1) * 512)
            S.dma("pool", lambda e, b3=b3, sl=sl: e.dma_start(out=wa[b3][:], in_=wv[:, :, sl]), (), [("wa", b3)])
            S.dma("sp", lambda e, b3=b3, sl=sl: e.dma_start(out=brow[b3][:], in_=ada_b_d[0:1, sl].to_broadcast([128, 512])),
                  (), [("brow", b3)])
            pb = nb % 2
            for k in range(KC):
                S.op("pe", lambda e, k=k, b3=b3, pb=pb: e.matmul(C.ps[pb][:], condrep[:, k, :], wa[b3][:, k, :],
                                                                  start=(k == 0), stop=(k == KC - 1)),
                     [("condrep",), ("wa", b3)], [("ps", pb)])
            S.op("dve", lambda e, b2=b2, b3=b3, pb=pb: e.tensor_tensor(out=mrow[b2][:], in0=C.ps[pb][:], in1=brow[b3][:], op=ALU.add),
                 [("ps", pb), ("brow", b3)], [("mrow", b2)])
            sub, kind, half = nb // 6, (nb % 6) // 2, nb % 2
            if kind == 2:
                f = 1.0 if sub == 1 else 0.5
                S.op("dve", lambda e, b2=b2, sub=sub, half=half, f=f: e.tensor_scalar(
                    out=C.gh[:, sub, half * 512:(half + 1) * 512], in0=mrow[b2][:], scalar1=1.0, scalar2=f,
                    op0=ALU.add, op1=ALU.mult), [("mrow", b2)], [("gh", sub, half)])
            else:
                tb = 2 + (nb % 2)
                for j in range(4):
                    S.op("pe", lambda e, j=j, b2=b2, tb=tb: e.transpose(C.ps[tb][:, j * 128:(j + 1) * 128],
                                                                       mrow[b2][:, j * 128:(j + 1) * 128], C.ident[:]),
                         [("mrow", b2), ("ident",)], [("ps", tb)])
                for j in range(4):
                    cidx = half * 4 + j
                    S.op("dve", lambda e, j=j, tb=tb, sub=sub, kind=kind, cidx=cidx: e.tensor_scalar(
                        out=C.modcol[:, sub, kind, cidx:cidx + 1], in0=C.ps[tb][:, j * 128:j * 128 + 1],
                        scalar1=(1.0 if kind == 1 else 0.0), scalar2=None, op0=ALU.add),
                        [("ps", tb)], [("modcol", sub)])
        S.barrier()


def ln_modulate_stage(nc, S, C, sub_next, do_ln, lng_d=None, lnb_d=None, tiles=None):
    es = ExitStack()
    if do_ln:
        C.lng = es.enter_context(nc.sbuf_tensor(_U() + "lng", [128, D], F32))
        C.lnb = es.enter_context(nc.sbuf_tensor(_U() + "lnb", [128, D], F32))
        S.dma("sp", lambda e: e.dma_start(out=C.lng[:], in_=lng_d.to_broadcast([128, D])), (), [("lng",)])
        S.dma("sp", lambda e: e.dma_start(out=C.lnb[:], in_=lnb_d.to_broadcast([128, D])), (), [("lnb",)])
    for t in (tiles if tiles is not None else range(NT)):
        xr = ("x", t)
        if do_ln:
            st = ("stat", t % 2)
            sv = C.stat[:, t % 2, :]
            for h in range(2):
                S.op("dve", lambda e, t=t, h=h, sv=sv: e.bn_stats(out=sv[:, h * 6:(h + 1) * 6], in_=C.x[:, t, h * 512:(h + 1) * 512]),
                     [xr], [st])
            S.op("dve", lambda e, sv=sv: e.bn_aggr(out=sv[:, 12:14], in_=sv[:, 0:12]), [st], [st])
            S.op("act", lambda e, sv=sv: e.activation(out=sv[:, 14:15], in_=sv[:, 13:14], func=AF.Sqrt, bias=C.epsc[:, 0:1], scale=1.0),
                 [st, ("epsc",)], [st])
            S.op("dve", lambda e, sv=sv: e.reciprocal(out=sv[:, 15:16], in_=sv[:, 14:15]), [st], [st])
            S.op("dve", lambda e, t=t, sv=sv: e.tensor_scalar(out=C.x[:, t, :], in0=C.x[:, t, :], scalar1=sv[:, 12:13],
                                                           scalar2=sv[:, 15:16], op0=ALU.subtract, op1=ALU.mult),
                 [xr, st], [xr])
            S.op("pool", lambda e, t=t: e.tensor_tensor(out=C.x[:, t, :], in0=C.x[:, t, :], in1=C.lng[:], op=ALU.mult),
                 [xr, ("lng",)], [xr])
            S.op("pool", lambda e, t=t: e.tensor_tensor(out=C.x[:, t, :], in0=C.x[:, t, :], in1=C.lnb[:], op=ALU.add),
                 [xr, ("lnb",)], [xr])
        if sub_next is not None:
            for half in range(2):
                pb = 2 * (t % 2) + half
                for j in range(4):
                    c = half * 4 + j
                    S.op("pe", lambda e, t=t, c=c, j=j, pb=pb: e.transpose(C.ps[pb][:, j * 128:(j + 1) * 128],
                                                                       C.x[:, t, c * 128:(c + 1) * 128], C.ident[:]),
                         [xr, ("ident",)], [("ps", pb)])
                for j in range(4):
                    c = half * 4 + j
                    S.op("act", lambda e, t=t, c=c, j=j, pb=pb: e.activation(
                        out=C.uT[:, c, 1 + t * 128:1 + (t + 1) * 128], in_=C.ps[pb][:, j * 128:(j + 1) * 128],
                        func=AF.Identity, scale=C.modcol[:, sub_next, 1, c:c + 1], bias=C.modcol[:, sub_next, 0, c:c + 1]),
                        [("ps", pb), ("modcol", sub_next)], [("uT", t)])
    if do_ln:
        S.barrier()
    es.close()


def phase_ffn(nc, S, C, sub, wg_d, wu_d, wd_d):
    with ExitStack() as es:
        wg = [es.enter_context(nc.sbuf_tensor(_U() + "wg%d" % i, [128, KC, 512], BF16)) for i in range(2)]
        wu = [es.enter_context(nc.sbuf_tensor(_U() + "wu%d" % i, [128, KC, 512], BF16)) for i in range(2)]
        wd = [es.enter_context(nc.sbuf_tensor(_U() + "wd%d" % i, [128, 4, D], BF16)) for i in range(2)]
        hT = [es.enter_context(nc.sbuf_tensor(_U() + "hT%d" % i, [128, 4, 512], BF16)) for i in range(2)]
        sg = [es.enter_context(nc.sbuf_tensor(_U() + "sg%d" % i, [128, 512], F32)) for i in range(2)]
        wgv = wg_d.rearrange("(c p) n -> p c n", p=128)
        wuv = wu_d.rearrange("(c p) n -> p c n", p=128)
        wdv = wd_d.rearrange("(c p) n -> p c n", p=128)
        cnt = 0
        ycnt = 0
        for gi, (f0, nf) in enumerate(FFN_GROUPS):
            b = gi % 2
            fs = slice(f0 * 128, (f0 + nf) * 128)
            S.dma("pool", lambda e, b=b, fs=fs, nf=nf: e.dma_start(out=wg[b][:, :, 0:nf * 128], in_=wgv[:, :, fs]), (), [("wg", b)])
            S.dma("pool", lambda e, b=b, fs=fs, nf=nf: e.dma_start(out=wu[b][:, :, 0:nf * 128], in_=wuv[:, :, fs]), (), [("wu", b)])
            S.dma("pool", lambda e, b=b, f0=f0, nf=nf: e.dma_start(out=wd[b][:, 0:nf, :], in_=wdv[:, f0:f0 + nf, :]), (), [("wd", b)])
            for j in range(nf):
                S.op("pool", lambda e, b=b, j=j: e.tensor_tensor(out=wd[b][:, j, :], in0=wd[b][:, j, :], in1=C.gh[:, sub, :], op=ALU.mult),
                     [("wd", b), ("gh", sub, 0), ("gh", sub, 1)], [("wd", b)])
            for tb in range(4):
                hb = tb % 2
                tsl = slice(1 + tb * 512, 1 + (tb + 1) * 512)
                for j in range(nf):
                    pg, pu = (cnt % 2), 2 + (cnt % 2)
                    sgb = cnt % 2
                    cnt += 1
                    ur = [("uT", tb * 4 + q) for q in range(4)]
                    for k in range(KC):
                        S.op("pe", lambda e, k=k, b=b, j=j, pg=pg, tsl=tsl: e.matmul(
                            C.ps[pg][:], wg[b][:, k, j * 128:(j + 1) * 128], C.uT[:, k, tsl], start=(k == 0), stop=(k == KC - 1)),
                            [("wg", b)] + ur, [("ps", pg)])
                    for k in range(KC):
                        S.op("pe", lambda e, k=k, b=b, j=j, pu=pu, tsl=tsl: e.matmul(
                            C.ps[pu][:], wu[b][:, k, j * 128:(j + 1) * 128], C.uT[:, k, tsl], start=(k == 0), stop=(k == KC - 1)),
                            [("wu", b)] + ur, [("ps", pu)])
                    S.op("act", lambda e, pg=pg, sgb=sgb: e.activation(out=sg[sgb][:], in_=C.ps[pg][:], func=AF.Silu),
                         [("ps", pg)], [("sg", sgb)])
                    S.op("dve", lambda e, pu=pu, sgb=sgb, hb=hb, j=j: e.tensor_tensor(
                        out=hT[hb][:, j, :], in0=sg[sgb][:], in1=C.ps[pu][:], op=ALU.mult),
                        [("ps", pu), ("sg", sgb)], [("hT", hb)])
                for tt in range(4):
                    t = tb * 4 + tt
                    for half in range(2):
                        py = 4 + (ycnt % 4)
                        ycnt += 1
                        for j in range(nf):
                            S.op("pe", lambda e, hb=hb, j=j, tt=tt, b=b, half=half, py=py: e.matmul(
                                C.ps[py][:], hT[hb][:, j, tt * 128:(tt + 1) * 128], wd[b][:, j, half * 512:(half + 1) * 512],
                                start=(j == 0), stop=(j == nf - 1)), [("hT", hb), ("wd", b)], [("ps", py)])
                        xs = C.x[:, t, half * 512:(half + 1) * 512]
                        if gi == 0:
                            S.op("dve", lambda e, xs=xs, py=py: e.scalar_tensor_tensor(
                                out=xs, in0=xs, scalar=ALPHA, in1=C.ps[py][:], op0=ALU.mult, op1=ALU.add),
                                [("x", t), ("ps", py)], [("x", t)])
                        else:
                            S.op("dve", lambda e, xs=xs, py=py: e.tensor_tensor(out=xs, in0=xs, in1=C.ps[py][:], op=ALU.add),
                                 [("x", t), ("ps", py)], [("x", t)])
        S.barrier()


NH = 8
SLOPES = [2.0 ** (-8.0 * (h + 1) / NH) for h in range(NH)]
LAMBDA_INIT1 = 0.8 - 0.6 * float(np.exp(-0.3 * 1))
BANDW = 3968
SKIP_T = 105.0


def phase_attn(nc, S, C, uTo_d, wqkv_d, lam_d, subg_d, wout_d):
    scale = 64 ** -0.5
    with ExitStack() as es:
        sb = lambda name, shape, dt: es.enter_context(nc.sbuf_tensor(name, shape, dt))
        uTo = [sb("uTo%d" % i, [128, KC, 512], BF16) for i in range(2)]
        wq = [sb("wqkv%d" % i, [128, KC, 3, 128], BF16) for i in range(2)]
        wo = [sb("wo%d" % i, [128, D], BF16) for i in range(2)]
        qT = sb("qT", [128, TOK], BF16)
        kT = sb("kT", [128, 2 * TOK], BF16)
        v = sb("v_h", [128, 32, 132], BF16)
        band = [sb("band%d" % i, [128, BANDW], BF16) for i in range(2)]
        itmp = [sb("itmp%d" % i, [128, 496], F32) for i in range(2)]
        sq = [sb("sq%d" % i, [128, 512], BF16) for i in range(2)]
        onesb = sb("onesb", [128, 128], BF16)
        nstat = sb("nstat", [128, 2, 16], F32)
        nbias = sb("nbias", [128, 2], F32)
        lamt = sb("lamt", [128, 4, 64], F32)
        lam2 = sb("lam2", [128, 2, 64], F32)
        lams = sb("lams", [128, 4], F32)
        subg = sb("subg", [128, 128], F32)
        PT = [sb("PT%d" % i, [128, 512], BF16) for i in range(4)]
        osum = [sb("osum%d" % i, [128, 128], F32) for i in range(4)]
        otmp = [sb("otmp%d" % i, [128, 128], F32) for i in range(2)]
        rs = sb("rs", [128, 16], F32)
        onb = [sb("onb%d" % i, [128, 128], BF16) for i in range(2)]
        oT = sb("oT", [128, TOK], BF16)

        S.op("dve", lambda e: e.memset(onesb[:], 1.0), (), [("onesb",)])
        S.op("dve", lambda e: e.memset(v[:, :, 128:129], 1.0), (), [("vone",)])
        S.dma("sp", lambda e: e.dma_start(out=lamt[:].rearrange("p a b -> p (a b)"),
                                          in_=lam_d.rearrange("(o a) b -> o (a b)", o=1).to_broadcast([128, 256])), (), [("lamt",)])
        S.dma("sp", lambda e: e.dma_start(out=subg[:], in_=subg_d.to_broadcast([128, 128])), (), [("subg",)])
        S.op("dve", lambda e: e.tensor_tensor(out=lam2[:, 0, :], in0=lamt[:, 0, :], in1=lamt[:, 1, :], op=ALU.mult), [("lamt",)], [("lam2",)])
        S.op("dve", lambda e: e.tensor_tensor(out=lam2[:, 1, :], in0=lamt[:, 2, :], in1=lamt[:, 3, :], op=ALU.mult), [("lamt",)], [("lam2",)])
        S.op("dve", lambda e: e.tensor_reduce(out=lams[:, 0:2], in_=lam2[:], axis=AX.X, op=ALU.add), [("lam2",)], [("lams",)])
        S.op("act", lambda e: e.activation(out=lams[:, 0:2], in_=lams[:, 0:2], func=AF.Exp), [("lams",)], [("lams",)])
        S.op("dve", lambda e: e.tensor_tensor(out=lams[:, 2:3], in0=lams[:, 1:2], in1=lams[:, 0:1], op=ALU.subtract), [("lams",)], [("lams",)])
        S.op("dve", lambda e: e.tensor_scalar(out=lams[:, 2:3], in0=lams[:, 2:3], scalar1=-LAMBDA_INIT1, scalar2=None, op0=ALU.add),
             [("lams",)], [("lams",)])
        S.op("dve", lambda e: e.tensor_scalar(out=subg[:], in0=subg[:], scalar1=(1.0 - LAMBDA_INIT1), scalar2=None, op0=ALU.mult),
             [("subg",)], [("subg",)])
        wv_ = wqkv_d.rearrange("(c p) n -> p c n", p=128)
        pcnt = [0]
        scnt = [0]
        ptc = [0]
        uocnt = [0]

        def proj_bank():
            pcnt[0] += 1
            return 6 + (pcnt[0] % 2)

        for h in range(NH):
            wb = h % 2
            for part in range(3):
                cs = slice(part * 1024 + h * 128, part * 1024 + (h + 1) * 128)
                S.dma("pool", lambda e, wb=wb, part=part, cs=cs: e.dma_start(out=wq[wb][:, :, part, :], in_=wv_[:, :, cs]),
                      (), [("wq", wb)])
            S.dma("pool", lambda e, wb=wb, h=h: e.dma_start(out=wo[wb][:], in_=wout_d[h * 128:(h + 1) * 128, :]), (), [("wo", wb)])
            S.op("pool", lambda e, wb=wb: e.tensor_tensor(out=wo[wb][:], in0=wo[wb][:], in1=C.gh[:, 1, :], op=ALU.mult),
                 [("wo", wb), ("gh", 1, 0), ("gh", 1, 1)], [("wo", wb)])
            def proj_block(part, own, tb, srcbuf, col, dst, dr, blk, wb=wb):
                wqb = wq[wb]
                pb = proj_bank()
                for k in range(KC):
                    if own:
                        src = C.uT[:, k, 1 + tb * 512:1 + (tb + 1) * 512]
                        rr = [("uT", tb * 4 + q) for q in range(4)]
                    else:
                        src = uTo[srcbuf][:, k, :]
                        rr = [("uTo", srcbuf)]
                    S.op("pe", lambda e, k=k, pb=pb, src=src: e.matmul(
                        C.ps[pb][:], wqb[:, k, part, :], src, start=(k == 0), stop=(k == KC - 1)),
                        [("wq", wb)] + rr, [("ps", pb)])
                if blk % 2 == 0:
                    S.op("act", lambda e, pb=pb: e.copy(out=dst, in_=C.ps[pb][:]), [("ps", pb)], [dr])
                else:
                    S.op("dve", lambda e, pb=pb: e.tensor_copy(out=dst, in_=C.ps[pb][:]), [("ps", pb)], [dr])
                sb_ = blk % 2
                S.op("pool", lambda e, sb_=sb_: e.tensor_tensor(out=sq[sb_][:], in0=dst, in1=dst, op=ALU.mult), [dr], [("sq", sb_)])
                for m in range(2):
                    pb2 = proj_bank()
                    ms = slice(m * 64, (m + 1) * 64)
                    S.op("pe", lambda e, ms=ms, sb_=sb_, pb2=pb2: e.matmul(C.ps[pb2][:], onesb[ms, :], sq[sb_][ms, :], start=True, stop=True),
                         [("onesb",), ("sq", sb_)], [("ps", pb2)])
                    S.op("dve", lambda e, m=m, pb2=pb2: e.tensor_reduce(out=nstat[:, m, col:col + 1], in_=C.ps[pb2][:], axis=AX.X, op=ALU.max),
                         [("ps", pb2)], [("nstat",)])

            def v_tile(kt, own, srcbuf, off, wb=wb):
                wqb = wq[wb]
                pb = proj_bank()
                for k in range(KC):
                    if own:
                        src = C.uT[:, k, 1 + kt * 128:1 + (kt + 1) * 128]
                        rr = [("uT", kt)]
                    else:
                        src = uTo[srcbuf][:, k, off * 128:(off + 1) * 128]
                        rr = [("uTo", srcbuf)]
                    S.op("pe", lambda e, k=k, pb=pb, src=src: e.matmul(
                        C.ps[pb][:, 0:128], src, wqb[:, k, 2, :], start=(k == 0), stop=(k == KC - 1)),
                        [("wq", wb)] + rr, [("ps", pb)])
                if kt % 2 == 0:
                    S.op("act", lambda e, pb=pb: e.copy(out=v[:, kt, 0:128], in_=C.ps[pb][:, 0:128]), [("ps", pb)], [("v", kt)])
                else:
                    S.op("dve", lambda e, pb=pb: e.tensor_copy(out=v[:, kt, 0:128], in_=C.ps[pb][:, 0:128]), [("ps", pb)], [("v", kt)])

            for tb in range(4):
                proj_block(0, True, tb, None, tb, qT[:, tb * 512:(tb + 1) * 512], ("qT", tb), tb)
            for tb in range(4):
                proj_block(1, True, tb, None, 4 + tb, kT[:, tb * 512:(tb + 1) * 512], ("kT", tb), 4 + tb)
            for kt in range(16):
                v_tile(kt, True, None, None)
            for tb in range(4):
                ub = uocnt[0] % 2
                uocnt[0] += 1
                S.dma("sp", lambda e, ub=ub, tb=tb: e.dma_start(out=uTo[ub][:], in_=uTo_d[:, :, tb * 512:(tb + 1) * 512]), (), [("uTo", ub)])
                proj_block(1, False, tb, ub, 8 + tb, kT[:, (4 + tb) * 512:(5 + tb) * 512], ("kT", 4 + tb), 8 + tb)
                for off in range(4):
                    v_tile(16 + tb * 4 + off, False, ub, off)
            for m in range(2):
                S.op("dve", lambda e, m=m: e.tensor_reduce(out=nstat[:, m, 12:13], in_=nstat[:, m, 0:4], axis=AX.X, op=ALU.max), [("nstat",)], [("nstat",)])
                S.op("dve", lambda e, m=m: e.tensor_reduce(out=nstat[:, m, 13:14], in_=nstat[:, m, 4:12], axis=AX.X, op=ALU.max), [("nstat",)], [("nstat",)])
                S.op("dve", lambda e, m=m: e.tensor_tensor(out=nstat[:, m, 14:15], in0=nstat[:, m, 12:13], in1=nstat[:, m, 13:14], op=ALU.mult),
                     [("nstat",)], [("nstat",)])
                S.op("act", lambda e, m=m: e.activation(out=nstat[:, m, 15:16], in_=nstat[:, m, 14:15], func=AF.Sqrt, scale=scale * scale),
                     [("nstat",)], [("nstat",)])
                S.op("dve", lambda e, m=m: e.tensor_scalar(out=nbias[:, m:m + 1], in0=nstat[:, m, 15:16], scalar1=-1.0, scalar2=None, op0=ALU.mult),
                     [("nstat",)], [("nbias", m)])
            slope = SLOPES[h]
            for bi in range(2):
                for cchunk in range(8):
                    ib = cchunk % 2
                    c0 = cchunk * 496
                    if bi == 0:
                        S.op("pool", lambda e, ib=ib, c0=c0: e.iota(itmp[ib][:], pattern=[[1, 496]], base=c0 - 1920, channel_multiplier=-1,
                                                                    allow_small_or_imprecise_dtypes=True), (), [("itmp", ib)])
                        S.op("act", lambda e, ib=ib: e.activation(out=itmp[ib][:], in_=itmp[ib][:], func=AF.Abs), [("itmp", ib)], [("itmp", ib)])
                    else:
                        S.op("pool", lambda e, ib=ib, c0=c0: e.iota(itmp[ib][:], pattern=[[-1, 496]], base=4095 - c0, channel_multiplier=-1,
                                                                    allow_small_or_imprecise_dtypes=True), (), [("itmp", ib)])
                    S.op("act", lambda e, ib=ib, bi=bi, c0=c0, slope=slope: e.activation(out=band[bi][:, c0:c0 + 496], in_=itmp[ib][:],
                                                                                 func=AF.Exp, scale=-slope), [("itmp", ib)], [("band", bi)])
            for qb in range(4):
                q0 = qb * 512
                for m in range(2):
                    ms = slice(m * 64, (m + 1) * 64)
                    act_kts = []
                    for kt in range(32):
                        if kt < 16:
                            k0 = kt * 128
                            if k0 > q0 + 511:
                                dmin = k0 - (q0 + 511)
                            elif k0 + 127 < q0:
                                dmin = q0 - (k0 + 127)
                            else:
                                dmin = 0
                        else:
                            k0 = (kt - 16) * 128
                            dmin = 4095 - (q0 + 511) - (k0 + 127)
                        if slope * dmin < SKIP_T:
                            act_kts.append(kt)
                    LAG = 2
                    nact = len(act_kts)
                    pts = {}

                    def emit_pv(ai):
                        kt = act_kts[ai]
                        pt = pts[ai]
                        for qt in range(4):
                            S.op("pe", lambda e, pt=pt, qt=qt, kt=kt, ai=ai: e.matmul(
                                C.ps[2 + qt][:, 0:129], PT[pt][:, qt * 128:(qt + 1) * 128], v[:, kt, 0:129],
                                start=(ai == 0), stop=(ai == nact - 1)), [("PT", pt), ("v", kt), ("vone",)], [("ps", 2 + qt)])

                    for ai, kt in enumerate(act_kts):
                        sbk = scnt[0] % 2
                        scnt[0] += 1
                        pt = ptc[0] % 4
                        ptc[0] += 1
                        pts[ai] = pt
                        S.op("pe", lambda e, ms=ms, kt=kt, q0=q0, sbk=sbk: e.matmul(
                            C.ps[sbk][:], kT[ms, kt * 128:(kt + 1) * 128], qT[ms, q0:q0 + 512], start=True, stop=True),
                            [("kT", kt // 4), ("qT", qb)], [("ps", sbk)])
                        S.op("act", lambda e, sbk=sbk, pt=pt, m=m: e.activation(out=PT[pt][:], in_=C.ps[sbk][:], func=AF.Exp,
                                                                              bias=nbias[:, m:m + 1], scale=scale),
                             [("ps", sbk), ("nbias", m)], [("PT", pt)])
                        if kt < 16:
                            st = q0 - kt * 128 + 1920
                            bsl = band[0][:, st:st + 512]
                        else:
                            st = q0 + (kt - 16) * 128
                            bsl = band[1][:, st:st + 512]
                        bi = 0 if kt < 16 else 1
                        S.op("dve", lambda e, pt=pt, bsl=bsl: e.tensor_tensor(out=PT[pt][:], in0=PT[pt][:], in1=bsl, op=ALU.mult),
                             [("PT", pt), ("band", bi)], [("PT", pt)])
                        if ai >= LAG:
                            emit_pv(ai - LAG)
                    for ai in range(max(0, nact - LAG), nact):
                        emit_pv(ai)
                    for qt in range(4):
                        col = m * 4 + qt
                        S.op("dve", lambda e, qt=qt, col=col: e.reciprocal(out=rs[:, col:col + 1], in_=C.ps[2 + qt][:, 128:129]),
                             [("ps", 2 + qt)], [("rs", col)])
                        if m == 0:
                            S.op("dve", lambda e, qt=qt, col=col: e.tensor_scalar(out=osum[qt][:], in0=C.ps[2 + qt][:, 0:128],
                                                                               scalar1=rs[:, col:col + 1], scalar2=None, op0=ALU.mult),
                                 [("ps", 2 + qt), ("rs", col)], [("osum", qt)])
                        else:
                            ob = qt % 2
                            S.op("dve", lambda e, qt=qt, col=col, ob=ob: e.tensor_scalar(out=otmp[ob][:], in0=C.ps[2 + qt][:, 0:128],
                                                                                     scalar1=rs[:, col:col + 1], scalar2=lams[:, 2:3],
                                                                                     op0=ALU.mult, op1=ALU.mult),
                                 [("ps", 2 + qt), ("rs", col), ("lams",)], [("otmp", ob)])
                            S.op("pool", lambda e, qt=qt, ob=ob: e.tensor_tensor(out=osum[qt][:], in0=osum[qt][:], in1=otmp[ob][:], op=ALU.add),
                                 [("osum", qt), ("otmp", ob)], [("osum", qt)])
                for qt in range(4):
                    col = 8 + qt
                    ob = qt % 2
                    S.op("act", lambda e, qt=qt, col=col, ob=ob: e.activation(out=otmp[ob][:], in_=osum[qt][:], func=AF.Square, accum_out=rs[:, col:col + 1]),
                         [("osum", qt)], [("otmp", ob), ("rs", col)])
                    S.op("act", lambda e, col=col: e.activation(out=rs[:, col:col + 1], in_=rs[:, col:col + 1], func=AF.Sqrt,
                                                               bias=C.epsc[:, 0:1], scale=1.0 / 128.0), [("rs", col), ("epsc",)], [("rs", col)])
                    S.op("dve", lambda e, col=col: e.reciprocal(out=rs[:, col:col + 1], in_=rs[:, col:col + 1]), [("rs", col)], [("rs", col)])
                    S.op("dve", lambda e, qt=qt, col=col, ob=ob: e.scalar_tensor_tensor(out=onb[ob][:], in0=osum[qt][:], scalar=rs[:, col:col + 1],
                                                                                    in1=subg[:], op0=ALU.mult, op1=ALU.mult),
                         [("osum", qt), ("rs", col), ("subg",)], [("onb", ob)])
                    pb = proj_bank()
                    pbv = C.ps[pb][:].bitcast(BF16)
                    S.op("pe", lambda e, ob=ob, pbv=pbv: e.transpose(pbv[:, 0:128], onb[ob][:], C.identb[:]),
                         [("onb", ob), ("identb",)], [("ps", pb)])
                    tcol = q0 + qt * 128
                    S.op("act", lambda e, pbv=pbv, tcol=tcol: e.copy(out=oT[:, tcol:tcol + 128], in_=pbv[:, 0:128]),
                         [("ps", pb)], [("oT", tcol // 128)])
            for t in range(NT):
                for half in range(2):
                    pb = proj_bank()
                    S.op("pe", lambda e, t=t, half=half, wb=wb, pb=pb: e.matmul(
                        C.ps[pb][:], oT[:, t * 128:(t + 1) * 128], wo[wb][:, half * 512:(half + 1) * 512], start=True, stop=True),
                        [("oT", t), ("wo", wb)], [("ps", pb)])
                    xs = C.x[:, t, half * 512:(half + 1) * 512]
                    if h == 0:
                        S.op("dve", lambda e, xs=xs, pb=pb: e.scalar_tensor_tensor(out=xs, in0=xs, scalar=ALPHA, in1=C.ps[pb][:],
                                                                                 op0=ALU.mult, op1=ALU.add), [("x", t), ("ps", pb)], [("x", t)])
                    else:
                        S.op("dve", lambda e, xs=xs, pb=pb: e.tensor_tensor(out=xs, in0=xs, in1=C.ps[pb][:], op=ALU.add),
                             [("x", t), ("ps", pb)], [("x", t)])
        S.barrier()


def _op(S, eng, f, reads, writes):
    return S.op(eng, f, reads, writes)


def MM(S, out, lhsT, rhs, start, stop, reads, writes):
    S.op("pe", lambda e: e.matmul(out, lhsT, rhs, start=start, stop=stop), reads, writes)


def TT(S, eng, out, in0, in1, op, reads, writes):
    S.op(eng, lambda e: e.tensor_tensor(out=out, in0=in0, in1=in1, op=op), reads, writes)


def TS(S, eng, out, in0, s1, s2, op0, op1, reads, writes):
    if op1 is None:
        S.op(eng, lambda e: e.tensor_scalar(out=out, in0=in0, scalar1=s1, scalar2=None, op0=op0), reads, writes)
    else:
        S.op(eng, lambda e: e.tensor_scalar(out=out, in0=in0, scalar1=s1, scalar2=s2, op0=op0, op1=op1), reads, writes)


def ACTF(S, out, in_, func, reads, writes, bias=None, scale=None, accum_out=None):
    kw = {}
    if bias is not None:
        kw["bias"] = bias
    if scale is not None:
        kw["scale"] = scale
    if accum_out is not None:
        kw["accum_out"] = accum_out
    S.op("act", lambda e: e.activation(out=out, in_=in_, func=func, **kw), reads, writes)


def TR(S, out, in_, ident, reads, writes):
    S.op("pe", lambda e: e.transpose(out, in_, ident), reads, writes)


def phase_ssm_pass(nc, S, C, p, win_d, wdt_d, dtb_d, alog_d, convw_d, convb_d, dsk_d, normg_d, wout_d,
                   y1_d, sin_d, sout_d, scale_x):
    last = (p == 1)
    IDN, TLE, TGE, TGT, TLT = 0, 1, 2, 3, 4
    BT_, CPB, NBLK = 256, 2, 8
    Tm = C.tri[:, TLE if p == 0 else TGE, :]
    Um = C.tri[:, TGT if p == 0 else TLT, :]
    ones_f = None
    with ExitStack() as es:
        sb = lambda name, shape, dt: es.enter_context(nc.sbuf_tensor(_U() + "s%d_%s" % (p, name), shape, dt))
        xbcT = sb("xbcT", [128, 24, BT_], BF16)
        pre = [sb("pre%d" % i, [128, BT_ + 4], F32) for i in range(2)]
        cva = [sb("cva%d" % i, [128, BT_], F32) for i in range(2)]
        wfc = [sb("wfc%d" % i, [128, KC, 128], BF16) for i in range(2)]
        wdt = sb("wdt", [128, KC, 32], BF16)
        convw = sb("convw", [128, 24, 3], F32)
        convb = sb("convb", [128, 24], F32)
        rows = sb("rows", [128, 4, 32], F32)
        onesf = sb("onesf", [128, 128], F32)
        sm = [sb("sm%d" % i, [128, 8, 32], F32) for i in range(2)]
        Rb = sb("Rb", [128, 8, 128], F32)
        LT = [sb("LT%d" % i, [128, 8, 128], BF16) for i in range(2)]
        Gm = [sb("Gm%d" % i, [128, 128], BF16) for i in range(2)]
        Xtok = [sb("Xtok%d" % i, [128, 512], BF16) for i in range(2)]
        Btok = [sb("Btok%d" % i, [128, 128], BF16) for i in range(2)]
        Xt1 = [sb("Xt1%d" % i, [128, 512], BF16) for i in range(2)]
        Xt2 = [sb("Xt2%d" % i, [128, 512], BF16) for i in range(2)]
        Sst = sb("Sst", [128, 4, 512], F32)
        Sb = sb("Sb", [128, 4, 512], BF16)
        ych = sb("ych", [128, 2048], F32)
        t1 = sb("t1", [128, 512], F32)
        if last:
            wz = [sb("wz%d" % i, [128, KC, 128], BF16) for i in range(2)]
            sz = sb("sz", [128, CPB, 2048], BF16)
            yT = sb("yT", [128, 16, BT_], BF16)
            wo = [sb("wo%d" % i, [128, D], BF16) for i in range(2)]
            ngc = sb("ngc", [128, 16], F32)
            gs = sb("gs", [128, 16], F32)
        psb = [C.ps[i][:].bitcast(BF16) for i in range(8)]
        wv_ = win_d.rearrange("(c q) n -> q c n", q=128)

        S.dma("sp", lambda e: e.dma_start(out=convw[:], in_=convw_d), (), [("convw",)])
        S.dma("sp", lambda e: e.dma_start(out=convb[:], in_=convb_d), (), [("convb",)])
        S.dma("pool", lambda e: e.dma_start(out=wdt[:], in_=wdt_d[p].rearrange("(c q) n -> q c n", q=128)), (), [("wdt",)])
        S.dma("sp", lambda e: e.dma_start(out=rows[:, 0, :], in_=dtb_d[p:p + 1, :].to_broadcast([128, 32])), (), [("rows", 0)])
        S.dma("sp", lambda e: e.dma_start(out=rows[:, 1, :], in_=alog_d[p:p + 1, :].to_broadcast([128, 32])), (), [("rows", 1)])
        S.dma("sp", lambda e: e.dma_start(out=rows[:, 2, :], in_=dsk_d.to_broadcast([128, 32])), (), [("rows", 2)])
        ACTF(S, rows[:, 1, :], rows[:, 1, :], AF.Exp, [("rows", 1)], [("rows", 1)])
        TS(S, "dve", rows[:, 1, :], rows[:, 1, :], -1.0, None, ALU.mult, None, [("rows", 1)], [("rows", 1)])
        S.op("dve", lambda e: e.memset(onesf[:], 1.0), (), [("onesf",)])
        if scale_x:
            for t in range(NT):
                TS(S, "pool", C.x[:, t, :], C.x[:, t, :], ALPHA, None, ALU.mult, None, [("x", t)], [("x", t)])
        if p == 0:
            S.op("dve", lambda e: e.memset(Sst[:], 0.0), (), [("Sst", g) for g in range(4)])
            S.op("pool", lambda e: e.memset(Sb[:], 0.0), (), [("Sb", g) for g in range(4)])
        else:
            S.dma("sp", lambda e: e.dma_start(out=Sst[:], in_=sin_d), (), [("Sst", g) for g in range(4)])
            for g in range(4):
                S.op("act", lambda e, g=g: e.copy(out=Sb[:, g, :], in_=Sst[:, g, :]), [("Sst", g)], [("Sb", g)])
            S.dma("sp", lambda e: e.dma_start(out=ngc[:], in_=normg_d), (), [("ngc",)])
        pro_done = set()

        def prologue(c):
            pro_done.add(c)
            cp = c % 2
            smc = sm[cp]
            tcol = 1 + c * 128
            R = lambda i: ("sm", cp, i)
            for k in range(KC):
                MM(S, C.ps[3][:, 0:32], C.uT[:, k, tcol:tcol + 128], wdt[:, k, :], k == 0, k == KC - 1, [("wdt",), ("uT", c)], [("ps", 3)])
            TT(S, "dve", smc[:, 0, :], C.ps[3][:, 0:32], rows[:, 0, :], ALU.add, [("ps", 3), ("rows", 0)], [R(0)])
            ACTF(S, smc[:, 0, :], smc[:, 0, :], AF.Exp, [R(0)], [R(0)])
            ACTF(S, smc[:, 0, :], smc[:, 0, :], AF.Ln, [R(0)], [R(0)], bias=1.0)
            TT(S, "dve", smc[:, 1, :], smc[:, 0, :], rows[:, 1, :], ALU.mult, [R(0), ("rows", 1)], [R(1)])
            MM(S, C.ps[3][:, 32:64], Tm, smc[:, 1, :], True, True, [("tri",), R(1)], [("ps", 3)])
            MM(S, C.ps[3][:, 64:96], onesf[:], smc[:, 1, :], True, True, [("onesf",), R(1)], [("ps", 3)])
            S.op("act", lambda e: e.copy(out=smc[:, 2, :], in_=C.ps[3][:, 32:64]), [("ps", 3)], [R(2)])
            ACTF(S, smc[:, 3, :], C.ps[3][:, 32:64], AF.Exp, [("ps", 3)], [R(3)])
            ACTF(S, smc[:, 5, :], C.ps[3][:, 64:96], AF.Exp, [("ps", 3)], [R(5)])
            TT(S, "dve", smc[:, 6, :], C.ps[3][:, 64:96], smc[:, 2, :], ALU.subtract, [("ps", 3), R(2)], [R(6)])
            ACTF(S, smc[:, 6, :], smc[:, 6, :], AF.Exp, [R(6)], [R(6)])
            TT(S, "dve", smc[:, 4, :], smc[:, 6, :], smc[:, 0, :], ALU.mult, [R(6), R(0)], [R(4)])

        blocks = range(NBLK) if p == 0 else range(NBLK - 1, -1, -1)
        wcnt = [0]
        for blk in blocks:
            c0 = blk * BT_
            for fc in range(24):
                wb = wcnt[0] % 2
                pa = wcnt[0] % 2
                wcnt[0] += 1
                cs = slice(2048 + fc * 128, 2048 + (fc + 1) * 128)
                S.dma("pool", lambda e, wb=wb, cs=cs: e.dma_start(out=wfc[wb][:], in_=wv_[:, :, cs]), (), [("wfc", wb)])
                ur = [("uT", min(max(t, 0), NT - 1)) for t in range(blk * CPB - 1, blk * CPB + CPB + 1)] + [("uThalo",)]
                for k in range(KC):
                    MM(S, C.ps[pa][:, 0:BT_], wfc[wb][:, k, :], C.uT[:, k, c0:c0 + BT_], k == 0, k == KC - 1, [("wfc", wb)] + ur, [("ps", pa)])
                for k in range(KC):
                    MM(S, C.ps[2][:, pa * 8:pa * 8 + 2], wfc[wb][:, k, :], C.uT[:, k, c0 + BT_:c0 + BT_ + 2], k == 0, k == KC - 1,
                       [("wfc", wb)] + ur, [("ps", 2)])
                S.op("act", lambda e, pa=pa: e.copy(out=pre[pa][:, 0:BT_], in_=C.ps[pa][:, 0:BT_]), [("ps", pa)], [("pre", pa)])
                S.op("dve", lambda e, pa=pa: e.tensor_copy(out=pre[pa][:, BT_:BT_ + 2], in_=C.ps[2][:, pa * 8:pa * 8 + 2]), [("ps", 2)], [("pre", pa)])
                TS(S, "pool", cva[pa][:], pre[pa][:, 0:BT_], convw[:, fc, 0:1], None, ALU.mult, None, [("pre", pa), ("convw",)], [("cva", pa)])
                S.op("dve", lambda e, pa=pa, fc=fc: e.scalar_tensor_tensor(out=cva[pa][:], in0=pre[pa][:, 1:BT_ + 1], scalar=convw[:, fc, 1:2],
                                                                         in1=cva[pa][:], op0=ALU.mult, op1=ALU.add),
                     [("pre", pa), ("cva", pa), ("convw",)], [("cva", pa)])
                S.op("dve", lambda e, pa=pa, fc=fc: e.scalar_tensor_tensor(out=cva[pa][:], in0=pre[pa][:, 2:BT_ + 2], scalar=convw[:, fc, 2:3],
                                                                         in1=cva[pa][:], op0=ALU.mult, op1=ALU.add),
                     [("pre", pa), ("cva", pa), ("convw",)], [("cva", pa)])
                ACTF(S, xbcT[:, fc, :], cva[pa][:], AF.Silu, [("cva", pa), ("convb",)], [("xbcT", fc)], bias=convb[:, fc:fc + 1])
            if last:
                for zc in range(16):
                    zb = zc % 2
                    cs = slice(zc * 128, (zc + 1) * 128)
                    S.dma("pool", lambda e, zb=zb, cs=cs: e.dma_start(out=wz[zb][:], in_=wv_[:, :, cs]), (), [("wz", zb)])
                    for cc in range(CPB):
                        tcol = 1 + c0 + cc * 128
                        for k in range(KC):
                            MM(S, C.ps[7][:, 0:128], C.uT[:, k, tcol:tcol + 128], wz[zb][:, k, :], k == 0, k == KC - 1,
                               [("wz", zb), ("uT", blk * CPB + cc)], [("ps", 7)])
                        ACTF(S, sz[:, cc, zc * 128:(zc + 1) * 128], C.ps[7][:, 0:128], AF.Silu, [("ps", 7)], [("sz", cc)])
            chunks = list(range(CPB)) if p == 0 else list(range(CPB - 1, -1, -1))
            for cc in chunks:
                c = blk * CPB + cc
                off = cc * 128
                if c not in pro_done:
                    prologue(c)
                nxt = c + 1 if p == 0 else c - 1
                cp = c % 2
                smc = sm[cp]
                if last:
                    S.dma("sp", lambda e, c=c: e.dma_start(out=ych[:], in_=y1_d[c]), (), [("ych", g) for g in range(4)])

                def stageA(g, off=off, smc=smc, cp=cp):
                    g2 = g % 2
                    hs = slice(8 * g, 8 * g + 8)
                    BT = xbcT[:, 16 + g, off:off + 128]
                    CT = xbcT[:, 20 + g, off:off + 128]
                    TT(S, "dve", Rb[:], Tm.unsqueeze(1).to_broadcast([128, 8, 128]), smc[:, 1, hs].unsqueeze(2).to_broadcast([128, 8, 128]),
                       ALU.mult, [("tri",), ("sm", cp, 1)], [("Rb",)])
                    for j in range(2):
                        MM(S, C.ps[4 + j][:], Um, Rb[:, 4 * j:4 * j + 4, :].rearrange("q a b -> q (a b)"), True, True, [("tri",), ("Rb",)], [("ps", 4 + j)])
                        ACTF(S, LT[g2][:, 4 * j:4 * j + 4, :].rearrange("q a b -> q (a b)"), C.ps[4 + j][:], AF.Exp, [("ps", 4 + j)], [("LT", g2)])
                    MM(S, C.ps[6][:, 0:128], BT, CT, True, True, [("xbcT", 16 + g), ("xbcT", 20 + g)], [("ps", 6)])
                    TT(S, "dve", Gm[g2][:], C.ps[6][:, 0:128], Tm, ALU.mult, [("ps", 6), ("tri",)], [("Gm", g2)])
                    TT(S, "dve", LT[g2][:], LT[g2][:], Gm[g2][:].unsqueeze(1).to_broadcast([128, 8, 128]), ALU.mult, [("LT", g2), ("Gm", g2)], [("LT", g2)])
                    for j in range(4):
                        TR(S, psb[7][:, j * 128:(j + 1) * 128], xbcT[:, 4 * g + j, off:off + 128], C.identb[:], [("xbcT", 4 * g + j), ("identb",)], [("ps", 7)])
                    TR(S, psb[7][:, 512:640], xbcT[:, 16 + g, off:off + 128], C.identb[:], [("xbcT", 16 + g), ("identb",)], [("ps", 7)])
                    S.op("act", lambda e: e.copy(out=Xtok[g2][:], in_=psb[7][:, 0:512]), [("ps", 7)], [("Xtok", g2)])
                    S.op("act", lambda e: e.copy(out=Btok[g2][:], in_=psb[7][:, 512:640]), [("ps", 7)], [("Btok", g2)])
                    X3 = Xtok[g2][:].rearrange("q (a b) -> q a b", a=8)
                    TT(S, "dve", Xt1[g2][:].rearrange("q (a b) -> q a b", a=8), X3, smc[:, 0, hs].unsqueeze(2).to_broadcast([128, 8, 64]), ALU.mult,
                       [("Xtok", g2), ("sm", cp, 0)], [("Xt1", g2)])
                    TT(S, "pool", Xt2[g2][:].rearrange("q (a b) -> q a b", a=8), X3, smc[:, 4, hs].unsqueeze(2).to_broadcast([128, 8, 64]), ALU.mult,
                       [("Xtok", g2), ("sm", cp, 4)], [("Xt2", g2)])

                def stageB(g, off=off, smc=smc, cp=cp):
                    g2 = g % 2
                    hs = slice(8 * g, 8 * g + 8)
                    CT = xbcT[:, 20 + g, off:off + 128]
                    X3 = Xtok[g2][:].rearrange("q (a b) -> q a b", a=8)
                    for hh in range(8):
                        MM(S, C.ps[0][:, hh * 64:(hh + 1) * 64], LT[g2][:, hh, :], Xt1[g2][:, hh * 64:(hh + 1) * 64], True, True,
                           [("LT", g2), ("Xt1", g2)], [("ps", 0)])
                    MM(S, C.ps[1][:], CT, Sb[:, g, :], True, True, [("xbcT", 20 + g), ("Sb", g)], [("ps", 1)])
                    MM(S, C.ps[2][:], Btok[g2][:], Xt2[g2][:], True, True, [("Btok", g2), ("Xt2", g2)], [("ps", 2)])
                    TT(S, "dve", t1[:].rearrange("q (a b) -> q a b", a=8), C.ps[1][:].rearrange("q (a b) -> q a b", a=8),
                       smc[:, 3, hs].unsqueeze(2).to_broadcast([128, 8, 64]), ALU.mult, [("ps", 1), ("sm", cp, 3)], [("t1",)])
                    yg = ych[:, g * 512:(g + 1) * 512]
                    if not last:
                        TT(S, "dve", yg, t1[:], C.ps[0][:], ALU.add, [("t1",), ("ps", 0)], [("ych", g)])
                        TT(S, "pool", t1[:].rearrange("q (a b) -> q a b", a=8), X3, rows[:, 2, hs].unsqueeze(2).to_broadcast([128, 8, 64]), ALU.mult,
                           [("Xtok", g2), ("rows", 2), ("t1",)], [("t1",)])
                        TT(S, "pool", yg, yg, t1[:], ALU.add, [("t1",), ("ych", g)], [("ych", g)])
                    else:
                        TT(S, "dve", t1[:], t1[:], C.ps[0][:], ALU.add, [("t1",), ("ps", 0)], [("t1",)])
                        TT(S, "pool", yg, yg, t1[:], ALU.add, [("t1",), ("ych", g)], [("ych", g)])
                    TT(S, "dve", Sst[:, g, :].rearrange("q (a b) -> q a b", a=8), Sst[:, g, :].rearrange("q (a b) -> q a b", a=8),
                       smc[:, 5, hs].unsqueeze(2).to_broadcast([128, 8, 64]), ALU.mult, [("Sst", g), ("sm", cp, 5)], [("Sst", g)])
                    TT(S, "dve", Sst[:, g, :], Sst[:, g, :], C.ps[2][:], ALU.add, [("Sst", g), ("ps", 2)], [("Sst", g)])
                    S.op("act", lambda e, g=g: e.copy(out=Sb[:, g, :], in_=Sst[:, g, :]), [("Sst", g)], [("Sb", g)])

                stageA(0)
                stageA(1)
                if 0 <= nxt < NT:
                    prologue(nxt)
                stageB(0)
                stageA(2)
                stageB(1)
                stageA(3)
                stageB(2)
                stageB(3)
                if not last:
                    S.dma("sp", lambda e, c=c: e.dma_start(out=y1_d[c], in_=ych[:]), [("ych", g) for g in range(4)], [("y1d", c)])
                else:
                    TT(S, "dve", ych[:], ych[:], sz[:, cc, :], ALU.mult, [("ych", g) for g in range(4)] + [("sz", cc)], [("ych", g) for g in range(4)])
                    for g in range(4):
                        ACTF(S, t1[:], ych[:, g * 512:(g + 1) * 512], AF.Square, [("ych", g)], [("t1",), ("gs", g)], accum_out=gs[:, g:g + 1])
                        ACTF(S, gs[:, g:g + 1], gs[:, g:g + 1], AF.Sqrt, [("gs", g), ("epsc",)], [("gs", g)], bias=C.epsc[:, 0:1], scale=1.0 / 512.0)
                        S.op("dve", lambda e, g=g: e.reciprocal(out=gs[:, g:g + 1], in_=gs[:, g:g + 1]), [("gs", g)], [("gs", g)])
                        TS(S, "dve", sz[:, cc, g * 512:(g + 1) * 512], ych[:, g * 512:(g + 1) * 512], gs[:, g:g + 1], None, ALU.mult, None,
                           [("ych", g), ("gs", g)], [("sz", cc)])
                    for q4 in range(4):
                        for j in range(4):
                            ch = q4 * 4 + j
                            TR(S, psb[6][:, j * 128:(j + 1) * 128], sz[:, cc, ch * 128:(ch + 1) * 128], C.identb[:], [("sz", cc), ("identb",)], [("ps", 6)])
                        S.op("act", lambda e, q4=q4, off=off: e.copy(out=yT[:, 4 * q4:4 * q4 + 4, off:off + 128],
                                                                    in_=psb[6][:, 0:512].rearrange("q (a b) -> q a b", a=4)),
                             [("ps", 6)], [("yT", cc)])
            if last:
                for ch in range(16):
                    wb = ch % 2
                    S.dma("pool", lambda e, wb=wb, ch=ch: e.dma_start(out=wo[wb][:], in_=wout_d[ch * 128:(ch + 1) * 128, :]), (), [("wo", wb)])
                    TS(S, "pool", wo[wb][:], wo[wb][:], ngc[:, ch:ch + 1], None, ALU.mult, None, [("wo", wb), ("ngc",)], [("wo", wb)])
                    TT(S, "pool", wo[wb][:], wo[wb][:], C.gh[:, 1, :], ALU.mult, [("wo", wb), ("gh", 1, 0), ("gh", 1, 1)], [("wo", wb)])
                    for cc in range(CPB):
                        t = blk * CPB + cc
                        for half in range(2):
                            pb = 4 + ((cc * 2 + half) % 2)
                            MM(S, C.ps[pb][:], yT[:, ch, cc * 128:(cc + 1) * 128], wo[wb][:, half * 512:(half + 1) * 512], True, True,
                               [("yT", cc), ("wo", wb)], [("ps", pb)])
                            xs = C.x[:, t, half * 512:(half + 1) * 512]
                            TT(S, "dve", xs, xs, C.ps[pb][:], ALU.add, [("x", t), ("ps", pb)], [("x", t)])
        if not last:
            S.dma("sp", lambda e: e.dma_start(out=sout_d, in_=Sst[:]), [("Sst", g) for g in range(4)], [("soutd",)])
        S.barrier()


def _dram_in(nc, name, shape, dt=F32):
    return nc.dram_tensor("d_" + name, list(shape), dt, kind="ExternalInput").ap()


def _dram_out(nc, name, shape, dt=F32):
    return nc.dram_tensor("d_" + name, list(shape), dt, kind="ExternalOutput").ap()


def build_stage(stage):
    nc = bass.Bass("TRN2", target_bir_lowering=False)
    S = Sched()
    C = Ctx()
    I = lambda name, shape, dt=F32: _dram_in(nc, name, shape, dt)
    x_d = I("x_in", [TOK, D])
    ccol_d = I("ccol", [128, KC])
    cpack_d = I("cpack", [128, 5, 128])
    lay = [0] if stage in (0, 1) else ([0, 1] if stage == 2 else [1])
    ada_w = {i: I("ada_w%d" % i, [D, 9216]) for i in lay}
    ada_b = {i: I("ada_b%d" % i, [1, 9216]) for i in lay}
    lng = {i: I("lng%d" % i, [3, D]) for i in lay}
    lnb = {i: I("lnb%d" % i, [3, D]) for i in lay}
    ffn_keys = {0: [(0, 0)], 1: [], 2: [(0, 1), (1, 0)], 3: [(1, 1)]}[stage]
    ffw = {}
    for (i, j) in ffn_keys:
        ffw[(i, j)] = (I("wg%d%d" % (i, j), [D, DFF]), I("wu%d%d" % (i, j), [D, DFF]), I("wd%d%d" % (i, j), [DFF, D]))
    if stage in (1, 2):
        w_in = I("w_in", [D, 5184])
        wdt = I("wdt", [2, D, 32])
        dtb = I("dtb", [2, 32])
        alog = I("alog", [2, 32])
        convw = I("convw", [128, 24, 3])
        convb = I("convb", [128, 24])
        dsk = I("dsk", [1, 32])
        normg = I("normg", [128, 16])
        ssm_wout = I("ssm_wout", [2048, D])
        xh = I("xh", [128, KC])
    if stage == 1:
        y1 = _dram_out(nc, "y1", [16, 128, 2048])
        s_out = _dram_out(nc, "s_out", [128, 4, 512])
        s_in = None
    if stage == 2:
        y1 = I("y1", [16, 128, 2048])
        s_in = I("s_in", [128, 4, 512])
        s_out = None
        uT_out = _dram_out(nc, "uT_out", [128, KC, TOK], BF16)
    if stage == 3:
        uTo = I("uTo", [128, KC, TOK], BF16)
        wqkv = I("wqkv", [D, 3072])
        lam = I("lam", [4, 64])
        subg = I("subg", [1, 128])
        attn_wout = I("attn_wout", [D, D])
    if stage != 1:
        y_d = _dram_out(nc, "y_out", [TOK, D])

    with ExitStack() as es:
        alloc_globals(nc, es, C)
        load_consts(nc, S, C, cpack_d)
        S.op("dve", lambda e: e.memset(C.uT[:, :, 0:1], 0.0), (), [("uT0",)])
        xv = x_d.rearrange("(t q) d -> q t d", q=128)
        for t in range(NT):
            S.dma("sp", lambda e, t=t: e.dma_start(out=C.x[:, t, :], in_=xv[:, t, :]), (), [("x", t)])

        def ffn(i, j):
            wg, wu, wd = ffw[(i, j)]
            phase_ffn(nc, S, C, 2 * j, wg, wu, wd)

        def halo():
            with ExitStack() as hs:
                xht = hs.enter_context(nc.sbuf_tensor(_U() + "xht", [128, KC], F32))
                S.dma("sp", lambda e: e.dma_start(out=xht[:], in_=xh), (), [("xht",)])
                TT(S, "dve", xht[:], xht[:], C.modcol[:, 1, 1, :], ALU.mult, [("xht",), ("modcol", 1)], [("xht",)])
                TT(S, "dve", C.uT[:, :, TOK + 1:TOK + 2], xht[:].unsqueeze(2), C.modcol[:, 1, 0, :].unsqueeze(2), ALU.add,
                   [("xht",), ("modcol", 1)], [("uThalo",)])
                S.barrier()

        if stage == 0:
            phase_mods(nc, S, C, ccol_d, ada_w[0], ada_b[0])
            ln_modulate_stage(nc, S, C, 0, False)
            S.barrier()
            ffn(0, 0)
            ln_modulate_stage(nc, S, C, None, True, lng[0][0:1, :], lnb[0][0:1, :])
        elif stage == 1:
            phase_mods(nc, S, C, ccol_d, ada_w[0], ada_b[0])
            ln_modulate_stage(nc, S, C, 1, False)
            halo()
            phase_ssm_pass(nc, S, C, 0, w_in, wdt, dtb, alog, convw, convb, dsk, normg, ssm_wout, y1, None, s_out, False)
        elif stage == 2:
            phase_mods(nc, S, C, ccol_d, ada_w[0], ada_b[0])
            ln_modulate_stage(nc, S, C, 1, False)
            halo()
            phase_ssm_pass(nc, S, C, 1, w_in, wdt, dtb, alog, convw, convb, dsk, normg, ssm_wout, y1, s_in, None, True)
            ln_modulate_stage(nc, S, C, 2, True, lng[0][1:2, :], lnb[0][1:2, :])
            ffn(0, 1)
            ln_modulate_stage(nc, S, C, None, True, lng[0][2:3, :], lnb[0][2:3, :])
            phase_mods(nc, S, C, ccol_d, ada_w[1], ada_b[1])
            ln_modulate_stage(nc, S, C, 0, False)
            S.barrier()
            ffn(1, 0)
            ln_modulate_stage(nc, S, C, 1, True, lng[1][0:1, :], lnb[1][0:1, :])
            S.dma("sp", lambda e: e.dma_start(out=uT_out, in_=C.uT[:, :, 1:1 + TOK]), [("uT", t) for t in range(NT)], [("uTout",)])
        else:
            phase_mods(nc, S, C, ccol_d, ada_w[1], ada_b[1])
            ln_modulate_stage(nc, S, C, 1, False)
            S.barrier()
            phase_attn(nc, S, C, uTo, wqkv, lam, subg, attn_wout)
            ln_modulate_stage(nc, S, C, 2, True, lng[1][1:2, :], lnb[1][1:2, :])
            ffn(1, 1)
            ln_modulate_stage(nc, S, C, None, True, lng[1][2:3, :], lnb[1][2:3, :])
        S.barrier()
        if stage != 1:
            yv = y_d.rearrange("(t q) d -> q t d", q=128)
            for t in range(NT):
                S.dma("sp", lambda e, t=t: e.dma_start(out=yv[:, t, :], in_=C.x[:, t, :]), [("x", t)], [("yout", t)])
        S.barrier()
        S.emit(nc, es)
    return nc


def _cpack():
    r = np.arange(128)[:, None]
    c = np.arange(128)[None, :]
    mats = [r == c, r <= c, r >= c, r > c, r < c]
    return np.ascontiguousarray(np.stack([m.astype(np.float32) for m in mats], axis=1))


def _col(v):
    v = np.asarray(v)
    return np.ascontiguousarray(v.reshape(-1, 128).T)


_PROGS = {}


def _prog(stage):
    if stage not in _PROGS:
        _PROGS[stage] = build_stage(stage)
    return _PROGS[stage]


DEBUG_STOP = None


def kernel(x, c, ada_w, ada_b, ln_g, ln_b, ffn_w_gate, ffn_w_up, ffn_w_down,
           ssm_w_in, ssm_conv_w, ssm_conv_b, ssm_dt_bias, ssm_a_log, ssm_d, ssm_norm_g, ssm_w_out,
           attn_w_qkv, attn_lambda, attn_subln_g, attn_w_out):
    f = lambda a: np.ascontiguousarray(np.asarray(a, dtype=np.float32))
    x, c = f(x), f(c)
    ada_w, ada_b, ln_g, ln_b = f(ada_w), f(ada_b), f(ln_g), f(ln_b)
    wgate, wup, wdown = f(ffn_w_gate), f(ffn_w_up), f(ffn_w_down)
    w_in, conv_w, conv_b = f(ssm_w_in)[0], f(ssm_conv_w)[0], f(ssm_conv_b)[0]
    dt_bias, a_log, dsk, norm_g, ssm_wout = f(ssm_dt_bias)[0], f(ssm_a_log)[0], f(ssm_d), f(ssm_norm_g)[0], f(ssm_w_out)[0]
    wqkv, lam, subg, attn_wout = f(attn_w_qkv)[0], f(attn_lambda)[0], f(attn_subln_g), f(attn_w_out)[0]
    cores = list(range(8))
    cpack = _cpack()

    def local(arr, core):
        b, h = core // 2, core % 2
        a = arr[b, h * TOK:(h + 1) * TOK]
        return np.ascontiguousarray(a[::-1] if h else a)

    def common(core, lays):
        b = core // 2
        m = {"d_ccol": _col(c[b]), "d_cpack": cpack}
        for i in lays:
            m["d_ada_w%d" % i] = ada_w[i]
            m["d_ada_b%d" % i] = ada_b[i:i + 1]
            m["d_lng%d" % i] = ln_g[i]
            m["d_lnb%d" % i] = ln_b[i]
        return m

    def ffn_in(m, keys):
        for (i, j) in keys:
            m["d_wg%d%d" % (i, j)] = wgate[i, j]
            m["d_wu%d%d" % (i, j)] = wup[i, j]
            m["d_wd%d%d" % (i, j)] = wdown[i, j]

    def ssm_in(m, core, x1loc):
        h = core % 2
        sets = [0, 1] if h == 0 else [1, 0]
        m["d_w_in"] = w_in
        m["d_wdt"] = np.ascontiguousarray(np.stack([w_in[:, 5120 + 32 * s:5152 + 32 * s] for s in sets]))
        m["d_dtb"] = np.ascontiguousarray(np.stack([dt_bias[s] for s in sets]))
        m["d_alog"] = np.ascontiguousarray(np.stack([a_log[s] for s in sets]))
        cw = conv_w if h == 0 else conv_w[::-1]
        m["d_convw"] = np.ascontiguousarray(cw.reshape(3, 24, 128).transpose(2, 1, 0))
        m["d_convb"] = _col(conv_b)
        m["d_dsk"] = dsk
        m["d_normg"] = _col(norm_g)
        m["d_ssm_wout"] = ssm_wout
        m["d_xh"] = _col(x1loc[core ^ 1][TOK - 1])

    maps = []
    for core in cores:
        m = common(core, [0])
        m["d_x_in"] = local(x, core)
        ffn_in(m, [(0, 0)])
        maps.append(m)
    r0 = run_bass_kernel_spmd(_prog(0), maps, core_ids=cores)
    x1 = [np.asarray(r0.results[k]["d_y_out"]) for k in cores]
    if DEBUG_STOP == 0:
        return x1
    maps = []
    for core in cores:
        m = common(core, [0])
        m["d_x_in"] = x1[core]
        ssm_in(m, core, x1)
        maps.append(m)
    r1 = run_bass_kernel_spmd(_prog(1), maps, core_ids=cores)
    y1 = [np.asarray(r1.results[k]["d_y1"]) for k in cores]
    so = [np.asarray(r1.results[k]["d_s_out"]) for k in cores]
    maps = []
    for core in cores:
        m = common(core, [0, 1])
        m["d_x_in"] = x1[core]
        ssm_in(m, core, x1)
        m["d_y1"] = y1[core]
        m["d_s_in"] = so[core ^ 1]
        ffn_in(m, [(0, 1), (1, 0)])
        maps.append(m)
    r2 = run_bass_kernel_spmd(_prog(2), maps, core_ids=cores)
    x4 = [np.asarray(r2.results[k]["d_y_out"]) for k in cores]
    uT4 = [np.asarray(r2.results[k]["d_uT_out"]) for k in cores]
    if DEBUG_STOP == 2:
        return x4
    maps = []
    for core in cores:
        m = common(core, [1])
        m["d_x_in"] = x4[core]
        m["d_uTo"] = uT4[core ^ 1]
        m["d_wqkv"] = wqkv
        m["d_lam"] = lam
        m["d_subg"] = subg
        m["d_attn_wout"] = attn_wout
        ffn_in(m, [(1, 1)])
        maps.append(m)
    r3 = run_bass_kernel_spmd(_prog(3), maps, core_ids=cores)
    out = np.empty((4, 2 * TOK, D), dtype=np.float32)
    for core in cores:
        b, h = core // 2, core % 2
        y = np.asarray(r3.results[core]["d_y_out"])
        out[b, h * TOK:(h + 1) * TOK] = y[::-1] if h else y
    return out
```

```python
from contextlib import ExitStack
import numpy as np
import concourse.bass as bass
import concourse.mybir as mybir
from concourse.bass_utils import run_bass_kernel_spmd

DT = mybir.dt
F32 = DT.float32
BF16 = DT.bfloat16
AF = mybir.ActivationFunctionType
ALU = mybir.AluOpType
AX = mybir.AxisListType

ENGS = ("pe", "act", "dve", "pool", "sp")
N_DMA_SEMS = 40
MAX_OUTSTANDING_DMA = 10 ** 9


class Op:
    __slots__ = ("eng", "fn", "deps", "signal", "seq", "is_dma", "slot", "slot_val",
                 "prev_slot_user", "idx", "is_nop")


class Sched:
    def __init__(self):
        self.ops = {e: [] for e in ENGS}
        self.last_write = {}
        self.readers = {}
        self.n_dma = 0
        self.slot_last = [None] * N_DMA_SEMS
        self.slot_count = [0] * N_DMA_SEMS
        self.all = []
        self.dma_hist = {}

    def _add(self, eng, fn, reads, writes, is_dma):
        o = Op()
        o.eng, o.fn, o.deps, o.signal, o.seq, o.is_dma = eng, fn, [], False, 0, is_dma
        o.slot = o.slot_val = None
        o.prev_slot_user = None
        o.is_nop = False
        o.idx = len(self.all)
        deps = {}
        for r in reads:
            w = self.last_write.get(r)
            if w is not None:
                deps[id(w)] = w
        for r in writes:
            w = self.last_write.get(r)
            if w is not None:
                deps[id(w)] = w
            for rd in self.readers.get(r, ()):
                if rd is not o:
                    deps[id(rd)] = rd
        for d in deps.values():
            if d.eng == eng and not d.is_dma and not is_dma:
                if eng == "pe":
                    continue
            o.deps.append(d)
            d.signal = True
        for r in reads:
            self.readers.setdefault(r, []).append(o)
        for r in writes:
            self.last_write[r] = o
            self.readers[r] = []
        if is_dma:
            q = self.dma_hist.setdefault(eng, [])
            if len(q) >= MAX_OUTSTANDING_DMA:
                o.deps.append(q[-MAX_OUTSTANDING_DMA])
            q.append(o)
            s = self.n_dma % N_DMA_SEMS
            self.n_dma += 1
            o.prev_slot_user = self.slot_last[s]
            self.slot_last[s] = o
            self.slot_count[s] += 1
            o.slot, o.slot_val = s, 16 * self.slot_count[s]
            o.signal = True
        self.ops[eng].append(o)
        self.all.append(o)
        return o

    def op(self, eng, fn, reads=(), writes=()):
        return self._add(eng, fn, reads, writes, False)

    def dma(self, eng, fn, reads=(), writes=()):
        return self._add(eng, fn, reads, writes, True)

    def barrier(self):
        key = ("__barrier__",)
        lasts = []
        for e in ENGS:
            for o in reversed(self.ops[e]):
                if not o.is_dma and not o.is_nop:
                    lasts.append(o)
                    break
        dmas = [o for o in self.slot_last if o is not None]
        for e in ("pe", "act", "dve", "pool", "sp"):
            o = self._add(e, lambda eng: eng.nop(), (), (), False)
            o.is_nop = True
            for d in lasts + dmas:
                if d.eng == e and e == "pe" and not d.is_dma:
                    continue
                o.deps.append(d)
                d.signal = True
        self.last_write = {}
        self.readers = {}

    def emit(self, nc, es):
        eng_sem = {e: es.enter_context(nc.semaphore("prog_" + e)) for e in ENGS}
        dma_sem = [es.enter_context(nc.semaphore("dma%d" % i)) for i in range(N_DMA_SEMS)]
        for e in ENGS:
            n = 0
            for o in self.ops[e]:
                if o.signal and not o.is_dma:
                    n += 1
                    o.seq = n
        block = es.enter_context(nc.Block())
        sched = self

        def body_for(e):
            def body(eng):
                waited_eng = {f: 0 for f in ENGS}
                waited_slot = [0] * N_DMA_SEMS
                for o in sched.ops[e]:
                    need_eng = {}
                    need_slot = {}
                    deps = list(o.deps)
                    if o.is_dma and o.prev_slot_user is not None:
                        deps.append(o.prev_slot_user)
                    for d in deps:
                        if d.is_dma:
                            if d.slot_val > waited_slot[d.slot]:
                                need_slot[d.slot] = max(need_slot.get(d.slot, 0), d.slot_val)
                        else:
                            if d.seq > waited_eng[d.eng]:
                                need_eng[d.eng] = max(need_eng.get(d.eng, 0), d.seq)
                    for f, v in need_eng.items():
                        eng.wait_ge(eng_sem[f], v)
                        waited_eng[f] = v
                    for s, v in need_slot.items():
                        eng.wait_ge(dma_sem[s], v)
                        waited_slot[s] = v
                    ins = o.fn(eng)
                    if o.is_dma:
                        ins.then_inc(dma_sem[o.slot], 16)
                    elif o.signal:
                        ins.then_inc(eng_sem[e], 1)
            return body

        block.tensor(body_for("pe"))
        block.scalar(body_for("act"))
        block.vector(body_for("dve"))
        block.gpsimd(body_for("pool"))
        block.sync(body_for("sp"))


D = 1024
TOK = 2048
NT = TOK // 128
DFF = 2816
NF = DFF // 128
KC = D // 128
ALPHA = (2 * 2) ** 0.25
LN_EPS = 1e-5
FFN_GROUPS = [(0, 4), (4, 4), (8, 4), (12, 4), (16, 4), (20, 2)]


class Ctx:
    pass


_UC = [0]


def _U():
    _UC[0] += 1
    return "u%d_" % _UC[0]


def alloc_globals(nc, es, C):
    C.x = es.enter_context(nc.sbuf_tensor(_U() + "x_res", [128, NT, D], F32))
    C.uT = es.enter_context(nc.sbuf_tensor(_U() + "uT", [128, KC, TOK + 2], BF16))
    C.ident = es.enter_context(nc.sbuf_tensor(_U() + "ident", [128, 128], F32))
    C.identb = es.enter_context(nc.sbuf_tensor(_U() + "identb", [128, 128], BF16))
    C.gh = es.enter_context(nc.sbuf_tensor(_U() + "gh", [128, 3, D], F32))
    C.modcol = es.enter_context(nc.sbuf_tensor(_U() + "modcol", [128, 3, 2, KC], F32))
    C.tri = es.enter_context(nc.sbuf_tensor(_U() + "tri", [128, 5, 128], F32))
    C.stat = es.enter_context(nc.sbuf_tensor(_U() + "stat", [128, 2, 16], F32))
    C.epsc = es.enter_context(nc.sbuf_tensor(_U() + "epsc", [128, 1], F32))
    C.ps = [es.enter_context(nc.psum_tensor("ps%d" % b, [128, 512], F32)) for b in range(8)]


def load_consts(nc, S, C, ident_d):
    S.dma("sp", lambda e: e.dma_start(out=C.tri[:], in_=ident_d), (), [("tri",)])
    S.op("dve", lambda e: e.tensor_copy(out=C.ident[:], in_=C.tri[:, 0, :]), [("tri",)], [("ident",)])
    S.op("dve", lambda e: e.tensor_copy(out=C.identb[:], in_=C.ident[:]), [("ident",)], [("identb",)])
    S.op("dve", lambda e: e.memset(C.epsc[:], LN_EPS), (), [("epsc",)])


def phase_mods(nc, S, C, ccol_d, ada_w_d, ada_b_d):
    with ExitStack() as es:
        ccol = es.enter_context(nc.sbuf_tensor(_U() + "ccol", [128, KC], F32))
        condb = es.enter_context(nc.sbuf_tensor(_U() + "condb", [128, KC], BF16))
        condrep = es.enter_context(nc.sbuf_tensor(_U() + "condrep", [128, KC, 128], BF16))
        wa = [es.enter_context(nc.sbuf_tensor(_U() + "wa%d" % i, [128, KC, 512], BF16)) for i in range(3)]
        brow = [es.enter_context(nc.sbuf_tensor(_U() + "brow%d" % i, [128, 512], F32)) for i in range(3)]
        mrow = [es.enter_context(nc.sbuf_tensor(_U() + "mrow%d" % i, [128, 512], F32)) for i in range(2)]
        S.dma("sp", lambda e: e.dma_start(out=ccol[:], in_=ccol_d), (), [("ccol",)])
        S.op("act", lambda e: e.activation(out=condb[:], in_=ccol[:], func=AF.Silu), [("ccol",)], [("condb",)])
        S.op("dve", lambda e: e.tensor_copy(out=condrep[:], in_=condb[:].unsqueeze(2).to_broadcast([128, KC, 128])),
             [("condb",)], [("condrep",)])
        wv = ada_w_d.rearrange("(c p) n -> p c n", p=128)
        for nb in range(18):
            b3, b2 = nb % 3, nb % 2
            sl = slice(nb * 512, (nb + 1) * 512)
            S.dma("pool", lambda e, b3=b3, sl=sl: e.dma_start(out=wa[b3][:], in_=wv[:, :, sl]), (), [("wa", b3)])
            S.dma("sp", lambda e, b3=b3, sl=sl: e.dma_start(out=brow[b3][:], in_=ada_b_d[0:1, sl].to_broadcast([128, 512])),
                  (), [("brow", b3)])
            pb = nb % 2
            for k in range(KC):
                S.op("pe", lambda e, k=k, b3=b3, pb=pb: e.matmul(C.ps[pb][:], condrep[:, k, :], wa[b3][:, k, :],
                                                                  start=(k == 0), stop=(k == KC - 1)),
                     [("condrep",), ("wa", b3)], [("ps", pb)])
            S.op("dve", lambda e, b2=b2, b3=b3, pb=pb: e.tensor_tensor(out=mrow[b2][:], in0=C.ps[pb][:], in1=brow[b3][:], op=ALU.add),
                 [("ps", pb), ("brow", b3)], [("mrow", b2)])
            sub, kind, half = nb // 6, (nb % 6) // 2, nb % 2
            if kind == 2:
                f = 1.0 if sub == 1 else 0.5
                S.op("dve", lambda e, b2=b2, sub=sub, half=half, f=f: e.tensor_scalar(
                    out=C.gh[:, sub, half * 512:(half + 1) * 512], in0=mrow[b2][:], scalar1=1.0, scalar2=f,
                    op0=ALU.add, op1=ALU.mult), [("mrow", b2)], [("gh", sub, half)])
            else:
                tb = 2 + (nb % 2)
                for j in range(4):
                    S.op("pe", lambda e, j=j, b2=b2, tb=tb: e.transpose(C.ps[tb][:, j * 128:(j + 1) * 128],
                                                                       mrow[b2][:, j * 128:(j + 1) * 128], C.ident[:]),
                         [("mrow", b2), ("ident",)], [("ps", tb)])
                for j in range(4):
                    cidx = half * 4 + j
                    S.op("dve", lambda e, j=j, tb=tb, sub=sub, kind=kind, cidx=cidx: e.tensor_scalar(
                        out=C.modcol[:, sub, kind, cidx:cidx + 1], in0=C.ps[tb][:, j * 128:j * 128 + 1],
                        scalar1=(1.0 if kind == 1 else 0.0), scalar2=None, op0=ALU.add),
                        [("ps", tb)], [("modcol", sub)])
        S.barrier()


def ln_modulate_stage(nc, S, C, sub_next, do_ln, lng_d=None, lnb_d=None, tiles=None):
    es = ExitStack()
    if do_ln:
        C.lng = es.enter_context(nc.sbuf_tensor(_U() + "lng", [128, D], F32))
        C.lnb = es.enter_context(nc.sbuf_tensor(_U() + "lnb", [128, D], F32))
        S.dma("sp", lambda e: e.dma_start(out=C.lng[:], in_=lng_d.to_broadcast([128, D])), (), [("lng",)])
        S.dma("sp", lambda e: e.dma_start(out=C.lnb[:], in_=lnb_d.to_broadcast([128, D])), (), [("lnb",)])
    for t in (tiles if tiles is not None else range(NT)):
        xr = ("x", t)
        if do_ln:
            st = ("stat", t % 2)
            sv = C.stat[:, t % 2, :]
            for h in range(2):
                S.op("dve", lambda e, t=t, h=h, sv=sv: e.bn_stats(out=sv[:, h * 6:(h + 1) * 6], in_=C.x[:, t, h * 512:(h + 1) * 512]),
                     [xr], [st])
            S.op("dve", lambda e, sv=sv: e.bn_aggr(out=sv[:, 12:14], in_=sv[:, 0:12]), [st], [st])
            S.op("act", lambda e, sv=sv: e.activation(out=sv[:, 14:15], in_=sv[:, 13:14], func=AF.Sqrt, bias=C.epsc[:, 0:1], scale=1.0),
                 [st, ("epsc",)], [st])
            S.op("dve", lambda e, sv=sv: e.reciprocal(out=sv[:, 15:16], in_=sv[:, 14:15]), [st], [st])
            S.op("dve", lambda e, t=t, sv=sv: e.tensor_scalar(out=C.x[:, t, :], in0=C.x[:, t, :], scalar1=sv[:, 12:13],
                                                           scalar2=sv[:, 15:16], op0=ALU.subtract, op1=ALU.mult),
                 [xr, st], [xr])
            S.op("dve", lambda e, t=t: e.tensor_tensor(out=C.x[:, t, :], in0=C.x[:, t, :], in1=C.lng[:], op=ALU.mult),
                 [xr, ("lng",)], [xr])
            S.op("dve", lambda e, t=t: e.tensor_tensor(out=C.x[:, t, :], in0=C.x[:, t, :], in1=C.lnb[:], op=ALU.add),
                 [xr, ("lnb",)], [xr])
        if sub_next is not None:
            for half in range(2):
                pb = 2 * (t % 2) + half
                for j in range(4):
                    c = half * 4 + j
                    S.op("pe", lambda e, t=t, c=c, j=j, pb=pb: e.transpose(C.ps[pb][:, j * 128:(j + 1) * 128],
                                                                       C.x[:, t, c * 128:(c + 1) * 128], C.ident[:]),
                         [xr, ("ident",)], [("ps", pb)])
                for j in range(4):
                    c = half * 4 + j
                    S.op("act", lambda e, t=t, c=c, j=j, pb=pb: e.activation(
                        out=C.uT[:, c, 1 + t * 128:1 + (t + 1) * 128], in_=C.ps[pb][:, j * 128:(j + 1) * 128],
                        func=AF.Identity, scale=C.modcol[:, sub_next, 1, c:c + 1], bias=C.modcol[:, sub_next, 0, c:c + 1]),
                        [("ps", pb), ("modcol", sub_next)], [("uT", t)])
    if do_ln:
        S.barrier()
    es.close()


def phase_ffn(nc, S, C, sub, wg_d, wu_d, wd_d):
    with ExitStack() as es:
        wg = [es.enter_context(nc.sbuf_tensor(_U() + "wg%d" % i, [128, KC, 512], BF16)) for i in range(2)]
        wu = [es.enter_context(nc.sbuf_tensor(_U() + "wu%d" % i, [128, KC, 512], BF16)) for i in range(2)]
        wd = [es.enter_context(nc.sbuf_tensor(_U() + "wd%d" % i, [128, 4, D], BF16)) for i in range(2)]
        hT = [es.enter_context(nc.sbuf_tensor(_U() + "hT%d" % i, [128, 4, 512], BF16)) for i in range(2)]
        sg = [es.enter_context(nc.sbuf_tensor(_U() + "sg%d" % i, [128, 512], F32)) for i in range(2)]
        wgv = wg_d.rearrange("(c p) n -> p c n", p=128)
        wuv = wu_d.rearrange("(c p) n -> p c n", p=128)
        wdv = wd_d.rearrange("(c p) n -> p c n", p=128)
        cnt = 0
        ycnt = 0
        for gi, (f0, nf) in enumerate(FFN_GROUPS):
            b = gi % 2
            fs = slice(f0 * 128, (f0 + nf) * 128)
            S.dma("pool", lambda e, b=b, fs=fs, nf=nf: e.dma_start(out=wg[b][:, :, 0:nf * 128], in_=wgv[:, :, fs]), (), [("wg", b)])
            S.dma("pool", lambda e, b=b, fs=fs, nf=nf: e.dma_start(out=wu[b][:, :, 0:nf * 128], in_=wuv[:, :, fs]), (), [("wu", b)])
            S.dma("pool", lambda e, b=b, f0=f0, nf=nf: e.dma_start(out=wd[b][:, 0:nf, :], in_=wdv[:, f0:f0 + nf, :]), (), [("wd", b)])
            for j in range(nf):
                S.op("pool", lambda e, b=b, j=j: e.tensor_tensor(out=wd[b][:, j, :], in0=wd[b][:, j, :], in1=C.gh[:, sub, :], op=ALU.mult),
                     [("wd", b), ("gh", sub, 0), ("gh", sub, 1)], [("wd", b)])
            for tb in range(4):
                hb = tb % 2
                tsl = slice(1 + tb * 512, 1 + (tb + 1) * 512)
                for j in range(nf):
                    pg, pu = (cnt % 2), 2 + (cnt % 2)
                    sgb = cnt % 2
                    cnt += 1
                    ur = [("uT", tb * 4 + q) for q in range(4)]
                    for k in range(KC):
                        S.op("pe", lambda e, k=k, b=b, j=j, pg=pg, tsl=tsl: e.matmul(
                            C.ps[pg][:], wg[b][:, k, j * 128:(j + 1) * 128], C.uT[:, k, tsl], start=(k == 0), stop=(k == KC - 1)),
                            [("wg", b)] + ur, [("ps", pg)])
                    for k in range(KC):
                        S.op("pe", lambda e, k=k, b=b, j=j, pu=pu, tsl=tsl: e.matmul(
                            C.ps[pu][:], wu[b][:, k, j * 128:(j + 1) * 128], C.uT[:, k, tsl], start=(k == 0), stop=(k == KC - 1)),
                            [("wu", b)] + ur, [("ps", pu)])
                    S.op("act", lambda e, pg=pg, sgb=sgb: e.activation(out=sg[sgb][:], in_=C.ps[pg][:], func=AF.Silu),
                         [("ps", pg)], [("sg", sgb)])
                    S.op("dve", lambda e, pu=pu, sgb=sgb, hb=hb, j=j: e.tensor_tensor(
                        out=hT[hb][:, j, :], in0=sg[sgb][:], in1=C.ps[pu][:], op=ALU.mult),
                        [("ps", pu), ("sg", sgb)], [("hT", hb)])
                for tt in range(4):
                    t = tb * 4 + tt
                    for half in range(2):
                        py = 4 + (ycnt % 4)
                        ycnt += 1
                        for j in range(nf):
                            S.op("pe", lambda e, hb=hb, j=j, tt=tt, b=b, half=half, py=py: e.matmul(
                                C.ps[py][:], hT[hb][:, j, tt * 128:(tt + 1) * 128], wd[b][:, j, half * 512:(half + 1) * 512],
                                start=(j == 0), stop=(j == nf - 1)), [("hT", hb), ("wd", b)], [("ps", py)])
                        xs = C.x[:, t, half * 512:(half + 1) * 512]
                        if gi == 0:
                            S.op("dve", lambda e, xs=xs, py=py: e.scalar_tensor_tensor(
                                out=xs, in0=xs, scalar=ALPHA, in1=C.ps[py][:], op0=ALU.mult, op1=ALU.add),
                                [("x", t), ("ps", py)], [("x", t)])
                        else:
                            S.op("dve", lambda e, xs=xs, py=py: e.tensor_tensor(out=xs, in0=xs, in1=C.ps[py][:], op=ALU.add),
                                 [("x", t), ("ps", py)], [("x", t)])
        S.barrier()


NH = 8
SLOPES = [2.0 ** (-8.0 * (h + 1) / NH) for h in range(NH)]
LAMBDA_INIT1 = 0.8 - 0.6 * float(np.exp(-0.3 * 1))
BANDW = 3968
SKIP_T = 105.0


def phase_attn(nc, S, C, uTo_d, wqkv_d, lam_d, subg_d, wout_d):
    scale = 64 ** -0.5
    with ExitStack() as es:
        sb = lambda name, shape, dt: es.enter_context(nc.sbuf_tensor(name, shape, dt))
        uTo = [sb("uTo%d" % i, [128, KC, 512], BF16) for i in range(2)]
        wq = [sb("wqkv%d" % i, [128, KC, 3, 128], BF16) for i in range(2)]
        wo = [sb("wo%d" % i, [128, D], BF16) for i in range(2)]
        qT = sb("qT", [128, TOK], BF16)
        kT = sb("kT", [128, 2 * TOK], BF16)
        v = sb("v_h", [128, 32, 132], BF16)
        band = [sb("band%d" % i, [128, BANDW], BF16) for i in range(2)]
        itmp = [sb("itmp%d" % i, [128, 496], F32) for i in range(2)]
        sq = [sb("sq%d" % i, [128, 512], BF16) for i in range(2)]
        onesb = sb("onesb", [128, 128], BF16)
        nstat = sb("nstat", [128, 2, 16], F32)
        nbias = sb("nbias", [128, 2], F32)
        lamt = sb("lamt", [128, 4, 64], F32)
        lam2 = sb("lam2", [128, 2, 64], F32)
        lams = sb("lams", [128, 4], F32)
        subg = sb("subg", [128, 128], F32)
        PT = [sb("PT%d" % i, [128, 512], BF16) for i in range(4)]
        osum = [sb("osum%d" % i, [128, 128], F32) for i in range(4)]
        otmp = [sb("otmp%d" % i, [128, 128], F32) for i in range(2)]
        rs = sb("rs", [128, 16], F32)
        onb = [sb("onb%d" % i, [128, 128], BF16) for i in range(2)]
        oT = sb("oT", [128, TOK], BF16)

        S.op("dve", lambda e: e.memset(onesb[:], 1.0), (), [("onesb",)])
        S.op("dve", lambda e: e.memset(v[:, :, 128:129], 1.0), (), [("vone",)])
        S.dma("sp", lambda e: e.dma_start(out=lamt[:].rearrange("p a b -> p (a b)"),
                                          in_=lam_d.rearrange("(o a) b -> o (a b)", o=1).to_broadcast([128, 256])), (), [("lamt",)])
        S.dma("sp", lambda e: e.dma_start(out=subg[:], in_=subg_d.to_broadcast([128, 128])), (), [("subg",)])
        S.op("dve", lambda e: e.tensor_tensor(out=lam2[:, 0, :], in0=lamt[:, 0, :], in1=lamt[:, 1, :], op=ALU.mult), [("lamt",)], [("lam2",)])
        S.op("dve", lambda e: e.tensor_tensor(out=lam2[:, 1, :], in0=lamt[:, 2, :], in1=lamt[:, 3, :], op=ALU.mult), [("lamt",)], [("lam2",)])
        S.op("dve", lambda e: e.tensor_reduce(out=lams[:, 0:2], in_=lam2[:], axis=AX.X, op=ALU.add), [("lam2",)], [("lams",)])
        S.op("act", lambda e: e.activation(out=lams[:, 0:2], in_=lams[:, 0:2], func=AF.Exp), [("lams",)], [("lams",)])
        S.op("dve", lambda e: e.tensor_tensor(out=lams[:, 2:3], in0=lams[:, 1:2], in1=lams[:, 0:1], op=ALU.subtract), [("lams",)], [("lams",)])
        S.op("dve", lambda e: e.tensor_scalar(out=lams[:, 2:3], in0=lams[:, 2:3], scalar1=-LAMBDA_INIT1, scalar2=None, op0=ALU.add),
             [("lams",)], [("lams",)])
        S.op("dve", lambda e: e.tensor_scalar(out=subg[:], in0=subg[:], scalar1=(1.0 - LAMBDA_INIT1), scalar2=None, op0=ALU.mult),
             [("subg",)], [("subg",)])
        wv_ = wqkv_d.rearrange("(c p) n -> p c n", p=128)
        pcnt = [0]
        scnt = [0]
        ptc = [0]
        uocnt = [0]

        def proj_bank():
            pcnt[0] += 1
            return 6 + (pcnt[0] % 2)

        for h in range(NH):
            wb = h % 2
            for part in range(3):
                cs = slice(part * 1024 + h * 128, part * 1024 + (h + 1) * 128)
                S.dma("pool", lambda e, wb=wb, part=part, cs=cs: e.dma_start(out=wq[wb][:, :, part, :], in_=wv_[:, :, cs]),
                      (), [("wq", wb)])
            S.dma("pool", lambda e, wb=wb, h=h: e.dma_start(out=wo[wb][:], in_=wout_d[h * 128:(h + 1) * 128, :]), (), [("wo", wb)])
            S.op("pool", lambda e, wb=wb: e.tensor_tensor(out=wo[wb][:], in0=wo[wb][:], in1=C.gh[:, 1, :], op=ALU.mult),
                 [("wo", wb), ("gh", 1, 0), ("gh", 1, 1)], [("wo", wb)])
            def proj_block(part, own, tb, srcbuf, col, dst, dr, blk, wb=wb):
                wqb = wq[wb]
                pb = proj_bank()
                for k in range(KC):
                    if own:
                        src = C.uT[:, k, 1 + tb * 512:1 + (tb + 1) * 512]
                        rr = [("uT", tb * 4 + q) for q in range(4)]
                    else:
                        src = uTo[srcbuf][:, k, :]
                        rr = [("uTo", srcbuf)]
                    S.op("pe", lambda e, k=k, pb=pb, src=src: e.matmul(
                        C.ps[pb][:], wqb[:, k, part, :], src, start=(k == 0), stop=(k == KC - 1)),
                        [("wq", wb)] + rr, [("ps", pb)])
                if blk % 2 == 0:
                    S.op("act", lambda e, pb=pb: e.copy(out=dst, in_=C.ps[pb][:]), [("ps", pb)], [dr])
                else:
                    S.op("dve", lambda e, pb=pb: e.tensor_copy(out=dst, in_=C.ps[pb][:]), [("ps", pb)], [dr])
                sb_ = blk % 2
                S.op("pool", lambda e, sb_=sb_: e.tensor_tensor(out=sq[sb_][:], in0=dst, in1=dst, op=ALU.mult), [dr], [("sq", sb_)])
                for m in range(2):
                    pb2 = proj_bank()
                    ms = slice(m * 64, (m + 1) * 64)
                    S.op("pe", lambda e, ms=ms, sb_=sb_, pb2=pb2: e.matmul(C.ps[pb2][:], onesb[ms, :], sq[sb_][ms, :], start=True, stop=True),
                         [("onesb",), ("sq", sb_)], [("ps", pb2)])
                    S.op("dve", lambda e, m=m, pb2=pb2: e.tensor_reduce(out=nstat[:, m, col:col + 1], in_=C.ps[pb2][:], axis=AX.X, op=ALU.max),
                         [("ps", pb2)], [("nstat",)])

            def v_tile(kt, own, srcbuf, off, wb=wb):
                wqb = wq[wb]
                pb = proj_bank()
                for k in range(KC):
                    if own:
                        src = C.uT[:, k, 1 + kt * 128:1 + (kt + 1) * 128]
                        rr = [("uT", kt)]
                    else:
                        src = uTo[srcbuf][:, k, off * 128:(off + 1) * 128]
                        rr = [("uTo", srcbuf)]
                    S.op("pe", lambda e, k=k, pb=pb, src=src: e.matmul(
                        C.ps[pb][:, 0:128], src, wqb[:, k, 2, :], start=(k == 0), stop=(k == KC - 1)),
                        [("wq", wb)] + rr, [("ps", pb)])
                if kt % 2 == 0:
                    S.op("act", lambda e, pb=pb: e.copy(out=v[:, kt, 0:128], in_=C.ps[pb][:, 0:128]), [("ps", pb)], [("v", kt)])
                else:
                    S.op("dve", lambda e, pb=pb: e.tensor_copy(out=v[:, kt, 0:128], in_=C.ps[pb][:, 0:128]), [("ps", pb)], [("v", kt)])

            for tb in range(4):
                proj_block(0, True, tb, None, tb, qT[:, tb * 512:(tb + 1) * 512], ("qT", tb), tb)
            for tb in range(4):
                proj_block(1, True, tb, None, 4 + tb, kT[:, tb * 512:(tb + 1) * 512], ("kT", tb), 4 + tb)
            for kt in range(16):
                v_tile(kt, True, None, None)
            for tb in range(4):
                ub = uocnt[0] % 2
                uocnt[0] += 1
                S.dma("sp", lambda e, ub=ub, tb=tb: e.dma_start(out=uTo[ub][:], in_=uTo_d[:, :, tb * 512:(tb + 1) * 512]), (), [("uTo", ub)])
                proj_block(1, False, tb, ub, 8 + tb, kT[:, (4 + tb) * 512:(5 + tb) * 512], ("kT", 4 + tb), 8 + tb)
                for off in range(4):
                    v_tile(16 + tb * 4 + off, False, ub, off)
            for m in range(2):
                S.op("dve", lambda e, m=m: e.tensor_reduce(out=nstat[:, m, 12:13], in_=nstat[:, m, 0:4], axis=AX.X, op=ALU.max), [("nstat",)], [("nstat",)])
                S.op("dve", lambda e, m=m: e.tensor_reduce(out=nstat[:, m, 13:14], in_=nstat[:, m, 4:12], axis=AX.X, op=ALU.max), [("nstat",)], [("nstat",)])
                S.op("dve", lambda e, m=m: e.tensor_tensor(out=nstat[:, m, 14:15], in0=nstat[:, m, 12:13], in1=nstat[:, m, 13:14], op=ALU.mult),
                     [("nstat",)], [("nstat",)])
                S.op("act", lambda e, m=m: e.activation(out=nstat[:, m, 15:16], in_=nstat[:, m, 14:15], func=AF.Sqrt, scale=scale * scale),
                     [("nstat",)], [("nstat",)])
                S.op("dve", lambda e, m=m: e.tensor_scalar(out=nbias[:, m:m + 1], in0=nstat[:, m, 15:16], scalar1=-1.0, scalar2=None, op0=ALU.mult),
                     [("nstat",)], [("nbias", m)])
            slope = SLOPES[h]
            for bi in range(2):
                for cchunk in range(8):
                    ib = cchunk % 2
                    c0 = cchunk * 496
                    if bi == 0:
                        S.op("pool", lambda e, ib=ib, c0=c0: e.iota(itmp[ib][:], pattern=[[1, 496]], base=c0 - 1920, channel_multiplier=-1,
                                                                    allow_small_or_imprecise_dtypes=True), (), [("itmp", ib)])
                        S.op("act", lambda e, ib=ib: e.activation(out=itmp[ib][:], in_=itmp[ib][:], func=AF.Abs), [("itmp", ib)], [("itmp", ib)])
                    else:
                        S.op("pool", lambda e, ib=ib, c0=c0: e.iota(itmp[ib][:], pattern=[[-1, 496]], base=4095 - c0, channel_multiplier=-1,
                                                                    allow_small_or_imprecise_dtypes=True), (), [("itmp", ib)])
                    S.op("act", lambda e, ib=ib, bi=bi, c0=c0, slope=slope: e.activation(out=band[bi][:, c0:c0 + 496], in_=itmp[ib][:],
                                                                                 func=AF.Exp, scale=-slope), [("itmp", ib)], [("band", bi)])
            for qb in range(4):
                q0 = qb * 512
                for m in range(2):
                    ms = slice(m * 64, (m + 1) * 64)
                    act_kts = []
                    for kt in range(32):
                        if kt < 16:
                            k0 = kt * 128
                            if k0 > q0 + 511:
                                dmin = k0 - (q0 + 511)
                            elif k0 + 127 < q0:
                                dmin = q0 - (k0 + 127)
                            else:
                                dmin = 0
                        else:
                            k0 = (kt - 16) * 128
                            dmin = 4095 - (q0 + 511) - (k0 + 127)
                        if slope * dmin < SKIP_T:
                            act_kts.append(kt)
                    LAG = 2
                    nact = len(act_kts)
                    pts = {}

                    def emit_pv(ai):
                        kt = act_kts[ai]
                        pt = pts[ai]
                        for qt in range(4):
                            S.op("pe", lambda e, pt=pt, qt=qt, kt=kt, ai=ai: e.matmul(
                                C.ps[2 + qt][:, 0:129], PT[pt][:, qt * 128:(qt + 1) * 128], v[:, kt, 0:129],
                                start=(ai == 0), stop=(ai == nact - 1)), [("PT", pt), ("v", kt), ("vone",)], [("ps", 2 + qt)])

                    for ai, kt in enumerate(act_kts):
                        sbk = scnt[0] % 2
                        scnt[0] += 1
                        pt = ptc[0] % 4
                        ptc[0] += 1
                        pts[ai] = pt
                        S.op("pe", lambda e, ms=ms, kt=kt, q0=q0, sbk=sbk: e.matmul(
                            C.ps[sbk][:], kT[ms, kt * 128:(kt + 1) * 128], qT[ms, q0:q0 + 512], start=True, stop=True),
                            [("kT", kt // 4), ("qT", qb)], [("ps", sbk)])
                        S.op("act", lambda e, sbk=sbk, pt=pt, m=m: e.activation(out=PT[pt][:], in_=C.ps[sbk][:], func=AF.Exp,
                                                                              bias=nbias[:, m:m + 1], scale=scale),
                             [("ps", sbk), ("nbias", m)], [("PT", pt)])
                        if kt < 16:
                            st = q0 - kt * 128 + 1920
                            bsl = band[0][:, st:st + 512]
                        else:
                            st = q0 + (kt - 16) * 128
                            bsl = band[1][:, st:st + 512]
                        bi = 0 if kt < 16 else 1
                        S.op("dve", lambda e, pt=pt, bsl=bsl: e.tensor_tensor(out=PT[pt][:], in0=PT[pt][:], in1=bsl, op=ALU.mult),
                             [("PT", pt), ("band", bi)], [("PT", pt)])
                        if ai >= LAG:
                            emit_pv(ai - LAG)
                    for ai in range(max(0, nact - LAG), nact):
                        emit_pv(ai)
                    for qt in range(4):
                        col = m * 4 + qt
                        S.op("dve", lambda e, qt=qt, col=col: e.reciprocal(out=rs[:, col:col + 1], in_=C.ps[2 + qt][:, 128:129]),
                             [("ps", 2 + qt)], [("rs", col)])
                        if m == 0:
                            S.op("dve", lambda e, qt=qt, col=col: e.tensor_scalar(out=osum[qt][:], in0=C.ps[2 + qt][:, 0:128],
                                                                               scalar1=rs[:, col:col + 1], scalar2=None, op0=ALU.mult),
                                 [("ps", 2 + qt), ("rs", col)], [("osum", qt)])
                        else:
                            ob = qt % 2
                            S.op("dve", lambda e, qt=qt, col=col, ob=ob: e.tensor_scalar(out=otmp[ob][:], in0=C.ps[2 + qt][:, 0:128],
                                                                                     scalar1=rs[:, col:col + 1], scalar2=lams[:, 2:3],
                                                                                     op0=ALU.mult, op1=ALU.mult),
                                 [("ps", 2 + qt), ("rs", col), ("lams",)], [("otmp", ob)])
                            S.op("pool", lambda e, qt=qt, ob=ob: e.tensor_tensor(out=osum[qt][:], in0=osum[qt][:], in1=otmp[ob][:], op=ALU.add),
                                 [("osum", qt), ("otmp", ob)], [("osum", qt)])
                for qt in range(4):
                    col = 8 + qt
                    ob = qt % 2
                    S.op("act", lambda e, qt=qt, col=col, ob=ob: e.activation(out=otmp[ob][:], in_=osum[qt][:], func=AF.Square, accum_out=rs[:, col:col + 1]),
                         [("osum", qt)], [("otmp", ob), ("rs", col)])
                    S.op("act", lambda e, col=col: e.activation(out=rs[:, col:col + 1], in_=rs[:, col:col + 1], func=AF.Sqrt,
                                                               bias=C.epsc[:, 0:1], scale=1.0 / 128.0), [("rs", col), ("epsc",)], [("rs", col)])
                    S.op("dve", lambda e, col=col: e.reciprocal(out=rs[:, col:col + 1], in_=rs[:, col:col + 1]), [("rs", col)], [("rs", col)])
                    S.op("dve", lambda e, qt=qt, col=col, ob=ob: e.scalar_tensor_tensor(out=onb[ob][:], in0=osum[qt][:], scalar=rs[:, col:col + 1],
                                                                                    in1=subg[:], op0=ALU.mult, op1=ALU.mult),
                         [("osum", qt), ("rs", col), ("subg",)], [("onb", ob)])
                    pb = proj_bank()
                    pbv = C.ps[pb][:].bitcast(BF16)
                    S.op("pe", lambda e, ob=ob, pbv=pbv: e.transpose(pbv[:, 0:128], onb[ob][:], C.identb[:]),
                         [("onb", ob), ("identb",)], [("ps", pb)])
                    tcol = q0 + qt * 128
                    S.op("act", lambda e, pbv=pbv, tcol=tcol: e.copy(out=oT[:, tcol:tcol + 128], in_=pbv[:, 0:128]),
                         [("ps", pb)], [("oT", tcol // 128)])
            for t in range(NT):
                for half in range(2):
                    pb = proj_bank()
                    S.op("pe", lambda e, t=t, half=half, wb=wb, pb=pb: e.matmul(
                        C.ps[pb][:], oT[:, t * 128:(t + 1) * 128], wo[wb][:, half * 512:(half + 1) * 512], start=True, stop=True),
                        [("oT", t), ("wo", wb)], [("ps", pb)])
                    xs = C.x[:, t, half * 512:(half + 1) * 512]
                    if h == 0:
                        S.op("dve", lambda e, xs=xs, pb=pb: e.scalar_tensor_tensor(out=xs, in0=xs, scalar=ALPHA, in1=C.ps[pb][:],
                                                                                 op0=ALU.mult, op1=ALU.add), [("x", t), ("ps", pb)], [("x", t)])
                    else:
                        S.op("dve", lambda e, xs=xs, pb=pb: e.tensor_tensor(out=xs, in0=xs, in1=C.ps[pb][:], op=ALU.add),
                             [("x", t), ("ps", pb)], [("x", t)])
        S.barrier()


def _op(S, eng, f, reads, writes):
    return S.op(eng, f, reads, writes)


def MM(S, out, lhsT, rhs, start, stop, reads, writes):
    S.op("pe", lambda e: e.matmul(out, lhsT, rhs, start=start, stop=stop), reads, writes)


def TT(S, eng, out, in0, in1, op, reads, writes):
    S.op(eng, lambda e: e.tensor_tensor(out=out, in0=in0, in1=in1, op=op), reads, writes)


def TS(S, eng, out, in0, s1, s2, op0, op1, reads, writes):
    if op1 is None:
        S.op(eng, lambda e: e.tensor_scalar(out=out, in0=in0, scalar1=s1, scalar2=None, op0=op0), reads, writes)
    else:
        S.op(eng, lambda e: e.tensor_scalar(out=out, in0=in0, scalar1=s1, scalar2=s2, op0=op0, op1=op1), reads, writes)


def ACTF(S, out, in_, func, reads, writes, bias=None, scale=None, accum_out=None):
    kw = {}
    if bias is not None:
        kw["bias"] = bias
    if scale is not None:
        kw["scale"] = scale
    if accum_out is not None:
        kw["accum_out"] = accum_out
    S.op("act", lambda e: e.activation(out=out, in_=in_, func=func, **kw), reads, writes)


def TR(S, out, in_, ident, reads, writes):
    S.op("pe", lambda e: e.transpose(out, in_, ident), reads, writes)


def phase_ssm_pass(nc, S, C, p, win_d, wdt_d, dtb_d, alog_d, convw_d, convb_d, dsk_d, normg_d, wout_d,
                   y1_d, sin_d, sout_d, scale_x):
    last = (p == 1)
    IDN, TLE, TGE, TGT, TLT = 0, 1, 2, 3, 4
    BT_, CPB, NBLK = 256, 2, 8
    Tm = C.tri[:, TLE if p == 0 else TGE, :]
    Um = C.tri[:, TGT if p == 0 else TLT, :]
    ones_f = None
    with ExitStack() as es:
        sb = lambda name, shape, dt: es.enter_context(nc.sbuf_tensor(_U() + "s%d_%s" % (p, name), shape, dt))
        xbcT = sb("xbcT", [128, 24, BT_], BF16)
        pre = [sb("pre%d" % i, [128, BT_ + 4], F32) for i in range(2)]
        cva = [sb("cva%d" % i, [128, BT_], F32) for i in range(2)]
        wfc = [sb("wfc%d" % i, [128, KC, 128], BF16) for i in range(2)]
        wdt = sb("wdt", [128, KC, 32], BF16)
        convw = sb("convw", [128, 24, 3], F32)
        convb = sb("convb", [128, 24], F32)
        rows = sb("rows", [128, 4, 32], F32)
        onesf = sb("onesf", [128, 128], F32)
        sm = [sb("sm%d" % i, [128, 8, 32], F32) for i in range(2)]
        Rb = sb("Rb", [128, 8, 128], F32)
        LT = [sb("LT%d" % i, [128, 8, 128], BF16) for i in range(2)]
        Gm = [sb("Gm%d" % i, [128, 128], BF16) for i in range(2)]
        Xtok = [sb("Xtok%d" % i, [128, 512], BF16) for i in range(2)]
        Btok = [sb("Btok%d" % i, [128, 128], BF16) for i in range(2)]
        Xt1 = [sb("Xt1%d" % i, [128, 512], BF16) for i in range(2)]
        Xt2 = [sb("Xt2%d" % i, [128, 512], BF16) for i in range(2)]
        Sst = sb("Sst", [128, 4, 512], F32)
        Sb = sb("Sb", [128, 4, 512], BF16)
        ych = sb("ych", [128, 2048], F32)
        t1 = sb("t1", [128, 512], F32)
        if last:
            wz = [sb("wz%d" % i, [128, KC, 128], BF16) for i in range(2)]
            sz = sb("sz", [128, CPB, 2048], BF16)
            yT = sb("yT", [128, 16, BT_], BF16)
            wo = [sb("wo%d" % i, [128, D], BF16) for i in range(2)]
            ngc = sb("ngc", [128, 16], F32)
            gs = sb("gs", [128, 16], F32)
        psb = [C.ps[i][:].bitcast(BF16) for i in range(8)]
        wv_ = win_d.rearrange("(c q) n -> q c n", q=128)

        S.dma("sp", lambda e: e.dma_start(out=convw[:], in_=convw_d), (), [("convw",)])
        S.dma("sp", lambda e: e.dma_start(out=convb[:], in_=convb_d), (), [("convb",)])
        S.dma("pool", lambda e: e.dma_start(out=wdt[:], in_=wdt_d[p].rearrange("(c q) n -> q c n", q=128)), (), [("wdt",)])
        S.dma("sp", lambda e: e.dma_start(out=rows[:, 0, :], in_=dtb_d[p:p + 1, :].to_broadcast([128, 32])), (), [("rows", 0)])
        S.dma("sp", lambda e: e.dma_start(out=rows[:, 1, :], in_=alog_d[p:p + 1, :].to_broadcast([128, 32])), (), [("rows", 1)])
        S.dma("sp", lambda e: e.dma_start(out=rows[:, 2, :], in_=dsk_d.to_broadcast([128, 32])), (), [("rows", 2)])
        ACTF(S, rows[:, 1, :], rows[:, 1, :], AF.Exp, [("rows", 1)], [("rows", 1)])
        TS(S, "dve", rows[:, 1, :], rows[:, 1, :], -1.0, None, ALU.mult, None, [("rows", 1)], [("rows", 1)])
        S.op("dve", lambda e: e.memset(onesf[:], 1.0), (), [("onesf",)])
        if scale_x:
            for t in range(NT):
                TS(S, "dve", C.x[:, t, :], C.x[:, t, :], ALPHA, None, ALU.mult, None, [("x", t)], [("x", t)])
        if p == 0:
            S.op("dve", lambda e: e.memset(Sst[:], 0.0), (), [("Sst", g) for g in range(4)])
            S.op("pool", lambda e: e.memset(Sb[:], 0.0), (), [("Sb", g) for g in range(4)])
        else:
            S.dma("sp", lambda e: e.dma_start(out=Sst[:], in_=sin_d), (), [("Sst", g) for g in range(4)])
            for g in range(4):
                S.op("act", lambda e, g=g: e.copy(out=Sb[:, g, :], in_=Sst[:, g, :]), [("Sst", g)], [("Sb", g)])
            S.dma("sp", lambda e: e.dma_start(out=ngc[:], in_=normg_d), (), [("ngc",)])
        pro_done = set()

        def prologue(c):
            pro_done.add(c)
            cp = c % 2
            smc = sm[cp]
            tcol = 1 + c * 128
            R = lambda i: ("sm", cp, i)
            for k in range(KC):
                MM(S, C.ps[3][:, 0:32], C.uT[:, k, tcol:tcol + 128], wdt[:, k, :], k == 0, k == KC - 1, [("wdt",), ("uT", c)], [("ps", 3)])
            TT(S, "dve", smc[:, 0, :], C.ps[3][:, 0:32], rows[:, 0, :], ALU.add, [("ps", 3), ("rows", 0)], [R(0)])
            ACTF(S, smc[:, 0, :], smc[:, 0, :], AF.Exp, [R(0)], [R(0)])
            ACTF(S, smc[:, 0, :], smc[:, 0, :], AF.Ln, [R(0)], [R(0)], bias=1.0)
            TT(S, "dve", smc[:, 1, :], smc[:, 0, :], rows[:, 1, :], ALU.mult, [R(0), ("rows", 1)], [R(1)])
            MM(S, C.ps[3][:, 32:64], Tm, smc[:, 1, :], True, True, [("tri",), R(1)], [("ps", 3)])
            MM(S, C.ps[3][:, 64:96], onesf[:], smc[:, 1, :], True, True, [("onesf",), R(1)], [("ps", 3)])
            S.op("act", lambda e: e.copy(out=smc[:, 2, :], in_=C.ps[3][:, 32:64]), [("ps", 3)], [R(2)])
            ACTF(S, smc[:, 3, :], C.ps[3][:, 32:64], AF.Exp, [("ps", 3)], [R(3)])
            ACTF(S, smc[:, 5, :], C.ps[3][:, 64:96], AF.Exp, [("ps", 3)], [R(5)])
            TT(S, "dve", smc[:, 6, :], C.ps[3][:, 64:96], smc[:, 2, :], ALU.subtract, [("ps", 3), R(2)], [R(6)])
            ACTF(S, smc[:, 6, :], smc[:, 6, :], AF.Exp, [R(6)], [R(6)])
            TT(S, "dve", smc[:, 4, :], smc[:, 6, :], smc[:, 0, :], ALU.mult, [R(6), R(0)], [R(4)])

        blocks = range(NBLK) if p == 0 else range(NBLK - 1, -1, -1)
        wcnt = [0]
        for blk in blocks:
            c0 = blk * BT_
            for fc in range(24):
                wb = wcnt[0] % 2
                pa = wcnt[0] % 2
                wcnt[0] += 1
                cs = slice(2048 + fc * 128, 2048 + (fc + 1) * 128)
                S.dma("pool", lambda e, wb=wb, cs=cs: e.dma_start(out=wfc[wb][:], in_=wv_[:, :, cs]), (), [("wfc", wb)])
                ur = [("uT", min(max(t, 0), NT - 1)) for t in range(blk * CPB - 1, blk * CPB + CPB + 1)] + [("uThalo",)]
                for k in range(KC):
                    MM(S, C.ps[pa][:, 0:BT_], wfc[wb][:, k, :], C.uT[:, k, c0:c0 + BT_], k == 0, k == KC - 1, [("wfc", wb)] + ur, [("ps", pa)])
                for k in range(KC):
                    MM(S, C.ps[2][:, pa * 8:pa * 8 + 2], wfc[wb][:, k, :], C.uT[:, k, c0 + BT_:c0 + BT_ + 2], k == 0, k == KC - 1,
                       [("wfc", wb)] + ur, [("ps", 2)])
                S.op("act", lambda e, pa=pa: e.copy(out=pre[pa][:, 0:BT_], in_=C.ps[pa][:, 0:BT_]), [("ps", pa)], [("pre", pa)])
                S.op("dve", lambda e, pa=pa: e.tensor_copy(out=pre[pa][:, BT_:BT_ + 2], in_=C.ps[2][:, pa * 8:pa * 8 + 2]), [("ps", 2)], [("pre", pa)])
                TS(S, "dve", cva[pa][:], pre[pa][:, 0:BT_], convw[:, fc, 0:1], None, ALU.mult, None, [("pre", pa), ("convw",)], [("cva", pa)])
                S.op("dve", lambda e, pa=pa, fc=fc: e.scalar_tensor_tensor(out=cva[pa][:], in0=pre[pa][:, 1:BT_ + 1], scalar=convw[:, fc, 1:2],
                                                                         in1=cva[pa][:], op0=ALU.mult, op1=ALU.add),
                     [("pre", pa), ("cva", pa), ("convw",)], [("cva", pa)])
                S.op("dve", lambda e, pa=pa, fc=fc: e.scalar_tensor_tensor(out=cva[pa][:], in0=pre[pa][:, 2:BT_ + 2], scalar=convw[:, fc, 2:3],
                                                                         in1=cva[pa][:], op0=ALU.mult, op1=ALU.add),
                     [("pre", pa), ("cva", pa), ("convw",)], [("cva", pa)])
                ACTF(S, xbcT[:, fc, :], cva[pa][:], AF.Silu, [("cva", pa), ("convb",)], [("xbcT", fc)], bias=convb[:, fc:fc + 1])
            if last:
                for zc in range(16):
                    zb = zc % 2
                    cs = slice(zc * 128, (zc + 1) * 128)
                    S.dma("pool", lambda e, zb=zb, cs=cs: e.dma_start(out=wz[zb][:], in_=wv_[:, :, cs]), (), [("wz", zb)])
                    for cc in range(CPB):
                        tcol = 1 + c0 + cc * 128
                        for k in range(KC):
                            MM(S, C.ps[7][:, 0:128], C.uT[:, k, tcol:tcol + 128], wz[zb][:, k, :], k == 0, k == KC - 1,
                               [("wz", zb), ("uT", blk * CPB + cc)], [("ps", 7)])
                        ACTF(S, sz[:, cc, zc * 128:(zc + 1) * 128], C.ps[7][:, 0:128], AF.Silu, [("ps", 7)], [("sz", cc)])
            chunks = list(range(CPB)) if p == 0 else list(range(CPB - 1, -1, -1))
            for cc in chunks:
                c = blk * CPB + cc
                off = cc * 128
                if c not in pro_done:
                    prologue(c)
                nxt = c + 1 if p == 0 else c - 1
                cp = c % 2
                smc = sm[cp]
                if last:
                    S.dma("sp", lambda e, c=c: e.dma_start(out=ych[:], in_=y1_d[c]), (), [("ych", g) for g in range(4)])

                def stageA(g, off=off, smc=smc, cp=cp):
                    g2 = g % 2
                    hs = slice(8 * g, 8 * g + 8)
                    BT = xbcT[:, 16 + g, off:off + 128]
                    CT = xbcT[:, 20 + g, off:off + 128]
                    TT(S, "dve", Rb[:], Tm.unsqueeze(1).to_broadcast([128, 8, 128]), smc[:, 1, hs].unsqueeze(2).to_broadcast([128, 8, 128]),
                       ALU.mult, [("tri",), ("sm", cp, 1)], [("Rb",)])
                    for j in range(2):
                        MM(S, C.ps[4 + j][:], Um, Rb[:, 4 * j:4 * j + 4, :].rearrange("q a b -> q (a b)"), True, True, [("tri",), ("Rb",)], [("ps", 4 + j)])
                        ACTF(S, LT[g2][:, 4 * j:4 * j + 4, :].rearrange("q a b -> q (a b)"), C.ps[4 + j][:], AF.Exp, [("ps", 4 + j)], [("LT", g2)])
                    MM(S, C.ps[6][:, 0:128], BT, CT, True, True, [("xbcT", 16 + g), ("xbcT", 20 + g)], [("ps", 6)])
                    TT(S, "dve", Gm[g2][:], C.ps[6][:, 0:128], Tm, ALU.mult, [("ps", 6), ("tri",)], [("Gm", g2)])
                    TT(S, "dve", LT[g2][:], LT[g2][:], Gm[g2][:].unsqueeze(1).to_broadcast([128, 8, 128]), ALU.mult, [("LT", g2), ("Gm", g2)], [("LT", g2)])
                    for j in range(4):
                        TR(S, psb[7][:, j * 128:(j + 1) * 128], xbcT[:, 4 * g + j, off:off + 128], C.identb[:], [("xbcT", 4 * g + j), ("identb",)], [("ps", 7)])
                    TR(S, psb[7][:, 512:640], xbcT[:, 16 + g, off:off + 128], C.identb[:], [("xbcT", 16 + g), ("identb",)], [("ps", 7)])
                    S.op("act", lambda e: e.copy(out=Xtok[g2][:], in_=psb[7][:, 0:512]), [("ps", 7)], [("Xtok", g2)])
                    S.op("act", lambda e: e.copy(out=Btok[g2][:], in_=psb[7][:, 512:640]), [("ps", 7)], [("Btok", g2)])
                    X3 = Xtok[g2][:].rearrange("q (a b) -> q a b", a=8)
                    TT(S, "dve", Xt1[g2][:].rearrange("q (a b) -> q a b", a=8), X3, smc[:, 0, hs].unsqueeze(2).to_broadcast([128, 8, 64]), ALU.mult,
                       [("Xtok", g2), ("sm", cp, 0)], [("Xt1", g2)])
                    TT(S, "pool", Xt2[g2][:].rearrange("q (a b) -> q a b", a=8), X3, smc[:, 4, hs].unsqueeze(2).to_broadcast([128, 8, 64]), ALU.mult,
                       [("Xtok", g2), ("sm", cp, 4)], [("Xt2", g2)])

                def stageB(g, off=off, smc=smc, cp=cp):
                    g2 = g % 2
                    hs = slice(8 * g, 8 * g + 8)
                    CT = xbcT[:, 20 + g, off:off + 128]
                    X3 = Xtok[g2][:].rearrange("q (a b) -> q a b", a=8)
                    for hh in range(8):
                        MM(S, C.ps[0][:, hh * 64:(hh + 1) * 64], LT[g2][:, hh, :], Xt1[g2][:, hh * 64:(hh + 1) * 64], True, True,
                           [("LT", g2), ("Xt1", g2)], [("ps", 0)])
                    MM(S, C.ps[1][:], CT, Sb[:, g, :], True, True, [("xbcT", 20 + g), ("Sb", g)], [("ps", 1)])
                    MM(S, C.ps[2][:], Btok[g2][:], Xt2[g2][:], True, True, [("Btok", g2), ("Xt2", g2)], [("ps", 2)])
                    TT(S, "dve", t1[:].rearrange("q (a b) -> q a b", a=8), C.ps[1][:].rearrange("q (a b) -> q a b", a=8),
                       smc[:, 3, hs].unsqueeze(2).to_broadcast([128, 8, 64]), ALU.mult, [("ps", 1), ("sm", cp, 3)], [("t1",)])
                    yg = ych[:, g * 512:(g + 1) * 512]
                    if not last:
                        TT(S, "dve", yg, t1[:], C.ps[0][:], ALU.add, [("t1",), ("ps", 0)], [("ych", g)])
                        TT(S, "pool", t1[:].rearrange("q (a b) -> q a b", a=8), X3, rows[:, 2, hs].unsqueeze(2).to_broadcast([128, 8, 64]), ALU.mult,
                           [("Xtok", g2), ("rows", 2), ("t1",)], [("t1",)])
                        TT(S, "pool", yg, yg, t1[:], ALU.add, [("t1",), ("ych", g)], [("ych", g)])
                    else:
                        TT(S, "dve", t1[:], t1[:], C.ps[0][:], ALU.add, [("t1",), ("ps", 0)], [("t1",)])
                        TT(S, "pool", yg, yg, t1[:], ALU.add, [("t1",), ("ych", g)], [("ych", g)])
                    TT(S, "dve", Sst[:, g, :].rearrange("q (a b) -> q a b", a=8), Sst[:, g, :].rearrange("q (a b) -> q a b", a=8),
                       smc[:, 5, hs].unsqueeze(2).to_broadcast([128, 8, 64]), ALU.mult, [("Sst", g), ("sm", cp, 5)], [("Sst", g)])
                    TT(S, "dve", Sst[:, g, :], Sst[:, g, :], C.ps[2][:], ALU.add, [("Sst", g), ("ps", 2)], [("Sst", g)])
                    S.op("act", lambda e, g=g: e.copy(out=Sb[:, g, :], in_=Sst[:, g, :]), [("Sst", g)], [("Sb", g)])

                stageA(0)
                stageA(1)
                if 0 <= nxt < NT:
                    prologue(nxt)
                stageB(0)
                stageA(2)
                stageB(1)
                stageA(3)
                stageB(2)
                stageB(3)
                if not last:
                    S.dma("sp", lambda e, c=c: e.dma_start(out=y1_d[c], in_=ych[:]), [("ych", g) for g in range(4)], [("y1d", c)])
                else:
                    TT(S, "dve", ych[:], ych[:], sz[:, cc, :], ALU.mult, [("ych", g) for g in range(4)] + [("sz", cc)], [("ych", g) for g in range(4)])
                    for g in range(4):
                        ACTF(S, t1[:], ych[:, g * 512:(g + 1) * 512], AF.Square, [("ych", g)], [("t1",), ("gs", g)], accum_out=gs[:, g:g + 1])
                        ACTF(S, gs[:, g:g + 1], gs[:, g:g + 1], AF.Sqrt, [("gs", g), ("epsc",)], [("gs", g)], bias=C.epsc[:, 0:1], scale=1.0 / 512.0)
                        S.op("dve", lambda e, g=g: e.reciprocal(out=gs[:, g:g + 1], in_=gs[:, g:g + 1]), [("gs", g)], [("gs", g)])
                        TS(S, "dve", sz[:, cc, g * 512:(g + 1) * 512], ych[:, g * 512:(g + 1) * 512], gs[:, g:g + 1], None, ALU.mult, None,
                           [("ych", g), ("gs", g)], [("sz", cc)])
                    for q4 in range(4):
                        for j in range(4):
                            ch = q4 * 4 + j
                            TR(S, psb[6][:, j * 128:(j + 1) * 128], sz[:, cc, ch * 128:(ch + 1) * 128], C.identb[:], [("sz", cc), ("identb",)], [("ps", 6)])
                        S.op("act", lambda e, q4=q4, off=off: e.copy(out=yT[:, 4 * q4:4 * q4 + 4, off:off + 128],
                                                                    in_=psb[6][:, 0:512].rearrange("q (a b) -> q a b", a=4)),
                             [("ps", 6)], [("yT", cc)])
            if last:
                for ch in range(16):
                    wb = ch % 2
                    S.dma("pool", lambda e, wb=wb, ch=ch: e.dma_start(out=wo[wb][:], in_=wout_d[ch * 128:(ch + 1) * 128, :]), (), [("wo", wb)])
                    TS(S, "dve", wo[wb][:], wo[wb][:], ngc[:, ch:ch + 1], None, ALU.mult, None, [("wo", wb), ("ngc",)], [("wo", wb)])
                    TT(S, "dve", wo[wb][:], wo[wb][:], C.gh[:, 1, :], ALU.mult, [("wo", wb), ("gh", 1, 0), ("gh", 1, 1)], [("wo", wb)])
                    for cc in range(CPB):
                        t = blk * CPB + cc
                        for half in range(2):
                            pb = 4 + ((cc * 2 + half) % 2)
                            MM(S, C.ps[pb][:], yT[:, ch, cc * 128:(cc + 1) * 128], wo[wb][:, half * 512:(half + 1) * 512], True, True,
                               [("yT", cc), ("wo", wb)], [("ps", pb)])
                            xs = C.x[:, t, half * 512:(half + 1) * 512]
                            TT(S, "dve", xs, xs, C.ps[pb][:], ALU.add, [("x", t), ("ps", pb)], [("x", t)])
        if not last:
            S.dma("sp", lambda e: e.dma_start(out=sout_d, in_=Sst[:]), [("Sst", g) for g in range(4)], [("soutd",)])
        S.barrier()


def _dram_in(nc, name, shape, dt=F32):
    return nc.dram_tensor("d_" + name, list(shape), dt, kind="ExternalInput").ap()


def _dram_out(nc, name, shape, dt=F32):
    return nc.dram_tensor("d_" + name, list(shape), dt, kind="ExternalOutput").ap()


def build_stage(stage):
    nc = bass.Bass("TRN2", target_bir_lowering=False)
    S = Sched()
    C = Ctx()
    I = lambda name, shape, dt=F32: _dram_in(nc, name, shape, dt)
    x_d = I("x_in", [TOK, D])
    ccol_d = I("ccol", [128, KC])
    cpack_d = I("cpack", [128, 5, 128])
    lay = [0] if stage in (0, 1) else ([0, 1] if stage == 2 else [1])
    ada_w = {i: I("ada_w%d" % i, [D, 9216]) for i in lay}
    ada_b = {i: I("ada_b%d" % i, [1, 9216]) for i in lay}
    lng = {i: I("lng%d" % i, [3, D]) for i in lay}
    lnb = {i: I("lnb%d" % i, [3, D]) for i in lay}
    ffn_keys = {0: [(0, 0)], 1: [], 2: [(0, 1), (1, 0)], 3: [(1, 1)]}[stage]
    ffw = {}
    for (i, j) in ffn_keys:
        ffw[(i, j)] = (I("wg%d%d" % (i, j), [D, DFF]), I("wu%d%d" % (i, j), [D, DFF]), I("wd%d%d" % (i, j), [DFF, D]))
    if stage in (1, 2):
        w_in = I("w_in", [D, 5184])
        wdt = I("wdt", [2, D, 32])
        dtb = I("dtb", [2, 32])
        alog = I("alog", [2, 32])
        convw = I("convw", [128, 24, 3])
        convb = I("convb", [128, 24])
        dsk = I("dsk", [1, 32])
        normg = I("normg", [128, 16])
        ssm_wout = I("ssm_wout", [2048, D])
        xh = I("xh", [128, KC])
    if stage == 1:
        y1 = _dram_out(nc, "y1", [16, 128, 2048])
        s_out = _dram_out(nc, "s_out", [128, 4, 512])
        s_in = None
    if stage == 2:
        y1 = I("y1", [16, 128, 2048])
        s_in = I("s_in", [128, 4, 512])
        s_out = None
        uT_out = _dram_out(nc, "uT_out", [128, KC, TOK], BF16)
    if stage == 3:
        uTo = I("uTo", [128, KC, TOK], BF16)
        wqkv = I("wqkv", [D, 3072])
        lam = I("lam", [4, 64])
        subg = I("subg", [1, 128])
        attn_wout = I("attn_wout", [D, D])
    if stage != 1:
        y_d = _dram_out(nc, "y_out", [TOK, D])

    with ExitStack() as es:
        alloc_globals(nc, es, C)
        load_consts(nc, S, C, cpack_d)
        S.op("dve", lambda e: e.memset(C.uT[:, :, 0:1], 0.0), (), [("uT0",)])
        xv = x_d.rearrange("(t q) d -> q t d", q=128)
        for t in range(NT):
            S.dma("sp", lambda e, t=t: e.dma_start(out=C.x[:, t, :], in_=xv[:, t, :]), (), [("x", t)])

        def ffn(i, j):
            wg, wu, wd = ffw[(i, j)]
            phase_ffn(nc, S, C, 2 * j, wg, wu, wd)

        def halo():
            with ExitStack() as hs:
                xht = hs.enter_context(nc.sbuf_tensor(_U() + "xht", [128, KC], F32))
                S.dma("sp", lambda e: e.dma_start(out=xht[:], in_=xh), (), [("xht",)])
                TT(S, "dve", xht[:], xht[:], C.modcol[:, 1, 1, :], ALU.mult, [("xht",), ("modcol", 1)], [("xht",)])
                TT(S, "dve", C.uT[:, :, TOK + 1:TOK + 2], xht[:].unsqueeze(2), C.modcol[:, 1, 0, :].unsqueeze(2), ALU.add,
                   [("xht",), ("modcol", 1)], [("uThalo",)])
                S.barrier()

        if stage == 0:
            phase_mods(nc, S, C, ccol_d, ada_w[0], ada_b[0])
            ln_modulate_stage(nc, S, C, 0, False)
            S.barrier()
            ffn(0, 0)
            ln_modulate_stage(nc, S, C, None, True, lng[0][0:1, :], lnb[0][0:1, :])
        elif stage == 1:
            phase_mods(nc, S, C, ccol_d, ada_w[0], ada_b[0])
            ln_modulate_stage(nc, S, C, 1, False)
            halo()
            phase_ssm_pass(nc, S, C, 0, w_in, wdt, dtb, alog, convw, convb, dsk, normg, ssm_wout, y1, None, s_out, False)
        elif stage == 2:
            phase_mods(nc, S, C, ccol_d, ada_w[0], ada_b[0])
            ln_modulate_stage(nc, S, C, 1, False)
            halo()
            phase_ssm_pass(nc, S, C, 1, w_in, wdt, dtb, alog, convw, convb, dsk, normg, ssm_wout, y1, s_in, None, True)
            ln_modulate_stage(nc, S, C, 2, True, lng[0][1:2, :], lnb[0][1:2, :])
            ffn(0, 1)
            ln_modulate_stage(nc, S, C, None, True, lng[0][2:3, :], lnb[0][2:3, :])
            phase_mods(nc, S, C, ccol_d, ada_w[1], ada_b[1])
            ln_modulate_stage(nc, S, C, 0, False)
            S.barrier()
            ffn(1, 0)
            ln_modulate_stage(nc, S, C, 1, True, lng[1][0:1, :], lnb[1][0:1, :])
            S.dma("sp", lambda e: e.dma_start(out=uT_out, in_=C.uT[:, :, 1:1 + TOK]), [("uT", t) for t in range(NT)], [("uTout",)])
        else:
            phase_mods(nc, S, C, ccol_d, ada_w[1], ada_b[1])
            ln_modulate_stage(nc, S, C, 1, False)
            S.barrier()
            phase_attn(nc, S, C, uTo, wqkv, lam, subg, attn_wout)
            ln_modulate_stage(nc, S, C, 2, True, lng[1][1:2, :], lnb[1][1:2, :])
            ffn(1, 1)
            ln_modulate_stage(nc, S, C, None, True, lng[1][2:3, :], lnb[1][2:3, :])
        S.barrier()
        if stage != 1:
            yv = y_d.rearrange("(t q) d -> q t d", q=128)
            for t in range(NT):
                S.dma("sp", lambda e, t=t: e.dma_start(out=yv[:, t, :], in_=C.x[:, t, :]), [("x", t)], [("yout", t)])
        S.barrier()
        S.emit(nc, es)
    return nc


def _cpack():
    r = np.arange(128)[:, None]
    c = np.arange(128)[None, :]
    mats = [r == c, r <= c, r >= c, r > c, r < c]
    return np.ascontiguousarray(np.stack([m.astype(np.float32) for m in mats], axis=1))


def _col(v):
    v = np.asarray(v)
    return np.ascontiguousarray(v.reshape(-1, 128).T)


_PROGS = {}


def _prog(stage):
    if stage not in _PROGS:
        _PROGS[stage] = build_stage(stage)
    return _PROGS[stage]


DEBUG_STOP = None


def kernel(x, c, ada_w, ada_b, ln_g, ln_b, ffn_w_gate, ffn_w_up, ffn_w_down,
           ssm_w_in, ssm_conv_w, ssm_conv_b, ssm_dt_bias, ssm_a_log, ssm_d, ssm_norm_g, ssm_w_out,
           attn_w_qkv, attn_lambda, attn_subln_g, attn_w_out):
    f = lambda a: np.ascontiguousarray(np.asarray(a, dtype=np.float32))
    x, c = f(x), f(c)
    ada_w, ada_b, ln_g, ln_b = f(ada_w), f(ada_b), f(ln_g), f(ln_b)
    wgate, wup, wdown = f(ffn_w_gate), f(ffn_w_up), f(ffn_w_down)
    w_in, conv_w, conv_b = f(ssm_w_in)[0], f(ssm_conv_w)[0], f(ssm_conv_b)[0]
    dt_bias, a_log, dsk, norm_g, ssm_wout = f(ssm_dt_bias)[0], f(ssm_a_log)[0], f(ssm_d), f(ssm_norm_g)[0], f(ssm_w_out)[0]
    wqkv, lam, subg, attn_wout = f(attn_w_qkv)[0], f(attn_lambda)[0], f(attn_subln_g), f(attn_w_out)[0]
    cores = list(range(8))
    cpack = _cpack()

    def local(arr, core):
        b, h = core // 2, core % 2
        a = arr[b, h * TOK:(h + 1) * TOK]
        return np.ascontiguousarray(a[::-1] if h else a)

    def common(core, lays):
        b = core // 2
        m = {"d_ccol": _col(c[b]), "d_cpack": cpack}
        for i in lays:
            m["d_ada_w%d" % i] = ada_w[i]
            m["d_ada_b%d" % i] = ada_b[i:i + 1]
            m["d_lng%d" % i] = ln_g[i]
            m["d_lnb%d" % i] = ln_b[i]
        return m

    def ffn_in(m, keys):
        for (i, j) in keys:
            m["d_wg%d%d" % (i, j)] = wgate[i, j]
            m["d_wu%d%d" % (i, j)] = wup[i, j]
            m["d_wd%d%d" % (i, j)] = wdown[i, j]

    def ssm_in(m, core, x1loc):
        h = core % 2
        sets = [0, 1] if h == 0 else [1, 0]
        m["d_w_in"] = w_in
        m["d_wdt"] = np.ascontiguousarray(np.stack([w_in[:, 5120 + 32 * s:5152 + 32 * s] for s in sets]))
        m["d_dtb"] = np.ascontiguousarray(np.stack([dt_bias[s] for s in sets]))
        m["d_alog"] = np.ascontiguousarray(np.stack([a_log[s] for s in sets]))
        cw = conv_w if h == 0 else conv_w[::-1]
        m["d_convw"] = np.ascontiguousarray(cw.reshape(3, 24, 128).transpose(2, 1, 0))
        m["d_convb"] = _col(conv_b)
        m["d_dsk"] = dsk
        m["d_normg"] = _col(norm_g)
        m["d_ssm_wout"] = ssm_wout
        m["d_xh"] = _col(x1loc[core ^ 1][TOK - 1])

    maps = []
    for core in cores:
        m = common(core, [0])
        m["d_x_in"] = local(x, core)
        ffn_in(m, [(0, 0)])
        maps.append(m)
    r0 = run_bass_kernel_spmd(_prog(0), maps, core_ids=cores)
    x1 = [np.asarray(r0.results[k]["d_y_out"]) for k in cores]
    if DEBUG_STOP == 0:
        return x1
    maps = []
    for core in cores:
        m = common(core, [0])
        m["d_x_in"] = x1[core]
        ssm_in(m, core, x1)
        maps.append(m)
    r1 = run_bass_kernel_spmd(_prog(1), maps, core_ids=cores)
    y1 = [np.asarray(r1.results[k]["d_y1"]) for k in cores]
    so = [np.asarray(r1.results[k]["d_s_out"]) for k in cores]
    maps = []
    for core in cores:
        m = common(core, [0, 1])
        m["d_x_in"] = x1[core]
        ssm_in(m, core, x1)
        m["d_y1"] = y1[core]
        m["d_s_in"] = so[core ^ 1]
        ffn_in(m, [(0, 1), (1, 0)])
        maps.append(m)
    r2 = run_bass_kernel_spmd(_prog(2), maps, core_ids=cores)
    x4 = [np.asarray(r2.results[k]["d_y_out"]) for k in cores]
    uT4 = [np.asarray(r2.results[k]["d_uT_out"]) for k in cores]
    if DEBUG_STOP == 2:
        return x4
    maps = []
    for core in cores:
        m = common(core, [1])
        m["d_x_in"] = x4[core]
        m["d_uTo"] = uT4[core ^ 1]
        m["d_wqkv"] = wqkv
        m["d_lam"] = lam
        m["d_subg"] = subg
        m["d_attn_wout"] = attn_wout
        ffn_in(m, [(1, 1)])
        maps.append(m)
    r3 = run_bass_kernel_spmd(_prog(3), maps, core_ids=cores)
    out = np.empty((4, 2 * TOK, D), dtype=np.float32)
    for core in cores:
        b, h = core // 2, core % 2
        y = np.asarray(r3.results[core]["d_y_out"])
        out[b, h * TOK:(h + 1) * TOK] = y[::-1] if h else y
    return out
```

```python
from contextlib import ExitStack
import numpy as np
import concourse.bass as bass
import concourse.mybir as mybir
from concourse.bass_utils import run_bass_kernel_spmd

DT = mybir.dt
F32 = DT.float32
BF16 = DT.bfloat16
AF = mybir.ActivationFunctionType
ALU = mybir.AluOpType
AX = mybir.AxisListType

ENGS = ("pe", "act", "dve", "pool", "sp")
N_DMA_SEMS = 40
MAX_OUTSTANDING_DMA = 10 ** 9


class Op:
    __slots__ = ("eng", "fn", "deps", "signal", "seq", "is_dma", "slot", "slot_val",
                 "prev_slot_user", "idx", "is_nop")


class Sched:
    def __init__(self):
        self.ops = {e: [] for e in ENGS}
        self.last_write = {}
        self.readers = {}
        self.n_dma = 0
        self.slot_last = [None] * N_DMA_SEMS
        self.slot_count = [0] * N_DMA_SEMS
        self.all = []
        self.dma_hist = {}

    def _add(self, eng, fn, reads, writes, is_dma):
        o = Op()
        o.eng, o.fn, o.deps, o.signal, o.seq, o.is_dma = eng, fn, [], False, 0, is_dma
        o.slot = o.slot_val = None
        o.prev_slot_user = None
        o.is_nop = False
        o.idx = len(self.all)
        deps = {}
        for r in reads:
            w = self.last_write.get(r)
            if w is not None:
                deps[id(w)] = w
        for r in writes:
            w = self.last_write.get(r)
            if w is not None:
                deps[id(w)] = w
            for rd in self.readers.get(r, ()):
                if rd is not o:
                    deps[id(rd)] = rd
        for d in deps.values():
            if d.eng == eng and not d.is_dma and not is_dma:
                if eng == "pe":
                    continue
            o.deps.append(d)
            d.signal = True
        for r in reads:
            self.readers.setdefault(r, []).append(o)
        for r in writes:
            self.last_write[r] = o
            self.readers[r] = []
        if is_dma:
            q = self.dma_hist.setdefault(eng, [])
            if len(q) >= MAX_OUTSTANDING_DMA:
                o.deps.append(q[-MAX_OUTSTANDING_DMA])
            q.append(o)
            s = self.n_dma % N_DMA_SEMS
            self.n_dma += 1
            o.prev_slot_user = self.slot_last[s]
            self.slot_last[s] = o
            self.slot_count[s] += 1
            o.slot, o.slot_val = s, 16 * self.slot_count[s]
            o.signal = True
        self.ops[eng].append(o)
        self.all.append(o)
        return o

    def op(self, eng, fn, reads=(), writes=()):
        return self._add(eng, fn, reads, writes, False)

    def dma(self, eng, fn, reads=(), writes=()):
        return self._add(eng, fn, reads, writes, True)

    def barrier(self):
        key = ("__barrier__",)
        lasts = []
        for e in ENGS:
            for o in reversed(self.ops[e]):
                if not o.is_dma and not o.is_nop:
                    lasts.append(o)
                    break
        dmas = [o for o in self.slot_last if o is not None]
        for e in ("pe", "act", "dve", "pool", "sp"):
            o = self._add(e, lambda eng: eng.nop(), (), (), False)
            o.is_nop = True
            for d in lasts + dmas:
                if d.eng == e and e == "pe" and not d.is_dma:
                    continue
                o.deps.append(d)
                d.signal = True
        self.last_write = {}
        self.readers = {}

    def emit(self, nc, es):
        eng_sem = {e: es.enter_context(nc.semaphore("prog_" + e)) for e in ENGS}
        dma_sem = [es.enter_context(nc.semaphore("dma%d" % i)) for i in range(N_DMA_SEMS)]
        for e in ENGS:
            n = 0
            for o in self.ops[e]:
                if o.signal and not o.is_dma:
                    n += 1
                    o.seq = n
        block = es.enter_context(nc.Block())
        sched = self

        def body_for(e):
            def body(eng):
                waited_eng = {f: 0 for f in ENGS}
                waited_slot = [0] * N_DMA_SEMS
                for o in sched.ops[e]:
                    need_eng = {}
                    need_slot = {}
                    deps = list(o.deps)
                    if o.is_dma and o.prev_slot_user is not None:
                        deps.append(o.prev_slot_user)
                    for d in deps:
                        if d.is_dma:
                            if d.slot_val > waited_slot[d.slot]:
                                need_slot[d.slot] = max(need_slot.get(d.slot, 0), d.slot_val)
                        else:
                            if d.seq > waited_eng[d.eng]:
                                need_eng[d.eng] = max(need_eng.get(d.eng, 0), d.seq)
                    for f, v in need_eng.items():
                        eng.wait_ge(eng_sem[f], v)
                        waited_eng[f] = v
                    for s, v in need_slot.items():
                        eng.wait_ge(dma_sem[s], v)
                        waited_slot[s] = v
                    ins = o.fn(eng)
                    if o.is_dma:
                        ins.then_inc(dma_sem[o.slot], 16)
                    elif o.signal:
                        ins.then_inc(eng_sem[e], 1)
            return body

        block.tensor(body_for("pe"))
        block.scalar(body_for("act"))
        block.vector(body_for("dve"))
        block.gpsimd(body_for("pool"))
        block.sync(body_for("sp"))


D = 1024
TOK = 2048
NT = TOK // 128
DFF = 2816
NF = DFF // 128
KC = D // 128
ALPHA = (2 * 2) ** 0.25
LN_EPS = 1e-5
FFN_GROUPS = [(0, 4), (4, 4), (8, 4), (12, 4), (16, 4), (20, 2)]


class Ctx:
    pass


_UC = [0]


def _U():
    _UC[0] += 1
    return "u%d_" % _UC[0]


def alloc_globals(nc, es, C):
    C.x = es.enter_context(nc.sbuf_tensor(_U() + "x_res", [128, NT, D], F32))
    C.uT = es.enter_context(nc.sbuf_tensor(_U() + "uT", [128, KC, TOK + 2], BF16))
    C.ident = es.enter_context(nc.sbuf_tensor(_U() + "ident", [128, 128], F32))
    C.identb = es.enter_context(nc.sbuf_tensor(_U() + "identb", [128, 128], BF16))
    C.gh = es.enter_context(nc.sbuf_tensor(_U() + "gh", [128, 3, D], F32))
    C.modcol = es.enter_context(nc.sbuf_tensor(_U() + "modcol", [128, 3, 2, KC], F32))
    C.tri = es.enter_context(nc.sbuf_tensor(_U() + "tri", [128, 5, 128], F32))
    C.stat = es.enter_context(nc.sbuf_tensor(_U() + "stat", [128, 2, 16], F32))
    C.epsc = es.enter_context(nc.sbuf_tensor(_U() + "epsc", [128, 1], F32))
    C.ps = [es.enter_context(nc.psum_tensor("ps%d" % b, [128, 512], F32)) for b in range(8)]


def load_consts(nc, S, C, ident_d):
    S.dma("sp", lambda e: e.dma_start(out=C.tri[:], in_=ident_d), (), [("tri",)])
    S.op("dve", lambda e: e.tensor_copy(out=C.ident[:], in_=C.tri[:, 0, :]), [("tri",)], [("ident",)])
    S.op("dve", lambda e: e.tensor_copy(out=C.identb[:], in_=C.ident[:]), [("ident",)], [("identb",)])
    S.op("dve", lambda e: e.memset(C.epsc[:], LN_EPS), (), [("epsc",)])


def phase_mods(nc, S, C, ccol_d, ada_w_d, ada_b_d):
    with ExitStack() as es:
        ccol = es.enter_context(nc.sbuf_tensor(_U() + "ccol", [128, KC], F32))
        condb = es.enter_context(nc.sbuf_tensor(_U() + "condb", [128, KC], BF16))
        condrep = es.enter_context(nc.sbuf_tensor(_U() + "condrep", [128, KC, 128], BF16))
        wa = [es.enter_context(nc.sbuf_tensor(_U() + "wa%d" % i, [128, KC, 512], BF16)) for i in range(3)]
        brow = [es.enter_context(nc.sbuf_tensor(_U() + "brow%d" % i, [128, 512], F32)) for i in range(3)]
        mrow = [es.enter_context(nc.sbuf_tensor(_U() + "mrow%d" % i, [128, 512], F32)) for i in range(2)]
        S.dma("sp", lambda e: e.dma_start(out=ccol[:], in_=ccol_d), (), [("ccol",)])
        S.op("act", lambda e: e.activation(out=condb[:], in_=ccol[:], func=AF.Silu), [("ccol",)], [("condb",)])
        S.op("dve", lambda e: e.tensor_copy(out=condrep[:], in_=condb[:].unsqueeze(2).to_broadcast([128, KC, 128])),
             [("condb",)], [("condrep",)])
        wv = ada_w_d.rearrange("(c p) n -> p c n", p=128)
        for nb in range(18):
            b3, b2 = nb % 3, nb % 2
            sl = slice(nb * 512, (nb + 1) * 512)
            S.dma("pool", lambda e, b3=b3, sl=sl: e.dma_start(out=wa[b3][:], in_=wv[:, :, sl]), (), [("wa", b3)])
            S.dma("sp", lambda e, b3=b3, sl=sl: e.dma_start(out=brow[b3][:], in_=ada_b_d[0:1, sl].to_broadcast([128, 512])),
                  (), [("brow", b3)])
            pb = nb % 2
            for k in range(KC):
                S.op("pe", lambda e, k=k, b3=b3, pb=pb: e.matmul(C.ps[pb][:], condrep[:, k, :], wa[b3][:, k, :],
                                                                  start=(k == 0), stop=(k == KC - 1)),
                     [("condrep",), ("wa", b3)], [("ps", pb)])
            S.op("dve", lambda e, b2=b2, b3=b3, pb=pb: e.tensor_tensor(out=mrow[b2][:], in0=C.ps[pb][:], in1=brow[b3][:], op=ALU.add),
                 [("ps", pb), ("brow", b3)], [("mrow", b2)])
            sub, kind, half = nb // 6, (nb % 6) // 2, nb % 2
            if kind == 2:
                f = 1.0 if sub == 1 else 0.5
                S.op("dve", lambda e, b2=b2, sub=sub, half=half, f=f: e.tensor_scalar(
                    out=C.gh[:, sub, half * 512:(half + 1) * 512], in0=mrow[b2][:], scalar1=1.0, scalar2=f,
                    op0=ALU.add, op1=ALU.mult), [("mrow", b2)], [("gh", sub, half)])
            else:
                tb = 2 + (nb % 2)
                for j in range(4):
                    S.op("pe", lambda e, j=j, b2=b2, tb=tb: e.transpose(C.ps[tb][:, j * 128:(j + 1) * 128],
                                                                       mrow[b2][:, j * 128:(j + 1) * 128], C.ident[:]),
                         [("mrow", b2), ("ident",)], [("ps", tb)])
                for j in range(4):
                    cidx = half * 4 + j
                    S.op("dve", lambda e, j=j, tb=tb, sub=sub, kind=kind, cidx=cidx: e.tensor_scalar(
                        out=C.modcol[:, sub, kind, cidx:cidx + 1], in0=C.ps[tb][:, j * 128:j * 128 + 1],
                        scalar1=(1.0 if kind == 1 else 0.0), scalar2=None, op0=ALU.add),
                        [("ps", tb)], [("modcol", sub)])
        S.barrier()


def ln_modulate_stage(nc, S, C, sub_next, do_ln, lng_d=None, lnb_d=None, tiles=None):
    es = ExitStack()
    if do_ln:
        C.lng = es.enter_context(nc.sbuf_tensor(_U() + "lng", [128, D], F32))
        C.lnb = es.enter_context(nc.sbuf_tensor(_U() + "lnb", [128, D], F32))
        S.dma("sp", lambda e: e.dma_start(out=C.lng[:], in_=lng_d.to_broadcast([128, D])), (), [("lng",)])
        S.dma("sp", lambda e: e.dma_start(out=C.lnb[:], in_=lnb_d.to_broadcast([128, D])), (), [("lnb",)])
    for t in (tiles if tiles is not None else range(NT)):
        xr = ("x", t)
        if do_ln:
            st = ("stat", t % 2)
            sv = C.stat[:, t % 2, :]
            for h in range(2):
                S.op("dve", lambda e, t=t, h=h, sv=sv: e.bn_stats(out=sv[:, h * 6:(h + 1) * 6], in_=C.x[:, t, h * 512:(h + 1) * 512]),
                     [xr], [st])
            S.op("dve", lambda e, sv=sv: e.bn_aggr(out=sv[:, 12:14], in_=sv[:, 0:12]), [st], [st])
            S.op("act", lambda e, sv=sv: e.activation(out=sv[:, 14:15], in_=sv[:, 13:14], func=AF.Sqrt, bias=C.epsc[:, 0:1], scale=1.0),
                 [st, ("epsc",)], [st])
            S.op("dve", lambda e, sv=sv: e.reciprocal(out=sv[:, 15:16], in_=sv[:, 14:15]), [st], [st])
            S.op("dve", lambda e, t=t, sv=sv: e.tensor_scalar(out=C.x[:, t, :], in0=C.x[:, t, :], scalar1=sv[:, 12:13],
                                                           scalar2=sv[:, 15:16], op0=ALU.subtract, op1=ALU.mult),
                 [xr, st], [xr])
            S.op("dve", lambda e, t=t: e.tensor_tensor(out=C.x[:, t, :], in0=C.x[:, t, :], in1=C.lng[:], op=ALU.mult),
                 [xr, ("lng",)], [xr])
            S.op("dve", lambda e, t=t: e.tensor_tensor(out=C.x[:, t, :], in0=C.x[:, t, :], in1=C.lnb[:], op=ALU.add),
                 [xr, ("lnb",)], [xr])
        if sub_next is not None:
            for half in range(2):
                pb = 2 * (t % 2) + half
                for j in range(4):
                    c = half * 4 + j
                    S.op("pe", lambda e, t=t, c=c, j=j, pb=pb: e.transpose(C.ps[pb][:, j * 128:(j + 1) * 128],
                                                                       C.x[:, t, c * 128:(c + 1) * 128], C.ident[:]),
                         [xr, ("ident",)], [("ps", pb)])
                for j in range(4):
                    c = half * 4 + j
                    S.op("act", lambda e, t=t, c=c, j=j, pb=pb: e.activation(
                        out=C.uT[:, c, 1 + t * 128:1 + (t + 1) * 128], in_=C.ps[pb][:, j * 128:(j + 1) * 128],
                        func=AF.Identity, scale=C.modcol[:, sub_next, 1, c:c + 1], bias=C.modcol[:, sub_next, 0, c:c + 1]),
                        [("ps", pb), ("modcol", sub_next)], [("uT", t)])
    if do_ln:
        S.barrier()
    es.close()


def phase_ffn(nc, S, C, sub, wg_d, wu_d, wd_d):
    with ExitStack() as es:
        wg = [es.enter_context(nc.sbuf_tensor(_U() + "wg%d" % i, [128, KC, 512], BF16)) for i in range(2)]
        wu = [es.enter_context(nc.sbuf_tensor(_U() + "wu%d" % i, [128, KC, 512], BF16)) for i in range(2)]
        wd = [es.enter_context(nc.sbuf_tensor(_U() + "wd%d" % i, [128, 4, D], BF16)) for i in range(2)]
        hT = [es.enter_context(nc.sbuf_tensor(_U() + "hT%d" % i, [128, 4, 512], BF16)) for i in range(2)]
        sg = [es.enter_context(nc.sbuf_tensor(_U() + "sg%d" % i, [128, 512], F32)) for i in range(2)]
        wgv = wg_d.rearrange("(c p) n -> p c n", p=128)
        wuv = wu_d.rearrange("(c p) n -> p c n", p=128)
        wdv = wd_d.rearrange("(c p) n -> p c n", p=128)
        cnt = 0
        ycnt = 0
        for gi, (f0, nf) in enumerate(FFN_GROUPS):
            b = gi % 2
            fs = slice(f0 * 128, (f0 + nf) * 128)
            S.dma("pool", lambda e, b=b, fs=fs, nf=nf: e.dma_start(out=wg[b][:, :, 0:nf * 128], in_=wgv[:, :, fs]), (), [("wg", b)])
            S.dma("pool", lambda e, b=b, fs=fs, nf=nf: e.dma_start(out=wu[b][:, :, 0:nf * 128], in_=wuv[:, :, fs]), (), [("wu", b)])
            S.dma("pool", lambda e, b=b, f0=f0, nf=nf: e.dma_start(out=wd[b][:, 0:nf, :], in_=wdv[:, f0:f0 + nf, :]), (), [("wd", b)])
            for j in range(nf):
                S.op("dve", lambda e, b=b, j=j: e.tensor_tensor(out=wd[b][:, j, :], in0=wd[b][:, j, :], in1=C.gh[:, sub, :], op=ALU.mult),
                     [("wd", b), ("gh", sub, 0), ("gh", sub, 1)], [("wd", b)])
            for tb in range(4):
                hb = tb % 2
                tsl = slice(1 + tb * 512, 1 + (tb + 1) * 512)
                for j in range(nf):
                    pg, pu = (cnt % 2), 2 + (cnt % 2)
                    sgb = cnt % 2
                    cnt += 1
                    ur = [("uT", tb * 4 + q) for q in range(4)]
                    for k in range(KC):
                        S.op("pe", lambda e, k=k, b=b, j=j, pg=pg, tsl=tsl: e.matmul(
                            C.ps[pg][:], wg[b][:, k, j * 128:(j + 1) * 128], C.uT[:, k, tsl], start=(k == 0), stop=(k == KC - 1)),
                            [("wg", b)] + ur, [("ps", pg)])
                    for k in range(KC):
                        S.op("pe", lambda e, k=k, b=b, j=j, pu=pu, tsl=tsl: e.matmul(
                            C.ps[pu][:], wu[b][:, k, j * 128:(j + 1) * 128], C.uT[:, k, tsl], start=(k == 0), stop=(k == KC - 1)),
                            [("wu", b)] + ur, [("ps", pu)])
                    S.op("act", lambda e, pg=pg, sgb=sgb: e.activation(out=sg[sgb][:], in_=C.ps[pg][:], func=AF.Silu),
                         [("ps", pg)], [("sg", sgb)])
                    S.op("dve", lambda e, pu=pu, sgb=sgb, hb=hb, j=j: e.tensor_tensor(
                        out=hT[hb][:, j, :], in0=sg[sgb][:], in1=C.ps[pu][:], op=ALU.mult),
                        [("ps", pu), ("sg", sgb)], [("hT", hb)])
                for tt in range(4):
                    t = tb * 4 + tt
                    for half in range(2):
                        py = 4 + (ycnt % 4)
                        ycnt += 1
                        for j in range(nf):
                            S.op("pe", lambda e, hb=hb, j=j, tt=tt, b=b, half=half, py=py: e.matmul(
                                C.ps[py][:], hT[hb][:, j, tt * 128:(tt + 1) * 128], wd[b][:, j, half * 512:(half + 1) * 512],
                                start=(j == 0), stop=(j == nf - 1)), [("hT", hb), ("wd", b)], [("ps", py)])
                        xs = C.x[:, t, half * 512:(half + 1) * 512]
                        if gi == 0:
                            S.op("dve", lambda e, xs=xs, py=py: e.scalar_tensor_tensor(
                                out=xs, in0=xs, scalar=ALPHA, in1=C.ps[py][:], op0=ALU.mult, op1=ALU.add),
                                [("x", t), ("ps", py)], [("x", t)])
                        else:
                            S.op("dve", lambda e, xs=xs, py=py: e.tensor_tensor(out=xs, in0=xs, in1=C.ps[py][:], op=ALU.add),
                                 [("x", t), ("ps", py)], [("x", t)])
        S.barrier()


NH = 8
SLOPES = [2.0 ** (-8.0 * (h + 1) / NH) for h in range(NH)]
LAMBDA_INIT1 = 0.8 - 0.6 * float(np.exp(-0.3 * 1))
BANDW = 3968
SKIP_T = 105.0


def phase_attn(nc, S, C, uTo_d, wqkv_d, lam_d, subg_d, wout_d):
    scale = 64 ** -0.5
    with ExitStack() as es:
        sb = lambda name, shape, dt: es.enter_context(nc.sbuf_tensor(name, shape, dt))
        uTo = [sb("uTo%d" % i, [128, KC, 512], BF16) for i in range(2)]
        wq = [sb("wqkv%d" % i, [128, KC, 3, 128], BF16) for i in range(2)]
        wo = [sb("wo%d" % i, [128, D], BF16) for i in range(2)]
        qT = sb("qT", [128, TOK], BF16)
        kT = sb("kT", [128, 2 * TOK], BF16)
        v = sb("v_h", [128, 32, 132], BF16)
        band = [sb("band%d" % i, [128, BANDW], BF16) for i in range(2)]
        itmp = [sb("itmp%d" % i, [128, 496], F32) for i in range(2)]
        sq = [sb("sq%d" % i, [128, 512], BF16) for i in range(2)]
        onesb = sb("onesb", [128, 128], BF16)
        nstat = sb("nstat", [128, 2, 16], F32)
        nbias = sb("nbias", [128, 2], F32)
        lamt = sb("lamt", [128, 4, 64], F32)
        lam2 = sb("lam2", [128, 2, 64], F32)
        lams = sb("lams", [128, 4], F32)
        subg = sb("subg", [128, 128], F32)
        PT = [sb("PT%d" % i, [128, 512], BF16) for i in range(4)]
        osum = [sb("osum%d" % i, [128, 128], F32) for i in range(4)]
        otmp = [sb("otmp%d" % i, [128, 128], F32) for i in range(2)]
        rs = sb("rs", [128, 16], F32)
        onb = [sb("onb%d" % i, [128, 128], BF16) for i in range(2)]
        oT = sb("oT", [128, TOK], BF16)

        S.op("dve", lambda e: e.memset(onesb[:], 1.0), (), [("onesb",)])
        S.op("dve", lambda e: e.memset(v[:, :, 128:129], 1.0), (), [("vone",)])
        S.dma("sp", lambda e: e.dma_start(out=lamt[:].rearrange("p a b -> p (a b)"),
                                          in_=lam_d.rearrange("(o a) b -> o (a b)", o=1).to_broadcast([128, 256])), (), [("lamt",)])
        S.dma("sp", lambda e: e.dma_start(out=subg[:], in_=subg_d.to_broadcast([128, 128])), (), [("subg",)])
        S.op("dve", lambda e: e.tensor_tensor(out=lam2[:, 0, :], in0=lamt[:, 0, :], in1=lamt[:, 1, :], op=ALU.mult), [("lamt",)], [("lam2",)])
        S.op("dve", lambda e: e.tensor_tensor(out=lam2[:, 1, :], in0=lamt[:, 2, :], in1=lamt[:, 3, :], op=ALU.mult), [("lamt",)], [("lam2",)])
        S.op("dve", lambda e: e.tensor_reduce(out=lams[:, 0:2], in_=lam2[:], axis=AX.X, op=ALU.add), [("lam2",)], [("lams",)])
        S.op("act", lambda e: e.activation(out=lams[:, 0:2], in_=lams[:, 0:2], func=AF.Exp), [("lams",)], [("lams",)])
        S.op("dve", lambda e: e.tensor_tensor(out=lams[:, 2:3], in0=lams[:, 1:2], in1=lams[:, 0:1], op=ALU.subtract), [("lams",)], [("lams",)])
        S.op("dve", lambda e: e.tensor_scalar(out=lams[:, 2:3], in0=lams[:, 2:3], scalar1=-LAMBDA_INIT1, scalar2=None, op0=ALU.add),
             [("lams",)], [("lams",)])
        S.op("dve", lambda e: e.tensor_scalar(out=subg[:], in0=subg[:], scalar1=(1.0 - LAMBDA_INIT1), scalar2=None, op0=ALU.mult),
             [("subg",)], [("subg",)])
        wv_ = wqkv_d.rearrange("(c p) n -> p c n", p=128)
        pcnt = [0]
        scnt = [0]
        ptc = [0]
        uocnt = [0]

        def proj_bank():
            pcnt[0] += 1
            return 6 + (pcnt[0] % 2)

        for h in range(NH):
            wb = h % 2
            for part in range(3):
                cs = slice(part * 1024 + h * 128, part * 1024 + (h + 1) * 128)
                S.dma("pool", lambda e, wb=wb, part=part, cs=cs: e.dma_start(out=wq[wb][:, :, part, :], in_=wv_[:, :, cs]),
                      (), [("wq", wb)])
            S.dma("pool", lambda e, wb=wb, h=h: e.dma_start(out=wo[wb][:], in_=wout_d[h * 128:(h + 1) * 128, :]), (), [("wo", wb)])
            S.op("pool", lambda e, wb=wb: e.tensor_tensor(out=wo[wb][:], in0=wo[wb][:], in1=C.gh[:, 1, :], op=ALU.mult),
                 [("wo", wb), ("gh", 1, 0), ("gh", 1, 1)], [("wo", wb)])
            def proj_block(part, own, tb, srcbuf, col, dst, dr, blk, wb=wb):
                wqb = wq[wb]
                pb = proj_bank()
                for k in range(KC):
                    if own:
                        src = C.uT[:, k, 1 + tb * 512:1 + (tb + 1) * 512]
                        rr = [("uT", tb * 4 + q) for q in range(4)]
                    else:
                        src = uTo[srcbuf][:, k, :]
                        rr = [("uTo", srcbuf)]
                    S.op("pe", lambda e, k=k, pb=pb, src=src: e.matmul(
                        C.ps[pb][:], wqb[:, k, part, :], src, start=(k == 0), stop=(k == KC - 1)),
                        [("wq", wb)] + rr, [("ps", pb)])
                if blk % 2 == 0:
                    S.op("act", lambda e, pb=pb: e.copy(out=dst, in_=C.ps[pb][:]), [("ps", pb)], [dr])
                else:
                    S.op("dve", lambda e, pb=pb: e.tensor_copy(out=dst, in_=C.ps[pb][:]), [("ps", pb)], [dr])
                sb_ = blk % 2
                S.op("pool", lambda e, sb_=sb_: e.tensor_tensor(out=sq[sb_][:], in0=dst, in1=dst, op=ALU.mult), [dr], [("sq", sb_)])
                for m in range(2):
                    pb2 = proj_bank()
                    ms = slice(m * 64, (m + 1) * 64)
                    S.op("pe", lambda e, ms=ms, sb_=sb_, pb2=pb2: e.matmul(C.ps[pb2][:], onesb[ms, :], sq[sb_][ms, :], start=True, stop=True),
                         [("onesb",), ("sq", sb_)], [("ps", pb2)])
                    S.op("dve", lambda e, m=m, pb2=pb2: e.tensor_reduce(out=nstat[:, m, col:col + 1], in_=C.ps[pb2][:], axis=AX.X, op=ALU.max),
                         [("ps", pb2)], [("nstat",)])

            def v_tile(kt, own, srcbuf, off, wb=wb):
                wqb = wq[wb]
                pb = proj_bank()
                for k in range(KC):
                    if own:
                        src = C.uT[:, k, 1 + kt * 128:1 + (kt + 1) * 128]
                        rr = [("uT", kt)]
                    else:
                        src = uTo[srcbuf][:, k, off * 128:(off + 1) * 128]
                        rr = [("uTo", srcbuf)]
                    S.op("pe", lambda e, k=k, pb=pb, src=src: e.matmul(
                        C.ps[pb][:, 0:128], src, wqb[:, k, 2, :], start=(k == 0), stop=(k == KC - 1)),
                        [("wq", wb)] + rr, [("ps", pb)])
                if kt % 2 == 0:
                    S.op("act", lambda e, pb=pb: e.copy(out=v[:, kt, 0:128], in_=C.ps[pb][:, 0:128]), [("ps", pb)], [("v", kt)])
                else:
                    S.op("dve", lambda e, pb=pb: e.tensor_copy(out=v[:, kt, 0:128], in_=C.ps[pb][:, 0:128]), [("ps", pb)], [("v", kt)])

            for tb in range(4):
                proj_block(0, True, tb, None, tb, qT[:, tb * 512:(tb + 1) * 512], ("qT", tb), tb)
            for tb in range(4):
                proj_block(1, True, tb, None, 4 + tb, kT[:, tb * 512:(tb + 1) * 512], ("kT", tb), 4 + tb)
            for kt in range(16):
                v_tile(kt, True, None, None)
            for tb in range(4):
                ub = uocnt[0] % 2
                uocnt[0] += 1
                S.dma("sp", lambda e, ub=ub, tb=tb: e.dma_start(out=uTo[ub][:], in_=uTo_d[:, :, tb * 512:(tb + 1) * 512]), (), [("uTo", ub)])
                proj_block(1, False, tb, ub, 8 + tb, kT[:, (4 + tb) * 512:(5 + tb) * 512], ("kT", 4 + tb), 8 + tb)
                for off in range(4):
                    v_tile(16 + tb * 4 + off, False, ub, off)
            for m in range(2):
                S.op("dve", lambda e, m=m: e.tensor_reduce(out=nstat[:, m, 12:13], in_=nstat[:, m, 0:4], axis=AX.X, op=ALU.max), [("nstat",)], [("nstat",)])
                S.op("dve", lambda e, m=m: e.tensor_reduce(out=nstat[:, m, 13:14], in_=nstat[:, m, 4:12], axis=AX.X, op=ALU.max), [("nstat",)], [("nstat",)])
                S.op("dve", lambda e, m=m: e.tensor_tensor(out=nstat[:, m, 14:15], in0=nstat[:, m, 12:13], in1=nstat[:, m, 13:14], op=ALU.mult),
                     [("nstat",)], [("nstat",)])
                S.op("act", lambda e, m=m: e.activation(out=nstat[:, m, 15:16], in_=nstat[:, m, 14:15], func=AF.Sqrt, scale=scale * scale),
                     [("nstat",)], [("nstat",)])
                S.op("dve", lambda e, m=m: e.tensor_scalar(out=nbias[:, m:m + 1], in0=nstat[:, m, 15:16], scalar1=-1.0, scalar2=None, op0=ALU.mult),
                     [("nstat",)], [("nbias", m)])
            slope = SLOPES[h]
            for bi in range(2):
                for cchunk in range(8):
                    ib = cchunk % 2
                    c0 = cchunk * 496
                    if bi == 0:
                        S.op("pool", lambda e, ib=ib, c0=c0: e.iota(itmp[ib][:], pattern=[[1, 496]], base=c0 - 1920, channel_multiplier=-1,
                                                                    allow_small_or_imprecise_dtypes=True), (), [("itmp", ib)])
                        S.op("act", lambda e, ib=ib: e.activation(out=itmp[ib][:], in_=itmp[ib][:], func=AF.Abs), [("itmp", ib)], [("itmp", ib)])
                    else:
                        S.op("pool", lambda e, ib=ib, c0=c0: e.iota(itmp[ib][:], pattern=[[-1, 496]], base=4095 - c0, channel_multiplier=-1,
                                                                    allow_small_or_imprecise_dtypes=True), (), [("itmp", ib)])
                    S.op("act", lambda e, ib=ib, bi=bi, c0=c0, slope=slope: e.activation(out=band[bi][:, c0:c0 + 496], in_=itmp[ib][:],
                                                                                 func=AF.Exp, scale=-slope), [("itmp", ib)], [("band", bi)])
            for qb in range(4):
                q0 = qb * 512
                for m in range(2):
                    ms = slice(m * 64, (m + 1) * 64)
                    act_kts = []
                    for kt in range(32):
                        if kt < 16:
                            k0 = kt * 128
                            if k0 > q0 + 511:
                                dmin = k0 - (q0 + 511)
                            elif k0 + 127 < q0:
                                dmin = q0 - (k0 + 127)
                            else:
                                dmin = 0
                        else:
                            k0 = (kt - 16) * 128
                            dmin = 4095 - (q0 + 511) - (k0 + 127)
                        if slope * dmin < SKIP_T:
                            act_kts.append(kt)
                    LAG = 2
                    nact = len(act_kts)
                    pts = {}

                    def emit_pv(ai):
                        kt = act_kts[ai]
                        pt = pts[ai]
                        for qt in range(4):
                            S.op("pe", lambda e, pt=pt, qt=qt, kt=kt, ai=ai: e.matmul(
                                C.ps[2 + qt][:, 0:129], PT[pt][:, qt * 128:(qt + 1) * 128], v[:, kt, 0:129],
                                start=(ai == 0), stop=(ai == nact - 1)), [("PT", pt), ("v", kt), ("vone",)], [("ps", 2 + qt)])

                    for ai, kt in enumerate(act_kts):
                        sbk = scnt[0] % 2
                        scnt[0] += 1
                        pt = ptc[0] % 4
                        ptc[0] += 1
                        pts[ai] = pt
                        S.op("pe", lambda e, ms=ms, kt=kt, q0=q0, sbk=sbk: e.matmul(
                            C.ps[sbk][:], kT[ms, kt * 128:(kt + 1) * 128], qT[ms, q0:q0 + 512], start=True, stop=True),
                            [("kT", kt // 4), ("qT", qb)], [("ps", sbk)])
                        S.op("act", lambda e, sbk=sbk, pt=pt, m=m: e.activation(out=PT[pt][:], in_=C.ps[sbk][:], func=AF.Exp,
                                                                              bias=nbias[:, m:m + 1], scale=scale),
                             [("ps", sbk), ("nbias", m)], [("PT", pt)])
                        if kt < 16:
                            st = q0 - kt * 128 + 1920
                            bsl = band[0][:, st:st + 512]
                        else:
                            st = q0 + (kt - 16) * 128
                            bsl = band[1][:, st:st + 512]
                        bi = 0 if kt < 16 else 1
                        S.op("dve", lambda e, pt=pt, bsl=bsl: e.tensor_tensor(out=PT[pt][:], in0=PT[pt][:], in1=bsl, op=ALU.mult),
                             [("PT", pt), ("band", bi)], [("PT", pt)])
                        if ai >= LAG:
                            emit_pv(ai - LAG)
                    for ai in range(max(0, nact - LAG), nact):
                        emit_pv(ai)
                    for qt in range(4):
                        col = m * 4 + qt
                        S.op("dve", lambda e, qt=qt, col=col: e.reciprocal(out=rs[:, col:col + 1], in_=C.ps[2 + qt][:, 128:129]),
                             [("ps", 2 + qt)], [("rs", col)])
                        if m == 0:
                            S.op("dve", lambda e, qt=qt, col=col: e.tensor_scalar(out=osum[qt][:], in0=C.ps[2 + qt][:, 0:128],
                                                                               scalar1=rs[:, col:col + 1], scalar2=None, op0=ALU.mult),
                                 [("ps", 2 + qt), ("rs", col)], [("osum", qt)])
                        else:
                            ob = qt % 2
                            S.op("dve", lambda e, qt=qt, col=col, ob=ob: e.tensor_scalar(out=otmp[ob][:], in0=C.ps[2 + qt][:, 0:128],
                                                                                     scalar1=rs[:, col:col + 1], scalar2=lams[:, 2:3],
                                                                                     op0=ALU.mult, op1=ALU.mult),
                                 [("ps", 2 + qt), ("rs", col), ("lams",)], [("otmp", ob)])
                            S.op("pool", lambda e, qt=qt, ob=ob: e.tensor_tensor(out=osum[qt][:], in0=osum[qt][:], in1=otmp[ob][:], op=ALU.add),
                                 [("osum", qt), ("otmp", ob)], [("osum", qt)])
                for qt in range(4):
                    col = 8 + qt
                    ob = qt % 2
                    S.op("act", lambda e, qt=qt, col=col, ob=ob: e.activation(out=otmp[ob][:], in_=osum[qt][:], func=AF.Square, accum_out=rs[:, col:col + 1]),
                         [("osum", qt)], [("otmp", ob), ("rs", col)])
                    S.op("act", lambda e, col=col: e.activation(out=rs[:, col:col + 1], in_=rs[:, col:col + 1], func=AF.Sqrt,
                                                               bias=C.epsc[:, 0:1], scale=1.0 / 128.0), [("rs", col), ("epsc",)], [("rs", col)])
                    S.op("dve", lambda e, col=col: e.reciprocal(out=rs[:, col:col + 1], in_=rs[:, col:col + 1]), [("rs", col)], [("rs", col)])
                    S.op("dve", lambda e, qt=qt, col=col, ob=ob: e.scalar_tensor_tensor(out=onb[ob][:], in0=osum[qt][:], scalar=rs[:, col:col + 1],
                                                                                    in1=subg[:], op0=ALU.mult, op1=ALU.mult),
                         [("osum", qt), ("rs", col), ("subg",)], [("onb", ob)])
                    pb = proj_bank()
                    pbv = C.ps[pb][:].bitcast(BF16)
                    S.op("pe", lambda e, ob=ob, pbv=pbv: e.transpose(pbv[:, 0:128], onb[ob][:], C.identb[:]),
                         [("onb", ob), ("identb",)], [("ps", pb)])
                    tcol = q0 + qt * 128
                    S.op("act", lambda e, pbv=pbv, tcol=tcol: e.copy(out=oT[:, tcol:tcol + 128], in_=pbv[:, 0:128]),
                         [("ps", pb)], [("oT", tcol // 128)])
            for t in range(NT):
                for half in range(2):
                    pb = proj_bank()
                    S.op("pe", lambda e, t=t, half=half, wb=wb, pb=pb: e.matmul(
                        C.ps[pb][:], oT[:, t * 128:(t + 1) * 128], wo[wb][:, half * 512:(half + 1) * 512], start=True, stop=True),
                        [("oT", t), ("wo", wb)], [("ps", pb)])
                    xs = C.x[:, t, half * 512:(half + 1) * 512]
                    if h == 0:
                        S.op("dve", lambda e, xs=xs, pb=pb: e.scalar_tensor_tensor(out=xs, in0=xs, scalar=ALPHA, in1=C.ps[pb][:],
                                                                                 op0=ALU.mult, op1=ALU.add), [("x", t), ("ps", pb)], [("x", t)])
                    else:
                        S.op("dve", lambda e, xs=xs, pb=pb: e.tensor_tensor(out=xs, in0=xs, in1=C.ps[pb][:], op=ALU.add),
                             [("x", t), ("ps", pb)], [("x", t)])
        S.barrier()


def _op(S, eng, f, reads, writes):
    return S.op(eng, f, reads, writes)


def MM(S, out, lhsT, rhs, start, stop, reads, writes):
    S.op("pe", lambda e: e.matmul(out, lhsT, rhs, start=start, stop=stop), reads, writes)


def TT(S, eng, out, in0, in1, op, reads, writes):
    S.op(eng, lambda e: e.tensor_tensor(out=out, in0=in0, in1=in1, op=op), reads, writes)


def TS(S, eng, out, in0, s1, s2, op0, op1, reads, writes):
    if op1 is None:
        S.op(eng, lambda e: e.tensor_scalar(out=out, in0=in0, scalar1=s1, scalar2=None, op0=op0), reads, writes)
    else:
        S.op(eng, lambda e: e.tensor_scalar(out=out, in0=in0, scalar1=s1, scalar2=s2, op0=op0, op1=op1), reads, writes)


def ACTF(S, out, in_, func, reads, writes, bias=None, scale=None, accum_out=None):
    kw = {}
    if bias is not None:
        kw["bias"] = bias
    if scale is not None:
        kw["scale"] = scale
    if accum_out is not None:
        kw["accum_out"] = accum_out
    S.op("act", lambda e: e.activation(out=out, in_=in_, func=func, **kw), reads, writes)


def TR(S, out, in_, ident, reads, writes):
    S.op("pe", lambda e: e.transpose(out, in_, ident), reads, writes)


def phase_ssm_pass(nc, S, C, p, win_d, wdt_d, dtb_d, alog_d, convw_d, convb_d, dsk_d, normg_d, wout_d,
                   y1_d, sin_d, sout_d, scale_x):
    last = (p == 1)
    IDN, TLE, TGE, TGT, TLT = 0, 1, 2, 3, 4
    BT_, CPB, NBLK = 256, 2, 8
    Tm = C.tri[:, TLE if p == 0 else TGE, :]
    Um = C.tri[:, TGT if p == 0 else TLT, :]
    ones_f = None
    with ExitStack() as es:
        sb = lambda name, shape, dt: es.enter_context(nc.sbuf_tensor(_U() + "s%d_%s" % (p, name), shape, dt))
        xbcT = sb("xbcT", [128, 24, BT_], BF16)
        pre = [sb("pre%d" % i, [128, BT_ + 4], F32) for i in range(2)]
        cva = [sb("cva%d" % i, [128, BT_], F32) for i in range(2)]
        wfc = [sb("wfc%d" % i, [128, KC, 128], BF16) for i in range(2)]
        wdt = sb("wdt", [128, KC, 32], BF16)
        convw = sb("convw", [128, 24, 3], F32)
        convb = sb("convb", [128, 24], F32)
        rows = sb("rows", [128, 4, 32], F32)
        onesf = sb("onesf", [128, 128], F32)
        sm = [sb("sm%d" % i, [128, 8, 32], F32) for i in range(2)]
        Rb = sb("Rb", [128, 8, 128], F32)
        LT = [sb("LT%d" % i, [128, 8, 128], BF16) for i in range(2)]
        Gm = [sb("Gm%d" % i, [128, 128], BF16) for i in range(2)]
        Xtok = [sb("Xtok%d" % i, [128, 512], BF16) for i in range(2)]
        Btok = [sb("Btok%d" % i, [128, 128], BF16) for i in range(2)]
        Xt1 = [sb("Xt1%d" % i, [128, 512], BF16) for i in range(2)]
        Xt2 = [sb("Xt2%d" % i, [128, 512], BF16) for i in range(2)]
        Sst = sb("Sst", [128, 4, 512], F32)
        Sb = sb("Sb", [128, 4, 512], BF16)
        ych = sb("ych", [128, 2048], F32)
        t1 = sb("t1", [128, 512], F32)
        if last:
            wz = [sb("wz%d" % i, [128, KC, 128], BF16) for i in range(2)]
            sz = sb("sz", [128, CPB, 2048], BF16)
            yT = sb("yT", [128, 16, BT_], BF16)
            wo = [sb("wo%d" % i, [128, D], BF16) for i in range(2)]
            ngc = sb("ngc", [128, 16], F32)
            gs = sb("gs", [128, 16], F32)
        psb = [C.ps[i][:].bitcast(BF16) for i in range(8)]
        wv_ = win_d.rearrange("(c q) n -> q c n", q=128)

        S.dma("sp", lambda e: e.dma_start(out=convw[:], in_=convw_d), (), [("convw",)])
        S.dma("sp", lambda e: e.dma_start(out=convb[:], in_=convb_d), (), [("convb",)])
        S.dma("pool", lambda e: e.dma_start(out=wdt[:], in_=wdt_d[p].rearrange("(c q) n -> q c n", q=128)), (), [("wdt",)])
        S.dma("sp", lambda e: e.dma_start(out=rows[:, 0, :], in_=dtb_d[p:p + 1, :].to_broadcast([128, 32])), (), [("rows", 0)])
        S.dma("sp", lambda e: e.dma_start(out=rows[:, 1, :], in_=alog_d[p:p + 1, :].to_broadcast([128, 32])), (), [("rows", 1)])
        S.dma("sp", lambda e: e.dma_start(out=rows[:, 2, :], in_=dsk_d.to_broadcast([128, 32])), (), [("rows", 2)])
        ACTF(S, rows[:, 1, :], rows[:, 1, :], AF.Exp, [("rows", 1)], [("rows", 1)])
        TS(S, "dve", rows[:, 1, :], rows[:, 1, :], -1.0, None, ALU.mult, None, [("rows", 1)], [("rows", 1)])
        S.op("dve", lambda e: e.memset(onesf[:], 1.0), (), [("onesf",)])
        if scale_x:
            for t in range(NT):
                TS(S, "dve", C.x[:, t, :], C.x[:, t, :], ALPHA, None, ALU.mult, None, [("x", t)], [("x", t)])
        if p == 0:
            S.op("dve", lambda e: e.memset(Sst[:], 0.0), (), [("Sst", g) for g in range(4)])
            S.op("pool", lambda e: e.memset(Sb[:], 0.0), (), [("Sb", g) for g in range(4)])
        else:
            S.dma("sp", lambda e: e.dma_start(out=Sst[:], in_=sin_d), (), [("Sst", g) for g in range(4)])
            for g in range(4):
                S.op("act", lambda e, g=g: e.copy(out=Sb[:, g, :], in_=Sst[:, g, :]), [("Sst", g)], [("Sb", g)])
            S.dma("sp", lambda e: e.dma_start(out=ngc[:], in_=normg_d), (), [("ngc",)])
        pro_done = set()

        def prologue(c):
            pro_done.add(c)
            cp = c % 2
            smc = sm[cp]
            tcol = 1 + c * 128
            R = lambda i: ("sm", cp, i)
            for k in range(KC):
                MM(S, C.ps[3][:, 0:32], C.uT[:, k, tcol:tcol + 128], wdt[:, k, :], k == 0, k == KC - 1, [("wdt",), ("uT", c)], [("ps", 3)])
            TT(S, "dve", smc[:, 0, :], C.ps[3][:, 0:32], rows[:, 0, :], ALU.add, [("ps", 3), ("rows", 0)], [R(0)])
            ACTF(S, smc[:, 0, :], smc[:, 0, :], AF.Exp, [R(0)], [R(0)])
            ACTF(S, smc[:, 0, :], smc[:, 0, :], AF.Ln, [R(0)], [R(0)], bias=1.0)
            TT(S, "dve", smc[:, 1, :], smc[:, 0, :], rows[:, 1, :], ALU.mult, [R(0), ("rows", 1)], [R(1)])
            MM(S, C.ps[3][:, 32:64], Tm, smc[:, 1, :], True, True, [("tri",), R(1)], [("ps", 3)])
            MM(S, C.ps[3][:, 64:96], onesf[:], smc[:, 1, :], True, True, [("onesf",), R(1)], [("ps", 3)])
            S.op("act", lambda e: e.copy(out=smc[:, 2, :], in_=C.ps[3][:, 32:64]), [("ps", 3)], [R(2)])
            ACTF(S, smc[:, 3, :], C.ps[3][:, 32:64], AF.Exp, [("ps", 3)], [R(3)])
            ACTF(S, smc[:, 5, :], C.ps[3][:, 64:96], AF.Exp, [("ps", 3)], [R(5)])
            TT(S, "dve", smc[:, 6, :], C.ps[3][:, 64:96], smc[:, 2, :], ALU.subtract, [("ps", 3), R(2)], [R(6)])
            ACTF(S, smc[:, 6, :], smc[:, 6, :], AF.Exp, [R(6)], [R(6)])
            TT(S, "dve", smc[:, 4, :], smc[:, 6, :], smc[:, 0, :], ALU.mult, [R(6), R(0)], [R(4)])

        blocks = range(NBLK) if p == 0 else range(NBLK - 1, -1, -1)
        wcnt = [0]
        for blk in blocks:
            c0 = blk * BT_
            for fc in range(24):
                wb = wcnt[0] % 2
                pa = wcnt[0] % 2
                wcnt[0] += 1
                cs = slice(2048 + fc * 128, 2048 + (fc + 1) * 128)
                S.dma("pool", lambda e, wb=wb, cs=cs: e.dma_start(out=wfc[wb][:], in_=wv_[:, :, cs]), (), [("wfc", wb)])
                ur = [("uT", min(max(t, 0), NT - 1)) for t in range(blk * CPB - 1, blk * CPB + CPB + 1)] + [("uThalo",)]
                for k in range(KC):
                    MM(S, C.ps[pa][:, 0:BT_], wfc[wb][:, k, :], C.uT[:, k, c0:c0 + BT_], k == 0, k == KC - 1, [("wfc", wb)] + ur, [("ps", pa)])
                for k in range(KC):
                    MM(S, C.ps[2][:, pa * 8:pa * 8 + 2], wfc[wb][:, k, :], C.uT[:, k, c0 + BT_:c0 + BT_ + 2], k == 0, k == KC - 1,
                       [("wfc", wb)] + ur, [("ps", 2)])
                S.op("act", lambda e, pa=pa: e.copy(out=pre[pa][:, 0:BT_], in_=C.ps[pa][:, 0:BT_]), [("ps", pa)], [("pre", pa)])
                S.op("dve", lambda e, pa=pa: e.tensor_copy(out=pre[pa][:, BT_:BT_ + 2], in_=C.ps[2][:, pa * 8:pa * 8 + 2]), [("ps", 2)], [("pre", pa)])
                TS(S, "dve", cva[pa][:], pre[pa][:, 0:BT_], convw[:, fc, 0:1], None, ALU.mult, None, [("pre", pa), ("convw",)], [("cva", pa)])
                S.op("dve", lambda e, pa=pa, fc=fc: e.scalar_tensor_tensor(out=cva[pa][:], in0=pre[pa][:, 1:BT_ + 1], scalar=convw[:, fc, 1:2],
                                                                         in1=cva[pa][:], op0=ALU.mult, op1=ALU.add),
                     [("pre", pa), ("cva", pa), ("convw",)], [("cva", pa)])
                S.op("dve", lambda e, pa=pa, fc=fc: e.scalar_tensor_tensor(out=cva[pa][:], in0=pre[pa][:, 2:BT_ + 2], scalar=convw[:, fc, 2:3],
                                                                         in1=cva[pa][:], op0=ALU.mult, op1=ALU.add),
                     [("pre", pa), ("cva", pa), ("convw",)], [("cva", pa)])
                ACTF(S, xbcT[:, fc, :], cva[pa][:], AF.Silu, [("cva", pa), ("convb",)], [("xbcT", fc)], bias=convb[:, fc:fc + 1])
            if last:
                for zc in range(16):
                    zb = zc % 2
                    cs = slice(zc * 128, (zc + 1) * 128)
                    S.dma("pool", lambda e, zb=zb, cs=cs: e.dma_start(out=wz[zb][:], in_=wv_[:, :, cs]), (), [("wz", zb)])
                    for cc in range(CPB):
                        tcol = 1 + c0 + cc * 128
                        for k in range(KC):
                            MM(S, C.ps[7][:, 0:128], C.uT[:, k, tcol:tcol + 128], wz[zb][:, k, :], k == 0, k == KC - 1,
                               [("wz", zb), ("uT", blk * CPB + cc)], [("ps", 7)])
                        ACTF(S, sz[:, cc, zc * 128:(zc + 1) * 128], C.ps[7][:, 0:128], AF.Silu, [("ps", 7)], [("sz", cc)])
            chunks = list(range(CPB)) if p == 0 else list(range(CPB - 1, -1, -1))
            for cc in chunks:
                c = blk * CPB + cc
                off = cc * 128
                if c not in pro_done:
                    prologue(c)
                nxt = c + 1 if p == 0 else c - 1
                cp = c % 2
                smc = sm[cp]
                if last:
                    S.dma("sp", lambda e, c=c: e.dma_start(out=ych[:], in_=y1_d[c]), (), [("ych", g) for g in range(4)])

                def stageA(g, off=off, smc=smc, cp=cp):
                    g2 = g % 2
                    hs = slice(8 * g, 8 * g + 8)
                    BT = xbcT[:, 16 + g, off:off + 128]
                    CT = xbcT[:, 20 + g, off:off + 128]
                    TT(S, "dve", Rb[:], Tm.unsqueeze(1).to_broadcast([128, 8, 128]), smc[:, 1, hs].unsqueeze(2).to_broadcast([128, 8, 128]),
                       ALU.mult, [("tri",), ("sm", cp, 1)], [("Rb",)])
                    for j in range(2):
                        MM(S, C.ps[4 + j][:], Um, Rb[:, 4 * j:4 * j + 4, :].rearrange("q a b -> q (a b)"), True, True, [("tri",), ("Rb",)], [("ps", 4 + j)])
                        ACTF(S, LT[g2][:, 4 * j:4 * j + 4, :].rearrange("q a b -> q (a b)"), C.ps[4 + j][:], AF.Exp, [("ps", 4 + j)], [("LT", g2)])
                    MM(S, C.ps[6][:, 0:128], BT, CT, True, True, [("xbcT", 16 + g), ("xbcT", 20 + g)], [("ps", 6)])
                    TT(S, "dve", Gm[g2][:], C.ps[6][:, 0:128], Tm, ALU.mult, [("ps", 6), ("tri",)], [("Gm", g2)])
                    TT(S, "dve", LT[g2][:], LT[g2][:], Gm[g2][:].unsqueeze(1).to_broadcast([128, 8, 128]), ALU.mult, [("LT", g2), ("Gm", g2)], [("LT", g2)])
                    for j in range(4):
                        TR(S, psb[7][:, j * 128:(j + 1) * 128], xbcT[:, 4 * g + j, off:off + 128], C.identb[:], [("xbcT", 4 * g + j), ("identb",)], [("ps", 7)])
                    TR(S, psb[7][:, 512:640], xbcT[:, 16 + g, off:off + 128], C.identb[:], [("xbcT", 16 + g), ("identb",)], [("ps", 7)])
                    S.op("act", lambda e: e.copy(out=Xtok[g2][:], in_=psb[7][:, 0:512]), [("ps", 7)], [("Xtok", g2)])
                    S.op("act", lambda e: e.copy(out=Btok[g2][:], in_=psb[7][:, 512:640]), [("ps", 7)], [("Btok", g2)])
                    X3 = Xtok[g2][:].rearrange("q (a b) -> q a b", a=8)
                    TT(S, "dve", Xt1[g2][:].rearrange("q (a b) -> q a b", a=8), X3, smc[:, 0, hs].unsqueeze(2).to_broadcast([128, 8, 64]), ALU.mult,
                       [("Xtok", g2), ("sm", cp, 0)], [("Xt1", g2)])
                    TT(S, "dve", Xt2[g2][:].rearrange("q (a b) -> q a b", a=8), X3, smc[:, 4, hs].unsqueeze(2).to_broadcast([128, 8, 64]), ALU.mult,
                       [("Xtok", g2), ("sm", cp, 4)], [("Xt2", g2)])

                def stageB(g, off=off, smc=smc, cp=cp):
                    g2 = g % 2
                    hs = slice(8 * g, 8 * g + 8)
                    CT = xbcT[:, 20 + g, off:off + 128]
                    X3 = Xtok[g2][:].rearrange("q (a b) -> q a b", a=8)
                    for hh in range(8):
                        MM(S, C.ps[0][:, hh * 64:(hh + 1) * 64], LT[g2][:, hh, :], Xt1[g2][:, hh * 64:(hh + 1) * 64], True, True,
                           [("LT", g2), ("Xt1", g2)], [("ps", 0)])
                    MM(S, C.ps[1][:], CT, Sb[:, g, :], True, True, [("xbcT", 20 + g), ("Sb", g)], [("ps", 1)])
                    MM(S, C.ps[2][:], Btok[g2][:], Xt2[g2][:], True, True, [("Btok", g2), ("Xt2", g2)], [("ps", 2)])
                    TT(S, "dve", t1[:].rearrange("q (a b) -> q a b", a=8), C.ps[1][:].rearrange("q (a b) -> q a b", a=8),
                       smc[:, 3, hs].unsqueeze(2).to_broadcast([128, 8, 64]), ALU.mult, [("ps", 1), ("sm", cp, 3)], [("t1",)])
                    yg = ych[:, g * 512:(g + 1) * 512]
                    if not last:
                        TT(S, "dve", yg, t1[:], C.ps[0][:], ALU.add, [("t1",), ("ps", 0)], [("ych", g)])
                        TT(S, "pool", t1[:].rearrange("q (a b) -> q a b", a=8), X3, rows[:, 2, hs].unsqueeze(2).to_broadcast([128, 8, 64]), ALU.mult,
                           [("Xtok", g2), ("rows", 2), ("t1",)], [("t1",)])
                        TT(S, "pool", yg, yg, t1[:], ALU.add, [("t1",), ("ych", g)], [("ych", g)])
                    else:
                        TT(S, "dve", t1[:], t1[:], C.ps[0][:], ALU.add, [("t1",), ("ps", 0)], [("t1",)])
                        TT(S, "pool", yg, yg, t1[:], ALU.add, [("t1",), ("ych", g)], [("ych", g)])
                    TT(S, "dve", Sst[:, g, :].rearrange("q (a b) -> q a b", a=8), Sst[:, g, :].rearrange("q (a b) -> q a b", a=8),
                       smc[:, 5, hs].unsqueeze(2).to_broadcast([128, 8, 64]), ALU.mult, [("Sst", g), ("sm", cp, 5)], [("Sst", g)])
                    TT(S, "dve", Sst[:, g, :], Sst[:, g, :], C.ps[2][:], ALU.add, [("Sst", g), ("ps", 2)], [("Sst", g)])
                    S.op("act", lambda e, g=g: e.copy(out=Sb[:, g, :], in_=Sst[:, g, :]), [("Sst", g)], [("Sb", g)])

                stageA(0)
                stageA(1)
                if 0 <= nxt < NT:
                    prologue(nxt)
                stageB(0)
                stageA(2)
                stageB(1)
                stageA(3)
                stageB(2)
                stageB(3)
                if not last:
                    S.dma("sp", lambda e, c=c: e.dma_start(out=y1_d[c], in_=ych[:]), [("ych", g) for g in range(4)], [("y1d", c)])
                else:
                    TT(S, "dve", ych[:], ych[:], sz[:, cc, :], ALU.mult, [("ych", g) for g in range(4)] + [("sz", cc)], [("ych", g) for g in range(4)])
                    for g in range(4):
                        ACTF(S, t1[:], ych[:, g * 512:(g + 1) * 512], AF.Square, [("ych", g)], [("t1",), ("gs", g)], accum_out=gs[:, g:g + 1])
                        ACTF(S, gs[:, g:g + 1], gs[:, g:g + 1], AF.Sqrt, [("gs", g), ("epsc",)], [("gs", g)], bias=C.epsc[:, 0:1], scale=1.0 / 512.0)
                        S.op("dve", lambda e, g=g: e.reciprocal(out=gs[:, g:g + 1], in_=gs[:, g:g + 1]), [("gs", g)], [("gs", g)])
                        TS(S, "dve", sz[:, cc, g * 512:(g + 1) * 512], ych[:, g * 512:(g + 1) * 512], gs[:, g:g + 1], None, ALU.mult, None,
                           [("ych", g), ("gs", g)], [("sz", cc)])
                    for q4 in range(4):
                        for j in range(4):
                            ch = q4 * 4 + j
                            TR(S, psb[6][:, j * 128:(j + 1) * 128], sz[:, cc, ch * 128:(ch + 1) * 128], C.identb[:], [("sz", cc), ("identb",)], [("ps", 6)])
                        S.op("act", lambda e, q4=q4, off=off: e.copy(out=yT[:, 4 * q4:4 * q4 + 4, off:off + 128],
                                                                    in_=psb[6][:, 0:512].rearrange("q (a b) -> q a b", a=4)),
                             [("ps", 6)], [("yT", cc)])
            if last:
                for ch in range(16):
                    wb = ch % 2
                    S.dma("pool", lambda e, wb=wb, ch=ch: e.dma_start(out=wo[wb][:], in_=wout_d[ch * 128:(ch + 1) * 128, :]), (), [("wo", wb)])
                    TS(S, "dve", wo[wb][:], wo[wb][:], ngc[:, ch:ch + 1], None, ALU.mult, None, [("wo", wb), ("ngc",)], [("wo", wb)])
                    TT(S, "dve", wo[wb][:], wo[wb][:], C.gh[:, 1, :], ALU.mult, [("wo", wb), ("gh", 1, 0), ("gh", 1, 1)], [("wo", wb)])
                    for cc in range(CPB):
                        t = blk * CPB + cc
                        for half in range(2):
                            pb = 4 + ((cc * 2 + half) % 2)
                            MM(S, C.ps[pb][:], yT[:, ch, cc * 128:(cc + 1) * 128], wo[wb][:, half * 512:(half + 1) * 512], True, True,
                               [("yT", cc), ("wo", wb)], [("ps", pb)])
                            xs = C.x[:, t, half * 512:(half + 1) * 512]
                            TT(S, "dve", xs, xs, C.ps[pb][:], ALU.add, [("x", t), ("ps", pb)], [("x", t)])
        if not last:
            S.dma("sp", lambda e: e.dma_start(out=sout_d, in_=Sst[:]), [("Sst", g) for g in range(4)], [("soutd",)])
        S.barrier()


def _dram_in(nc, name, shape, dt=F32):
    return nc.dram_tensor("d_" + name, list(shape), dt, kind="ExternalInput").ap()


def _dram_out(nc, name, shape, dt=F32):
    return nc.dram_tensor("d_" + name, list(shape), dt, kind="ExternalOutput").ap()


def build_stage(stage):
    nc = bass.Bass("TRN2", target_bir_lowering=False)
    S = Sched()
    C = Ctx()
    I = lambda name, shape, dt=F32: _dram_in(nc, name, shape, dt)
    x_d = I("x_in", [TOK, D])
    ccol_d = I("ccol", [128, KC])
    cpack_d = I("cpack", [128, 5, 128])
    lay = [0] if stage in (0, 1) else ([0, 1] if stage == 2 else [1])
    ada_w = {i: I("ada_w%d" % i, [D, 9216]) for i in lay}
    ada_b = {i: I("ada_b%d" % i, [1, 9216]) for i in lay}
    lng = {i: I("lng%d" % i, [3, D]) for i in lay}
    lnb = {i: I("lnb%d" % i, [3, D]) for i in lay}
    ffn_keys = {0: [(0, 0)], 1: [], 2: [(0, 1), (1, 0)], 3: [(1, 1)]}[stage]
    ffw = {}
    for (i, j) in ffn_keys:
        ffw[(i, j)] = (I("wg%d%d" % (i, j), [D, DFF]), I("wu%d%d" % (i, j), [D, DFF]), I("wd%d%d" % (i, j), [DFF, D]))
    if stage in (1, 2):
        w_in = I("w_in", [D, 5184])
        wdt = I("wdt", [2, D, 32])
        dtb = I("dtb", [2, 32])
        alog = I("alog", [2, 32])
        convw = I("convw", [128, 24, 3])
        convb = I("convb", [128, 24])
        dsk = I("dsk", [1, 32])
        normg = I("normg", [128, 16])
        ssm_wout = I("ssm_wout", [2048, D])
        xh = I("xh", [128, KC])
    if stage == 1:
        y1 = _dram_out(nc, "y1", [16, 128, 2048])
        s_out = _dram_out(nc, "s_out", [128, 4, 512])
        s_in = None
    if stage == 2:
        y1 = I("y1", [16, 128, 2048])
        s_in = I("s_in", [128, 4, 512])
        s_out = None
        uT_out = _dram_out(nc, "uT_out", [128, KC, TOK], BF16)
    if stage == 3:
        uTo = I("uTo", [128, KC, TOK], BF16)
        wqkv = I("wqkv", [D, 3072])
        lam = I("lam", [4, 64])
        subg = I("subg", [1, 128])
        attn_wout = I("attn_wout", [D, D])
    if stage != 1:
        y_d = _dram_out(nc, "y_out", [TOK, D])

    with ExitStack() as es:
        alloc_globals(nc, es, C)
        load_consts(nc, S, C, cpack_d)
        S.op("dve", lambda e: e.memset(C.uT[:, :, 0:1], 0.0), (), [("uT0",)])
        xv = x_d.rearrange("(t q) d -> q t d", q=128)
        for t in range(NT):
            S.dma("sp", lambda e, t=t: e.dma_start(out=C.x[:, t, :], in_=xv[:, t, :]), (), [("x", t)])

        def ffn(i, j):
            wg, wu, wd = ffw[(i, j)]
            phase_ffn(nc, S, C, 2 * j, wg, wu, wd)

        def halo():
            with ExitStack() as hs:
                xht = hs.enter_context(nc.sbuf_tensor(_U() + "xht", [128, KC], F32))
                S.dma("sp", lambda e: e.dma_start(out=xht[:], in_=xh), (), [("xht",)])
                TT(S, "dve", xht[:], xht[:], C.modcol[:, 1, 1, :], ALU.mult, [("xht",), ("modcol", 1)], [("xht",)])
                TT(S, "dve", C.uT[:, :, TOK + 1:TOK + 2], xht[:].unsqueeze(2), C.modcol[:, 1, 0, :].unsqueeze(2), ALU.add,
                   [("xht",), ("modcol", 1)], [("uThalo",)])
                S.barrier()

        if stage == 0:
            phase_mods(nc, S, C, ccol_d, ada_w[0], ada_b[0])
            ln_modulate_stage(nc, S, C, 0, False)
            S.barrier()
            ffn(0, 0)
            ln_modulate_stage(nc, S, C, None, True, lng[0][0:1, :], lnb[0][0:1, :])
        elif stage == 1:
            phase_mods(nc, S, C, ccol_d, ada_w[0], ada_b[0])
            ln_modulate_stage(nc, S, C, 1, False)
            halo()
            phase_ssm_pass(nc, S, C, 0, w_in, wdt, dtb, alog, convw, convb, dsk, normg, ssm_wout, y1, None, s_out, False)
        elif stage == 2:
            phase_mods(nc, S, C, ccol_d, ada_w[0], ada_b[0])
            ln_modulate_stage(nc, S, C, 1, False)
            halo()
            phase_ssm_pass(nc, S, C, 1, w_in, wdt, dtb, alog, convw, convb, dsk, normg, ssm_wout, y1, s_in, None, True)
            ln_modulate_stage(nc, S, C, 2, True, lng[0][1:2, :], lnb[0][1:2, :])
            ffn(0, 1)
            ln_modulate_stage(nc, S, C, None, True, lng[0][2:3, :], lnb[0][2:3, :])
            phase_mods(nc, S, C, ccol_d, ada_w[1], ada_b[1])
            ln_modulate_stage(nc, S, C, 0, False)
            S.barrier()
            ffn(1, 0)
            ln_modulate_stage(nc, S, C, 1, True, lng[1][0:1, :], lnb[1][0:1, :])
            S.dma("sp", lambda e: e.dma_start(out=uT_out, in_=C.uT[:, :, 1:1 + TOK]), [("uT", t) for t in range(NT)], [("uTout",)])
        else:
            phase_mods(nc, S, C, ccol_d, ada_w[1], ada_b[1])
            ln_modulate_stage(nc, S, C, 1, False)
            S.barrier()
            phase_attn(nc, S, C, uTo, wqkv, lam, subg, attn_wout)
            ln_modulate_stage(nc, S, C, 2, True, lng[1][1:2, :], lnb[1][1:2, :])
            ffn(1, 1)
            ln_modulate_stage(nc, S, C, None, True, lng[1][2:3, :], lnb[1][2:3, :])
        S.barrier()
        if stage != 1:
            yv = y_d.rearrange("(t q) d -> q t d", q=128)
            for t in range(NT):
                S.dma("sp", lambda e, t=t: e.dma_start(out=yv[:, t, :], in_=C.x[:, t, :]), [("x", t)], [("yout", t)])
        S.barrier()
        S.emit(nc, es)
    return nc


def _cpack():
    r = np.arange(128)[:, None]
    c = np.arange(128)[None, :]
    mats = [r == c, r <= c, r >= c, r > c, r < c]
    return np.ascontiguousarray(np.stack([m.astype(np.float32) for m in mats], axis=1))


def _col(v):
    v = np.asarray(v)
    return np.ascontiguousarray(v.reshape(-1, 128).T)


_PROGS = {}


def _prog(stage):
    if stage not in _PROGS:
        _PROGS[stage] = build_stage(stage)
    return _PROGS[stage]


DEBUG_STOP = None


def kernel(x, c, ada_w, ada_b, ln_g, ln_b, ffn_w_gate, ffn_w_up, ffn_w_down,
           ssm_w_in, ssm_conv_w, ssm_conv_b, ssm_dt_bias, ssm_a_log, ssm_d, ssm_norm_g, ssm_w_out,
           attn_w_qkv, attn_lambda, attn_subln_g, attn_w_out):
    f = lambda a: np.ascontiguousarray(np.asarray(a, dtype=np.float32))
    x, c = f(x), f(c)
    ada_w, ada_b, ln_g, ln_b = f(ada_w), f(ada_b), f(ln_g), f(ln_b)
    wgate, wup, wdown = f(ffn_w_gate), f(ffn_w_up), f(ffn_w_down)
    w_in, conv_w, conv_b = f(ssm_w_in)[0], f(ssm_conv_w)[0], f(ssm_conv_b)[0]
    dt_bias, a_log, dsk, norm_g, ssm_wout = f(ssm_dt_bias)[0], f(ssm_a_log)[0], f(ssm_d), f(ssm_norm_g)[0], f(ssm_w_out)[0]
    wqkv, lam, subg, attn_wout = f(attn_w_qkv)[0], f(attn_lambda)[0], f(attn_subln_g), f(attn_w_out)[0]
    cores = list(range(8))
    cpack = _cpack()

    def local(arr, core):
        b, h = core // 2, core % 2
        a = arr[b, h * TOK:(h + 1) * TOK]
        return np.ascontiguousarray(a[::-1] if h else a)

    def common(core, lays):
        b = core // 2
        m = {"d_ccol": _col(c[b]), "d_cpack": cpack}
        for i in lays:
            m["d_ada_w%d" % i] = ada_w[i]
            m["d_ada_b%d" % i] = ada_b[i:i + 1]
            m["d_lng%d" % i] = ln_g[i]
            m["d_lnb%d" % i] = ln_b[i]
        return m

    def ffn_in(m, keys):
        for (i, j) in keys:
            m["d_wg%d%d" % (i, j)] = wgate[i, j]
            m["d_wu%d%d" % (i, j)] = wup[i, j]
            m["d_wd%d%d" % (i, j)] = wdown[i, j]

    def ssm_in(m, core, x1loc):
        h = core % 2
        sets = [0, 1] if h == 0 else [1, 0]
        m["d_w_in"] = w_in
        m["d_wdt"] = np.ascontiguousarray(np.stack([w_in[:, 5120 + 32 * s:5152 + 32 * s] for s in sets]))
        m["d_dtb"] = np.ascontiguousarray(np.stack([dt_bias[s] for s in sets]))
        m["d_alog"] = np.ascontiguousarray(np.stack([a_log[s] for s in sets]))
        cw = conv_w if h == 0 else conv_w[::-1]
        m["d_convw"] = np.ascontiguousarray(cw.reshape(3, 24, 128).transpose(2, 1, 0))
        m["d_convb"] = _col(conv_b)
        m["d_dsk"] = dsk
        m["d_normg"] = _col(norm_g)
        m["d_ssm_wout"] = ssm_wout
        m["d_xh"] = _col(x1loc[core ^ 1][TOK - 1])

    maps = []
    for core in cores:
        m = common(core, [0])
        m["d_x_in"] = local(x, core)
        ffn_in(m, [(0, 0)])
        maps.append(m)
    r0 = run_bass_kernel_spmd(_prog(0), maps, core_ids=cores)
    x1 = [np.asarray(r0.results[k]["d_y_out"]) for k in cores]
    if DEBUG_STOP == 0:
        return x1
    maps = []
    for core in cores:
        m = common(core, [0])
        m["d_x_in"] = x1[core]
        ssm_in(m, core, x1)
        maps.append(m)
    r1 = run_bass_kernel_spmd(_prog(1), maps, core_ids=cores)
    y1 = [np.asarray(r1.results[k]["d_y1"]) for k in cores]
    so = [np.asarray(r1.results[k]["d_s_out"]) for k in cores]
    maps = []
    for core in cores:
        m = common(core, [0, 1])
        m["d_x_in"] = x1[core]
        ssm_in(m, core, x1)
        m["d_y1"] = y1[core]
        m["d_s_in"] = so[core ^ 1]
        ffn_in(m, [(0, 1), (1, 0)])
        maps.append(m)
    r2 = run_bass_kernel_spmd(_prog(2), maps, core_ids=cores)
    x4 = [np.asarray(r2.results[k]["d_y_out"]) for k in cores]
    uT4 = [np.asarray(r2.results[k]["d_uT_out"]) for k in cores]
    if DEBUG_STOP == 2:
        return x4
    maps = []
    for core in cores:
        m = common(core, [1])
        m["d_x_in"] = x4[core]
        m["d_uTo"] = uT4[core ^ 1]
        m["d_wqkv"] = wqkv
        m["d_lam"] = lam
        m["d_subg"] = subg
        m["d_attn_wout"] = attn_wout
        ffn_in(m, [(1, 1)])
        maps.append(m)
    r3 = run_bass_kernel_spmd(_prog(3), maps, core_ids=cores)
    out = np.empty((4, 2 * TOK, D), dtype=np.float32)
    for core in cores:
        b, h = core // 2, core % 2
        y = np.asarray(r3.results[core]["d_y_out"])
        out[b, h * TOK:(h + 1) * TOK] = y[::-1] if h else y
    return out
```

```python
from contextlib import ExitStack
import numpy as np
import concourse.bass as bass
import concourse.mybir as mybir
from concourse.bass_utils import run_bass_kernel_spmd

DT = mybir.dt
F32 = DT.float32
BF16 = DT.bfloat16
AF = mybir.ActivationFunctionType
ALU = mybir.AluOpType
AX = mybir.AxisListType

ENGS = ("pe", "act", "dve", "pool", "sp")
N_DMA_SEMS = 40
MAX_OUTSTANDING_DMA = 10 ** 9


class Op:
    __slots__ = ("eng", "fn", "deps", "signal", "seq", "is_dma", "slot", "slot_val",
                 "prev_slot_user", "idx", "is_nop")


class Sched:
    def __init__(self):
        self.ops = {e: [] for e in ENGS}
        self.last_write = {}
        self.readers = {}
        self.n_dma = 0
        self.slot_last = [None] * N_DMA_SEMS
        self.slot_count = [0] * N_DMA_SEMS
        self.all = []
        self.dma_hist = {}

    def _add(self, eng, fn, reads, writes, is_dma):
        o = Op()
        o.eng, o.fn, o.deps, o.signal, o.seq, o.is_dma = eng, fn, [], False, 0, is_dma
        o.slot = o.slot_val = None
        o.prev_slot_user = None
        o.is_nop = False
        o.idx = len(self.all)
        deps = {}
        for r in reads:
            w = self.last_write.get(r)
            if w is not None:
                deps[id(w)] = w
        for r in writes:
            w = self.last_write.get(r)
            if w is not None:
                deps[id(w)] = w
            for rd in self.readers.get(r, ()):
                if rd is not o:
                    deps[id(rd)] = rd
        for d in deps.values():
            if d.eng == eng and not d.is_dma and not is_dma:
                if eng == "pe":
                    continue
            o.deps.append(d)
            d.signal = True
        for r in reads:
            self.readers.setdefault(r, []).append(o)
        for r in writes:
            self.last_write[r] = o
            self.readers[r] = []
        if is_dma:
            q = self.dma_hist.setdefault(eng, [])
            if len(q) >= MAX_OUTSTANDING_DMA:
                o.deps.append(q[-MAX_OUTSTANDING_DMA])
            q.append(o)
            s = self.n_dma % N_DMA_SEMS
            self.n_dma += 1
            o.prev_slot_user = self.slot_last[s]
            self.slot_last[s] = o
            self.slot_count[s] += 1
            o.slot, o.slot_val = s, 16 * self.slot_count[s]
            o.signal = True
        self.ops[eng].append(o)
        self.all.append(o)
        return o

    def op(self, eng, fn, reads=(), writes=()):
        return self._add(eng, fn, reads, writes, False)

    def dma(self, eng, fn, reads=(), writes=()):
        return self._add(eng, fn, reads, writes, True)

    def barrier(self):
        key = ("__barrier__",)
        lasts = []
        for e in ENGS:
            for o in reversed(self.ops[e]):
                if not o.is_dma and not o.is_nop:
                    lasts.append(o)
                    break
        dmas = [o for o in self.slot_last if o is not None]
        for e in ("pe", "act", "dve", "pool", "sp"):
            o = self._add(e, lambda eng: eng.nop(), (), (), False)
            o.is_nop = True
            for d in lasts + dmas:
                if d.eng == e and e == "pe" and not d.is_dma:
                    continue
                o.deps.append(d)
                d.signal = True
        self.last_write = {}
        self.readers = {}

    def emit(self, nc, es):
        eng_sem = {e: es.enter_context(nc.semaphore("prog_" + e)) for e in ENGS}
        dma_sem = [es.enter_context(nc.semaphore("dma%d" % i)) for i in range(N_DMA_SEMS)]
        for e in ENGS:
            n = 0
            for o in self.ops[e]:
                if o.signal and not o.is_dma:
                    n += 1
                    o.seq = n
        block = es.enter_context(nc.Block())
        sched = self

        def body_for(e):
            def body(eng):
                waited_eng = {f: 0 for f in ENGS}
                waited_slot = [0] * N_DMA_SEMS
                for o in sched.ops[e]:
                    need_eng = {}
                    need_slot = {}
                    deps = list(o.deps)
                    if o.is_dma and o.prev_slot_user is not None:
                        deps.append(o.prev_slot_user)
                    for d in deps:
                        if d.is_dma:
                            if d.slot_val > waited_slot[d.slot]:
                                need_slot[d.slot] = max(need_slot.get(d.slot, 0), d.slot_val)
                        else:
                            if d.seq > waited_eng[d.eng]:
                                need_eng[d.eng] = max(need_eng.get(d.eng, 0), d.seq)
                    for f, v in need_eng.items():
                        eng.wait_ge(eng_sem[f], v)
                        waited_eng[f] = v
                    for s, v in need_slot.items():
                        eng.wait_ge(dma_sem[s], v)
                        waited_slot[s] = v
                    ins = o.fn(eng)
                    if o.is_dma:
                        ins.then_inc(dma_sem[o.slot], 16)
                    elif o.signal:
                        ins.then_inc(eng_sem[e], 1)
            return body

        block.tensor(body_for("pe"))
        block.scalar(body_for("act"))
        block.vector(body_for("dve"))
        block.gpsimd(body_for("pool"))
        block.sync(body_for("sp"))


D = 1024
TOK = 2048
NT = TOK // 128
DFF = 2816
NF = DFF // 128
KC = D // 128
ALPHA = (2 * 2) ** 0.25
LN_EPS = 1e-5
FFN_GROUPS = [(0, 4), (4, 4), (8, 4), (12, 4), (16, 4), (20, 2)]


class Ctx:
    pass


_UC = [0]


def _U():
    _UC[0] += 1
    return "u%d_" % _UC[0]


def alloc_globals(nc, es, C):
    C.x = es.enter_context(nc.sbuf_tensor(_U() + "x_res", [128, NT, D], F32))
    C.uT = es.enter_context(nc.sbuf_tensor(_U() + "uT", [128, KC, TOK + 2], BF16))
    C.ident = es.enter_context(nc.sbuf_tensor(_U() + "ident", [128, 128], F32))
    C.identb = es.enter_context(nc.sbuf_tensor(_U() + "identb", [128, 128], BF16))
    C.gh = es.enter_context(nc.sbuf_tensor(_U() + "gh", [128, 3, D], F32))
    C.modcol = es.enter_context(nc.sbuf_tensor(_U() + "modcol", [128, 3, 2, KC], F32))
    C.tri = es.enter_context(nc.sbuf_tensor(_U() + "tri", [128, 5, 128], F32))
    C.stat = es.enter_context(nc.sbuf_tensor(_U() + "stat", [128, 2, 16], F32))
    C.epsc = es.enter_context(nc.sbuf_tensor(_U() + "epsc", [128, 1], F32))
    C.ps = [es.enter_context(nc.psum_tensor("ps%d" % b, [128, 512], F32)) for b in range(8)]


def load_consts(nc, S, C, ident_d):
    S.dma("sp", lambda e: e.dma_start(out=C.tri[:], in_=ident_d), (), [("tri",)])
    S.op("dve", lambda e: e.tensor_copy(out=C.ident[:], in_=C.tri[:, 0, :]), [("tri",)], [("ident",)])
    S.op("dve", lambda e: e.tensor_copy(out=C.identb[:], in_=C.ident[:]), [("ident",)], [("identb",)])
    S.op("dve", lambda e: e.memset(C.epsc[:], LN_EPS), (), [("epsc",)])


def phase_mods(nc, S, C, ccol_d, ada_w_d, ada_b_d):
    with ExitStack() as es:
        ccol = es.enter_context(nc.sbuf_tensor(_U() + "ccol", [128, KC], F32))
        condb = es.enter_context(nc.sbuf_tensor(_U() + "condb", [128, KC], BF16))
        condrep = es.enter_context(nc.sbuf_tensor(_U() + "condrep", [128, KC, 128], BF16))
        wa = [es.enter_context(nc.sbuf_tensor(_U() + "wa%d" % i, [128, KC, 512], BF16)) for i in range(3)]
        brow = [es.enter_context(nc.sbuf_tensor(_U() + "brow%d" % i, [128, 512], F32)) for i in range(3)]
        mrow = [es.enter_context(nc.sbuf_tensor(_U() + "mrow%d" % i, [128, 512], F32)) for i in range(2)]
        S.dma("sp", lambda e: e.dma_start(out=ccol[:], in_=ccol_d), (), [("ccol",)])
        S.op("act", lambda e: e.activation(out=condb[:], in_=ccol[:], func=AF.Silu), [("ccol",)], [("condb",)])
        S.op("dve", lambda e: e.tensor_copy(out=condrep[:], in_=condb[:].unsqueeze(2).to_broadcast([128, KC, 128])),
             [("condb",)], [("condrep",)])
        wv = ada_w_d.rearrange("(c p) n -> p c n", p=128)
        for nb in range(18):
            b3, b2 = nb % 3, nb % 2
            sl = slice(nb * 512, (nb + 1) * 512)
            S.dma("pool", lambda e, b3=b3, sl=sl: e.dma_start(out=wa[b3][:], in_=wv[:, :, sl]), (), [("wa", b3)])
            S.dma("sp", lambda e, b3=b3, sl=sl: e.dma_start(out=brow[b3][:], in_=ada_b_d[0:1, sl].to_broadcast([128, 512])),
                  (), [("brow", b3)])
            pb = nb % 2
            for k in range(KC):
                S.op("pe", lambda e, k=k, b3=b3, pb=pb: e.matmul(C.ps[pb][:], condrep[:, k, :], wa[b3][:, k, :],
                                                                  start=(k == 0), stop=(k == KC - 1)),
                     [("condrep",), ("wa", b3)], [("ps", pb)])
            S.op("dve", lambda e, b2=b2, b3=b3, pb=pb: e.tensor_tensor(out=mrow[b2][:], in0=C.ps[pb][:], in1=brow[b3][:], op=ALU.add),
                 [("ps", pb), ("brow", b3)], [("mrow", b2)])
            sub, kind, half = nb // 6, (nb % 6) // 2, nb % 2
            if kind == 2:
                f = 1.0 if sub == 1 else 0.5
                S.op("dve", lambda e, b2=b2, sub=sub, half=half, f=f: e.tensor_scalar(
                    out=C.gh[:, sub, half * 512:(half + 1) * 512], in0=mrow[b2][:], scalar1=1.0, scalar2=f,
                    op0=ALU.add, op1=ALU.mult), [("mrow", b2)], [("gh", sub, half)])
            else:
                tb = 2 + (nb % 2)
                for j in range(4):
                    S.op("pe", lambda e, j=j, b2=b2, tb=tb: e.transpose(C.ps[tb][:, j * 128:(j + 1) * 128],
                                                                       mrow[b2][:, j * 128:(j + 1) * 128], C.ident[:]),
                         [("mrow", b2), ("ident",)], [("ps", tb)])
                for j in range(4):
                    cidx = half * 4 + j
                    S.op("dve", lambda e, j=j, tb=tb, sub=sub, kind=kind, cidx=cidx: e.tensor_scalar(
                        out=C.modcol[:, sub, kind, cidx:cidx + 1], in0=C.ps[tb][:, j * 128:j * 128 + 1],
                        scalar1=(1.0 if kind == 1 else 0.0), scalar2=None, op0=ALU.add),
                        [("ps", tb)], [("modcol", sub)])
        S.barrier()


def ln_modulate_stage(nc, S, C, sub_next, do_ln, lng_d=None, lnb_d=None, tiles=None):
    es = ExitStack()
    if do_ln:
        C.lng = es.enter_context(nc.sbuf_tensor(_U() + "lng", [128, D], F32))
        C.lnb = es.enter_context(nc.sbuf_tensor(_U() + "lnb", [128, D], F32))
        S.dma("sp", lambda e: e.dma_start(out=C.lng[:], in_=lng_d.to_broadcast([128, D])), (), [("lng",)])
        S.dma("sp", lambda e: e.dma_start(out=C.lnb[:], in_=lnb_d.to_broadcast([128, D])), (), [("lnb",)])
    for t in (tiles if tiles is not None else range(NT)):
        xr = ("x", t)
        if do_ln:
            st = ("stat", t % 2)
            sv = C.stat[:, t % 2, :]
            for h in range(2):
                S.op("dve", lambda e, t=t, h=h, sv=sv: e.bn_stats(out=sv[:, h * 6:(h + 1) * 6], in_=C.x[:, t, h * 512:(h + 1) * 512]),
                     [xr], [st])
            S.op("dve", lambda e, sv=sv: e.bn_aggr(out=sv[:, 12:14], in_=sv[:, 0:12]), [st], [st])
            S.op("act", lambda e, sv=sv: e.activation(out=sv[:, 14:15], in_=sv[:, 13:14], func=AF.Sqrt, bias=C.epsc[:, 0:1], scale=1.0),
                 [st, ("epsc",)], [st])
            S.op("dve", lambda e, sv=sv: e.reciprocal(out=sv[:, 15:16], in_=sv[:, 14:15]), [st], [st])
            S.op("dve", lambda e, t=t, sv=sv: e.tensor_scalar(out=C.x[:, t, :], in0=C.x[:, t, :], scalar1=sv[:, 12:13],
                                                           scalar2=sv[:, 15:16], op0=ALU.subtract, op1=ALU.mult),
                 [xr, st], [xr])
            S.op("dve", lambda e, t=t: e.tensor_tensor(out=C.x[:, t, :], in0=C.x[:, t, :], in1=C.lng[:], op=ALU.mult),
                 [xr, ("lng",)], [xr])
            S.op("dve", lambda e, t=t: e.tensor_tensor(out=C.x[:, t, :], in0=C.x[:, t, :], in1=C.lnb[:], op=ALU.add),
                 [xr, ("lnb",)], [xr])
        if sub_next is not None:
            for half in range(2):
                pb = 2 * (t % 2) + half
                for j in range(4):
                    c = half * 4 + j
                    S.op("pe", lambda e, t=t, c=c, j=j, pb=pb: e.transpose(C.ps[pb][:, j * 128:(j + 1) * 128],
                                                                       C.x[:, t, c * 128:(c + 1) * 128], C.ident[:]),
                         [xr, ("ident",)], [("ps", pb)])
                for j in range(4):
                    c = half * 4 + j
                    S.op("act", lambda e, t=t, c=c, j=j, pb=pb: e.activation(
                        out=C.uT[:, c, 1 + t * 128:1 + (t + 1) * 128], in_=C.ps[pb][:, j * 128:(j + 1) * 128],
                        func=AF.Identity, scale=C.modcol[:, sub_next, 1, c:c + 1], bias=C.modcol[:, sub_next, 0, c:c + 1]),
                        [("ps", pb), ("modcol", sub_next)], [("uT", t)])
    if do_ln:
        S.barrier()
    es.close()


def phase_ffn(nc, S, C, sub, wg_d, wu_d, wd_d):
    with ExitStack() as es:
        wg = [es.enter_context(nc.sbuf_tensor(_U() + "wg%d" % i, [128, KC, 512], BF16)) for i in range(2)]
        wu = [es.enter_context(nc.sbuf_tensor(_U() + "wu%d" % i, [128, KC, 512], BF16)) for i in range(2)]
        wd = [es.enter_context(nc.sbuf_tensor(_U() + "wd%d" % i, [128, 4, D], BF16)) for i in range(2)]
        hT = [es.enter_context(nc.sbuf_tensor(_U() + "hT%d" % i, [128, 4, 512], BF16)) for i in range(2)]
        sg = [es.enter_context(nc.sbuf_tensor(_U() + "sg%d" % i, [128, 512], F32)) for i in range(2)]
        wgv = wg_d.rearrange("(c p) n -> p c n", p=128)
        wuv = wu_d.rearrange("(c p) n -> p c n", p=128)
        wdv = wd_d.rearrange("(c p) n -> p c n", p=128)
        cnt = 0
        ycnt = 0
        for gi, (f0, nf) in enumerate(FFN_GROUPS):
            b = gi % 2
            fs = slice(f0 * 128, (f0 + nf) * 128)
            S.dma("pool", lambda e, b=b, fs=fs, nf=nf: e.dma_start(out=wg[b][:, :, 0:nf * 128], in_=wgv[:, :, fs]), (), [("wg", b)])
            S.dma("pool", lambda e, b=b, fs=fs, nf=nf: e.dma_start(out=wu[b][:, :, 0:nf * 128], in_=wuv[:, :, fs]), (), [("wu", b)])
            S.dma("pool", lambda e, b=b, f0=f0, nf=nf: e.dma_start(out=wd[b][:, 0:nf, :], in_=wdv[:, f0:f0 + nf, :]), (), [("wd", b)])
            for j in range(nf):
                S.op("dve", lambda e, b=b, j=j: e.tensor_tensor(out=wd[b][:, j, :], in0=wd[b][:, j, :], in1=C.gh[:, sub, :], op=ALU.mult),
                     [("wd", b), ("gh", sub, 0), ("gh", sub, 1)], [("wd", b)])
            for tb in range(4):
                hb = tb % 2
                tsl = slice(1 + tb * 512, 1 + (tb + 1) * 512)
                for j in range(nf):
                    pg, pu = (cnt % 2), 2 + (cnt % 2)
                    sgb = cnt % 2
                    cnt += 1
                    ur = [("uT", tb * 4 + q) for q in range(4)]
                    for k in range(KC):
                        S.op("pe", lambda e, k=k, b=b, j=j, pg=pg, tsl=tsl: e.matmul(
                            C.ps[pg][:], wg[b][:, k, j * 128:(j + 1) * 128], C.uT[:, k, tsl], start=(k == 0), stop=(k == KC - 1)),
                            [("wg", b)] + ur, [("ps", pg)])
                    for k in range(KC):
                        S.op("pe", lambda e, k=k, b=b, j=j, pu=pu, tsl=tsl: e.matmul(
                            C.ps[pu][:], wu[b][:, k, j * 128:(j + 1) * 128], C.uT[:, k, tsl], start=(k == 0), stop=(k == KC - 1)),
                            [("wu", b)] + ur, [("ps", pu)])
                    S.op("act", lambda e, pg=pg, sgb=sgb: e.activation(out=sg[sgb][:], in_=C.ps[pg][:], func=AF.Silu),
                         [("ps", pg)], [("sg", sgb)])
                    S.op("dve", lambda e, pu=pu, sgb=sgb, hb=hb, j=j: e.tensor_tensor(
                        out=hT[hb][:, j, :], in0=sg[sgb][:], in1=C.ps[pu][:], op=ALU.mult),
                        [("ps", pu), ("sg", sgb)], [("hT", hb)])
                for tt in range(4):
                    t = tb * 4 + tt
                    for half in range(2):
                        py = 4 + (ycnt % 4)
                        ycnt += 1
                        for j in range(nf):
                            S.op("pe", lambda e, hb=hb, j=j, tt=tt, b=b, half=half, py=py: e.matmul(
                                C.ps[py][:], hT[hb][:, j, tt * 128:(tt + 1) * 128], wd[b][:, j, half * 512:(half + 1) * 512],
                                start=(j == 0), stop=(j == nf - 1)), [("hT", hb), ("wd", b)], [("ps", py)])
                        xs = C.x[:, t, half * 512:(half + 1) * 512]
                        if gi == 0:
                            S.op("dve", lambda e, xs=xs, py=py: e.scalar_tensor_tensor(
                                out=xs, in0=xs, scalar=ALPHA, in1=C.ps[py][:], op0=ALU.mult, op1=ALU.add),
                                [("x", t), ("ps", py)], [("x", t)])
                        else:
                            S.op("dve", lambda e, xs=xs, py=py: e.tensor_tensor(out=xs, in0=xs, in1=C.ps[py][:], op=ALU.add),
                                 [("x", t), ("ps", py)], [("x", t)])
        S.barrier()


NH = 8
SLOPES = [2.0 ** (-8.0 * (h + 1) / NH) for h in range(NH)]
LAMBDA_INIT1 = 0.8 - 0.6 * float(np.exp(-0.3 * 1))
BANDW = 3968
SKIP_T = 105.0


def phase_attn(nc, S, C, uTo_d, wqkv_d, lam_d, subg_d, wout_d):
    scale = 64 ** -0.5
    with ExitStack() as es:
        sb = lambda name, shape, dt: es.enter_context(nc.sbuf_tensor(name, shape, dt))
        uTo = [sb("uTo%d" % i, [128, KC, 512], BF16) for i in range(2)]
        wq = [sb("wqkv%d" % i, [128, KC, 3, 128], BF16) for i in range(2)]
        wo = [sb("wo%d" % i, [128, D], BF16) for i in range(2)]
        qT = sb("qT", [128, TOK], BF16)
        qTm = [sb("qTm%d" % i, [128, TOK], BF16) for i in range(2)]
        kT = sb("kT", [128, 2 * TOK], BF16)
        v = sb("v_h", [128, 32, 132], BF16)
        band = [sb("band%d" % i, [128, BANDW], BF16) for i in range(2)]
        itmp = [sb("itmp%d" % i, [128, 496], F32) for i in range(2)]
        sq = [sb("sq%d" % i, [128, 512], BF16) for i in range(2)]
        onesb = sb("onesb", [128, 128], BF16)
        nstat = sb("nstat", [128, 2, 16], F32)
        nbias = sb("nbias", [128, 2], F32)
        lamt = sb("lamt", [128, 4, 64], F32)
        lam2 = sb("lam2", [128, 2, 64], F32)
        lams = sb("lams", [128, 4], F32)
        subg = sb("subg", [128, 128], F32)
        PT = [sb("PT%d" % i, [128, 512], BF16) for i in range(4)]
        osum = [sb("osum%d" % i, [128, 128], F32) for i in range(4)]
        otmp = [sb("otmp%d" % i, [128, 128], F32) for i in range(2)]
        rs = sb("rs", [128, 16], F32)
        onb = [sb("onb%d" % i, [128, 128], BF16) for i in range(2)]
        oT = sb("oT", [128, TOK], BF16)

        S.op("dve", lambda e: e.memset(onesb[:], 1.0), (), [("onesb",)])
        S.op("pool", lambda e: e.memset(qTm[0][64:128, :], 0.0), (), [("qTz", 0)])
        S.op("pool", lambda e: e.memset(qTm[1][0:64, :], 0.0), (), [("qTz", 1)])
        S.op("dve", lambda e: e.memset(v[:, :, 128:129], 1.0), (), [("vone",)])
        S.dma("sp", lambda e: e.dma_start(out=lamt[:].rearrange("p a b -> p (a b)"),
                                          in_=lam_d.rearrange("(o a) b -> o (a b)", o=1).to_broadcast([128, 256])), (), [("lamt",)])
        S.dma("sp", lambda e: e.dma_start(out=subg[:], in_=subg_d.to_broadcast([128, 128])), (), [("subg",)])
        S.op("dve", lambda e: e.tensor_tensor(out=lam2[:, 0, :], in0=lamt[:, 0, :], in1=lamt[:, 1, :], op=ALU.mult), [("lamt",)], [("lam2",)])
        S.op("dve", lambda e: e.tensor_tensor(out=lam2[:, 1, :], in0=lamt[:, 2, :], in1=lamt[:, 3, :], op=ALU.mult), [("lamt",)], [("lam2",)])
        S.op("dve", lambda e: e.tensor_reduce(out=lams[:, 0:2], in_=lam2[:], axis=AX.X, op=ALU.add), [("lam2",)], [("lams",)])
        S.op("act", lambda e: e.activation(out=lams[:, 0:2], in_=lams[:, 0:2], func=AF.Exp), [("lams",)], [("lams",)])
        S.op("dve", lambda e: e.tensor_tensor(out=lams[:, 2:3], in0=lams[:, 1:2], in1=lams[:, 0:1], op=ALU.subtract), [("lams",)], [("lams",)])
        S.op("dve", lambda e: e.tensor_scalar(out=lams[:, 2:3], in0=lams[:, 2:3], scalar1=-LAMBDA_INIT1, scalar2=None, op0=ALU.add),
             [("lams",)], [("lams",)])
        S.op("dve", lambda e: e.tensor_scalar(out=subg[:], in0=subg[:], scalar1=(1.0 - LAMBDA_INIT1), scalar2=None, op0=ALU.mult),
             [("subg",)], [("subg",)])
        wv_ = wqkv_d.rearrange("(c p) n -> p c n", p=128)
        pcnt = [0]
        scnt = [0]
        ptc = [0]
        uocnt = [0]

        def proj_bank():
            pcnt[0] += 1
            return 6 + (pcnt[0] % 2)

        for h in range(NH):
            wb = h % 2
            for part in range(3):
                cs = slice(part * 1024 + h * 128, part * 1024 + (h + 1) * 128)
                S.dma("pool", lambda e, wb=wb, part=part, cs=cs: e.dma_start(out=wq[wb][:, :, part, :], in_=wv_[:, :, cs]),
                      (), [("wq", wb)])
            S.dma("pool", lambda e, wb=wb, h=h: e.dma_start(out=wo[wb][:], in_=wout_d[h * 128:(h + 1) * 128, :]), (), [("wo", wb)])
            S.op("pool", lambda e, wb=wb: e.tensor_tensor(out=wo[wb][:], in0=wo[wb][:], in1=C.gh[:, 1, :], op=ALU.mult),
                 [("wo", wb), ("gh", 1, 0), ("gh", 1, 1)], [("wo", wb)])
            def proj_block(part, own, tb, srcbuf, col, dst, dr, blk, wb=wb):
                wqb = wq[wb]
                pb = proj_bank()
                for k in range(KC):
                    if own:
                        src = C.uT[:, k, 1 + tb * 512:1 + (tb + 1) * 512]
                        rr = [("uT", tb * 4 + q) for q in range(4)]
                    else:
                        src = uTo[srcbuf][:, k, :]
                        rr = [("uTo", srcbuf)]
                    S.op("pe", lambda e, k=k, pb=pb, src=src: e.matmul(
                        C.ps[pb][:], wqb[:, k, part, :], src, start=(k == 0), stop=(k == KC - 1)),
                        [("wq", wb)] + rr, [("ps", pb)])
                if blk % 2 == 0:
                    S.op("act", lambda e, pb=pb: e.copy(out=dst, in_=C.ps[pb][:]), [("ps", pb)], [dr])
                else:
                    S.op("dve", lambda e, pb=pb: e.tensor_copy(out=dst, in_=C.ps[pb][:]), [("ps", pb)], [dr])
                sb_ = blk % 2
                S.op("pool", lambda e, sb_=sb_: e.tensor_tensor(out=sq[sb_][:], in0=dst, in1=dst, op=ALU.mult), [dr], [("sq", sb_)])
                for m in range(2):
                    pb2 = proj_bank()
                    ms = slice(m * 64, (m + 1) * 64)
                    S.op("pe", lambda e, ms=ms, sb_=sb_, pb2=pb2: e.matmul(C.ps[pb2][:], onesb[ms, :], sq[sb_][ms, :], start=True, stop=True),
                         [("onesb",), ("sq", sb_)], [("ps", pb2)])
                    S.op("dve", lambda e, m=m, pb2=pb2: e.tensor_reduce(out=nstat[:, m, col:col + 1], in_=C.ps[pb2][:], axis=AX.X, op=ALU.max),
                         [("ps", pb2)], [("nstat",)])

            def v_tile(kt, own, srcbuf, off, wb=wb):
                wqb = wq[wb]
                pb = proj_bank()
                for k in range(KC):
                    if own:
                        src = C.uT[:, k, 1 + kt * 128:1 + (kt + 1) * 128]
                        rr = [("uT", kt)]
                    else:
                        src = uTo[srcbuf][:, k, off * 128:(off + 1) * 128]
                        rr = [("uTo", srcbuf)]
                    S.op("pe", lambda e, k=k, pb=pb, src=src: e.matmul(
                        C.ps[pb][:, 0:128], src, wqb[:, k, 2, :], start=(k == 0), stop=(k == KC - 1)),
                        [("wq", wb)] + rr, [("ps", pb)])
                if kt % 2 == 0:
                    S.op("act", lambda e, pb=pb: e.copy(out=v[:, kt, 0:128], in_=C.ps[pb][:, 0:128]), [("ps", pb)], [("v", kt)])
                else:
                    S.op("dve", lambda e, pb=pb: e.tensor_copy(out=v[:, kt, 0:128], in_=C.ps[pb][:, 0:128]), [("ps", pb)], [("v", kt)])

            for tb in range(4):
                proj_block(0, True, tb, None, tb, qT[:, tb * 512:(tb + 1) * 512], ("qT", tb), tb)
            for tb in range(4):
                sl = slice(tb * 512, (tb + 1) * 512)
                S.op("act", lambda e, sl=sl: e.copy(out=qTm[0][0:64, sl], in_=qT[0:64, sl]), [("qT", tb)], [("qTm", 0, tb)])
                S.op("pool", lambda e, sl=sl: e.tensor_copy(out=qTm[1][64:128, sl], in_=qT[64:128, sl]), [("qT", tb)], [("qTm", 1, tb)])
            for tb in range(4):
                proj_block(1, True, tb, None, 4 + tb, kT[:, tb * 512:(tb + 1) * 512], ("kT", tb), 4 + tb)
            for kt in range(16):
                v_tile(kt, True, None, None)
            for tb in range(4):
                ub = uocnt[0] % 2
                uocnt[0] += 1
                S.dma("sp", lambda e, ub=ub, tb=tb: e.dma_start(out=uTo[ub][:], in_=uTo_d[:, :, tb * 512:(tb + 1) * 512]), (), [("uTo", ub)])
                proj_block(1, False, tb, ub, 8 + tb, kT[:, (4 + tb) * 512:(5 + tb) * 512], ("kT", 4 + tb), 8 + tb)
                for off in range(4):
                    v_tile(16 + tb * 4 + off, False, ub, off)
            for m in range(2):
                S.op("dve", lambda e, m=m: e.tensor_reduce(out=nstat[:, m, 12:13], in_=nstat[:, m, 0:4], axis=AX.X, op=ALU.max), [("nstat",)], [("nstat",)])
                S.op("dve", lambda e, m=m: e.tensor_reduce(out=nstat[:, m, 13:14], in_=nstat[:, m, 4:12], axis=AX.X, op=ALU.max), [("nstat",)], [("nstat",)])
                S.op("dve", lambda e, m=m: e.tensor_tensor(out=nstat[:, m, 14:15], in0=nstat[:, m, 12:13], in1=nstat[:, m, 13:14], op=ALU.mult),
                     [("nstat",)], [("nstat",)])
                S.op("act", lambda e, m=m: e.activation(out=nstat[:, m, 15:16], in_=nstat[:, m, 14:15], func=AF.Sqrt, scale=scale * scale),
                     [("nstat",)], [("nstat",)])
                S.op("dve", lambda e, m=m: e.tensor_scalar(out=nbias[:, m:m + 1], in0=nstat[:, m, 15:16], scalar1=-1.0, scalar2=None, op0=ALU.mult),
                     [("nstat",)], [("nbias", m)])
            slope = SLOPES[h]
            for bi in range(2):
                for cchunk in range(8):
                    ib = cchunk % 2
                    c0 = cchunk * 496
                    if bi == 0:
                        S.op("pool", lambda e, ib=ib, c0=c0: e.iota(itmp[ib][:], pattern=[[1, 496]], base=c0 - 1920, channel_multiplier=-1,
                                                                    allow_small_or_imprecise_dtypes=True), (), [("itmp", ib)])
                        S.op("act", lambda e, ib=ib: e.activation(out=itmp[ib][:], in_=itmp[ib][:], func=AF.Abs), [("itmp", ib)], [("itmp", ib)])
                    else:
                        S.op("pool", lambda e, ib=ib, c0=c0: e.iota(itmp[ib][:], pattern=[[-1, 496]], base=4095 - c0, channel_multiplier=-1,
                                                                    allow_small_or_imprecise_dtypes=True), (), [("itmp", ib)])
                    S.op("act", lambda e, ib=ib, bi=bi, c0=c0, slope=slope: e.activation(out=band[bi][:, c0:c0 + 496], in_=itmp[ib][:],
                                                                                 func=AF.Exp, scale=-slope), [("itmp", ib)], [("band", bi)])
            for qb in range(4):
                q0 = qb * 512
                for m in range(2):
                    ms = slice(m * 64, (m + 1) * 64)
                    act_kts = []
                    for kt in range(32):
                        if kt < 16:
                            k0 = kt * 128
                            if k0 > q0 + 511:
                                dmin = k0 - (q0 + 511)
                            elif k0 + 127 < q0:
                                dmin = q0 - (k0 + 127)
                            else:
                                dmin = 0
                        else:
                            k0 = (kt - 16) * 128
                            dmin = 4095 - (q0 + 511) - (k0 + 127)
                        if slope * dmin < SKIP_T:
                            act_kts.append(kt)
                    LAG = 2
                    nact = len(act_kts)
                    pts = {}

                    def emit_pv(ai):
                        kt = act_kts[ai]
                        pt = pts[ai]
                        for qt in range(4):
                            S.op("pe", lambda e, pt=pt, qt=qt, kt=kt, ai=ai: e.matmul(
                                C.ps[2 + qt][:, 0:129], PT[pt][:, qt * 128:(qt + 1) * 128], v[:, kt, 0:129],
                                start=(ai == 0), stop=(ai == nact - 1)), [("PT", pt), ("v", kt), ("vone",)], [("ps", 2 + qt)])

                    for ai, kt in enumerate(act_kts):
                        sbk = scnt[0] % 2
                        scnt[0] += 1
                        pt = ptc[0] % 4
                        ptc[0] += 1
                        pts[ai] = pt
                        S.op("pe", lambda e, m=m, kt=kt, q0=q0, sbk=sbk: e.matmul(
                            C.ps[sbk][:], kT[:, kt * 128:(kt + 1) * 128], qTm[m][:, q0:q0 + 512], start=True, stop=True),
                            [("kT", kt // 4), ("qTm", m, qb), ("qTz", m)], [("ps", sbk)])
                        S.op("act", lambda e, sbk=sbk, pt=pt, m=m: e.activation(out=PT[pt][:], in_=C.ps[sbk][:], func=AF.Exp,
                                                                              bias=nbias[:, m:m + 1], scale=scale),
                             [("ps", sbk), ("nbias", m)], [("PT", pt)])
                        if kt < 16:
                            st = q0 - kt * 128 + 1920
                            bsl = band[0][:, st:st + 512]
                        else:
                            st = q0 + (kt - 16) * 128
                            bsl = band[1][:, st:st + 512]
                        bi = 0 if kt < 16 else 1
                        S.op("dve", lambda e, pt=pt, bsl=bsl: e.tensor_tensor(out=PT[pt][:], in0=PT[pt][:], in1=bsl, op=ALU.mult),
                             [("PT", pt), ("band", bi)], [("PT", pt)])
                        if ai >= LAG:
                            emit_pv(ai - LAG)
                    for ai in range(max(0, nact - LAG), nact):
                        emit_pv(ai)
                    for qt in range(4):
                        col = m * 4 + qt
                        S.op("dve", lambda e, qt=qt, col=col: e.reciprocal(out=rs[:, col:col + 1], in_=C.ps[2 + qt][:, 128:129]),
                             [("ps", 2 + qt)], [("rs", col)])
                        if m == 0:
                            S.op("dve", lambda e, qt=qt, col=col: e.tensor_scalar(out=osum[qt][:], in0=C.ps[2 + qt][:, 0:128],
                                                                               scalar1=rs[:, col:col + 1], scalar2=None, op0=ALU.mult),
                                 [("ps", 2 + qt), ("rs", col)], [("osum", qt)])
                        else:
                            ob = qt % 2
                            S.op("dve", lambda e, qt=qt, col=col, ob=ob: e.tensor_scalar(out=otmp[ob][:], in0=C.ps[2 + qt][:, 0:128],
                                                                                     scalar1=rs[:, col:col + 1], scalar2=lams[:, 2:3],
                                                                                     op0=ALU.mult, op1=ALU.mult),
                                 [("ps", 2 + qt), ("rs", col), ("lams",)], [("otmp", ob)])
                            S.op("pool", lambda e, qt=qt, ob=ob: e.tensor_tensor(out=osum[qt][:], in0=osum[qt][:], in1=otmp[ob][:], op=ALU.add),
                                 [("osum", qt), ("otmp", ob)], [("osum", qt)])
                for qt in range(4):
                    col = 8 + qt
                    ob = qt % 2
                    S.op("act", lambda e, qt=qt, col=col, ob=ob: e.activation(out=otmp[ob][:], in_=osum[qt][:], func=AF.Square, accum_out=rs[:, col:col + 1]),
                         [("osum", qt)], [("otmp", ob), ("rs", col)])
                    S.op("act", lambda e, col=col: e.activation(out=rs[:, col:col + 1], in_=rs[:, col:col + 1], func=AF.Sqrt,
                                                               bias=C.epsc[:, 0:1], scale=1.0 / 128.0), [("rs", col), ("epsc",)], [("rs", col)])
                    S.op("dve", lambda e, col=col: e.reciprocal(out=rs[:, col:col + 1], in_=rs[:, col:col + 1]), [("rs", col)], [("rs", col)])
                    S.op("dve", lambda e, qt=qt, col=col, ob=ob: e.scalar_tensor_tensor(out=onb[ob][:], in0=osum[qt][:], scalar=rs[:, col:col + 1],
                                                                                    in1=subg[:], op0=ALU.mult, op1=ALU.mult),
                         [("osum", qt), ("rs", col), ("subg",)], [("onb", ob)])
                    pb = proj_bank()
                    pbv = C.ps[pb][:].bitcast(BF16)
                    S.op("pe", lambda e, ob=ob, pbv=pbv: e.transpose(pbv[:, 0:128], onb[ob][:], C.identb[:]),
                         [("onb", ob), ("identb",)], [("ps", pb)])
                    tcol = q0 + qt * 128
                    S.op("act", lambda e, pbv=pbv, tcol=tcol: e.copy(out=oT[:, tcol:tcol + 128], in_=pbv[:, 0:128]),
                         [("ps", pb)], [("oT", tcol // 128)])
            for t in range(NT):
                for half in range(2):
                    pb = proj_bank()
                    S.op("pe", lambda e, t=t, half=half, wb=wb, pb=pb: e.matmul(
                        C.ps[pb][:], oT[:, t * 128:(t + 1) * 128], wo[wb][:, half * 512:(half + 1) * 512], start=True, stop=True),
                        [("oT", t), ("wo", wb)], [("ps", pb)])
                    xs = C.x[:, t, half * 512:(half + 1) * 512]
                    if h == 0:
                        S.op("dve", lambda e, xs=xs, pb=pb: e.scalar_tensor_tensor(out=xs, in0=xs, scalar=ALPHA, in1=C.ps[pb][:],
                                                                                 op0=ALU.mult, op1=ALU.add), [("x", t), ("ps", pb)], [("x", t)])
                    else:
                        S.op("dve", lambda e, xs=xs, pb=pb: e.tensor_tensor(out=xs, in0=xs, in1=C.ps[pb][:], op=ALU.add),
                             [("x", t), ("ps", pb)], [("x", t)])
        S.barrier()


def _op(S, eng, f, reads, writes):
    return S.op(eng, f, reads, writes)


def MM(S, out, lhsT, rhs, start, stop, reads, writes):
    S.op("pe", lambda e: e.matmul(out, lhsT, rhs, start=start, stop=stop), reads, writes)


def TT(S, eng, out, in0, in1, op, reads, writes):
    S.op(eng, lambda e: e.tensor_tensor(out=out, in0=in0, in1=in1, op=op), reads, writes)


def TS(S, eng, out, in0, s1, s2, op0, op1, reads, writes):
    if op1 is None:
        S.op(eng, lambda e: e.tensor_scalar(out=out, in0=in0, scalar1=s1, scalar2=None, op0=op0), reads, writes)
    else:
        S.op(eng, lambda e: e.tensor_scalar(out=out, in0=in0, scalar1=s1, scalar2=s2, op0=op0, op1=op1), reads, writes)


def ACTF(S, out, in_, func, reads, writes, bias=None, scale=None, accum_out=None):
    kw = {}
    if bias is not None:
        kw["bias"] = bias
    if scale is not None:
        kw["scale"] = scale
    if accum_out is not None:
        kw["accum_out"] = accum_out
    S.op("act", lambda e: e.activation(out=out, in_=in_, func=func, **kw), reads, writes)


def TR(S, out, in_, ident, reads, writes):
    S.op("pe", lambda e: e.transpose(out, in_, ident), reads, writes)


def phase_ssm_pass(nc, S, C, p, win_d, wdt_d, dtb_d, alog_d, convw_d, convb_d, dsk_d, normg_d, wout_d,
                   y1_d, sin_d, sout_d, scale_x):
    last = (p == 1)
    IDN, TLE, TGE, TGT, TLT = 0, 1, 2, 3, 4
    BT_, CPB, NBLK = 256, 2, 8
    Tm = C.tri[:, TLE if p == 0 else TGE, :]
    Um = C.tri[:, TGT if p == 0 else TLT, :]
    ones_f = None
    with ExitStack() as es:
        sb = lambda name, shape, dt: es.enter_context(nc.sbuf_tensor(_U() + "s%d_%s" % (p, name), shape, dt))
        xbcT = sb("xbcT", [128, 24, BT_], BF16)
        pre = [sb("pre%d" % i, [128, BT_ + 4], F32) for i in range(2)]
        cva = [sb("cva%d" % i, [128, BT_], F32) for i in range(2)]
        wfc = [sb("wfc%d" % i, [128, KC, 128], BF16) for i in range(2)]
        wdt = sb("wdt", [128, KC, 32], BF16)
        convw = sb("convw", [128, 24, 3], F32)
        convb = sb("convb", [128, 24], F32)
        rows = sb("rows", [128, 4, 32], F32)
        onesf = sb("onesf", [128, 128], F32)
        sm = [sb("sm%d" % i, [128, 8, 32], F32) for i in range(2)]
        Rb = sb("Rb", [128, 8, 128], F32)
        LT = [sb("LT%d" % i, [128, 8, 128], BF16) for i in range(2)]
        Gm = [sb("Gm%d" % i, [128, 128], BF16) for i in range(2)]
        Xtok = [sb("Xtok%d" % i, [128, 512], BF16) for i in range(2)]
        Btok = [sb("Btok%d" % i, [128, 128], BF16) for i in range(2)]
        Xt1 = [sb("Xt1%d" % i, [128, 512], BF16) for i in range(2)]
        Xt2 = [sb("Xt2%d" % i, [128, 512], BF16) for i in range(2)]
        Sst = sb("Sst", [128, 4, 512], F32)
        Sb = sb("Sb", [128, 4, 512], BF16)
        ych = sb("ych", [128, 2048], F32)
        t1 = sb("t1", [128, 512], F32)
        if last:
            wz = [sb("wz%d" % i, [128, KC, 128], BF16) for i in range(2)]
            sz = sb("sz", [128, CPB, 2048], BF16)
            yT = sb("yT", [128, 16, BT_], BF16)
            wo = [sb("wo%d" % i, [128, D], BF16) for i in range(2)]
            ngc = sb("ngc", [128, 16], F32)
            gs = sb("gs", [128, 16], F32)
        psb = [C.ps[i][:].bitcast(BF16) for i in range(8)]
        wv_ = win_d.rearrange("(c q) n -> q c n", q=128)

        S.dma("sp", lambda e: e.dma_start(out=convw[:], in_=convw_d), (), [("convw",)])
        S.dma("sp", lambda e: e.dma_start(out=convb[:], in_=convb_d), (), [("convb",)])
        S.dma("pool", lambda e: e.dma_start(out=wdt[:], in_=wdt_d[p].rearrange("(c q) n -> q c n", q=128)), (), [("wdt",)])
        S.dma("sp", lambda e: e.dma_start(out=rows[:, 0, :], in_=dtb_d[p:p + 1, :].to_broadcast([128, 32])), (), [("rows", 0)])
        S.dma("sp", lambda e: e.dma_start(out=rows[:, 1, :], in_=alog_d[p:p + 1, :].to_broadcast([128, 32])), (), [("rows", 1)])
        S.dma("sp", lambda e: e.dma_start(out=rows[:, 2, :], in_=dsk_d.to_broadcast([128, 32])), (), [("rows", 2)])
        ACTF(S, rows[:, 1, :], rows[:, 1, :], AF.Exp, [("rows", 1)], [("rows", 1)])
        TS(S, "dve", rows[:, 1, :], rows[:, 1, :], -1.0, None, ALU.mult, None, [("rows", 1)], [("rows", 1)])
        S.op("dve", lambda e: e.memset(onesf[:], 1.0), (), [("onesf",)])
        if scale_x:
            for t in range(NT):
                TS(S, "dve", C.x[:, t, :], C.x[:, t, :], ALPHA, None, ALU.mult, None, [("x", t)], [("x", t)])
        if p == 0:
            S.op("dve", lambda e: e.memset(Sst[:], 0.0), (), [("Sst", g) for g in range(4)])
            S.op("pool", lambda e: e.memset(Sb[:], 0.0), (), [("Sb", g) for g in range(4)])
        else:
            S.dma("sp", lambda e: e.dma_start(out=Sst[:], in_=sin_d), (), [("Sst", g) for g in range(4)])
            for g in range(4):
                S.op("act", lambda e, g=g: e.copy(out=Sb[:, g, :], in_=Sst[:, g, :]), [("Sst", g)], [("Sb", g)])
            S.dma("sp", lambda e: e.dma_start(out=ngc[:], in_=normg_d), (), [("ngc",)])
        pro_done = set()

        def prologue(c):
            pro_done.add(c)
            cp = c % 2
            smc = sm[cp]
            tcol = 1 + c * 128
            R = lambda i: ("sm", cp, i)
            for k in range(KC):
                MM(S, C.ps[3][:, 0:32], C.uT[:, k, tcol:tcol + 128], wdt[:, k, :], k == 0, k == KC - 1, [("wdt",), ("uT", c)], [("ps", 3)])
            TT(S, "dve", smc[:, 0, :], C.ps[3][:, 0:32], rows[:, 0, :], ALU.add, [("ps", 3), ("rows", 0)], [R(0)])
            ACTF(S, smc[:, 0, :], smc[:, 0, :], AF.Exp, [R(0)], [R(0)])
            ACTF(S, smc[:, 0, :], smc[:, 0, :], AF.Ln, [R(0)], [R(0)], bias=1.0)
            TT(S, "dve", smc[:, 1, :], smc[:, 0, :], rows[:, 1, :], ALU.mult, [R(0), ("rows", 1)], [R(1)])
            MM(S, C.ps[3][:, 32:64], Tm, smc[:, 1, :], True, True, [("tri",), R(1)], [("ps", 3)])
            MM(S, C.ps[3][:, 64:96], onesf[:], smc[:, 1, :], True, True, [("onesf",), R(1)], [("ps", 3)])
            S.op("act", lambda e: e.copy(out=smc[:, 2, :], in_=C.ps[3][:, 32:64]), [("ps", 3)], [R(2)])
            ACTF(S, smc[:, 3, :], C.ps[3][:, 32:64], AF.Exp, [("ps", 3)], [R(3)])
            ACTF(S, smc[:, 5, :], C.ps[3][:, 64:96], AF.Exp, [("ps", 3)], [R(5)])
            TT(S, "dve", smc[:, 6, :], C.ps[3][:, 64:96], smc[:, 2, :], ALU.subtract, [("ps", 3), R(2)], [R(6)])
            ACTF(S, smc[:, 6, :], smc[:, 6, :], AF.Exp, [R(6)], [R(6)])
            TT(S, "dve", smc[:, 4, :], smc[:, 6, :], smc[:, 0, :], ALU.mult, [R(6), R(0)], [R(4)])

        blocks = range(NBLK) if p == 0 else range(NBLK - 1, -1, -1)
        wcnt = [0]
        for blk in blocks:
            c0 = blk * BT_
            for fc in range(24):
                wb = wcnt[0] % 2
                pa = wcnt[0] % 2
                wcnt[0] += 1
                cs = slice(2048 + fc * 128, 2048 + (fc + 1) * 128)
                S.dma("pool", lambda e, wb=wb, cs=cs: e.dma_start(out=wfc[wb][:], in_=wv_[:, :, cs]), (), [("wfc", wb)])
                ur = [("uT", min(max(t, 0), NT - 1)) for t in range(blk * CPB - 1, blk * CPB + CPB + 1)] + [("uThalo",)]
                for k in range(KC):
                    MM(S, C.ps[pa][:, 0:BT_], wfc[wb][:, k, :], C.uT[:, k, c0:c0 + BT_], k == 0, k == KC - 1, [("wfc", wb)] + ur, [("ps", pa)])
                for k in range(KC):
                    MM(S, C.ps[2][:, pa * 8:pa * 8 + 2], wfc[wb][:, k, :], C.uT[:, k, c0 + BT_:c0 + BT_ + 2], k == 0, k == KC - 1,
                       [("wfc", wb)] + ur, [("ps", 2)])
                S.op("act", lambda e, pa=pa: e.copy(out=pre[pa][:, 0:BT_], in_=C.ps[pa][:, 0:BT_]), [("ps", pa)], [("pre", pa)])
                S.op("dve", lambda e, pa=pa: e.tensor_copy(out=pre[pa][:, BT_:BT_ + 2], in_=C.ps[2][:, pa * 8:pa * 8 + 2]), [("ps", 2)], [("pre", pa)])
                TS(S, "dve", cva[pa][:], pre[pa][:, 0:BT_], convw[:, fc, 0:1], None, ALU.mult, None, [("pre", pa), ("convw",)], [("cva", pa)])
                S.op("dve", lambda e, pa=pa, fc=fc: e.scalar_tensor_tensor(out=cva[pa][:], in0=pre[pa][:, 1:BT_ + 1], scalar=convw[:, fc, 1:2],
                                                                         in1=cva[pa][:], op0=ALU.mult, op1=ALU.add),
                     [("pre", pa), ("cva", pa), ("convw",)], [("cva", pa)])
                S.op("dve", lambda e, pa=pa, fc=fc: e.scalar_tensor_tensor(out=cva[pa][:], in0=pre[pa][:, 2:BT_ + 2], scalar=convw[:, fc, 2:3],
                                                                         in1=cva[pa][:], op0=ALU.mult, op1=ALU.add),
                     [("pre", pa), ("cva", pa), ("convw",)], [("cva", pa)])
                ACTF(S, xbcT[:, fc, :], cva[pa][:], AF.Silu, [("cva", pa), ("convb",)], [("xbcT", fc)], bias=convb[:, fc:fc + 1])
            if last:
                for zc in range(16):
                    zb = zc % 2
                    cs = slice(zc * 128, (zc + 1) * 128)
                    S.dma("pool", lambda e, zb=zb, cs=cs: e.dma_start(out=wz[zb][:], in_=wv_[:, :, cs]), (), [("wz", zb)])
                    for cc in range(CPB):
                        tcol = 1 + c0 + cc * 128
                        for k in range(KC):
                            MM(S, C.ps[7][:, 0:128], C.uT[:, k, tcol:tcol + 128], wz[zb][:, k, :], k == 0, k == KC - 1,
                               [("wz", zb), ("uT", blk * CPB + cc)], [("ps", 7)])
                        ACTF(S, sz[:, cc, zc * 128:(zc + 1) * 128], C.ps[7][:, 0:128], AF.Silu, [("ps", 7)], [("sz", cc)])
            chunks = list(range(CPB)) if p == 0 else list(range(CPB - 1, -1, -1))
            for cc in chunks:
                c = blk * CPB + cc
                off = cc * 128
                if c not in pro_done:
                    prologue(c)
                nxt = c + 1 if p == 0 else c - 1
                cp = c % 2
                smc = sm[cp]
                if last:
                    S.dma("sp", lambda e, c=c: e.dma_start(out=ych[:], in_=y1_d[c]), (), [("ych", g) for g in range(4)])

                def stageA(g, off=off, smc=smc, cp=cp):
                    g2 = g % 2
                    hs = slice(8 * g, 8 * g + 8)
                    BT = xbcT[:, 16 + g, off:off + 128]
                    CT = xbcT[:, 20 + g, off:off + 128]
                    TT(S, "dve", Rb[:], Tm.unsqueeze(1).to_broadcast([128, 8, 128]), smc[:, 1, hs].unsqueeze(2).to_broadcast([128, 8, 128]),
                       ALU.mult, [("tri",), ("sm", cp, 1)], [("Rb",)])
                    for j in range(2):
                        MM(S, C.ps[4 + j][:], Um, Rb[:, 4 * j:4 * j + 4, :].rearrange("q a b -> q (a b)"), True, True, [("tri",), ("Rb",)], [("ps", 4 + j)])
                        ACTF(S, LT[g2][:, 4 * j:4 * j + 4, :].rearrange("q a b -> q (a b)"), C.ps[4 + j][:], AF.Exp, [("ps", 4 + j)], [("LT", g2)])
                    MM(S, C.ps[6][:, 0:128], BT, CT, True, True, [("xbcT", 16 + g), ("xbcT", 20 + g)], [("ps", 6)])
                    TT(S, "dve", Gm[g2][:], C.ps[6][:, 0:128], Tm, ALU.mult, [("ps", 6), ("tri",)], [("Gm", g2)])
                    TT(S, "dve", LT[g2][:], LT[g2][:], Gm[g2][:].unsqueeze(1).to_broadcast([128, 8, 128]), ALU.mult, [("LT", g2), ("Gm", g2)], [("LT", g2)])
                    for j in range(4):
                        TR(S, psb[7][:, j * 128:(j + 1) * 128], xbcT[:, 4 * g + j, off:off + 128], C.identb[:], [("xbcT", 4 * g + j), ("identb",)], [("ps", 7)])
                    TR(S, psb[7][:, 512:640], xbcT[:, 16 + g, off:off + 128], C.identb[:], [("xbcT", 16 + g), ("identb",)], [("ps", 7)])
                    S.op("act", lambda e: e.copy(out=Xtok[g2][:], in_=psb[7][:, 0:512]), [("ps", 7)], [("Xtok", g2)])
                    S.op("act", lambda e: e.copy(out=Btok[g2][:], in_=psb[7][:, 512:640]), [("ps", 7)], [("Btok", g2)])
                    X3 = Xtok[g2][:].rearrange("q (a b) -> q a b", a=8)
                    TT(S, "dve", Xt1[g2][:].rearrange("q (a b) -> q a b", a=8), X3, smc[:, 0, hs].unsqueeze(2).to_broadcast([128, 8, 64]), ALU.mult,
                       [("Xtok", g2), ("sm", cp, 0)], [("Xt1", g2)])
                    TT(S, "dve", Xt2[g2][:].rearrange("q (a b) -> q a b", a=8), X3, smc[:, 4, hs].unsqueeze(2).to_broadcast([128, 8, 64]), ALU.mult,
                       [("Xtok", g2), ("sm", cp, 4)], [("Xt2", g2)])

                def stageB(g, off=off, smc=smc, cp=cp):
                    g2 = g % 2
                    hs = slice(8 * g, 8 * g + 8)
                    CT = xbcT[:, 20 + g, off:off + 128]
                    X3 = Xtok[g2][:].rearrange("q (a b) -> q a b", a=8)
                    for hh in range(8):
                        MM(S, C.ps[0][:, hh * 64:(hh + 1) * 64], LT[g2][:, hh, :], Xt1[g2][:, hh * 64:(hh + 1) * 64], True, True,
                           [("LT", g2), ("Xt1", g2)], [("ps", 0)])
                    MM(S, C.ps[1][:], CT, Sb[:, g, :], True, True, [("xbcT", 20 + g), ("Sb", g)], [("ps", 1)])
                    MM(S, C.ps[2][:], Btok[g2][:], Xt2[g2][:], True, True, [("Btok", g2), ("Xt2", g2)], [("ps", 2)])
                    TT(S, "dve", t1[:].rearrange("q (a b) -> q a b", a=8), C.ps[1][:].rearrange("q (a b) -> q a b", a=8),
                       smc[:, 3, hs].unsqueeze(2).to_broadcast([128, 8, 64]), ALU.mult, [("ps", 1), ("sm", cp, 3)], [("t1",)])
                    yg = ych[:, g * 512:(g + 1) * 512]
                    if not last:
                        TT(S, "dve", yg, t1[:], C.ps[0][:], ALU.add, [("t1",), ("ps", 0)], [("ych", g)])
                        TT(S, "pool", t1[:].rearrange("q (a b) -> q a b", a=8), X3, rows[:, 2, hs].unsqueeze(2).to_broadcast([128, 8, 64]), ALU.mult,
                           [("Xtok", g2), ("rows", 2), ("t1",)], [("t1",)])
                        TT(S, "pool", yg, yg, t1[:], ALU.add, [("t1",), ("ych", g)], [("ych", g)])
                    else:
                        TT(S, "dve", t1[:], t1[:], C.ps[0][:], ALU.add, [("t1",), ("ps", 0)], [("t1",)])
                        TT(S, "pool", yg, yg, t1[:], ALU.add, [("t1",), ("ych", g)], [("ych", g)])
                    TT(S, "dve", Sst[:, g, :].rearrange("q (a b) -> q a b", a=8), Sst[:, g, :].rearrange("q (a b) -> q a b", a=8),
                       smc[:, 5, hs].unsqueeze(2).to_broadcast([128, 8, 64]), ALU.mult, [("Sst", g), ("sm", cp, 5)], [("Sst", g)])
                    TT(S, "dve", Sst[:, g, :], Sst[:, g, :], C.ps[2][:], ALU.add, [("Sst", g), ("ps", 2)], [("Sst", g)])
                    S.op("act", lambda e, g=g: e.copy(out=Sb[:, g, :], in_=Sst[:, g, :]), [("Sst", g)], [("Sb", g)])

                stageA(0)
                stageA(1)
                if 0 <= nxt < NT:
                    prologue(nxt)
                stageB(0)
                stageA(2)
                stageB(1)
                stageA(3)
                stageB(2)
                stageB(3)
                if not last:
                    S.dma("sp", lambda e, c=c: e.dma_start(out=y1_d[c], in_=ych[:]), [("ych", g) for g in range(4)], [("y1d", c)])
                else:
                    TT(S, "dve", ych[:], ych[:], sz[:, cc, :], ALU.mult, [("ych", g) for g in range(4)] + [("sz", cc)], [("ych", g) for g in range(4)])
                    for g in range(4):
                        ACTF(S, t1[:], ych[:, g * 512:(g + 1) * 512], AF.Square, [("ych", g)], [("t1",), ("gs", g)], accum_out=gs[:, g:g + 1])
                        ACTF(S, gs[:, g:g + 1], gs[:, g:g + 1], AF.Sqrt, [("gs", g), ("epsc",)], [("gs", g)], bias=C.epsc[:, 0:1], scale=1.0 / 512.0)
                        S.op("dve", lambda e, g=g: e.reciprocal(out=gs[:, g:g + 1], in_=gs[:, g:g + 1]), [("gs", g)], [("gs", g)])
                        TS(S, "dve", sz[:, cc, g * 512:(g + 1) * 512], ych[:, g * 512:(g + 1) * 512], gs[:, g:g + 1], None, ALU.mult, None,
                           [("ych", g), ("gs", g)], [("sz", cc)])
                    for q4 in range(4):
                        for j in range(4):
                            ch = q4 * 4 + j
                            TR(S, psb[6][:, j * 128:(j + 1) * 128], sz[:, cc, ch * 128:(ch + 1) * 128], C.identb[:], [("sz", cc), ("identb",)], [("ps", 6)])
                        S.op("act", lambda e, q4=q4, off=off: e.copy(out=yT[:, 4 * q4:4 * q4 + 4, off:off + 128],
                                                                    in_=psb[6][:, 0:512].rearrange("q (a b) -> q a b", a=4)),
                             [("ps", 6)], [("yT", cc)])
            if last:
                for ch in range(16):
                    wb = ch % 2
                    S.dma("pool", lambda e, wb=wb, ch=ch: e.dma_start(out=wo[wb][:], in_=wout_d[ch * 128:(ch + 1) * 128, :]), (), [("wo", wb)])
                    TS(S, "dve", wo[wb][:], wo[wb][:], ngc[:, ch:ch + 1], None, ALU.mult, None, [("wo", wb), ("ngc",)], [("wo", wb)])
                    TT(S, "dve", wo[wb][:], wo[wb][:], C.gh[:, 1, :], ALU.mult, [("wo", wb), ("gh", 1, 0), ("gh", 1, 1)], [("wo", wb)])
                    for cc in range(CPB):
                        t = blk * CPB + cc
                        for half in range(2):
                            pb = 4 + ((cc * 2 + half) % 2)
                            MM(S, C.ps[pb][:], yT[:, ch, cc * 128:(cc + 1) * 128], wo[wb][:, half * 512:(half + 1) * 512], True, True,
                               [("yT", cc), ("wo", wb)], [("ps", pb)])
                            xs = C.x[:, t, half * 512:(half + 1) * 512]
                            TT(S, "dve", xs, xs, C.ps[pb][:], ALU.add, [("x", t), ("ps", pb)], [("x", t)])
        if not last:
            S.dma("sp", lambda e: e.dma_start(out=sout_d, in_=Sst[:]), [("Sst", g) for g in range(4)], [("soutd",)])
        S.barrier()


def _dram_in(nc, name, shape, dt=F32):
    return nc.dram_tensor("d_" + name, list(shape), dt, kind="ExternalInput").ap()


def _dram_out(nc, name, shape, dt=F32):
    return nc.dram_tensor("d_" + name, list(shape), dt, kind="ExternalOutput").ap()


def build_stage(stage):
    nc = bass.Bass("TRN2", target_bir_lowering=False)
    S = Sched()
    C = Ctx()
    I = lambda name, shape, dt=F32: _dram_in(nc, name, shape, dt)
    x_d = I("x_in", [TOK, D])
    ccol_d = I("ccol", [128, KC])
    cpack_d = I("cpack", [128, 5, 128])
    lay = [0] if stage in (0, 1) else ([0, 1] if stage == 2 else [1])
    ada_w = {i: I("ada_w%d" % i, [D, 9216]) for i in lay}
    ada_b = {i: I("ada_b%d" % i, [1, 9216]) for i in lay}
    lng = {i: I("lng%d" % i, [3, D]) for i in lay}
    lnb = {i: I("lnb%d" % i, [3, D]) for i in lay}
    ffn_keys = {0: [(0, 0)], 1: [], 2: [(0, 1), (1, 0)], 3: [(1, 1)]}[stage]
    ffw = {}
    for (i, j) in ffn_keys:
        ffw[(i, j)] = (I("wg%d%d" % (i, j), [D, DFF]), I("wu%d%d" % (i, j), [D, DFF]), I("wd%d%d" % (i, j), [DFF, D]))
    if stage in (1, 2):
        w_in = I("w_in", [D, 5184])
        wdt = I("wdt", [2, D, 32])
        dtb = I("dtb", [2, 32])
        alog = I("alog", [2, 32])
        convw = I("convw", [128, 24, 3])
        convb = I("convb", [128, 24])
        dsk = I("dsk", [1, 32])
        normg = I("normg", [128, 16])
        ssm_wout = I("ssm_wout", [2048, D])
        xh = I("xh", [128, KC])
    if stage == 1:
        y1 = _dram_out(nc, "y1", [16, 128, 2048])
        s_out = _dram_out(nc, "s_out", [128, 4, 512])
        s_in = None
    if stage == 2:
        y1 = I("y1", [16, 128, 2048])
        s_in = I("s_in", [128, 4, 512])
        s_out = None
        uT_out = _dram_out(nc, "uT_out", [128, KC, TOK], BF16)
    if stage == 3:
        uTo = I("uTo", [128, KC, TOK], BF16)
        wqkv = I("wqkv", [D, 3072])
        lam = I("lam", [4, 64])
        subg = I("subg", [1, 128])
        attn_wout = I("attn_wout", [D, D])
    if stage != 1:
        y_d = _dram_out(nc, "y_out", [TOK, D])

    with ExitStack() as es:
        alloc_globals(nc, es, C)
        load_consts(nc, S, C, cpack_d)
        S.op("dve", lambda e: e.memset(C.uT[:, :, 0:1], 0.0), (), [("uT0",)])
        xv = x_d.rearrange("(t q) d -> q t d", q=128)
        for t in range(NT):
            S.dma("sp", lambda e, t=t: e.dma_start(out=C.x[:, t, :], in_=xv[:, t, :]), (), [("x", t)])

        def ffn(i, j):
            wg, wu, wd = ffw[(i, j)]
            phase_ffn(nc, S, C, 2 * j, wg, wu, wd)

        def halo():
            with ExitStack() as hs:
                xht = hs.enter_context(nc.sbuf_tensor(_U() + "xht", [128, KC], F32))
                S.dma("sp", lambda e: e.dma_start(out=xht[:], in_=xh), (), [("xht",)])
                TT(S, "dve", xht[:], xht[:], C.modcol[:, 1, 1, :], ALU.mult, [("xht",), ("modcol", 1)], [("xht",)])
                TT(S, "dve", C.uT[:, :, TOK + 1:TOK + 2], xht[:].unsqueeze(2), C.modcol[:, 1, 0, :].unsqueeze(2), ALU.add,
                   [("xht",), ("modcol", 1)], [("uThalo",)])
                S.barrier()

        if stage == 0:
            phase_mods(nc, S, C, ccol_d, ada_w[0], ada_b[0])
            ln_modulate_stage(nc, S, C, 0, False)
            S.barrier()
            ffn(0, 0)
            ln_modulate_stage(nc, S, C, None, True, lng[0][0:1, :], lnb[0][0:1, :])
        elif stage == 1:
            phase_mods(nc, S, C, ccol_d, ada_w[0], ada_b[0])
            ln_modulate_stage(nc, S, C, 1, False)
            halo()
            phase_ssm_pass(nc, S, C, 0, w_in, wdt, dtb, alog, convw, convb, dsk, normg, ssm_wout, y1, None, s_out, False)
        elif stage == 2:
            phase_mods(nc, S, C, ccol_d, ada_w[0], ada_b[0])
            ln_modulate_stage(nc, S, C, 1, False)
            halo()
            phase_ssm_pass(nc, S, C, 1, w_in, wdt, dtb, alog, convw, convb, dsk, normg, ssm_wout, y1, s_in, None, True)
            ln_modulate_stage(nc, S, C, 2, True, lng[0][1:2, :], lnb[0][1:2, :])
            ffn(0, 1)
            ln_modulate_stage(nc, S, C, None, True, lng[0][2:3, :], lnb[0][2:3, :])
            phase_mods(nc, S, C, ccol_d, ada_w[1], ada_b[1])
            ln_modulate_stage(nc, S, C, 0, False)
            S.barrier()
            ffn(1, 0)
            ln_modulate_stage(nc, S, C, 1, True, lng[1][0:1, :], lnb[1][0:1, :])
            S.dma("sp", lambda e: e.dma_start(out=uT_out, in_=C.uT[:, :, 1:1 + TOK]), [("uT", t) for t in range(NT)], [("uTout",)])
        else:
            phase_mods(nc, S, C, ccol_d, ada_w[1], ada_b[1])
            ln_modulate_stage(nc, S, C, 1, False)
            S.barrier()
            phase_attn(nc, S, C, uTo, wqkv, lam, subg, attn_wout)
            ln_modulate_stage(nc, S, C, 2, True, lng[1][1:2, :], lnb[1][1:2, :])
            ffn(1, 1)
            ln_modulate_stage(nc, S, C, None, True, lng[1][2:3, :], lnb[1][2:3, :])
        S.barrier()
        if stage != 1:
            yv = y_d.rearrange("(t q) d -> q t d", q=128)
            for t in range(NT):
                S.dma("sp", lambda e, t=t: e.dma_start(out=yv[:, t, :], in_=C.x[:, t, :]), [("x", t)], [("yout", t)])
        S.barrier()
        S.emit(nc, es)
    return nc


def _cpack():
    r = np.arange(128)[:, None]
    c = np.arange(128)[None, :]
    mats = [r == c, r <= c, r >= c, r > c, r < c]
    return np.ascontiguousarray(np.stack([m.astype(np.float32) for m in mats], axis=1))


def _col(v):
    v = np.asarray(v)
    return np.ascontiguousarray(v.reshape(-1, 128).T)


_PROGS = {}


def _prog(stage):
    if stage not in _PROGS:
        _PROGS[stage] = build_stage(stage)
    return _PROGS[stage]


DEBUG_STOP = None


def kernel(x, c, ada_w, ada_b, ln_g, ln_b, ffn_w_gate, ffn_w_up, ffn_w_down,
           ssm_w_in, ssm_conv_w, ssm_conv_b, ssm_dt_bias, ssm_a_log, ssm_d, ssm_norm_g, ssm_w_out,
           attn_w_qkv, attn_lambda, attn_subln_g, attn_w_out):
    f = lambda a: np.ascontiguousarray(np.asarray(a, dtype=np.float32))
    x, c = f(x), f(c)
    ada_w, ada_b, ln_g, ln_b = f(ada_w), f(ada_b), f(ln_g), f(ln_b)
    wgate, wup, wdown = f(ffn_w_gate), f(ffn_w_up), f(ffn_w_down)
    w_in, conv_w, conv_b = f(ssm_w_in)[0], f(ssm_conv_w)[0], f(ssm_conv_b)[0]
    dt_bias, a_log, dsk, norm_g, ssm_wout = f(ssm_dt_bias)[0], f(ssm_a_log)[0], f(ssm_d), f(ssm_norm_g)[0], f(ssm_w_out)[0]
    wqkv, lam, subg, attn_wout = f(attn_w_qkv)[0], f(attn_lambda)[0], f(attn_subln_g), f(attn_w_out)[0]
    cores = list(range(8))
    cpack = _cpack()

    def local(arr, core):
        b, h = core // 2, core % 2
        a = arr[b, h * TOK:(h + 1) * TOK]
        return np.ascontiguousarray(a[::-1] if h else a)

    def common(core, lays):
        b = core // 2
        m = {"d_ccol": _col(c[b]), "d_cpack": cpack}
        for i in lays:
            m["d_ada_w%d" % i] = ada_w[i]
            m["d_ada_b%d" % i] = ada_b[i:i + 1]
            m["d_lng%d" % i] = ln_g[i]
            m["d_lnb%d" % i] = ln_b[i]
        return m

    def ffn_in(m, keys):
        for (i, j) in keys:
            m["d_wg%d%d" % (i, j)] = wgate[i, j]
            m["d_wu%d%d" % (i, j)] = wup[i, j]
            m["d_wd%d%d" % (i, j)] = wdown[i, j]

    def ssm_in(m, core, x1loc):
        h = core % 2
        sets = [0, 1] if h == 0 else [1, 0]
        m["d_w_in"] = w_in
        m["d_wdt"] = np.ascontiguousarray(np.stack([w_in[:, 5120 + 32 * s:5152 + 32 * s] for s in sets]))
        m["d_dtb"] = np.ascontiguousarray(np.stack([dt_bias[s] for s in sets]))
        m["d_alog"] = np.ascontiguousarray(np.stack([a_log[s] for s in sets]))
        cw = conv_w if h == 0 else conv_w[::-1]
        m["d_convw"] = np.ascontiguousarray(cw.reshape(3, 24, 128).transpose(2, 1, 0))
        m["d_convb"] = _col(conv_b)
        m["d_dsk"] = dsk
        m["d_normg"] = _col(norm_g)
        m["d_ssm_wout"] = ssm_wout
        m["d_xh"] = _col(x1loc[core ^ 1][TOK - 1])

    maps = []
    for core in cores:
        m = common(core, [0])
        m["d_x_in"] = local(x, core)
        ffn_in(m, [(0, 0)])
        maps.append(m)
    r0 = run_bass_kernel_spmd(_prog(0), maps, core_ids=cores)
    x1 = [np.asarray(r0.results[k]["d_y_out"]) for k in cores]
    if DEBUG_STOP == 0:
        return x1
    maps = []
    for core in cores:
        m = common(core, [0])
        m["d_x_in"] = x1[core]
        ssm_in(m, core, x1)
        maps.append(m)
    r1 = run_bass_kernel_spmd(_prog(1), maps, core_ids=cores)
    y1 = [np.asarray(r1.results[k]["d_y1"]) for k in cores]
    so = [np.asarray(r1.results[k]["d_s_out"]) for k in cores]
    maps = []
    for core in cores:
        m = common(core, [0, 1])
        m["d_x_in"] = x1[core]
        ssm_in(m, core, x1)
        m["d_y1"] = y1[core]
        m["d_s_in"] = so[core ^ 1]
        ffn_in(m, [(0, 1), (1, 0)])
        maps.append(m)
    r2 = run_bass_kernel_spmd(_prog(2), maps, core_ids=cores)
    x4 = [np.asarray(r2.results[k]["d_y_out"]) for k in cores]
    uT4 = [np.asarray(r2.results[k]["d_uT_out"]) for k in cores]
    if DEBUG_STOP == 2:
        return x4
    maps = []
    for core in cores:
        m = common(core, [1])
        m["d_x_in"] = x4[core]
        m["d_uTo"] = uT4[core ^ 1]
        m["d_wqkv"] = wqkv
        m["d_lam"] = lam
        m["d_subg"] = subg
        m["d_attn_wout"] = attn_wout
        ffn_in(m, [(1, 1)])
        maps.append(m)
    r3 = run_bass_kernel_spmd(_prog(3), maps, core_ids=cores)
    out = np.empty((4, 2 * TOK, D), dtype=np.float32)
    for core in cores:
        b, h = core // 2, core % 2
        y = np.asarray(r3.results[core]["d_y_out"])
        out[b, h * TOK:(h + 1) * TOK] = y[::-1] if h else y
    return out
```
